# Optimizing a Trainium2 kernel written in Bass

```python
import math
import jax, jax.numpy as jnp
from jax import lax
import numpy as np

D_MODEL = 1024
BATCH = 4
SEQ = 4096
DEPTH = 2

HEAD_DIM = 64
HA = 8
DIL_PATTERNS = ((128, 1), (512, 4), (2048, 16))
HB = 4
C_CONV = 512
CONV_K = 31
D_FF = 2816
N_BUCKETS = 32
REL_MAX_DIST = 2048
BLK = 128
MIX_W = 512
A_W = HA * HEAD_DIM
B_W = HB * 2 * HEAD_DIM
IN_W = 3 * A_W + 3 * B_W + 2 * C_CONV
N_BRANCH = 3
EPS = 1e-6
NEG = -1e30

kernel_name = "hybrid_gated_dilated_diff_conformer_block"


def rmsnorm(x, g):
    xf = x.astype(jnp.float32)
    y = xf * lax.rsqrt(jnp.mean(xf * xf, axis=-1, keepdims=True) + EPS)
    return (y * g.astype(jnp.float32)).astype(x.dtype)


def layernorm(x, g, b):
    xf = x.astype(jnp.float32)
    mu = jnp.mean(xf, axis=-1, keepdims=True)
    var = jnp.mean(jnp.square(xf - mu), axis=-1, keepdims=True)
    y = (xf - mu) * lax.rsqrt(var + EPS) * g.astype(jnp.float32) + b.astype(jnp.float32)
    return y.astype(x.dtype)


def modulate(h, shift, scale):
    return h * (1.0 + scale[:, None, :]) + shift[:, None, :]


def swiglu(h, w_up, w_down):
    gate, up = jnp.split(h @ w_up, 2, axis=-1)
    return (jax.nn.silu(gate) * up) @ w_down


def t5_bucket(dist):
    max_exact = N_BUCKETS // 2
    d = jnp.maximum(dist.astype(jnp.float32), 1.0)
    large = max_exact + (jnp.log(d / max_exact) / math.log(REL_MAX_DIST / max_exact)
                         * (N_BUCKETS - max_exact)).astype(jnp.int32)
    large = jnp.minimum(large, N_BUCKETS - 1)
    return jnp.where(dist < max_exact, dist, large)


def dilated_window_attn(q, k, v, window, dilation, bias_table):
    b, s, h, hd = q.shape
    steps = window // dilation
    L = s // dilation
    nb = -(-L // BLK)
    Lp = nb * BLK

    def to_sub(t):
        t = t.reshape(b, L, dilation, h, hd).transpose(0, 2, 3, 1, 4)
        t = jnp.pad(t, ((0, 0), (0, 0), (0, 0), (0, Lp - L), (0, 0)))
        return t.reshape(b, dilation, h, nb, BLK, hd)

    def with_prev(t):
        prev = jnp.pad(t[:, :, :, :-1], ((0, 0), (0, 0), (0, 0), (1, 0), (0, 0), (0, 0)))
        return jnp.concatenate([prev, t], axis=-2)

    qs = to_sub(q)
    kw = with_prev(to_sub(k))
    vw = with_prev(to_sub(v))
    logits = jnp.einsum('bdhnqc,bdhnkc->bdhnqk', qs, kw).astype(jnp.float32) / math.sqrt(hd)
    i = jnp.arange(BLK)[:, None]
    j = jnp.arange(2 * BLK)[None, :]
    rel = i + BLK - j
    valid = ((rel >= 0) & (rel <= steps))[None] & ((jnp.arange(nb)[:, None, None] > 0) | (j >= BLK)[None])
    bias = jnp.moveaxis(bias_table[t5_bucket(jnp.maximum(rel, 0) * dilation)], -1, 0)
    logits = jnp.where(valid, logits + bias[:, None].astype(jnp.float32), NEG)
    m = jnp.max(logits, axis=-1, keepdims=True)
    p = jnp.exp(logits - m)
    den = jnp.sum(p, axis=-1, keepdims=True)
    o = jnp.einsum('bdhnqk,bdhnkc->bdhnqc', p.astype(vw.dtype), vw).astype(jnp.float32) / den
    lse = (m + jnp.log(den))[..., 0]
    o = o.reshape(b, dilation, h, Lp, hd)[:, :, :, :L].transpose(0, 3, 1, 2, 4).reshape(b, s, h, hd)
    lse = lse.reshape(b, dilation, h, Lp)[:, :, :, :L].transpose(0, 3, 1, 2).reshape(b, s, h)
    return o, lse


def dilated_mixer(q, k, v, bias_table):
    outs, lses = [], []
    for window, dilation in DIL_PATTERNS:
        o, lse = dilated_window_attn(q, k, v, window, dilation, bias_table)
        outs.append(o)
        lses.append(lse)
    w = jax.nn.softmax(jnp.stack(lses), axis=0)
    o = jnp.sum(w[..., None] * jnp.stack(outs), axis=0)
    return o.astype(q.dtype)


def diff_attention(q, k, v, lam, bias_table):
    b, s, h, _, hd = q.shape
    nb = s // BLK
    qb = q.reshape(b, nb, BLK, h, 2, hd).transpose(1, 0, 3, 4, 2, 5)
    kt = k.transpose(0, 2, 3, 1, 4)
    vt = v.transpose(0, 2, 1, 3)
    kpos = jnp.arange(s)
    scale = 1.0 / math.sqrt(hd)

    def block(args):
        n, qblk = args
        qpos = n * BLK + jnp.arange(BLK)
        dist = qpos[:, None] - kpos[None, :]
        logits = jnp.einsum('bhcqd,bhckd->bhcqk', qblk, kt).astype(jnp.float32) * scale
        bias = jnp.moveaxis(bias_table[t5_bucket(jnp.maximum(dist, 0))], -1, 0)
        logits = jnp.where(dist >= 0, logits + bias[None, :, None].astype(jnp.float32), NEG)
        p = jax.nn.softmax(logits, axis=-1)
        a = p[:, :, 0] - lam * p[:, :, 1]
        return jnp.einsum('bhqk,bhkd->bhqd', a.astype(vt.dtype), vt)

    out = lax.map(block, (jnp.arange(nb), qb))
    return out.transpose(1, 0, 3, 2, 4).reshape(b, s, h, 2 * hd)


def conv_module(u, conv_w, conv_b, ln_g, ln_b):
    u1, u2 = jnp.split(u, 2, axis=-1)
    u = u1 * jax.nn.sigmoid(u2)
    y = lax.conv_general_dilated(u, conv_w[:, None, :].astype(u.dtype), window_strides=(1,),
                                 padding=[(CONV_K - 1, 0)],
                                 dimension_numbers=('NWC', 'WIO', 'NWC'),
                                 feature_group_count=u.shape[-1]) + conv_b
    return jax.nn.silu(layernorm(y, ln_g, ln_b))


def token_mixer(h, l, rel_bias, w_in, qk_gain, lambda_vec, subln_g, conv_w, conv_b,
                conv_ln_g, conv_ln_b, w_branch, w_gate, b_gate, w_out):
    b, s, d = h.shape
    proj = h @ w_in
    qa, ka, va, qb, kb, vb, u = jnp.split(
        proj, [A_W, 2 * A_W, 3 * A_W, 3 * A_W + B_W, 3 * A_W + 2 * B_W, 3 * A_W + 3 * B_W], axis=-1)
    qa = rmsnorm(qa.reshape(b, s, HA, HEAD_DIM), qk_gain[0])
    ka = rmsnorm(ka.reshape(b, s, HA, HEAD_DIM), qk_gain[1])
    va = va.reshape(b, s, HA, HEAD_DIM)
    y_a = dilated_mixer(qa, ka, va, rel_bias[:, :HA]).reshape(b, s, A_W)
    lam_init = 0.8 - 0.6 * math.exp(-0.3 * l)
    lv = lambda_vec.astype(jnp.float32)
    lam = jnp.exp(jnp.sum(lv[0] * lv[1])) - jnp.exp(jnp.sum(lv[2] * lv[3])) + lam_init
    qb = rmsnorm(qb.reshape(b, s, HB, 2, HEAD_DIM), qk_gain[2:4])
    kb = rmsnorm(kb.reshape(b, s, HB, 2, HEAD_DIM), qk_gain[4:6])
    vb = vb.reshape(b, s, HB, 2 * HEAD_DIM)
    ob = diff_attention(qb, kb, vb, lam, rel_bias[:, HA:])
    y_b = (rmsnorm(ob, subln_g) * (1.0 - lam_init)).reshape(b, s, B_W)
    y_c = conv_module(u, conv_w, conv_b, conv_ln_g, conv_ln_b)
    ys = jnp.einsum('gbsm,gmd->bsgd', jnp.stack([y_a, y_b, y_c]), w_branch)
    gates = jax.nn.sigmoid(h @ w_gate + b_gate).reshape(b, s, N_BRANCH, d)
    return jnp.sum(gates * ys, axis=2) @ w_out


def setup_inputs(seed: int = 0) -> dict:
    key = jax.random.key(seed)
    ks = jax.random.split(key, 20)

    def nrm(k, shape, scale):
        return jax.random.normal(k, shape, jnp.float32) * scale

    D = D_MODEL
    return {
        "x": nrm(ks[0], (BATCH, SEQ, D), 1.0),
        "c": nrm(ks[1], (BATCH, D), 1.0),
        "rel_bias": nrm(ks[2], (N_BUCKETS, HA + HB), 0.2),
        "w_ada": nrm(ks[3], (DEPTH, D, 9 * D), 0.5 * D ** -0.5),
        "b_ada": nrm(ks[4], (DEPTH, 9 * D), 0.02),
        "norm_g": 1.0 + nrm(ks[5], (DEPTH, 3, D), 0.02),
        "w_ffn_in": nrm(ks[6], (DEPTH, 2, D, 2 * D_FF), D ** -0.5),
        "w_ffn_out": nrm(ks[7], (DEPTH, 2, D_FF, D), D_FF ** -0.5),
        "w_in": nrm(ks[8], (DEPTH, D, IN_W), D ** -0.5),
        "qk_gain": 1.0 + nrm(ks[9], (DEPTH, 6, HEAD_DIM), 0.02),
        "lambda_vec": nrm(ks[10], (DEPTH, 4, HEAD_DIM), 0.1),
        "subln_g": 1.0 + nrm(ks[11], (DEPTH, 2 * HEAD_DIM), 0.02),
        "conv_w": nrm(ks[12], (DEPTH, CONV_K, C_CONV), CONV_K ** -0.5),
        "conv_b": nrm(ks[13], (DEPTH, C_CONV), 0.02),
        "conv_ln_g": 1.0 + nrm(ks[14], (DEPTH, C_CONV), 0.02),
        "conv_ln_b": nrm(ks[15], (DEPTH, C_CONV), 0.02),
        "w_branch": nrm(ks[16], (DEPTH, N_BRANCH, MIX_W, D), MIX_W ** -0.5),
        "w_gate": nrm(ks[17], (DEPTH, D, N_BRANCH * D), D ** -0.5),
        "b_gate": nrm(ks[18], (DEPTH, N_BRANCH * D), 0.02),
        "w_out": nrm(ks[19], (DEPTH, D, D), D ** -0.5),
    }


def reference(x, c, rel_bias, w_ada, b_ada, norm_g, w_ffn_in, w_ffn_out, w_in, qk_gain,
              lambda_vec, subln_g, conv_w, conv_b, conv_ln_g, conv_ln_b, w_branch, w_gate,
              b_gate, w_out):
    b, s, d = x.shape
    for l in range(DEPTH):
        mod = (jnp.einsum('bd,de->be', jax.nn.silu(c), w_ada[l]) + b_ada[l]).reshape(b, 9, d)
        h = modulate(rmsnorm(x, norm_g[l, 0]), mod[:, 0], mod[:, 1])
        x = x + 0.5 * mod[:, 2][:, None] * swiglu(h, w_ffn_in[l, 0], w_ffn_out[l, 0])
        h = modulate(rmsnorm(x, norm_g[l, 1]), mod[:, 3], mod[:, 4])
        y = token_mixer(h, l, rel_bias, w_in[l], qk_gain[l], lambda_vec[l], subln_g[l],
                        conv_w[l], conv_b[l], conv_ln_g[l], conv_ln_b[l], w_branch[l],
                        w_gate[l], b_gate[l], w_out[l])
        x = x + mod[:, 5][:, None] * y
        h = modulate(rmsnorm(x, norm_g[l, 2]), mod[:, 6], mod[:, 7])
        x = x + 0.5 * mod[:, 8][:, None] * swiglu(h, w_ffn_in[l, 1], w_ffn_out[l, 1])
    return x
```

```python
import numpy as np
import concourse.bass as bass
import concourse.mybir as mybir
from concourse.bass_utils import run_bass_kernel_spmd

F32 = mybir.dt.float32
BF16 = mybir.dt.bfloat16
AF = mybir.ActivationFunctionType
ALU = mybir.AluOpType

D = 1024
DC = 8
T = 2048
TT = 512
NT = T // TT
DFF = 2816
FC = 22
NCORES = 8


class _Op:
    __slots__ = ("eng", "fn", "deps", "is_dma", "sig", "tok", "idx")


class Prog:
    ENGS = ("pe", "act", "dve", "pool", "sp")
    NDMA = 6

    def __init__(self, nc):
        self.nc = nc
        self.ops = []
        self.last_w = {}
        self.readers = {}
        self.last_c = {}
        self.dma_hist = {}
        self.pending = {}

    def _add(self, eng, fn, reads, writes, is_dma):
        op = _Op()
        op.eng, op.fn, op.is_dma, op.sig, op.tok = eng, fn, is_dma, False, None
        op.idx = len(self.ops)
        deps = {}
        for r in reads:
            w = self.last_w.get(r)
            if w is not None:
                deps[w.idx] = (w, True)
        for wkey in writes:
            w = self.last_w.get(wkey)
            if w is not None and w.idx not in deps:
                deps[w.idx] = (w, False)
            for rd in self.readers.get(wkey, ()):
                if rd.idx not in deps:
                    deps[rd.idx] = (rd, False)
        keep = []
        for d in self.pending.pop(eng, ()):
            if d.eng == eng and not d.is_dma and eng == "pe":
                continue
            if d.idx not in deps:
                keep.append(d)
                d.sig = True
        for d, raw in deps.values():
            if d.eng == eng and not d.is_dma and not is_dma:
                if eng == "pe" or not raw:
                    continue
            keep.append(d)
            d.sig = True
        op.deps = keep
        for r in reads:
            self.readers.setdefault(r, []).append(op)
        for wkey in writes:
            self.last_w[wkey] = op
            self.readers[wkey] = []
        self.ops.append(op)
        if is_dma:
            self.dma_hist.setdefault(eng, []).append(op)
        else:
            self.last_c[eng] = op
        return op

    def barrier(self):
        B = list(self.last_c.values())
        for q, h in self.dma_hist.items():
            B.extend(h[-self.NDMA:])
        for e in self.ENGS:
            self.pending[e] = list(self.pending.get(e, ())) + B

    def op(self, eng, fn, reads=(), writes=()):
        reads, writes = tuple(reads), tuple(writes)
        extra = tuple(r for r in reads if (r == "psb" or (isinstance(r, tuple) and r[0] == "ps")) and r not in writes)
        return self._add(eng, fn, reads, writes + extra, False)

    def dma(self, eng, fn, reads=(), writes=()):
        return self._add(eng, fn, tuple(reads), tuple(writes), True)

    def emit(self, final_keys):
        nc = self.nc
        import contextlib
        with contextlib.ExitStack() as st:
            esem = {e: st.enter_context(nc.semaphore("s_" + e)) for e in self.ENGS}
            dsem = {e: [st.enter_context(nc.semaphore("d_%s%d" % (e, i))) for i in range(self.NDMA)]
                    for e in ("sp", "pool", "act")}
            ecnt = {e: 0 for e in self.ENGS}
            dcnt = {e: [0] * self.NDMA for e in dsem}
            drr = {e: 0 for e in dsem}
            finals = [self.last_w[k] for k in final_keys]
            for f in finals:
                f.sig = True
            prewait = {}
            for op in self.ops:
                if op.is_dma:
                    k = drr[op.eng] % self.NDMA
                    drr[op.eng] += 1
                    prewait[op.idx] = (dsem[op.eng][k], dcnt[op.eng][k])
                    dcnt[op.eng][k] += 16
                    op.tok = (dsem[op.eng][k], dcnt[op.eng][k])
                elif op.sig:
                    ecnt[op.eng] += 1
                    op.tok = (esem[op.eng], ecnt[op.eng])
            assert max(ecnt.values()) < 60000, ecnt
            per = {e: [o for o in self.ops if o.eng == e] for e in self.ENGS}
            block = st.enter_context(nc.Block())

            def run(eng_obj, ename, tail):
                waited = {}

                def w(sem, val):
                    if val <= 0:
                        return
                    key = id(sem)
                    if waited.get(key, 0) >= val:
                        return
                    waited[key] = val
                    eng_obj.wait_ge(sem, val)
                for op in per[ename]:
                    for d in op.deps:
                        w(*d.tok)
                    if op.is_dma:
                        w(*prewait[op.idx])
                    ins = op.fn(eng_obj)
                    if op.tok is not None:
                        ins.then_inc(op.tok[0], 16 if op.is_dma else 1)
                if tail:
                    for f in finals:
                        w(*f.tok)

            @block.tensor
            def _(e):
                run(e, "pe", False)

            @block.scalar
            def _(e):
                run(e, "act", False)

            @block.vector
            def _(e):
                run(e, "dve", False)

            @block.gpsimd
            def _(e):
                run(e, "pool", False)

            @block.sync
            def _(e):
                run(e, "sp", True)


def _rs(ap, shape):
    shape = list(shape)
    if len(shape) == 1:
        return ap
    names = "abcd"[:len(shape)]
    kw = {names[i]: shape[i] for i in range(len(shape))}
    return ap.rearrange("p (%s) -> p %s" % (" ".join(names), " ".join(names)), **kw)


class Arena:
    def __init__(self, ap_f32, nwords):
        self.base, self.n, self.off = ap_f32, nwords, 0

    def reset(self):
        self.off = 0

    def f32(self, *shape):
        n = int(np.prod(shape))
        a = self.base[:, self.off:self.off + n]
        self.off += n
        assert self.off <= self.n, (self.off, self.n)
        return _rs(a, shape)

    def bf(self, *shape):
        n = int(np.prod(shape))
        w = (n + 1) // 2
        a = self.base[:, self.off:self.off + w].bitcast(BF16)
        self.off += w
        assert self.off <= self.n, (self.off, self.n)
        return _rs(a[:, 0:n], shape)


PV_NG = 0
PV_BADA = 24
PV_GAIN = 96
PV_CW = 100
PV_CB = 224
PV_LG = 228
PV_LB = 232
PV_BG = 236
PV_CS = 260
PV_PF = 268
PV_PB = 269
NPV = 272

LW = 2688
EPS = 1e-6
DIL = ((128, 1), (512, 4), (2048, 16))


def _t5_bucket_np(dist):
    dist = np.asarray(dist, np.int64)
    dd = np.maximum(dist.astype(np.float32), np.float32(1.0))
    large = 16 + (np.log(dd / np.float32(16.0)) / np.float32(np.log(2048.0 / 16.0)) * np.float32(16.0)).astype(np.int32)
    large = np.minimum(large, 31)
    return np.where(dist < 16, dist, large).astype(np.int64)


def _host_bias_tables(rel_bias):
    NEG = np.float32(-1e30)
    k = np.arange(128)[:, None]
    dbias = np.empty((128, 8, 3, 256), np.float32)
    for p, (win, d) in enumerate(DIL):
        j = np.arange(256)[None, :]
        rel = np.where(j < 128, j + 128 - k, j - 128 - k)
        valid = (rel >= 0) & (rel <= 128)
        b = _t5_bucket_np(np.maximum(rel, 0) * d)
        for h in range(8):
            dbias[:, h, p, :] = np.where(valid, rel_bias[b, h], NEG)
    j = np.arange(LW)[None, :]
    dist = j - 384 - k
    b = _t5_bucket_np(np.maximum(dist, 0))
    fbias = np.empty((128, 4, LW), np.float32)
    for h in range(4):
        fbias[:, h, :] = np.where(dist >= 0, rel_bias[b, 8 + h], NEG)
    return dbias, fbias


def _host_pv(inp, l, core):
    b = core // 2
    pv = np.zeros((128, NPV), np.float32)
    pv[:, PV_NG:PV_NG + 24] = inp["norm_g"][l].reshape(3, 8, 128).transpose(2, 0, 1).reshape(128, 24)
    pv[:, PV_BADA:PV_BADA + 72] = inp["b_ada"][l].reshape(9, 8, 128).transpose(2, 0, 1).reshape(128, 72)
    g = inp["qk_gain"][l]
    pv[:, PV_GAIN + 0] = np.concatenate([g[0], g[0]])
    pv[:, PV_GAIN + 1] = np.concatenate([g[1], g[1]])
    pv[:, PV_GAIN + 2] = np.concatenate([g[2], g[3]])
    pv[:, PV_GAIN + 3] = np.concatenate([g[4], g[5]])
    pv[:, PV_CW:PV_CW + 124] = inp["conv_w"][l].reshape(31, 4, 128).transpose(2, 1, 0).reshape(128, 124)
    pv[:, PV_CB:PV_CB + 4] = inp["conv_b"][l].reshape(4, 128).T
    pv[:, PV_LG:PV_LG + 4] = inp["conv_ln_g"][l].reshape(4, 128).T
    pv[:, PV_LB:PV_LB + 4] = inp["conv_ln_b"][l].reshape(4, 128).T
    pv[:, PV_BG:PV_BG + 24] = inp["b_gate"][l].reshape(3, 8, 128).transpose(2, 0, 1).reshape(128, 24)
    pv[:, PV_CS:PV_CS + 8] = inp["c"][b].reshape(8, 128).T
    pv[:, PV_PF] = 1.0 if core % 2 == 1 else 0.0
    pv[:, PV_PB] = 0.0 if core % 2 == 1 else -30000.0
    return pv


def build(stage, dbg=False):
    import contextlib
    nc = bass.Bass("TRN2", target_bir_lowering=False)
    la = {1: None, 2: 0, 3: 1}[stage]
    lb = {1: 0, 2: 1, 3: None}[stage]

    def din(name, shape, dt=F32):
        return nc.dram_tensor(name, list(shape), dt, kind="ExternalInput").ap()

    def dout(name, shape, dt=F32):
        return nc.dram_tensor(name, list(shape), dt, kind="ExternalOutput").ap()

    def dscr(name, shape, dt=F32):
        return nc.dram_tensor(name, list(shape), dt, kind=("ExternalOutput" if dbg else "Internal")).ap()

    I = {}
    if stage == 1:
        I["x"] = din("x", [T, D])
    else:
        I["xT_in"] = din("xT_in", [128, DC, T])
        I["modA"] = din("modA", [128, 72])
        I["pvA"] = din("pvA", [128, NPV])
        for n, s in (("w_inA", [D, 4096]), ("w_gateA", [D, 3072]), ("w_brA", [1536, D]), ("w_outA", [D, D]),
                     ("w_fiA", [D, 2 * DFF]), ("w_foA", [DFF, D]), ("lamA", [1, 256]), ("subgA", [1, 128]),
                     ("dbias", [128, 8, 3, 256]), ("fbias", [128, 4, LW]), ("ut_p", [128, 4, 32])):
            I[n] = din(n, s)
        for n in ("kaT_o", "kbT_o", "kaT_p", "kbT_p"):
            I[n] = din(n, [128, 4, T], BF16)
        for n in ("va_o", "vb_o", "va_p", "vb_p"):
            I[n] = din(n, [T, 512], BF16)
    if lb is not None:
        I["pvB"] = din("pvB", [128, NPV])
        for n, s in (("w_adaB", [D, 9 * D]), ("w_fiB", [D, 2 * DFF]), ("w_foB", [DFF, D]), ("w_inB", [D, 4096])):
            I[n] = din(n, s)
    O = {}
    if stage < 3:
        O["xT_out"] = dout("xT_out", [128, DC, T])
        O["modB"] = dout("modB", [128, 72])
        O["kaT"] = dout("kaT", [128, 4, T], BF16)
        O["kbT"] = dout("kbT", [128, 4, T], BF16)
        O["va"] = dout("va", [T, 512], BF16)
        O["vb"] = dout("vb", [T, 512], BF16)
        O["ut"] = dout("ut", [128, 4, 32])
    else:
        O["out"] = dout("out", [T, D])
    S = {}
    if la is not None:
        S["h2T"] = dscr("h2T_s", [128, DC, T], BF16)
        S["qaT"] = dscr("qaT_s", [128, 4, T], BF16)
        S["qbT"] = dscr("qbT_s", [128, 4, T], BF16)
        S["uT"] = dscr("uT_s", [128, 4, 32 + T])
        S["yaT"] = dscr("yaT_s", [512, T], BF16)
        S["ybT"] = dscr("ybT_s", [512, T], BF16)
        S["ycT"] = dscr("ycT_s", [128, 4, T], BF16)
        S["vaf"] = dscr("vaf_s", [2 * T, 512], BF16)
        S["vbf"] = dscr("vbf_s", [2 * T, 512], BF16)

    with contextlib.ExitStack() as st:
        def sb(name, shape, dt):
            return st.enter_context(nc.sbuf_tensor(name, shape, dt))
        xT = sb("xT", [128, DC, T], F32)
        identF = sb("identF", [128, 128], F32)
        onesF = sb("onesF", [128, 128], F32)
        identB = sb("identB", [128, 128], BF16)
        onesB = sb("onesB", [128, 128], BF16)
        blkB = sb("blkB", [128, 128], BF16)
        pvA = sb("pvA_t", [128, NPV], F32)
        pvB = sb("pvB_t", [128, NPV], F32)
        modA = sb("modA_t", [128, 72], F32)
        modB = sb("modB_t", [128, 72], F32)
        sclA = sb("sclA", [128, 3, 8], F32)
        gatA = sb("gatA", [128, 3, 8], F32)
        sclB = sb("sclB", [128, 3, 8], F32)
        gatB = sb("gatB", [128, 3, 8], F32)
        neglam = sb("neglam", [128, 4], F32)
        NA = 33600
        arena_t = sb("arena", [128, NA], F32)
        A = Arena(arena_t[:, :], NA)
        ps = [st.enter_context(nc.psum_tensor("ps%d" % i, [128, 512], F32)) for i in range(7)]
        psb = st.enter_context(nc.psum_tensor("psb", [128, 1024], BF16))
        P = Prog(nc)
        rr = [0]

        def pb(lo=0, hi=7):
            i = lo + rr[0] % (hi - lo)
            rr[0] += 1
            return ps[i], ("ps", i)

        P.op("pool", lambda e: e.memset(identF[:], 0.0), writes=["identF"])
        P.op("pool", lambda e: e.affine_select(out=identF[:], in_=identF[:], pattern=[[-1, 128]],
                                                compare_op=ALU.not_equal, fill=1.0, base=0, channel_multiplier=1),
             reads=["identF"], writes=["identF"])
        P.op("pool", lambda e: e.memset(onesF[:], 1.0), writes=["onesF"])
        P.op("pool", lambda e: e.memset(onesB[:], 1.0), writes=["onesB"])
        P.op("pool", lambda e: e.memset(blkB[:], 0.0), writes=["blkB"])
        P.op("pool", lambda e: e.memset(blkB[0:64, 0:64], 1.0), reads=["blkB"], writes=["blkB"])
        P.op("pool", lambda e: e.memset(blkB[64:128, 64:128], 1.0), reads=["blkB"], writes=["blkB"])
        P.op("dve", lambda e: e.tensor_copy(out=identB[:], in_=identF[:]), reads=["identF"], writes=["identB"])
        if la is not None:
            P.dma("sp", lambda e: e.dma_start(out=pvA[:], in_=I["pvA"][:, :]), writes=["pvA"])
            P.dma("sp", lambda e: e.dma_start(out=modA[:], in_=I["modA"][:, :]), writes=["modA"])
        if lb is not None:
            P.dma("sp", lambda e: e.dma_start(out=pvB[:], in_=I["pvB"][:, :]), writes=["pvB"])

        def derive(pv, pvk, mod, modk, scl, gat, tag):
            for n in range(3):
                P.op("dve", lambda e, n=n: e.scalar_tensor_tensor(
                    out=scl[:, n, :], in0=mod[:, (3 * n + 1) * 8:(3 * n + 2) * 8], scalar=1.0,
                    in1=pv[:, PV_NG + n * 8:PV_NG + n * 8 + 8], op0=ALU.add, op1=ALU.mult),
                    reads=[modk, pvk], writes=["scl" + tag])
                P.op("dve", lambda e, n=n: e.tensor_scalar(
                    out=gat[:, n, :], in0=mod[:, (3 * n + 2) * 8:(3 * n + 3) * 8],
                    scalar1=(1.0 if n == 1 else 0.5), scalar2=None, op0=ALU.mult),
                    reads=[modk], writes=["gat" + tag])

        def fresh(mark=0):
            P.barrier()
            A.off = mark

        def ph_load_x():
            A.reset()
            xin = [A.f32(1024) for _ in range(2)]
            for i in range(16):
                s = i % 2
                P.dma("sp", lambda e, i=i, s=s: e.dma_start(out=xin[s], in_=I["x"][i * 128:(i + 1) * 128, :]),
                      writes=[("xin", s)])
                for hb in range(2):
                    bk, bkey = pb()

                    def tr(e, s=s, hb=hb, bk=bk):
                        for c4 in range(4):
                            c = hb * 4 + c4
                            ins = e.transpose(out=bk[:, c4 * 128:(c4 + 1) * 128],
                                              in_=xin[s][:, c * 128:(c + 1) * 128], identity=identF[:])
                        return ins
                    P.op("pe", tr, reads=[("xin", s), "identF"], writes=[bkey])
                    dst = xT[:, hb * 4:(hb + 1) * 4, i * 128:(i + 1) * 128]
                    if hb == 0:
                        P.op("dve", lambda e, dst=dst, bk=bk: e.tensor_copy(out=dst, in_=_rs(bk[:, :], [4, 128])),
                             reads=[bkey], writes=[("xT", i // 4, hb)])
                    else:
                        P.op("act", lambda e, dst=dst, bk=bk: e.activation(out=dst, in_=_rs(bk[:, :], [4, 128]),
                                                                          func=AF.Identity),
                             reads=[bkey], writes=[("xT", i // 4, hb)])

        def xkeys(tt):
            return [("xT", tt, 0), ("xT", tt, 1)]

        def ph_load_xT():
            for c in range(DC):
                P.dma("sp", lambda e, c=c: e.dma_start(out=xT[:, c, :], in_=I["xT_in"][:, c, :]),
                      writes=[("xT", tt, hb) for tt in range(NT) for hb in range(2)])

        def ph_adaln(pv, pvk, w_ada, mod, modk):
            A.reset()
            P.barrier()
            csb = A.bf(8)
            wa = [A.bf(8, 1024) for _ in range(2)]
            P.op("act", lambda e: e.activation(out=csb, in_=pv[:, PV_CS:PV_CS + 8], func=AF.Silu),
                 reads=[pvk], writes=["csb"])
            bk, bkey = ps[6], ("ps", 6)
            for j in range(9):
                s = j % 2
                P.dma("pool", lambda e, j=j, s=s: e.dma_start(
                    out=wa[s], in_=w_ada[:, j * 1024:(j + 1) * 1024].rearrange("(c p) n -> p c n", p=128)),
                    writes=[("wa", s)])

                def mm(e, j=j, s=s):
                    for cb in range(8):
                        for kc in range(8):
                            ins = e.matmul(bk[:, j * 8 + cb:j * 8 + cb + 1], lhsT=wa[s][:, kc, cb * 128:(cb + 1) * 128],
                                           rhs=csb[:, kc:kc + 1], start=(kc == 0), stop=(kc == 7))
                    return ins
                P.op("pe", mm, reads=[("wa", s), "csb"], writes=[bkey])
            P.op("dve", lambda e: e.tensor_tensor(out=mod[:, :], in0=bk[:, 0:72], in1=pv[:, PV_BADA:PV_BADA + 72],
                                                  op=ALU.add), reads=[bkey, pvk], writes=[modk])

        def ph_norm(n, pv, pvk, mod, modk, scl, sclk, hT):
            sqb = A.bf(8, TT)
            rs = [A.f32(TT) for _ in range(2)]
            tmpf = [A.f32(TT) for _ in range(2)]
            k = 0
            for tt in range(NT):
                sl = slice(tt * TT, (tt + 1) * TT)
                P.op("act", lambda e, sl=sl: e.activation(out=sqb, in_=xT[:, :, sl], func=AF.Square),
                     reads=xkeys(tt), writes=["sqb"])
                bk, bkey = pb()

                def mm(e, bk=bk):
                    for c in range(DC):
                        ins = e.matmul(bk[:, :], lhsT=onesB[:], rhs=sqb[:, c, :], start=(c == 0), stop=(c == DC - 1))
                    return ins
                P.op("pe", mm, reads=["sqb", "onesB"], writes=[bkey])
                r = rs[tt % 2]
                rk = ("rs", tt % 2)
                P.op("act", lambda e, r=r, bk=bk: e.activation(out=r, in_=bk[:, :], func=AF.Ln, scale=1.0 / D, bias=EPS),
                     reads=[bkey], writes=[rk])
                P.op("act", lambda e, r=r: e.activation(out=r, in_=r, func=AF.Exp, scale=-0.5), reads=[rk], writes=[rk])
                for c in range(DC):
                    tf = tmpf[k % 2]
                    tk = ("tmpf", k % 2)
                    k += 1
                    P.op("dve", lambda e, c=c, sl=sl, tf=tf, r=r: e.tensor_tensor(out=tf, in0=xT[:, c, sl], in1=r, op=ALU.mult),
                         reads=xkeys(tt) + [rk], writes=[tk])
                    P.op("act", lambda e, c=c, sl=sl, tf=tf: e.activation(
                        out=hT[:, c, sl], in_=tf, func=AF.Identity, scale=scl[:, n, c:c + 1],
                        bias=mod[:, 3 * n * 8 + c:3 * n * 8 + c + 1]),
                        reads=[tk, sclk, modk], writes=[("h", tt)])

        def ph_ffn(w_up, w_dn, gat, gatk, n, hT):
            groups = [(0, 6), (6, 6), (12, 6), (18, 4)]
            actT = A.bf(6, T)
            wup = [A.bf(2, 8, 256) for _ in range(2)]
            wdn = [A.bf(6, D) for _ in range(2)]
            sgf = [A.f32(TT) for _ in range(2)]
            k = 0
            npair = 0
            for gi, (c0, gn) in enumerate(groups):
                ws = gi % 2
                P.dma("pool", lambda e, c0=c0, gn=gn, ws=ws: e.dma_start(
                    out=wdn[ws][:, 0:gn, :], in_=w_dn[c0 * 128:(c0 + gn) * 128, :].rearrange("(i p) n -> p i n", p=128)),
                    writes=[("wdn", ws)])
                for pi in range(gn // 2):
                    cpair = c0 + 2 * pi
                    us = npair % 2
                    npair += 1
                    for gu in range(2):
                        col = gu * DFF + cpair * 128
                        P.dma("pool", lambda e, us=us, gu=gu, col=col: e.dma_start(
                            out=wup[us][:, gu, :, :], in_=w_up[:, col:col + 256].rearrange("(c p) n -> p c n", p=128)),
                            writes=[("wup", us, gu)])
                    for ci in range(2):
                        il = 2 * pi + ci
                        for tt in range(NT):
                            sl = slice(tt * TT, (tt + 1) * TT)
                            bg, bgk = pb()
                            bu, buk = pb()

                            def mmg(e, us=us, ci=ci, sl=sl, bg=bg, gu=0):
                                for kc in range(DC):
                                    ins = e.matmul(bg[:, :], lhsT=wup[us][:, gu, kc, ci * 128:(ci + 1) * 128], rhs=hT[:, kc, sl],
                                                   start=(kc == 0), stop=(kc == DC - 1))
                                return ins
                            P.op("pe", mmg, reads=[("wup", us, 0), ("h", tt)], writes=[bgk])
                            P.op("pe", lambda e, us=us, ci=ci, sl=sl, bu=bu: mmg(e, us, ci, sl, bu, 1),
                                 reads=[("wup", us, 1), ("h", tt)], writes=[buk])
                            sg = sgf[k % 2]
                            sk = ("sgf", k % 2)
                            k += 1
                            P.op("act", lambda e, sg=sg, bg=bg: e.activation(out=sg, in_=bg[:, :], func=AF.Silu),
                                 reads=[bgk], writes=[sk])
                            P.op("dve", lambda e, sg=sg, bu=bu, il=il, sl=sl: e.tensor_tensor(
                                out=actT[:, il, sl], in0=bu[:, :], in1=sg, op=ALU.mult),
                                reads=[buk, sk], writes=[("actT", tt)])
                for dc in range(DC):
                    for tt in range(NT):
                        sl = slice(tt * TT, (tt + 1) * TT)
                        bd, bdk = pb()

                        def mmd(e, dc=dc, sl=sl, bd=bd, gn=gn, ws=ws):
                            for i in range(gn):
                                ins = e.matmul(bd[:, :], lhsT=wdn[ws][:, i, dc * 128:(dc + 1) * 128], rhs=actT[:, i, sl],
                                               start=(i == 0), stop=(i == gn - 1))
                            return ins
                        P.op("pe", mmd, reads=[("wdn", ws), ("actT", tt)], writes=[bdk])
                        P.op("dve", lambda e, dc=dc, sl=sl, bd=bd: e.scalar_tensor_tensor(
                            out=xT[:, dc, sl], in0=bd[:, :], scalar=gat[:, n, dc:dc + 1], in1=xT[:, dc, sl],
                            op0=ALU.mult, op1=ALU.add),
                            reads=[bdk, gatk, ("xT", tt, dc // 4)], writes=[("xT", tt, dc // 4)])

        def proj_norm(w_in, colbase, pv, pvk, gaincol, hT, dst):
            wq = A.bf(8, 512)
            kst = [A.bf(T) for _ in range(2)]
            sq = [A.bf(TT) for _ in range(2)]
            qf = [A.f32(TT) for _ in range(2)]
            rs = [A.f32(TT) for _ in range(2)]
            P.dma("pool", lambda e: e.dma_start(out=wq, in_=w_in[:, colbase:colbase + 512].rearrange("(c p) n -> p c n", p=128)),
                  writes=["wq"])
            k = 0
            for ch in range(4):
                ks = kst[ch % 2]
                kk = ("kst", ch % 2)
                for tt in range(NT):
                    sl = slice(tt * TT, (tt + 1) * TT)
                    u = k % 2
                    k += 1
                    bq, bqk = pb()

                    def mm(e, ch=ch, sl=sl, bq=bq):
                        for kc in range(DC):
                            ins = e.matmul(bq[:, :], lhsT=wq[:, kc, ch * 128:(ch + 1) * 128], rhs=hT[:, kc, sl],
                                           start=(kc == 0), stop=(kc == DC - 1))
                        return ins
                    P.op("pe", mm, reads=["wq", ("h", tt)], writes=[bqk])
                    P.op("act", lambda e, u=u, bq=bq: e.activation(out=sq[u], in_=bq[:, :], func=AF.Square),
                         reads=[bqk], writes=[("sq", u)])
                    P.op("dve", lambda e, u=u, bq=bq: e.tensor_copy(out=qf[u], in_=bq[:, :]), reads=[bqk], writes=[("qf", u)])
                    bs, bsk = pb()
                    P.op("pe", lambda e, u=u, bs=bs: e.matmul(bs[:, :], lhsT=blkB[:], rhs=sq[u], start=True, stop=True),
                         reads=[("sq", u), "blkB"], writes=[bsk])
                    P.op("act", lambda e, u=u, bs=bs: e.activation(out=rs[u], in_=bs[:, :], func=AF.Ln, scale=1.0 / 64, bias=EPS),
                         reads=[bsk], writes=[("prs", u)])
                    P.op("act", lambda e, u=u: e.activation(out=rs[u], in_=rs[u], func=AF.Exp, scale=-0.5),
                         reads=[("prs", u)], writes=[("prs", u)])
                    P.op("dve", lambda e, u=u, ks=ks, sl=sl: e.scalar_tensor_tensor(
                        out=ks[:, sl], in0=qf[u], scalar=pv[:, PV_GAIN + gaincol:PV_GAIN + gaincol + 1], in1=rs[u],
                        op0=ALU.mult, op1=ALU.mult), reads=[("qf", u), ("prs", u), pvk], writes=[kk])
                P.dma("sp", lambda e, ch=ch, ks=ks: e.dma_start(out=dst(ch), in_=ks), reads=[kk], writes=[("dst", colbase, ch)])

        def v_proj(w_in, colbase, hT, dst):
            wv = A.bf(8, 512)
            vst = [A.bf(512) for _ in range(2)]
            P.dma("pool", lambda e: e.dma_start(out=wv, in_=w_in[:, colbase:colbase + 512].rearrange("(c p) n -> p c n", p=128)),
                  writes=["wv"])
            for i in range(16):
                u = i % 2
                bv, bvk = pb()

                def mm(e, i=i, bv=bv):
                    for kc in range(DC):
                        ins = e.matmul(bv[:, :], lhsT=hT[:, kc, i * 128:(i + 1) * 128], rhs=wv[:, kc, :],
                                       start=(kc == 0), stop=(kc == DC - 1))
                    return ins
                P.op("pe", mm, reads=["wv", ("h", i // 4)], writes=[bvk])
                P.op("act", lambda e, u=u, bv=bv: e.activation(out=vst[u], in_=bv[:, :], func=AF.Identity),
                     reads=[bvk], writes=[("vst", u)])
                P.dma("sp", lambda e, i=i, u=u: e.dma_start(out=dst[i * 128:(i + 1) * 128, :], in_=vst[u]),
                      reads=[("vst", u)], writes=[("vdst", colbase, i)])

        def glu(w_in, hT, tts, sink):
            wu = A.bf(8, 1024)
            sgf = [A.f32(TT) for _ in range(2)]
            uf = [A.f32(TT) for _ in range(2)]
            P.dma("pool", lambda e: e.dma_start(out=wu, in_=w_in[:, 3072:4096].rearrange("(c p) n -> p c n", p=128)),
                  writes=["wu"])
            k = 0
            for ch in range(4):
                for tt in tts:
                    sl = slice(tt * TT, (tt + 1) * TT)
                    u = k % 2
                    k += 1
                    b1, b1k = pb()
                    b2, b2k = pb()

                    def mm(e, col, bk, sl=sl):
                        for kc in range(DC):
                            ins = e.matmul(bk[:, :], lhsT=wu[:, kc, col:col + 128], rhs=hT[:, kc, sl],
                                           start=(kc == 0), stop=(kc == DC - 1))
                        return ins
                    P.op("pe", lambda e, ch=ch, b1=b1, mm=mm: mm(e, ch * 128, b1), reads=["wu", ("h", tt)], writes=[b1k])
                    P.op("pe", lambda e, ch=ch, b2=b2, mm=mm: mm(e, 512 + ch * 128, b2), reads=["wu", ("h", tt)], writes=[b2k])
                    P.op("act", lambda e, u=u, b2=b2: e.activation(out=sgf[u], in_=b2[:, :], func=AF.Sigmoid),
                         reads=[b2k], writes=[("gsg", u)])
                    P.op("dve", lambda e, u=u, b1=b1: e.tensor_tensor(out=uf[u], in0=b1[:, :], in1=sgf[u], op=ALU.mult),
                         reads=[b1k, ("gsg", u)], writes=[("guf", u)])
                    sink(ch, tt, uf[u], ("guf", u))

        def ph_kv(pv, pvk, mod, modk, scl, sclk, w_in):
            A.reset()
            P.barrier()
            hT = A.bf(8, T)
            mark = A.off
            kvl = _DBG.get("kv", 9)
            ph_norm(1, pv, pvk, mod, modk, scl, sclk, hT)
            fresh(mark)
            if kvl >= 1:
                proj_norm(w_in, 512, pv, pvk, 1, hT, lambda ch: O["kaT"][:, ch, :])
                fresh(mark)
            if kvl >= 2:
                proj_norm(w_in, 2048, pv, pvk, 3, hT, lambda ch: O["kbT"][:, ch, :])
                fresh(mark)
            if kvl >= 3:
                v_proj(w_in, 1024, hT, O["va"])
                fresh(mark)
            if kvl >= 4:
                v_proj(w_in, 2560, hT, O["vb"])
                fresh(mark)
            if kvl < 5:
                return

            def sink(ch, tt, uf, key):
                P.dma("sp", lambda e, ch=ch, uf=uf: e.dma_start(out=O["ut"][:, ch, :], in_=uf[:, TT - 32:TT]),
                      reads=[key], writes=[("utout", ch)])
            glu(w_in, hT, [NT - 1], sink)

        def ph_store_x():
            for c in range(DC):
                P.dma("sp", lambda e, c=c: e.dma_start(out=O["xT_out"][:, c, :], in_=xT[:, c, :]),
                      reads=[("xT", tt, c // 4) for tt in range(NT)], writes=[("xout", c)])

        def ph_ffn_full(n, pv, pvk, mod, modk, scl, sclk, gat, gatk, w_up, w_dn):
            fresh()
            hT = A.bf(8, T)
            mark = A.off
            ph_norm(n, pv, pvk, mod, modk, scl, sclk, hT)
            fresh(mark)
            ph_ffn(w_up, w_dn, gat, gatk, n, hT)

        def ph_qu(pv, pvk, mod, modk, scl, sclk, w_in):
            fresh()
            hT = A.bf(8, T)
            mark = A.off
            ph_norm(1, pv, pvk, mod, modk, scl, sclk, hT)
            for c in range(DC):
                P.dma("sp", lambda e, c=c: e.dma_start(out=S["h2T"][:, c, :], in_=hT[:, c, :]),
                      reads=[("h", tt) for tt in range(NT)], writes=[("h2Ts", c)])
            fresh(mark)
            proj_norm(w_in, 0, pv, pvk, 0, hT, lambda ch: S["qaT"][:, ch, :])
            fresh(mark)
            proj_norm(w_in, 1536, pv, pvk, 2, hT, lambda ch: S["qbT"][:, ch, :])
            fresh(mark)
            utp = A.f32(4, 32)
            P.dma("sp", lambda e: e.dma_start(out=utp, in_=I["ut_p"][:, :, :]), writes=["utp"])
            P.op("dve", lambda e: e.tensor_scalar(out=utp, in0=utp, scalar1=pv[:, PV_PF:PV_PF + 1], scalar2=None, op0=ALU.mult),
                 reads=["utp", pvk], writes=["utp"])
            P.dma("sp", lambda e: e.dma_start(out=S["uT"][:, :, 0:32], in_=utp), reads=["utp"], writes=["uTs"])

            def sink(ch, tt, uf, key):
                P.dma("sp", lambda e, ch=ch, tt=tt, uf=uf: e.dma_start(out=S["uT"][:, ch, 32 + tt * TT:32 + (tt + 1) * TT], in_=uf),
                      reads=[key], writes=[("uTs", ch, tt)])
            glu(w_in, hT, list(range(NT)), sink)
            for nm, src_p, src_o in (("vaf", "va_p", "va_o"), ("vbf", "vb_p", "vb_o")):
                P.dma("sp", lambda e, nm=nm, src_p=src_p: e.dma_start(out=S[nm][0:T, :], in_=I[src_p][:, :]), writes=[(nm, 0)])
                P.dma("sp", lambda e, nm=nm, src_o=src_o: e.dma_start(out=S[nm][T:2 * T, :], in_=I[src_o][:, :]), writes=[(nm, 1)])

        def ph_dil(pv, pvk):
            fresh()
            qh = A.bf(T)
            kh = A.bf(2 * T)
            vbuf = [A.bf(32, 256) for _ in range(2)]
            pT = [A.bf(2, 256) for _ in range(2)]
            tmp = [A.f32(2, 256) for _ in range(2)]
            acc = A.f32(2, T)
            db = A.f32(2, 3, 256)
            rlow = A.f32(T)
            yn = A.bf(T)
            for s in range(2):
                P.op("pool", lambda e, s=s: e.memset(vbuf[s], 1.0), writes=[("vbuf", s, r) for r in range(16)])
            unit = 0
            allacc = [("acc", b) for b in range(16)]
            for hp in range(4):
                P.dma("sp", lambda e, hp=hp: e.dma_start(out=qh, in_=S["qaT"][:, hp, :]), writes=["qh"])
                P.dma("sp", lambda e, hp=hp: e.dma_start(out=kh[:, 0:T], in_=I["kaT_p"][:, hp, :]), writes=[("kh", 0)])
                P.dma("sp", lambda e, hp=hp: e.dma_start(out=kh[:, T:2 * T], in_=I["kaT_o"][:, hp, :]), writes=[("kh", 1)])
                P.dma("sp", lambda e, hp=hp: e.dma_start(out=db, in_=I["dbias"][:, 2 * hp:2 * hp + 2, :, :]), writes=["db"])
                for p, (win, d) in enumerate(DIL):
                    vs = (hp * 3 + p) % 2
                    vb = vbuf[vs]
                    nb = 16 // d
                    vview = S["vaf"].rearrange("(i d) c -> i d c", d=d)
                    i0 = T // d - 128
                    for r in range(d):
                        for h in range(2):
                            src = vview[i0:i0 + 128 * (nb + 1), r, hp * 128 + h * 64:hp * 128 + h * 64 + 64].rearrange(
                                "(j p) c -> p j c", p=128)
                            P.dma("sp", lambda e, vb=vb, r=r, h=h, nb=nb, src=src: e.dma_start(
                                out=vb[:, r * (nb + 1):(r + 1) * (nb + 1), h * 128:h * 128 + 64], in_=src),
                                reads=[("vaf", 0), ("vaf", 1)], writes=[("vbuf", vs, r)])
                    khv = _rs(kh, [2 * T // d, d])
                    qhv = _rs(qh, [T // d, d])
                    for r in range(d):
                        for m in range(nb):
                            u = unit % 2
                            unit += 1
                            bS = [pb(), pb()]
                            bO, bOk = pb()

                            def st_(e, r=r, m=m, d=d, bS=bS, khv=khv, qhv=qhv):
                                for h in range(2):
                                    for jj in range(2):
                                        ki = T // d + 128 * (m - 1 + jj)
                                        ins = e.matmul(bS[h][0][:, jj * 128:(jj + 1) * 128],
                                                       lhsT=khv[h * 64:(h + 1) * 64, ki:ki + 128, r],
                                                       rhs=qhv[h * 64:(h + 1) * 64, 128 * m:128 * m + 128, r],
                                                       start=True, stop=True)
                                return ins
                            P.op("pe", st_, reads=["qh", ("kh", 0), ("kh", 1)], writes=[bS[0][1], bS[1][1]])
                            for h in range(2):
                                P.op("dve", lambda e, h=h, u=u, p=p, bS=bS: e.scalar_tensor_tensor(
                                    out=tmp[u][:, h, :], in0=bS[h][0][:, 0:256], scalar=0.125, in1=db[:, h, p, :],
                                    op0=ALU.mult, op1=ALU.add), reads=[bS[h][1], "db"], writes=[("dtmp", u, h)])
                                if m == 0:
                                    P.op("act", lambda e, h=h, u=u: e.activation(
                                        out=pT[u][:, h, 0:128], in_=tmp[u][:, h, 0:128], func=AF.Exp, bias=pv[:, PV_PB:PV_PB + 1]),
                                        reads=[("dtmp", u, h), pvk], writes=[("dpT", u, h)])
                                    P.op("act", lambda e, h=h, u=u: e.activation(
                                        out=pT[u][:, h, 128:256], in_=tmp[u][:, h, 128:256], func=AF.Exp),
                                        reads=[("dtmp", u, h)], writes=[("dpT", u, h)])
                                else:
                                    P.op("act", lambda e, h=h, u=u: e.activation(out=pT[u][:, h, :], in_=tmp[u][:, h, :], func=AF.Exp),
                                         reads=[("dtmp", u, h)], writes=[("dpT", u, h)])

                            def pv_(e, r=r, m=m, nb=nb, u=u, vb=vb, bO=bO):
                                for h in range(2):
                                    for jj in range(2):
                                        ins = e.matmul(bO[:, h * 128:(h + 1) * 128],
                                                       lhsT=vb[:, r * (nb + 1) + m + jj, h * 128:(h + 1) * 128],
                                                       rhs=pT[u][:, h, jj * 128:(jj + 1) * 128], start=(jj == 0), stop=(jj == 1))
                                return ins
                            P.op("pe", pv_, reads=[("dpT", u, 0), ("dpT", u, 1), ("vbuf", vs, r)], writes=[bOk])
                            av = acc.rearrange("p h (i d) -> p h i d", d=d)[:, :, 128 * m:128 * m + 128, r]
                            ak = [("acc", b) for b in range(d * m, d * (m + 1))]
                            if p == 0:
                                P.op("dve", lambda e, av=av, bO=bO: e.tensor_copy(out=av, in_=_rs(bO[:, 0:256], [2, 128])),
                                     reads=[bOk], writes=ak)
                            else:
                                P.op("dve", lambda e, av=av, bO=bO: e.tensor_tensor(out=av, in0=_rs(bO[:, 0:256], [2, 128]), in1=av, op=ALU.add),
                                     reads=[bOk] + ak, writes=ak)
                for h in range(2):
                    P.op("dve", lambda e, h=h: e.reciprocal(out=acc[64:128, h, :], in_=acc[64:128, h, :]), reads=allacc, writes=allacc)
                    P.op("dve", lambda e, h=h: e.tensor_copy(out=rlow[0:64, :], in_=acc[64:128, h, :]), reads=allacc, writes=["rlow"])
                    P.op("dve", lambda e, h=h: e.tensor_tensor(out=yn[0:64, :], in0=acc[0:64, h, :], in1=rlow[0:64, :], op=ALU.mult),
                         reads=allacc + ["rlow"], writes=["yn"])
                    P.dma("sp", lambda e, hp=hp, h=h: e.dma_start(out=S["yaT"][(hp * 2 + h) * 64:(hp * 2 + h + 1) * 64, :], in_=yn[0:64, :]),
                          reads=["yn"], writes=[("yaTs", hp, h)])

        def ph_lambda(l):
            import math
            lam_init = 0.8 - 0.6 * math.exp(-0.3 * l)
            fresh()
            lv = A.f32(256)
            t = A.f32(128)
            s2 = A.f32(2)
            P.dma("sp", lambda e: e.dma_start(out=lv, in_=I["lamA"][0:1, :].partition_broadcast(128)), writes=["lv"])
            P.op("dve", lambda e: e.tensor_tensor(out=_rs(t, [2, 64]), in0=_rs(lv, [2, 2, 64])[:, :, 0, :],
                                                  in1=_rs(lv, [2, 2, 64])[:, :, 1, :], op=ALU.mult), reads=["lv"], writes=["lvt"])
            P.op("dve", lambda e: e.tensor_reduce(out=s2, in_=_rs(t, [2, 64]), axis=mybir.AxisListType.X, op=ALU.add),
                 reads=["lvt"], writes=["lvs"])
            P.op("act", lambda e: e.activation(out=s2, in_=s2, func=AF.Exp), reads=["lvs"], writes=["lvs"])
            P.op("dve", lambda e: e.tensor_tensor(out=neglam[:, 1:2], in0=s2[:, 1:2], in1=s2[:, 0:1], op=ALU.subtract),
                 reads=["lvs"], writes=["neglam1"])
            P.op("dve", lambda e: e.tensor_scalar(out=neglam[:, 0:1], in0=neglam[:, 1:2], scalar1=-lam_init, scalar2=None, op0=ALU.add),
                 reads=["neglam1"], writes=["neglam"])
            return lam_init

        def ph_diff(pv, pvk, lam_init):
            fresh()
            qh = A.bf(T)
            kh = A.bf(2 * T)
            vaug = A.bf(32, 130)
            W = A.f32(LW)
            tmp = [A.f32(TT) for _ in range(2)]
            pT = [A.bf(TT) for _ in range(2)]
            o1 = A.f32(128)
            of = A.f32(128)
            sm = A.f32(8)
            ybt = [A.bf(128) for _ in range(2)]
            ybst = A.bf(TT)
            gsub = A.f32(128)
            P.op("pool", lambda e: e.memset(vaug, 1.0), writes=["vaug"])
            P.dma("sp", lambda e: e.dma_start(out=gsub, in_=I["subgA"][0:1, :].partition_broadcast(128)), writes=["gsub"])
            P.op("dve", lambda e: e.tensor_scalar(out=gsub, in0=gsub, scalar1=(1.0 - lam_init), scalar2=None, op0=ALU.mult),
                 reads=["gsub"], writes=["gsub"])
            accb = [ps[2], ps[3], ps[4], ps[5]]
            acck = [("ps", 2), ("ps", 3), ("ps", 4), ("ps", 5)]
            cnt = 0
            for h in range(4):
                P.dma("sp", lambda e, h=h: e.dma_start(out=qh, in_=S["qbT"][:, h, :]), writes=["qh"])
                P.dma("sp", lambda e, h=h: e.dma_start(out=kh[:, 0:T], in_=I["kbT_p"][:, h, :]), writes=[("kh", 0)])
                P.dma("sp", lambda e, h=h: e.dma_start(out=kh[:, T:2 * T], in_=I["kbT_o"][:, h, :]), writes=[("kh", 1)])
                P.dma("sp", lambda e, h=h: e.dma_start(
                    out=vaug[:, :, 0:128], in_=S["vbf"][:, h * 128:(h + 1) * 128].rearrange("(j p) c -> p j c", p=128)),
                    reads=[("vbf", 0), ("vbf", 1)], writes=["vaug"])
                P.dma("sp", lambda e, h=h: e.dma_start(out=W, in_=I["fbias"][:, h, :]), writes=["W"])
                for g in range(4):
                    for c in range(2):
                        nkb = 16 + 4 * g + 4
                        first = [True, True]
                        for kbi in range(nkb):
                            u = cnt % 2
                            cnt += 1
                            bS, bSk = ps[u], ("ps", u)
                            P.op("pe", lambda e, c=c, kbi=kbi, g=g, bS=bS: e.matmul(
                                bS[:, :], lhsT=kh[c * 64:(c + 1) * 64, kbi * 128:(kbi + 1) * 128],
                                rhs=qh[c * 64:(c + 1) * 64, g * TT:(g + 1) * TT], start=True, stop=True),
                                reads=["qh", ("kh", 0), ("kh", 1)], writes=[bSk])
                            delta = (T + TT * g) - 128 * kbi
                            off = delta + 384 if delta < 1792 else 2176
                            P.op("dve", lambda e, u=u, off=off, bS=bS: e.scalar_tensor_tensor(
                                out=tmp[u], in0=bS[:, :], scalar=0.125, in1=W[:, off:off + TT], op0=ALU.mult, op1=ALU.add),
                                reads=[bSk, "W"], writes=[("ftmp", u)])
                            if kbi < 16:
                                P.op("act", lambda e, u=u: e.activation(out=pT[u], in_=tmp[u], func=AF.Exp, bias=pv[:, PV_PB:PV_PB + 1]),
                                     reads=[("ftmp", u), pvk], writes=[("fpT", u)])
                            else:
                                P.op("act", lambda e, u=u: e.activation(out=pT[u], in_=tmp[u], func=AF.Exp),
                                     reads=[("ftmp", u)], writes=[("fpT", u)])
                            plan = []
                            for qb in range(4):
                                if kbi >= 16 and (4 * g + qb) < (kbi - 16):
                                    continue
                                stf = first[qb // 2]
                                first[qb // 2] = False
                                last = (kbi == 16 + 4 * g + qb)
                                plan.append((qb, stf, last))

                            def pv_(e, plan=plan, c=c, u=u, kbi=kbi):
                                for qb, stf, last in plan:
                                    col = (qb % 2) * 256
                                    ins = e.matmul(accb[2 * c + qb // 2][:, col:col + 129], lhsT=pT[u][:, qb * 128:(qb + 1) * 128],
                                                   rhs=vaug[:, kbi, 0:129], start=stf, stop=last, skip_group_check=True)
                                return ins
                            P.op("pe", pv_, reads=[("fpT", u), "vaug"], writes=[acck[2 * c], acck[2 * c + 1]])
                    for qb in range(4):
                        col = (qb % 2) * 256
                        b1, b1k = accb[qb // 2], acck[qb // 2]
                        b2, b2k = accb[2 + qb // 2], acck[2 + qb // 2]
                        yb = ybt[qb % 2]
                        P.op("dve", lambda e, b1=b1, col=col: e.reciprocal(out=sm[:, 0:1], in_=b1[:, col + 128:col + 129]),
                             reads=[b1k], writes=["sm0"])
                        P.op("dve", lambda e, b2=b2, col=col: e.reciprocal(out=sm[:, 1:2], in_=b2[:, col + 128:col + 129]),
                             reads=[b2k], writes=["sm1"])
                        P.op("dve", lambda e: e.tensor_tensor(out=sm[:, 2:3], in0=sm[:, 1:2], in1=neglam[:, 0:1], op=ALU.mult),
                             reads=["sm1", "neglam"], writes=["sm2"])
                        P.op("act", lambda e, b1=b1, col=col: e.activation(out=o1, in_=b1[:, col:col + 128], func=AF.Identity, scale=sm[:, 0:1]),
                             reads=[b1k, "sm0"], writes=["o1"])
                        P.op("dve", lambda e, b2=b2, col=col: e.scalar_tensor_tensor(
                            out=of, in0=b2[:, col:col + 128], scalar=sm[:, 2:3], in1=o1, op0=ALU.mult, op1=ALU.add),
                            reads=[b2k, "sm2", "o1"], writes=["of"])
                        P.op("act", lambda e: e.activation(out=o1, in_=of, func=AF.Square, accum_out=sm[:, 3:4]),
                             reads=["of", "o1"], writes=["o1", "sm3"])
                        P.op("act", lambda e: e.activation(out=sm[:, 4:5], in_=sm[:, 3:4], func=AF.Ln, scale=1.0 / 128, bias=EPS),
                             reads=["sm3"], writes=["sm4"])
                        P.op("act", lambda e: e.activation(out=sm[:, 4:5], in_=sm[:, 4:5], func=AF.Exp, scale=-0.5),
                             reads=["sm4"], writes=["sm4"])
                        P.op("dve", lambda e, yb=yb: e.scalar_tensor_tensor(
                            out=yb, in0=of, scalar=sm[:, 4:5], in1=gsub, op0=ALU.mult, op1=ALU.mult),
                            reads=["of", "sm4", "gsub"], writes=[("ybt", qb % 2)])
                        P.op("pe", lambda e, yb=yb, qb=qb: e.transpose(out=psb[:, qb * 128:(qb + 1) * 128], in_=yb, identity=identB[:]),
                             reads=[("ybt", qb % 2), "identB"], writes=["psb"])
                    P.op("act", lambda e: e.activation(out=ybst, in_=psb[:, 0:TT], func=AF.Identity), reads=["psb"], writes=["ybst"])
                    P.dma("sp", lambda e, h=h, g=g: e.dma_start(out=S["ybT"][h * 128:(h + 1) * 128, g * TT:(g + 1) * TT], in_=ybst),
                          reads=["ybst"], writes=[("ybTs", h, g)])

        def ph_conv(pv, pvk):
            fresh()
            ub = A.f32(32 + T)
            ycf = A.f32(4, T)
            ycb = A.bf(4, T)
            sqf = A.f32(4, TT)
            mf = A.f32(TT)
            vf = A.f32(TT)
            tf = [A.f32(TT) for _ in range(2)]
            for cc in range(4):
                P.dma("sp", lambda e, cc=cc: e.dma_start(out=ub, in_=S["uT"][:, cc, :]),
                      reads=["uTs"] + [("uTs", cc, tt) for tt in range(NT)], writes=["ub"])
                for j in range(31):
                    wcol = pv[:, PV_CW + cc * 31 + j:PV_CW + cc * 31 + j + 1]
                    src = ub[:, 2 + j:2 + j + T]
                    if j == 0:
                        P.op("dve", lambda e, cc=cc, wcol=wcol, src=src: e.tensor_scalar(
                            out=ycf[:, cc, :], in0=src, scalar1=wcol, scalar2=pv[:, PV_CB + cc:PV_CB + cc + 1],
                            op0=ALU.mult, op1=ALU.add), reads=["ub", pvk], writes=[("ycf", cc)])
                    else:
                        P.op("dve", lambda e, cc=cc, wcol=wcol, src=src: e.scalar_tensor_tensor(
                            out=ycf[:, cc, :], in0=src, scalar=wcol, in1=ycf[:, cc, :], op0=ALU.mult, op1=ALU.add),
                            reads=["ub", pvk, ("ycf", cc)], writes=[("ycf", cc)])
            k = 0
            allycf = [("ycf", cc) for cc in range(4)]
            for tt in range(NT):
                sl = slice(tt * TT, (tt + 1) * TT)
                P.op("act", lambda e, sl=sl: e.activation(out=sqf, in_=ycf[:, :, sl], func=AF.Square), reads=allycf, writes=["sqf"])
                b1, b1k = pb()
                b2, b2k = pb()

                def mm1(e, sl=sl, b1=b1):
                    for cc in range(4):
                        ins = e.matmul(b1[:, :], lhsT=onesF[:], rhs=ycf[:, cc, sl], start=(cc == 0), stop=(cc == 3))
                    return ins

                def mm2(e, b2=b2):
                    for cc in range(4):
                        ins = e.matmul(b2[:, :], lhsT=onesF[:], rhs=sqf[:, cc, :], start=(cc == 0), stop=(cc == 3))
                    return ins
                P.op("pe", mm1, reads=allycf + ["onesF"], writes=[b1k])
                P.op("pe", mm2, reads=["sqf", "onesF"], writes=[b2k])
                P.op("dve", lambda e, b1=b1: e.tensor_scalar(out=mf, in0=b1[:, :], scalar1=1.0 / 512, scalar2=None, op0=ALU.mult),
                     reads=[b1k], writes=["mf"])
                P.op("dve", lambda e: e.tensor_tensor(out=vf, in0=mf, in1=mf, op=ALU.mult), reads=["mf"], writes=["vf"])
                P.op("dve", lambda e, b2=b2: e.scalar_tensor_tensor(out=vf, in0=b2[:, :], scalar=1.0 / 512, in1=vf,
                                                                    op0=ALU.mult, op1=ALU.subtract),
                     reads=[b2k, "vf"], writes=["vf"])
                P.op("act", lambda e: e.activation(out=vf, in_=vf, func=AF.Ln, bias=EPS), reads=["vf"], writes=["vf"])
                P.op("act", lambda e: e.activation(out=vf, in_=vf, func=AF.Exp, scale=-0.5), reads=["vf"], writes=["vf"])
                for cc in range(4):
                    t_ = tf[k % 2]
                    tk = ("ctf", k % 2)
                    k += 1
                    P.op("dve", lambda e, cc=cc, sl=sl, t_=t_: e.tensor_tensor(out=t_, in0=ycf[:, cc, sl], in1=mf, op=ALU.subtract),
                         reads=allycf + ["mf"], writes=[tk])
                    P.op("dve", lambda e, t_=t_: e.tensor_tensor(out=t_, in0=t_, in1=vf, op=ALU.mult), reads=[tk, "vf"], writes=[tk])
                    P.op("act", lambda e, cc=cc, sl=sl, t_=t_: e.activation(
                        out=ycb[:, cc, sl], in_=t_, func=AF.Silu, scale=pv[:, PV_LG + cc:PV_LG + cc + 1],
                        bias=pv[:, PV_LB + cc:PV_LB + cc + 1]), reads=[tk, pvk], writes=[("ycb", cc)])
            for cc in range(4):
                P.dma("sp", lambda e, cc=cc: e.dma_start(out=S["ycT"][:, cc, :], in_=ycb[:, cc, :]), reads=[("ycb", cc)], writes=[("ycTs", cc)])

        def ph_merge(pv, pvk, gat, gatk, w_gate, w_br, w_out):
            fresh()
            wo = A.bf(8, D)
            hh = A.bf(8, 1024)
            yy = A.bf(3, 4, 1024)
            zT = A.bf(8, 1024)
            wg = [A.bf(3, 8, 128) for _ in range(2)]
            wb = [A.bf(3, 4, 128) for _ in range(2)]
            gsb = [A.f32(TT) for _ in range(2)]
            zacc = [A.f32(TT) for _ in range(2)]
            prod = [A.f32(TT) for _ in range(2)]
            P.dma("pool", lambda e: e.dma_start(out=wo, in_=w_out.rearrange("(c p) n -> p c n", p=128)), writes=["wo"])
            ysrc = [S["yaT"].rearrange("(c p) t -> p c t", p=128), S["ybT"].rearrange("(c p) t -> p c t", p=128), S["ycT"]]
            ykeys = [[("yaTs", hp, h) for hp in range(4) for h in range(2)],
                     [("ybTs", h, g) for h in range(4) for g in range(4)],
                     [("ycTs", cc) for cc in range(4)]]
            cnt = 0
            k = 0
            for half in range(2):
                hs = slice(half * 1024, (half + 1) * 1024)
                P.dma("sp", lambda e, hs=hs: e.dma_start(out=hh, in_=S["h2T"][:, :, hs]),
                      reads=[("h2Ts", c) for c in range(DC)], writes=["hh"])
                for i in range(3):
                    P.dma("sp", lambda e, i=i, hs=hs: e.dma_start(out=yy[:, i, :, :], in_=ysrc[i][:, :, hs]),
                          reads=ykeys[i], writes=[("yy", i)])
                for j in range(DC):
                    s = cnt % 2
                    cnt += 1
                    for i in range(3):
                        col = i * 1024 + j * 128
                        P.dma("pool", lambda e, s=s, i=i, col=col: e.dma_start(
                            out=wg[s][:, i, :, :], in_=w_gate[:, col:col + 128].rearrange("(c p) n -> p c n", p=128)),
                            writes=[("wg", s, i)])
                        P.dma("pool", lambda e, s=s, i=i, j=j: e.dma_start(
                            out=wb[s][:, i, :, :], in_=w_br[i * 512:(i + 1) * 512, j * 128:(j + 1) * 128].rearrange("(c p) n -> p c n", p=128)),
                            writes=[("wb", s, i)])
                    for t2 in range(2):
                        sl = slice(t2 * TT, (t2 + 1) * TT)
                        za = zacc[(2 * j + t2) % 2]
                        zk = ("zacc", (2 * j + t2) % 2)
                        for i in range(3):
                            u = k % 2
                            k += 1
                            bg, bgk = pb()
                            by, byk = pb()

                            def mg(e, s=s, i=i, sl=sl, bg=bg):
                                for kc in range(DC):
                                    ins = e.matmul(bg[:, :], lhsT=wg[s][:, i, kc, :], rhs=hh[:, kc, sl], start=(kc == 0), stop=(kc == DC - 1))
                                return ins

                            def my(e, s=s, i=i, sl=sl, by=by):
                                for kc in range(4):
                                    ins = e.matmul(by[:, :], lhsT=wb[s][:, i, kc, :], rhs=yy[:, i, kc, sl], start=(kc == 0), stop=(kc == 3))
                                return ins
                            P.op("pe", mg, reads=[("wg", s, i), "hh"], writes=[bgk])
                            P.op("pe", my, reads=[("wb", s, i), ("yy", i)], writes=[byk])
                            P.op("act", lambda e, u=u, bg=bg, i=i, j=j: e.activation(
                                out=gsb[u], in_=bg[:, :], func=AF.Sigmoid, bias=pv[:, PV_BG + i * 8 + j:PV_BG + i * 8 + j + 1]),
                                reads=[bgk, pvk], writes=[("gsb", u)])
                            if i == 0:
                                P.op("dve", lambda e, u=u, by=by, za=za: e.tensor_tensor(out=za, in0=by[:, :], in1=gsb[u], op=ALU.mult),
                                     reads=[byk, ("gsb", u)], writes=[zk])
                            else:
                                P.op("dve", lambda e, u=u, by=by: e.tensor_tensor(out=prod[u], in0=by[:, :], in1=gsb[u], op=ALU.mult),
                                     reads=[byk, ("gsb", u)], writes=[("prod", u)])
                                if i == 1:
                                    P.op("pool", lambda e, u=u, za=za: e.tensor_tensor(out=za, in0=za, in1=prod[u], op=ALU.add),
                                         reads=[zk, ("prod", u)], writes=[zk])
                                else:
                                    P.op("pool", lambda e, u=u, za=za, j=j, sl=sl: e.tensor_tensor(out=zT[:, j, sl], in0=za, in1=prod[u], op=ALU.add),
                                         reads=[zk, ("prod", u)], writes=[("zT", t2)])
                for dj in range(DC):
                    for t2 in range(2):
                        sl = slice(t2 * TT, (t2 + 1) * TT)
                        xs = slice(half * 1024 + t2 * TT, half * 1024 + (t2 + 1) * TT)
                        tt = half * 2 + t2
                        bd, bdk = pb()

                        def mo(e, dj=dj, sl=sl, bd=bd):
                            for kc in range(DC):
                                ins = e.matmul(bd[:, :], lhsT=wo[:, kc, dj * 128:(dj + 1) * 128], rhs=zT[:, kc, sl], start=(kc == 0), stop=(kc == DC - 1))
                            return ins
                        P.op("pe", mo, reads=["wo", ("zT", t2)], writes=[bdk])
                        P.op("dve", lambda e, dj=dj, xs=xs, bd=bd: e.scalar_tensor_tensor(
                            out=xT[:, dj, xs], in0=bd[:, :], scalar=gat[:, 1, dj:dj + 1], in1=xT[:, dj, xs], op0=ALU.mult, op1=ALU.add),
                            reads=[bdk, gatk, ("xT", tt, dj // 4)], writes=[("xT", tt, dj // 4)])

        def ph_out():
            fresh()
            ot = [A.f32(D) for _ in range(2)]
            for i in range(16):
                s = i % 2
                for hb in range(2):
                    bk, bkey = pb()

                    def tr(e, i=i, hb=hb, bk=bk):
                        for c4 in range(4):
                            c = hb * 4 + c4
                            ins = e.transpose(out=bk[:, c4 * 128:(c4 + 1) * 128], in_=xT[:, c, i * 128:(i + 1) * 128], identity=identF[:])
                        return ins
                    P.op("pe", tr, reads=[("xT", i // 4, hb), "identF"], writes=[bkey])
                    if hb == 0:
                        P.op("dve", lambda e, s=s, bk=bk: e.tensor_copy(out=ot[s][:, 0:512], in_=bk[:, :]), reads=[bkey], writes=[("ot", s, 0)])
                    else:
                        P.op("act", lambda e, s=s, bk=bk: e.activation(out=ot[s][:, 512:1024], in_=bk[:, :], func=AF.Identity),
                             reads=[bkey], writes=[("ot", s, 1)])
                P.dma("sp", lambda e, i=i, s=s: e.dma_start(out=O["out"][i * 128:(i + 1) * 128, :], in_=ot[s]),
                      reads=[("ot", s, 0), ("ot", s, 1)], writes=[("outd", i)])

        finals = []
        if stage == 1:
            ph_load_x()
        else:
            ph_load_xT()
            derive(pvA, "pvA", modA, "modA", sclA, gatA, "A")
            s2 = _DBG.get("s2", 9)
            ph_qu(pvA, "pvA", modA, "modA", sclA, "sclA", I["w_inA"])
            lam_init = ph_lambda(la)
            if s2 >= 2:
                ph_dil(pvA, "pvA")
            if s2 >= 3:
                ph_diff(pvA, "pvA", lam_init)
            if s2 >= 4:
                ph_conv(pvA, "pvA")
            if s2 >= 5:
                ph_merge(pvA, "pvA", gatA, "gatA", I["w_gateA"], I["w_brA"], I["w_outA"])
            if s2 >= 6:
                ph_ffn_full(2, pvA, "pvA", modA, "modA", sclA, "sclA", gatA, "gatA", I["w_fiA"], I["w_foA"])
        if lb is not None:
            up = _DBG.get("upto", 9) if _DBG.get("s2", 9) >= 7 else 0
            if up >= 1:
                ph_adaln(pvB, "pvB", I["w_adaB"], modB, "modB")
                derive(pvB, "pvB", modB, "modB", sclB, gatB, "B")
            if up >= 2:
                ph_ffn_full(0, pvB, "pvB", modB, "modB", sclB, "sclB", gatB, "gatB", I["w_fiB"], I["w_foB"])
            if up >= 3:
                ph_kv(pvB, "pvB", modB, "modB", sclB, "sclB", I["w_inB"])
            ph_store_x()
            P.dma("sp", lambda e: e.dma_start(out=O["modB"][:, :], in_=modB[:]), reads=["modB"], writes=["modBout"])
        else:
            ph_out()
        P.barrier()
        P.op("pool", lambda e: e.memset(neglam[:, 3:4], 0.0), writes=["__tail"])
        P.emit(["__tail"])
    return nc


_CACHE = {}
_DBG = {}


def _prog(stage):
    if stage not in _CACHE:
        _CACHE[stage] = build(stage)
    return _CACHE[stage]


def _mixer_inputs(inp, l, prev_res, dbias, fbias):
    maps = []
    for c in range(NCORES):
        own = prev_res[c]
        prv = prev_res[c - 1] if c % 2 == 1 else prev_res[c]
        maps.append({
            "xT_in": own["xT_out"], "modA": own["modB"], "pvA": _host_pv(inp, l, c),
            "w_inA": inp["w_in"][l], "w_gateA": inp["w_gate"][l], "w_brA": inp["w_branch"][l].reshape(1536, D),
            "w_outA": inp["w_out"][l], "w_fiA": inp["w_ffn_in"][l, 1], "w_foA": inp["w_ffn_out"][l, 1],
            "lamA": inp["lambda_vec"][l].reshape(1, 256), "subgA": inp["subln_g"][l].reshape(1, 128),
            "dbias": dbias, "fbias": fbias, "ut_p": prv["ut"],
            "kaT_o": own["kaT"], "kbT_o": own["kbT"], "va_o": own["va"], "vb_o": own["vb"],
            "kaT_p": prv["kaT"], "kbT_p": prv["kbT"], "va_p": prv["va"], "vb_p": prv["vb"],
        })
    return maps


def _ffn_kv_inputs(inp, l, c):
    return {"pvB": _host_pv(inp, l, c), "w_adaB": inp["w_ada"][l], "w_fiB": inp["w_ffn_in"][l, 0],
            "w_foB": inp["w_ffn_out"][l, 0], "w_inB": inp["w_in"][l]}


def kernel(**inputs):
    inp = {k: np.ascontiguousarray(np.asarray(v, dtype=np.float32)) for k, v in inputs.items()}
    dbias, fbias = _host_bias_tables(inp["rel_bias"])
    cores = list(range(NCORES))
    m1 = []
    for c in cores:
        b, hf = c // 2, c % 2
        d = {"x": np.ascontiguousarray(inp["x"][b, hf * T:(hf + 1) * T, :])}
        d.update(_ffn_kv_inputs(inp, 0, c))
        m1.append(d)
    r1 = run_bass_kernel_spmd(_prog(1), m1, core_ids=cores).results
    m2 = _mixer_inputs(inp, 0, r1, dbias, fbias)
    for c in cores:
        m2[c].update(_ffn_kv_inputs(inp, 1, c))
    r2 = run_bass_kernel_spmd(_prog(2), m2, core_ids=cores).results
    m3 = _mixer_inputs(inp, 1, r2, dbias, fbias)
    r3 = run_bass_kernel_spmd(_prog(3), m3, core_ids=cores).results
    out = np.empty((4, 2 * T, D), np.float32)
    for c in cores:
        out[c // 2, (c % 2) * T:(c % 2 + 1) * T, :] = r3[c]["out"]
    return out
```

```python
import numpy as np
import concourse.bass as bass
import concourse.mybir as mybir
from concourse.bass_utils import run_bass_kernel_spmd

F32 = mybir.dt.float32
BF16 = mybir.dt.bfloat16
AF = mybir.ActivationFunctionType
ALU = mybir.AluOpType

D = 1024
DC = 8
T = 2048
TT = 512
NT = T // TT
DFF = 2816
FC = 22
NCORES = 8


class _Op:
    __slots__ = ("eng", "fn", "deps", "is_dma", "sig", "tok", "idx")


class Prog:
    ENGS = ("pe", "act", "dve", "pool", "sp")
    NDMA = 6

    def __init__(self, nc):
        self.nc = nc
        self.ops = []
        self.last_w = {}
        self.readers = {}
        self.last_c = {}
        self.dma_hist = {}
        self.pending = {}

    def _add(self, eng, fn, reads, writes, is_dma):
        op = _Op()
        op.eng, op.fn, op.is_dma, op.sig, op.tok = eng, fn, is_dma, False, None
        op.idx = len(self.ops)
        deps = {}
        for r in reads:
            w = self.last_w.get(r)
            if w is not None:
                deps[w.idx] = (w, True)
        for wkey in writes:
            w = self.last_w.get(wkey)
            if w is not None and w.idx not in deps:
                deps[w.idx] = (w, False)
            for rd in self.readers.get(wkey, ()):
                if rd.idx not in deps:
                    deps[rd.idx] = (rd, False)
        keep = []
        for d in self.pending.pop(eng, ()):
            if d.eng == eng and not d.is_dma and eng == "pe":
                continue
            if d.idx not in deps:
                keep.append(d)
                d.sig = True
        for d, raw in deps.values():
            if d.eng == eng and not d.is_dma and not is_dma:
                if eng == "pe" or not raw:
                    continue
            keep.append(d)
            d.sig = True
        op.deps = keep
        for r in reads:
            self.readers.setdefault(r, []).append(op)
        for wkey in writes:
            self.last_w[wkey] = op
            self.readers[wkey] = []
        self.ops.append(op)
        if is_dma:
            self.dma_hist.setdefault(eng, []).append(op)
        else:
            self.last_c[eng] = op
        return op

    def barrier(self):
        B = list(self.last_c.values())
        for q, h in self.dma_hist.items():
            B.extend(h[-self.NDMA:])
        for e in self.ENGS:
            self.pending[e] = list(self.pending.get(e, ())) + B

    def op(self, eng, fn, reads=(), writes=()):
        reads, writes = tuple(reads), tuple(writes)
        extra = tuple(r for r in reads if (r == "psb" or (isinstance(r, tuple) and r[0] == "ps")) and r not in writes)
        return self._add(eng, fn, reads, writes + extra, False)

    def dma(self, eng, fn, reads=(), writes=()):
        return self._add(eng, fn, tuple(reads), tuple(writes), True)

    def emit(self, final_keys):
        nc = self.nc
        import contextlib
        with contextlib.ExitStack() as st:
            esem = {e: st.enter_context(nc.semaphore("s_" + e)) for e in self.ENGS}
            dsem = {e: [st.enter_context(nc.semaphore("d_%s%d" % (e, i))) for i in range(self.NDMA)]
                    for e in ("sp", "pool", "act")}
            ecnt = {e: 0 for e in self.ENGS}
            dcnt = {e: [0] * self.NDMA for e in dsem}
            drr = {e: 0 for e in dsem}
            finals = [self.last_w[k] for k in final_keys]
            for f in finals:
                f.sig = True
            prewait = {}
            for op in self.ops:
                if op.is_dma:
                    k = drr[op.eng] % self.NDMA
                    drr[op.eng] += 1
                    prewait[op.idx] = (dsem[op.eng][k], dcnt[op.eng][k])
                    dcnt[op.eng][k] += 16
                    op.tok = (dsem[op.eng][k], dcnt[op.eng][k])
                elif op.sig:
                    ecnt[op.eng] += 1
                    op.tok = (esem[op.eng], ecnt[op.eng])
            assert max(ecnt.values()) < 60000, ecnt
            per = {e: [o for o in self.ops if o.eng == e] for e in self.ENGS}
            block = st.enter_context(nc.Block())

            def run(eng_obj, ename, tail):
                waited = {}

                def w(sem, val):
                    if val <= 0:
                        return
                    key = id(sem)
                    if waited.get(key, 0) >= val:
                        return
                    waited[key] = val
                    eng_obj.wait_ge(sem, val)
                for op in per[ename]:
                    for d in op.deps:
                        w(*d.tok)
                    if op.is_dma:
                        w(*prewait[op.idx])
                    ins = op.fn(eng_obj)
                    if op.tok is not None:
                        ins.then_inc(op.tok[0], 16 if op.is_dma else 1)
                if tail:
                    for f in finals:
                        w(*f.tok)

            @block.tensor
            def _(e):
                run(e, "pe", False)

            @block.scalar
            def _(e):
                run(e, "act", False)

            @block.vector
            def _(e):
                run(e, "dve", False)

            @block.gpsimd
            def _(e):
                run(e, "pool", False)

            @block.sync
            def _(e):
                run(e, "sp", True)


def _rs(ap, shape):
    shape = list(shape)
    if len(shape) == 1:
        return ap
    names = "abcd"[:len(shape)]
    kw = {names[i]: shape[i] for i in range(len(shape))}
    return ap.rearrange("p (%s) -> p %s" % (" ".join(names), " ".join(names)), **kw)


class Arena:
    def __init__(self, ap_f32, nwords):
        self.base, self.n, self.off = ap_f32, nwords, 0

    def reset(self):
        self.off = 0

    def f32(self, *shape):
        n = int(np.prod(shape))
        a = self.base[:, self.off:self.off + n]
        self.off += n
        assert self.off <= self.n, (self.off, self.n)
        return _rs(a, shape)

    def bf(self, *shape):
        n = int(np.prod(shape))
        w = (n + 1) // 2
        a = self.base[:, self.off:self.off + w].bitcast(BF16)
        self.off += w
        assert self.off <= self.n, (self.off, self.n)
        return _rs(a[:, 0:n], shape)


PV_NG = 0
PV_BADA = 24
PV_GAIN = 96
PV_CW = 100
PV_CB = 224
PV_LG = 228
PV_LB = 232
PV_BG = 236
PV_CS = 260
PV_PF = 268
PV_PB = 269
NPV = 272

LW = 2688
EPS = 1e-6
DIL = ((128, 1), (512, 4), (2048, 16))


def _t5_bucket_np(dist):
    dist = np.asarray(dist, np.int64)
    dd = np.maximum(dist.astype(np.float32), np.float32(1.0))
    large = 16 + (np.log(dd / np.float32(16.0)) / np.float32(np.log(2048.0 / 16.0)) * np.float32(16.0)).astype(np.int32)
    large = np.minimum(large, 31)
    return np.where(dist < 16, dist, large).astype(np.int64)


def _host_bias_tables(rel_bias):
    NEG = np.float32(-1e30)
    k = np.arange(128)[:, None]
    dbias = np.empty((128, 8, 3, 256), np.float32)
    for p, (win, d) in enumerate(DIL):
        j = np.arange(256)[None, :]
        rel = np.where(j < 128, j + 128 - k, j - 128 - k)
        valid = (rel >= 0) & (rel <= 128)
        b = _t5_bucket_np(np.maximum(rel, 0) * d)
        for h in range(8):
            dbias[:, h, p, :] = np.where(valid, rel_bias[b, h], NEG)
    j = np.arange(LW)[None, :]
    dist = j - 384 - k
    b = _t5_bucket_np(np.maximum(dist, 0))
    fbias = np.empty((128, 4, LW), np.float32)
    for h in range(4):
        fbias[:, h, :] = np.where(dist >= 0, rel_bias[b, 8 + h], NEG)
    return dbias, fbias


def _host_pv(inp, l, core):
    b = core // 2
    pv = np.zeros((128, NPV), np.float32)
    pv[:, PV_NG:PV_NG + 24] = inp["norm_g"][l].reshape(3, 8, 128).transpose(2, 0, 1).reshape(128, 24)
    pv[:, PV_BADA:PV_BADA + 72] = inp["b_ada"][l].reshape(9, 8, 128).transpose(2, 0, 1).reshape(128, 72)
    g = inp["qk_gain"][l]
    pv[:, PV_GAIN + 0] = np.concatenate([g[0], g[0]])
    pv[:, PV_GAIN + 1] = np.concatenate([g[1], g[1]])
    pv[:, PV_GAIN + 2] = np.concatenate([g[2], g[3]])
    pv[:, PV_GAIN + 3] = np.concatenate([g[4], g[5]])
    pv[:, PV_CW:PV_CW + 124] = inp["conv_w"][l].reshape(31, 4, 128).transpose(2, 1, 0).reshape(128, 124)
    pv[:, PV_CB:PV_CB + 4] = inp["conv_b"][l].reshape(4, 128).T
    pv[:, PV_LG:PV_LG + 4] = inp["conv_ln_g"][l].reshape(4, 128).T
    pv[:, PV_LB:PV_LB + 4] = inp["conv_ln_b"][l].reshape(4, 128).T
    pv[:, PV_BG:PV_BG + 24] = inp["b_gate"][l].reshape(3, 8, 128).transpose(2, 0, 1).reshape(128, 24)
    pv[:, PV_CS:PV_CS + 8] = inp["c"][b].reshape(8, 128).T
    pv[:, PV_PF] = 1.0 if core % 2 == 1 else 0.0
    pv[:, PV_PB] = 0.0 if core % 2 == 1 else -30000.0
    return pv


def build(stage, dbg=False):
    import contextlib
    nc = bass.Bass("TRN2", target_bir_lowering=False)
    la = {1: None, 2: 0, 3: 1}[stage]
    lb = {1: 0, 2: 1, 3: None}[stage]

    def din(name, shape, dt=F32):
        return nc.dram_tensor(name, list(shape), dt, kind="ExternalInput").ap()

    def dout(name, shape, dt=F32):
        return nc.dram_tensor(name, list(shape), dt, kind="ExternalOutput").ap()

    def dscr(name, shape, dt=F32):
        return nc.dram_tensor(name, list(shape), dt, kind=("ExternalOutput" if dbg else "Internal")).ap()

    I = {}
    if stage == 1:
        I["x"] = din("x", [T, D])
    else:
        I["xT_in"] = din("xT_in", [128, DC, T])
        I["modA"] = din("modA", [128, 72])
        I["pvA"] = din("pvA", [128, NPV])
        for n, s in (("w_inA", [D, 4096]), ("w_gateA", [D, 3072]), ("w_brA", [1536, D]), ("w_outA", [D, D]),
                     ("w_fiA", [D, 2 * DFF]), ("w_foA", [DFF, D]), ("lamA", [1, 256]), ("subgA", [1, 128]),
                     ("dbias", [128, 8, 3, 256]), ("fbias", [128, 4, LW]), ("ut_p", [128, 4, 32])):
            I[n] = din(n, s)
        for n in ("kaT_o", "kbT_o", "kaT_p", "kbT_p"):
            I[n] = din(n, [128, 4, T], BF16)
        for n in ("va_o", "vb_o", "va_p", "vb_p"):
            I[n] = din(n, [T, 512], BF16)
    if lb is not None:
        I["pvB"] = din("pvB", [128, NPV])
        for n, s in (("w_adaB", [D, 9 * D]), ("w_fiB", [D, 2 * DFF]), ("w_foB", [DFF, D]), ("w_inB", [D, 4096])):
            I[n] = din(n, s)
    O = {}
    if stage < 3:
        O["xT_out"] = dout("xT_out", [128, DC, T])
        O["modB"] = dout("modB", [128, 72])
        O["kaT"] = dout("kaT", [128, 4, T], BF16)
        O["kbT"] = dout("kbT", [128, 4, T], BF16)
        O["va"] = dout("va", [T, 512], BF16)
        O["vb"] = dout("vb", [T, 512], BF16)
        O["ut"] = dout("ut", [128, 4, 32])
    else:
        O["out"] = dout("out", [T, D])
    S = {}
    if la is not None:
        S["h2T"] = dscr("h2T_s", [128, DC, T], BF16)
        S["qaT"] = dscr("qaT_s", [128, 4, T], BF16)
        S["qbT"] = dscr("qbT_s", [128, 4, T], BF16)
        S["uT"] = dscr("uT_s", [128, 4, 32 + T])
        S["yaT"] = dscr("yaT_s", [512, T], BF16)
        S["ybT"] = dscr("ybT_s", [512, T], BF16)
        S["ycT"] = dscr("ycT_s", [128, 4, T], BF16)
        S["vaf"] = dscr("vaf_s", [2 * T, 512], BF16)
        S["vbf"] = dscr("vbf_s", [2 * T, 512], BF16)

    with contextlib.ExitStack() as st:
        def sb(name, shape, dt):
            return st.enter_context(nc.sbuf_tensor(name, shape, dt))
        xT = sb("xT", [128, DC, T], F32)
        identF = sb("identF", [128, 128], F32)
        onesF = sb("onesF", [128, 128], F32)
        identB = sb("identB", [128, 128], BF16)
        onesB = sb("onesB", [128, 128], BF16)
        blkB = sb("blkB", [128, 128], BF16)
        pvA = sb("pvA_t", [128, NPV], F32)
        pvB = sb("pvB_t", [128, NPV], F32)
        modA = sb("modA_t", [128, 72], F32)
        modB = sb("modB_t", [128, 72], F32)
        sclA = sb("sclA", [128, 3, 8], F32)
        gatA = sb("gatA", [128, 3, 8], F32)
        sclB = sb("sclB", [128, 3, 8], F32)
        gatB = sb("gatB", [128, 3, 8], F32)
        neglam = sb("neglam", [128, 4], F32)
        NA = 33600
        arena_t = sb("arena", [128, NA], F32)
        A = Arena(arena_t[:, :], NA)
        ps = [st.enter_context(nc.psum_tensor("ps%d" % i, [128, 512], F32)) for i in range(7)]
        psb = st.enter_context(nc.psum_tensor("psb", [128, 1024], BF16))
        P = Prog(nc)
        rr = [0]

        def pb(lo=0, hi=7):
            i = lo + rr[0] % (hi - lo)
            rr[0] += 1
            return ps[i], ("ps", i)

        P.op("pool", lambda e: e.memset(identF[:], 0.0), writes=["identF"])
        P.op("pool", lambda e: e.affine_select(out=identF[:], in_=identF[:], pattern=[[-1, 128]],
                                                compare_op=ALU.not_equal, fill=1.0, base=0, channel_multiplier=1),
             reads=["identF"], writes=["identF"])
        P.op("pool", lambda e: e.memset(onesF[:], 1.0), writes=["onesF"])
        P.op("pool", lambda e: e.memset(onesB[:], 1.0), writes=["onesB"])
        P.op("pool", lambda e: e.memset(blkB[:], 0.0), writes=["blkB"])
        P.op("pool", lambda e: e.memset(blkB[0:64, 0:64], 1.0), reads=["blkB"], writes=["blkB"])
        P.op("pool", lambda e: e.memset(blkB[64:128, 64:128], 1.0), reads=["blkB"], writes=["blkB"])
        P.op("dve", lambda e: e.tensor_copy(out=identB[:], in_=identF[:]), reads=["identF"], writes=["identB"])
        if la is not None:
            P.dma("sp", lambda e: e.dma_start(out=pvA[:], in_=I["pvA"][:, :]), writes=["pvA"])
            P.dma("sp", lambda e: e.dma_start(out=modA[:], in_=I["modA"][:, :]), writes=["modA"])
        if lb is not None:
            P.dma("sp", lambda e: e.dma_start(out=pvB[:], in_=I["pvB"][:, :]), writes=["pvB"])

        def derive(pv, pvk, mod, modk, scl, gat, tag):
            for n in range(3):
                P.op("dve", lambda e, n=n: e.scalar_tensor_tensor(
                    out=scl[:, n, :], in0=mod[:, (3 * n + 1) * 8:(3 * n + 2) * 8], scalar=1.0,
                    in1=pv[:, PV_NG + n * 8:PV_NG + n * 8 + 8], op0=ALU.add, op1=ALU.mult),
                    reads=[modk, pvk], writes=["scl" + tag])
                P.op("dve", lambda e, n=n: e.tensor_scalar(
                    out=gat[:, n, :], in0=mod[:, (3 * n + 2) * 8:(3 * n + 3) * 8],
                    scalar1=(1.0 if n == 1 else 0.5), scalar2=None, op0=ALU.mult),
                    reads=[modk], writes=["gat" + tag])

        def fresh(mark=0):
            P.barrier()
            A.off = mark

        def ph_load_x():
            A.reset()
            xin = [A.f32(1024) for _ in range(2)]
            for i in range(16):
                s = i % 2
                P.dma("sp", lambda e, i=i, s=s: e.dma_start(out=xin[s], in_=I["x"][i * 128:(i + 1) * 128, :]),
                      writes=[("xin", s)])
                for hb in range(2):
                    bk, bkey = pb()

                    def tr(e, s=s, hb=hb, bk=bk):
                        for c4 in range(4):
                            c = hb * 4 + c4
                            ins = e.transpose(out=bk[:, c4 * 128:(c4 + 1) * 128],
                                              in_=xin[s][:, c * 128:(c + 1) * 128], identity=identF[:])
                        return ins
                    P.op("pe", tr, reads=[("xin", s), "identF"], writes=[bkey])
                    dst = xT[:, hb * 4:(hb + 1) * 4, i * 128:(i + 1) * 128]
                    if hb == 0:
                        P.op("dve", lambda e, dst=dst, bk=bk: e.tensor_copy(out=dst, in_=_rs(bk[:, :], [4, 128])),
                             reads=[bkey], writes=[("xT", i // 4, hb)])
                    else:
                        P.op("act", lambda e, dst=dst, bk=bk: e.activation(out=dst, in_=_rs(bk[:, :], [4, 128]),
                                                                          func=AF.Identity),
                             reads=[bkey], writes=[("xT", i // 4, hb)])

        def xkeys(tt):
            return [("xT", tt, 0), ("xT", tt, 1)]

        def ph_load_xT():
            for c in range(DC):
                P.dma("sp", lambda e, c=c: e.dma_start(out=xT[:, c, :], in_=I["xT_in"][:, c, :]),
                      writes=[("xT", tt, hb) for tt in range(NT) for hb in range(2)])

        def ph_adaln(pv, pvk, w_ada, mod, modk):
            A.reset()
            P.barrier()
            csb = A.bf(8)
            wa = [A.bf(8, 1024) for _ in range(2)]
            P.op("act", lambda e: e.activation(out=csb, in_=pv[:, PV_CS:PV_CS + 8], func=AF.Silu),
                 reads=[pvk], writes=["csb"])
            bk, bkey = ps[6], ("ps", 6)
            for j in range(9):
                s = j % 2
                P.dma("pool", lambda e, j=j, s=s: e.dma_start(
                    out=wa[s], in_=w_ada[:, j * 1024:(j + 1) * 1024].rearrange("(c p) n -> p c n", p=128)),
                    writes=[("wa", s)])

                def mm(e, j=j, s=s):
                    for cb in range(8):
                        for kc in range(8):
                            ins = e.matmul(bk[:, j * 8 + cb:j * 8 + cb + 1], lhsT=wa[s][:, kc, cb * 128:(cb + 1) * 128],
                                           rhs=csb[:, kc:kc + 1], start=(kc == 0), stop=(kc == 7))
                    return ins
                P.op("pe", mm, reads=[("wa", s), "csb"], writes=[bkey])
            P.op("dve", lambda e: e.tensor_tensor(out=mod[:, :], in0=bk[:, 0:72], in1=pv[:, PV_BADA:PV_BADA + 72],
                                                  op=ALU.add), reads=[bkey, pvk], writes=[modk])

        def ph_norm(n, pv, pvk, mod, modk, scl, sclk, hT):
            sqb = A.bf(8, TT)
            rs = [A.f32(TT) for _ in range(2)]
            tmpf = [A.f32(TT) for _ in range(2)]
            k = 0
            for tt in range(NT):
                sl = slice(tt * TT, (tt + 1) * TT)
                P.op("act", lambda e, sl=sl: e.activation(out=sqb, in_=xT[:, :, sl], func=AF.Square),
                     reads=xkeys(tt), writes=["sqb"])
                bk, bkey = pb()

                def mm(e, bk=bk):
                    for c in range(DC):
                        ins = e.matmul(bk[:, :], lhsT=onesB[:], rhs=sqb[:, c, :], start=(c == 0), stop=(c == DC - 1))
                    return ins
                P.op("pe", mm, reads=["sqb", "onesB"], writes=[bkey])
                r = rs[tt % 2]
                rk = ("rs", tt % 2)
                P.op("act", lambda e, r=r, bk=bk: e.activation(out=r, in_=bk[:, :], func=AF.Ln, scale=1.0 / D, bias=EPS),
                     reads=[bkey], writes=[rk])
                P.op("act", lambda e, r=r: e.activation(out=r, in_=r, func=AF.Exp, scale=-0.5), reads=[rk], writes=[rk])
                for c in range(DC):
                    tf = tmpf[k % 2]
                    tk = ("tmpf", k % 2)
                    k += 1
                    P.op("dve", lambda e, c=c, sl=sl, tf=tf, r=r: e.tensor_tensor(out=tf, in0=xT[:, c, sl], in1=r, op=ALU.mult),
                         reads=xkeys(tt) + [rk], writes=[tk])
                    P.op("act", lambda e, c=c, sl=sl, tf=tf: e.activation(
                        out=hT[:, c, sl], in_=tf, func=AF.Identity, scale=scl[:, n, c:c + 1],
                        bias=mod[:, 3 * n * 8 + c:3 * n * 8 + c + 1]),
                        reads=[tk, sclk, modk], writes=[("h", tt)])

        def ph_ffn(w_up, w_dn, gat, gatk, n, hT):
            groups = [(0, 6), (6, 6), (12, 6), (18, 4)]
            actT = A.bf(6, T)
            wup = [A.bf(2, 8, 256) for _ in range(2)]
            wdn = [A.bf(6, D) for _ in range(2)]
            sgf = [A.f32(TT) for _ in range(2)]
            k = 0
            npair = 0
            for gi, (c0, gn) in enumerate(groups):
                ws = gi % 2
                P.dma("pool", lambda e, c0=c0, gn=gn, ws=ws: e.dma_start(
                    out=wdn[ws][:, 0:gn, :], in_=w_dn[c0 * 128:(c0 + gn) * 128, :].rearrange("(i p) n -> p i n", p=128)),
                    writes=[("wdn", ws)])
                for pi in range(gn // 2):
                    cpair = c0 + 2 * pi
                    us = npair % 2
                    npair += 1
                    for gu in range(2):
                        col = gu * DFF + cpair * 128
                        P.dma("pool", lambda e, us=us, gu=gu, col=col: e.dma_start(
                            out=wup[us][:, gu, :, :], in_=w_up[:, col:col + 256].rearrange("(c p) n -> p c n", p=128)),
                            writes=[("wup", us, gu)])
                    for ci in range(2):
                        il = 2 * pi + ci
                        for tt in range(NT):
                            sl = slice(tt * TT, (tt + 1) * TT)
                            bg, bgk = pb()
                            bu, buk = pb()

                            def mmg(e, us=us, ci=ci, sl=sl, bg=bg, gu=0):
                                for kc in range(DC):
                                    ins = e.matmul(bg[:, :], lhsT=wup[us][:, gu, kc, ci * 128:(ci + 1) * 128], rhs=hT[:, kc, sl],
                                                   start=(kc == 0), stop=(kc == DC - 1))
                                return ins
                            P.op("pe", mmg, reads=[("wup", us, 0), ("h", tt)], writes=[bgk])
                            P.op("pe", lambda e, us=us, ci=ci, sl=sl, bu=bu: mmg(e, us, ci, sl, bu, 1),
                                 reads=[("wup", us, 1), ("h", tt)], writes=[buk])
                            sg = sgf[k % 2]
                            sk = ("sgf", k % 2)
                            k += 1
                            P.op("act", lambda e, sg=sg, bg=bg: e.activation(out=sg, in_=bg[:, :], func=AF.Silu),
                                 reads=[bgk], writes=[sk])
                            P.op("dve", lambda e, sg=sg, bu=bu, il=il, sl=sl: e.tensor_tensor(
                                out=actT[:, il, sl], in0=bu[:, :], in1=sg, op=ALU.mult),
                                reads=[buk, sk], writes=[("actT", tt)])
                for dc in range(DC):
                    for tt in range(NT):
                        sl = slice(tt * TT, (tt + 1) * TT)
                        bd, bdk = pb()

                        def mmd(e, dc=dc, sl=sl, bd=bd, gn=gn, ws=ws):
                            for i in range(gn):
                                ins = e.matmul(bd[:, :], lhsT=wdn[ws][:, i, dc * 128:(dc + 1) * 128], rhs=actT[:, i, sl],
                                               start=(i == 0), stop=(i == gn - 1))
                            return ins
                        P.op("pe", mmd, reads=[("wdn", ws), ("actT", tt)], writes=[bdk])
                        P.op("dve", lambda e, dc=dc, sl=sl, bd=bd: e.scalar_tensor_tensor(
                            out=xT[:, dc, sl], in0=bd[:, :], scalar=gat[:, n, dc:dc + 1], in1=xT[:, dc, sl],
                            op0=ALU.mult, op1=ALU.add),
                            reads=[bdk, gatk, ("xT", tt, dc // 4)], writes=[("xT", tt, dc // 4)])

        def proj_norm(w_in, colbase, pv, pvk, gaincol, hT, dst):
            wq = A.bf(8, 512)
            kst = [A.bf(T) for _ in range(2)]
            sq = [A.bf(TT) for _ in range(2)]
            qf = [A.f32(TT) for _ in range(2)]
            rs = [A.f32(TT) for _ in range(2)]
            P.dma("pool", lambda e: e.dma_start(out=wq, in_=w_in[:, colbase:colbase + 512].rearrange("(c p) n -> p c n", p=128)),
                  writes=["wq"])
            k = 0
            for ch in range(4):
                ks = kst[ch % 2]
                kk = ("kst", ch % 2)
                for tt in range(NT):
                    sl = slice(tt * TT, (tt + 1) * TT)
                    u = k % 2
                    k += 1
                    bq, bqk = pb()

                    def mm(e, ch=ch, sl=sl, bq=bq):
                        for kc in range(DC):
                            ins = e.matmul(bq[:, :], lhsT=wq[:, kc, ch * 128:(ch + 1) * 128], rhs=hT[:, kc, sl],
                                           start=(kc == 0), stop=(kc == DC - 1))
                        return ins
                    P.op("pe", mm, reads=["wq", ("h", tt)], writes=[bqk])
                    P.op("act", lambda e, u=u, bq=bq: e.activation(out=sq[u], in_=bq[:, :], func=AF.Square),
                         reads=[bqk], writes=[("sq", u)])
                    P.op("dve", lambda e, u=u, bq=bq: e.tensor_copy(out=qf[u], in_=bq[:, :]), reads=[bqk], writes=[("qf", u)])
                    bs, bsk = pb()
                    P.op("pe", lambda e, u=u, bs=bs: e.matmul(bs[:, :], lhsT=blkB[:], rhs=sq[u], start=True, stop=True),
                         reads=[("sq", u), "blkB"], writes=[bsk])
                    P.op("act", lambda e, u=u, bs=bs: e.activation(out=rs[u], in_=bs[:, :], func=AF.Ln, scale=1.0 / 64, bias=EPS),
                         reads=[bsk], writes=[("prs", u)])
                    P.op("act", lambda e, u=u: e.activation(out=rs[u], in_=rs[u], func=AF.Exp, scale=-0.5),
                         reads=[("prs", u)], writes=[("prs", u)])
                    P.op("dve", lambda e, u=u, ks=ks, sl=sl: e.scalar_tensor_tensor(
                        out=ks[:, sl], in0=qf[u], scalar=pv[:, PV_GAIN + gaincol:PV_GAIN + gaincol + 1], in1=rs[u],
                        op0=ALU.mult, op1=ALU.mult), reads=[("qf", u), ("prs", u), pvk], writes=[kk])
                P.dma("sp", lambda e, ch=ch, ks=ks: e.dma_start(out=dst(ch), in_=ks), reads=[kk], writes=[("dst", colbase, ch)])

        def v_proj(w_in, colbase, hT, dst):
            wv = A.bf(8, 512)
            vst = [A.bf(512) for _ in range(2)]
            P.dma("pool", lambda e: e.dma_start(out=wv, in_=w_in[:, colbase:colbase + 512].rearrange("(c p) n -> p c n", p=128)),
                  writes=["wv"])
            for i in range(16):
                u = i % 2
                bv, bvk = pb()

                def mm(e, i=i, bv=bv):
                    for kc in range(DC):
                        ins = e.matmul(bv[:, :], lhsT=hT[:, kc, i * 128:(i + 1) * 128], rhs=wv[:, kc, :],
                                       start=(kc == 0), stop=(kc == DC - 1))
                    return ins
                P.op("pe", mm, reads=["wv", ("h", i // 4)], writes=[bvk])
                P.op("act", lambda e, u=u, bv=bv: e.activation(out=vst[u], in_=bv[:, :], func=AF.Identity),
                     reads=[bvk], writes=[("vst", u)])
                P.dma("sp", lambda e, i=i, u=u: e.dma_start(out=dst[i * 128:(i + 1) * 128, :], in_=vst[u]),
                      reads=[("vst", u)], writes=[("vdst", colbase, i)])

        def glu(w_in, hT, tts, sink):
            wu = A.bf(8, 1024)
            sgf = [A.f32(TT) for _ in range(2)]
            uf = [A.f32(TT) for _ in range(2)]
            P.dma("pool", lambda e: e.dma_start(out=wu, in_=w_in[:, 3072:4096].rearrange("(c p) n -> p c n", p=128)),
                  writes=["wu"])
            k = 0
            for ch in range(4):
                for tt in tts:
                    sl = slice(tt * TT, (tt + 1) * TT)
                    u = k % 2
                    k += 1
                    b1, b1k = pb()
                    b2, b2k = pb()

                    def mm(e, col, bk, sl=sl):
                        for kc in range(DC):
                            ins = e.matmul(bk[:, :], lhsT=wu[:, kc, col:col + 128], rhs=hT[:, kc, sl],
                                           start=(kc == 0), stop=(kc == DC - 1))
                        return ins
                    P.op("pe", lambda e, ch=ch, b1=b1, mm=mm: mm(e, ch * 128, b1), reads=["wu", ("h", tt)], writes=[b1k])
                    P.op("pe", lambda e, ch=ch, b2=b2, mm=mm: mm(e, 512 + ch * 128, b2), reads=["wu", ("h", tt)], writes=[b2k])
                    P.op("act", lambda e, u=u, b2=b2: e.activation(out=sgf[u], in_=b2[:, :], func=AF.Sigmoid),
                         reads=[b2k], writes=[("gsg", u)])
                    P.op("dve", lambda e, u=u, b1=b1: e.tensor_tensor(out=uf[u], in0=b1[:, :], in1=sgf[u], op=ALU.mult),
                         reads=[b1k, ("gsg", u)], writes=[("guf", u)])
                    sink(ch, tt, uf[u], ("guf", u))

        def ph_kv(pv, pvk, mod, modk, scl, sclk, w_in):
            A.reset()
            P.barrier()
            hT = A.bf(8, T)
            mark = A.off
            kvl = _DBG.get("kv", 9)
            ph_norm(1, pv, pvk, mod, modk, scl, sclk, hT)
            fresh(mark)
            if kvl >= 1:
                proj_norm(w_in, 512, pv, pvk, 1, hT, lambda ch: O["kaT"][:, ch, :])
                fresh(mark)
            if kvl >= 2:
                proj_norm(w_in, 2048, pv, pvk, 3, hT, lambda ch: O["kbT"][:, ch, :])
                fresh(mark)
            if kvl >= 3:
                v_proj(w_in, 1024, hT, O["va"])
                fresh(mark)
            if kvl >= 4:
                v_proj(w_in, 2560, hT, O["vb"])
                fresh(mark)
            if kvl < 5:
                return

            def sink(ch, tt, uf, key):
                P.dma("sp", lambda e, ch=ch, uf=uf: e.dma_start(out=O["ut"][:, ch, :], in_=uf[:, TT - 32:TT]),
                      reads=[key], writes=[("utout", ch)])
            glu(w_in, hT, [NT - 1], sink)

        def ph_store_x():
            for c in range(DC):
                P.dma("sp", lambda e, c=c: e.dma_start(out=O["xT_out"][:, c, :], in_=xT[:, c, :]),
                      reads=[("xT", tt, c // 4) for tt in range(NT)], writes=[("xout", c)])

        def ph_ffn_full(n, pv, pvk, mod, modk, scl, sclk, gat, gatk, w_up, w_dn):
            fresh()
            hT = A.bf(8, T)
            mark = A.off
            ph_norm(n, pv, pvk, mod, modk, scl, sclk, hT)
            fresh(mark)
            ph_ffn(w_up, w_dn, gat, gatk, n, hT)

        def ph_qu(pv, pvk, mod, modk, scl, sclk, w_in):
            fresh()
            hT = A.bf(8, T)
            mark = A.off
            ph_norm(1, pv, pvk, mod, modk, scl, sclk, hT)
            for c in range(DC):
                P.dma("sp", lambda e, c=c: e.dma_start(out=S["h2T"][:, c, :], in_=hT[:, c, :]),
                      reads=[("h", tt) for tt in range(NT)], writes=[("h2Ts", c)])
            fresh(mark)
            proj_norm(w_in, 0, pv, pvk, 0, hT, lambda ch: S["qaT"][:, ch, :])
            fresh(mark)
            proj_norm(w_in, 1536, pv, pvk, 2, hT, lambda ch: S["qbT"][:, ch, :])
            fresh(mark)
            utp = A.f32(4, 32)
            P.dma("sp", lambda e: e.dma_start(out=utp, in_=I["ut_p"][:, :, :]), writes=["utp"])
            P.op("dve", lambda e: e.tensor_scalar(out=utp, in0=utp, scalar1=pv[:, PV_PF:PV_PF + 1], scalar2=None, op0=ALU.mult),
                 reads=["utp", pvk], writes=["utp"])
            P.dma("sp", lambda e: e.dma_start(out=S["uT"][:, :, 0:32], in_=utp), reads=["utp"], writes=["uTs"])

            def sink(ch, tt, uf, key):
                P.dma("sp", lambda e, ch=ch, tt=tt, uf=uf: e.dma_start(out=S["uT"][:, ch, 32 + tt * TT:32 + (tt + 1) * TT], in_=uf),
                      reads=[key], writes=[("uTs", ch, tt)])
            glu(w_in, hT, list(range(NT)), sink)
            for nm, src_p, src_o in (("vaf", "va_p", "va_o"), ("vbf", "vb_p", "vb_o")):
                P.dma("sp", lambda e, nm=nm, src_p=src_p: e.dma_start(out=S[nm][0:T, :], in_=I[src_p][:, :]), writes=[(nm, 0)])
                P.dma("sp", lambda e, nm=nm, src_o=src_o: e.dma_start(out=S[nm][T:2 * T, :], in_=I[src_o][:, :]), writes=[(nm, 1)])

        def ph_dil(pv, pvk):
            fresh()
            qh = A.bf(T)
            kh = A.bf(2 * T)
            vbuf = [A.bf(32, 256) for _ in range(2)]
            pT = [A.bf(2, 256) for _ in range(2)]
            tmp = [A.f32(2, 256) for _ in range(2)]
            acc = A.f32(2, T)
            db = A.f32(2, 3, 256)
            rlow = A.f32(T)
            yn = A.bf(T)
            for s in range(2):
                P.op("pool", lambda e, s=s: e.memset(vbuf[s], 1.0), writes=[("vbuf", s, r) for r in range(16)])
            unit = 0
            allacc = [("acc", b) for b in range(16)]
            for hp in range(4):
                P.dma("sp", lambda e, hp=hp: e.dma_start(out=qh, in_=S["qaT"][:, hp, :]), writes=["qh"])
                P.dma("sp", lambda e, hp=hp: e.dma_start(out=kh[:, 0:T], in_=I["kaT_p"][:, hp, :]), writes=[("kh", 0)])
                P.dma("sp", lambda e, hp=hp: e.dma_start(out=kh[:, T:2 * T], in_=I["kaT_o"][:, hp, :]), writes=[("kh", 1)])
                P.dma("sp", lambda e, hp=hp: e.dma_start(out=db, in_=I["dbias"][:, 2 * hp:2 * hp + 2, :, :]), writes=["db"])
                pend = []

                def flush(n):
                    while len(pend) > n:
                        pend.pop(0)()
                for p, (win, d) in enumerate(DIL):
                    vs = (hp * 3 + p) % 2
                    vb = vbuf[vs]
                    nb = 16 // d
                    vview = S["vaf"].rearrange("(i d) c -> i d c", d=d)
                    i0_ = T // d - 128
                    for r in range(d):
                        for h in range(2):
                            src = vview[i0_:i0_ + 128 * (nb + 1), r, hp * 128 + h * 64:hp * 128 + h * 64 + 64].rearrange(
                                "(j p) c -> p j c", p=128)
                            P.dma("sp", lambda e, vb=vb, r=r, h=h, nb=nb, src=src: e.dma_start(
                                out=vb[:, r * (nb + 1):(r + 1) * (nb + 1), h * 128:h * 128 + 64], in_=src),
                                reads=[("vaf", 0), ("vaf", 1)], writes=[("vbuf", vs, r)])
                    khv = _rs(kh, [2 * T // d, d])
                    qhv = _rs(qh, [T // d, d])
                    for r in range(d):
                        for m in range(nb):
                            u = unit % 2
                            unit += 1
                            bS = [pb(), pb()]
                            bO, bOk = pb()

                            def st_(e, r=r, m=m, d=d, bS=bS, khv=khv, qhv=qhv):
                                for h in range(2):
                                    for jj in range(2):
                                        ki = T // d + 128 * (m - 1 + jj)
                                        ins = e.matmul(bS[h][0][:, jj * 128:(jj + 1) * 128],
                                                       lhsT=khv[h * 64:(h + 1) * 64, ki:ki + 128, r],
                                                       rhs=qhv[h * 64:(h + 1) * 64, 128 * m:128 * m + 128, r],
                                                       start=True, stop=True)
                                return ins
                            P.op("pe", st_, reads=["qh", ("kh", 0), ("kh", 1)], writes=[bS[0][1], bS[1][1]])
                            for h in range(2):
                                P.op("dve", lambda e, h=h, u=u, p=p, bS=bS: e.scalar_tensor_tensor(
                                    out=tmp[u][:, h, :], in0=bS[h][0][:, 0:256], scalar=0.125, in1=db[:, h, p, :],
                                    op0=ALU.mult, op1=ALU.add), reads=[bS[h][1], "db"], writes=[("dtmp", u, h)])
                                if m == 0:
                                    P.op("act", lambda e, h=h, u=u: e.activation(
                                        out=pT[u][:, h, 0:128], in_=tmp[u][:, h, 0:128], func=AF.Exp, bias=pv[:, PV_PB:PV_PB + 1]),
                                        reads=[("dtmp", u, h), pvk], writes=[("dpT", u, h)])
                                    P.op("act", lambda e, h=h, u=u: e.activation(
                                        out=pT[u][:, h, 128:256], in_=tmp[u][:, h, 128:256], func=AF.Exp),
                                        reads=[("dtmp", u, h)], writes=[("dpT", u, h)])
                                else:
                                    P.op("act", lambda e, h=h, u=u: e.activation(out=pT[u][:, h, :], in_=tmp[u][:, h, :], func=AF.Exp),
                                         reads=[("dtmp", u, h)], writes=[("dpT", u, h)])

                            def back(r=r, m=m, nb=nb, u=u, vb=vb, vs=vs, bO=bO, bOk=bOk, d=d, p=p):
                                def pv_(e):
                                    for h in range(2):
                                        for jj in range(2):
                                            ins = e.matmul(bO[:, h * 128:(h + 1) * 128],
                                                           lhsT=vb[:, r * (nb + 1) + m + jj, h * 128:(h + 1) * 128],
                                                           rhs=pT[u][:, h, jj * 128:(jj + 1) * 128], start=(jj == 0), stop=(jj == 1))
                                    return ins
                                P.op("pe", pv_, reads=[("dpT", u, 0), ("dpT", u, 1), ("vbuf", vs, r)], writes=[bOk])
                                av = acc.rearrange("p h (i d) -> p h i d", d=d)[:, :, 128 * m:128 * m + 128, r]
                                ak = [("acc", b) for b in range(d * m, d * (m + 1))]
                                if p == 0:
                                    P.op("dve", lambda e: e.tensor_copy(out=av, in_=_rs(bO[:, 0:256], [2, 128])),
                                         reads=[bOk], writes=ak)
                                else:
                                    P.op("dve", lambda e: e.tensor_tensor(out=av, in0=_rs(bO[:, 0:256], [2, 128]), in1=av, op=ALU.add),
                                         reads=[bOk] + ak, writes=ak)
                            pend.append(back)
                            flush(1)
                flush(0)
                for h in range(2):
                    P.op("dve", lambda e, h=h: e.reciprocal(out=acc[64:128, h, :], in_=acc[64:128, h, :]), reads=allacc, writes=allacc)
                    P.op("dve", lambda e, h=h: e.tensor_copy(out=rlow[0:64, :], in_=acc[64:128, h, :]), reads=allacc, writes=["rlow"])
                    P.op("dve", lambda e, h=h: e.tensor_tensor(out=yn[0:64, :], in0=acc[0:64, h, :], in1=rlow[0:64, :], op=ALU.mult),
                         reads=allacc + ["rlow"], writes=["yn"])
                    P.dma("sp", lambda e, hp=hp, h=h: e.dma_start(out=S["yaT"][(hp * 2 + h) * 64:(hp * 2 + h + 1) * 64, :], in_=yn[0:64, :]),
                          reads=["yn"], writes=[("yaTs", hp, h)])

        def ph_lambda(l):
            import math
            lam_init = 0.8 - 0.6 * math.exp(-0.3 * l)
            fresh()
            lv = A.f32(256)
            t = A.f32(128)
            s2 = A.f32(2)
            P.dma("sp", lambda e: e.dma_start(out=lv, in_=I["lamA"][0:1, :].partition_broadcast(128)), writes=["lv"])
            P.op("dve", lambda e: e.tensor_tensor(out=_rs(t, [2, 64]), in0=_rs(lv, [2, 2, 64])[:, :, 0, :],
                                                  in1=_rs(lv, [2, 2, 64])[:, :, 1, :], op=ALU.mult), reads=["lv"], writes=["lvt"])
            P.op("dve", lambda e: e.tensor_reduce(out=s2, in_=_rs(t, [2, 64]), axis=mybir.AxisListType.X, op=ALU.add),
                 reads=["lvt"], writes=["lvs"])
            P.op("act", lambda e: e.activation(out=s2, in_=s2, func=AF.Exp), reads=["lvs"], writes=["lvs"])
            P.op("dve", lambda e: e.tensor_tensor(out=neglam[:, 1:2], in0=s2[:, 1:2], in1=s2[:, 0:1], op=ALU.subtract),
                 reads=["lvs"], writes=["neglam1"])
            P.op("dve", lambda e: e.tensor_scalar(out=neglam[:, 0:1], in0=neglam[:, 1:2], scalar1=-lam_init, scalar2=None, op0=ALU.add),
                 reads=["neglam1"], writes=["neglam"])
            return lam_init

        def ph_diff(pv, pvk, lam_init):
            fresh()
            qh = A.bf(T)
            kh = A.bf(2 * T)
            vaug = A.bf(32, 130)
            W = A.f32(LW)
            NS = 3
            tmp = [A.f32(TT) for _ in range(NS)]
            pT = [A.bf(TT) for _ in range(NS)]
            sbank = [(ps[0], ("ps", 0)), (ps[1], ("ps", 1)), (ps[6], ("ps", 6))]
            o1 = A.f32(128)
            of = A.f32(128)
            sm = A.f32(8)
            ybt = [A.bf(128) for _ in range(4)]
            ybst = A.bf(TT)
            gsub = A.f32(128)
            P.op("pool", lambda e: e.memset(vaug, 1.0), writes=["vaug"])
            P.dma("sp", lambda e: e.dma_start(out=gsub, in_=I["subgA"][0:1, :].partition_broadcast(128)), writes=["gsub"])
            P.op("dve", lambda e: e.tensor_scalar(out=gsub, in0=gsub, scalar1=(1.0 - lam_init), scalar2=None, op0=ALU.mult),
                 reads=["gsub"], writes=["gsub"])
            accb = [ps[2], ps[3], ps[4], ps[5]]
            acck = [("ps", 2), ("ps", 3), ("ps", 4), ("ps", 5)]
            cnt = 0
            for h in range(4):
                P.dma("sp", lambda e, h=h: e.dma_start(out=qh, in_=S["qbT"][:, h, :]), writes=["qh"])
                P.dma("sp", lambda e, h=h: e.dma_start(out=kh[:, 0:T], in_=I["kbT_p"][:, h, :]), writes=[("kh", 0)])
                P.dma("sp", lambda e, h=h: e.dma_start(out=kh[:, T:2 * T], in_=I["kbT_o"][:, h, :]), writes=[("kh", 1)])
                P.dma("sp", lambda e, h=h: e.dma_start(
                    out=vaug[:, :, 0:128], in_=S["vbf"][:, h * 128:(h + 1) * 128].rearrange("(j p) c -> p j c", p=128)),
                    reads=[("vbf", 0), ("vbf", 1)], writes=["vaug"])
                P.dma("sp", lambda e, h=h: e.dma_start(out=W, in_=I["fbias"][:, h, :]), writes=["W"])
                pend = []
                finb = []

                def flush(n):
                    while len(pend) > n:
                        pend.pop(0)()
                since = 0
                for g in range(4):
                    for c in range(2):
                        nkb = 16 + 4 * g + 4
                        first = [True, True]
                        for kbi in range(nkb):
                            u = cnt % NS
                            cnt += 1
                            bS, bSk = sbank[u]
                            P.op("pe", lambda e, c=c, kbi=kbi, g=g, bS=bS: e.matmul(
                                bS[:, :], lhsT=kh[c * 64:(c + 1) * 64, kbi * 128:(kbi + 1) * 128],
                                rhs=qh[c * 64:(c + 1) * 64, g * TT:(g + 1) * TT], start=True, stop=True),
                                reads=["qh", ("kh", 0), ("kh", 1)], writes=[bSk])
                            delta = (T + TT * g) - 128 * kbi
                            off = delta + 384 if delta < 1792 else 2176
                            P.op("dve", lambda e, u=u, off=off, bS=bS: e.scalar_tensor_tensor(
                                out=tmp[u], in0=bS[:, :], scalar=0.125, in1=W[:, off:off + TT], op0=ALU.mult, op1=ALU.add),
                                reads=[bSk, "W"], writes=[("ftmp", u)])
                            if kbi < 16:
                                P.op("act", lambda e, u=u: e.activation(out=pT[u], in_=tmp[u], func=AF.Exp, bias=pv[:, PV_PB:PV_PB + 1]),
                                     reads=[("ftmp", u), pvk], writes=[("fpT", u)])
                            else:
                                P.op("act", lambda e, u=u: e.activation(out=pT[u], in_=tmp[u], func=AF.Exp),
                                     reads=[("ftmp", u)], writes=[("fpT", u)])
                            plan = []
                            for qb in range(4):
                                if kbi >= 16 and (4 * g + qb) < (kbi - 16):
                                    continue
                                stf = first[qb // 2]
                                first[qb // 2] = False
                                last = (kbi == 16 + 4 * g + qb)
                                plan.append((qb, stf, last))

                            def back(plan=plan, c=c, u=u, kbi=kbi):
                                def pv_(e):
                                    for qb, stf, last in plan:
                                        col = (qb % 2) * 256
                                        ins = e.matmul(accb[2 * c + qb // 2][:, col:col + 129], lhsT=pT[u][:, qb * 128:(qb + 1) * 128],
                                                       rhs=vaug[:, kbi, 0:129], start=stf, stop=last, skip_group_check=True)
                                    return ins
                                P.op("pe", pv_, reads=[("fpT", u), "vaug"], writes=[acck[2 * c], acck[2 * c + 1]])
                            pend.append(back)
                            flush(NS - 1)
                            since += 1
                            if finb and since >= 3:
                                finb.pop(0)()
                    flush(0)
                    if finb:
                        finb.pop(0)()
                    for qb in range(4):
                        col = (qb % 2) * 256
                        b1, b1k = accb[qb // 2], acck[qb // 2]
                        b2, b2k = accb[2 + qb // 2], acck[2 + qb // 2]
                        yb = ybt[qb]
                        P.op("dve", lambda e, b1=b1, col=col: e.reciprocal(out=sm[:, 0:1], in_=b1[:, col + 128:col + 129]),
                             reads=[b1k], writes=["sm0"])
                        P.op("dve", lambda e, b2=b2, col=col: e.reciprocal(out=sm[:, 1:2], in_=b2[:, col + 128:col + 129]),
                             reads=[b2k], writes=["sm1"])
                        P.op("dve", lambda e: e.tensor_tensor(out=sm[:, 2:3], in0=sm[:, 1:2], in1=neglam[:, 0:1], op=ALU.mult),
                             reads=["sm1", "neglam"], writes=["sm2"])
                        P.op("act", lambda e, b1=b1, col=col: e.activation(out=o1, in_=b1[:, col:col + 128], func=AF.Identity, scale=sm[:, 0:1]),
                             reads=[b1k, "sm0"], writes=["o1"])
                        P.op("dve", lambda e, b2=b2, col=col: e.scalar_tensor_tensor(
                            out=of, in0=b2[:, col:col + 128], scalar=sm[:, 2:3], in1=o1, op0=ALU.mult, op1=ALU.add),
                            reads=[b2k, "sm2", "o1"], writes=["of"])
                        P.op("act", lambda e: e.activation(out=o1, in_=of, func=AF.Square, accum_out=sm[:, 3:4]),
                             reads=["of", "o1"], writes=["o1", "sm3"])
                        P.op("act", lambda e: e.activation(out=sm[:, 4:5], in_=sm[:, 3:4], func=AF.Ln, scale=1.0 / 128, bias=EPS),
                             reads=["sm3"], writes=["sm4"])
                        P.op("act", lambda e: e.activation(out=sm[:, 4:5], in_=sm[:, 4:5], func=AF.Exp, scale=-0.5),
                             reads=["sm4"], writes=["sm4"])
                        P.op("dve", lambda e, yb=yb: e.scalar_tensor_tensor(
                            out=yb, in0=of, scalar=sm[:, 4:5], in1=gsub, op0=ALU.mult, op1=ALU.mult),
                            reads=["of", "sm4", "gsub"], writes=[("ybt", qb)])

                    def fin_b(h=h, g=g):
                        def tr(e):
                            for qb in range(4):
                                ins = e.transpose(out=psb[:, qb * 128:(qb + 1) * 128], in_=ybt[qb], identity=identB[:])
                            return ins
                        P.op("pe", tr, reads=[("ybt", qb) for qb in range(4)] + ["identB"], writes=["psb"])
                        P.op("act", lambda e: e.activation(out=ybst, in_=psb[:, 0:TT], func=AF.Identity), reads=["psb"], writes=["ybst"])
                        P.dma("sp", lambda e: e.dma_start(out=S["ybT"][h * 128:(h + 1) * 128, g * TT:(g + 1) * TT], in_=ybst),
                              reads=["ybst"], writes=[("ybTs", h, g)])
                    finb.append(fin_b)
                    since = 0
                while finb:
                    finb.pop(0)()

        def ph_conv(pv, pvk):
            fresh()
            ub = A.f32(32 + T)
            ycf = A.f32(4, T)
            ycb = A.bf(4, T)
            sqf = A.f32(4, TT)
            mf = A.f32(TT)
            vf = A.f32(TT)
            tf = [A.f32(TT) for _ in range(2)]
            for cc in range(4):
                P.dma("sp", lambda e, cc=cc: e.dma_start(out=ub, in_=S["uT"][:, cc, :]),
                      reads=["uTs"] + [("uTs", cc, tt) for tt in range(NT)], writes=["ub"])
                for j in range(31):
                    wcol = pv[:, PV_CW + cc * 31 + j:PV_CW + cc * 31 + j + 1]
                    src = ub[:, 2 + j:2 + j + T]
                    if j == 0:
                        P.op("dve", lambda e, cc=cc, wcol=wcol, src=src: e.tensor_scalar(
                            out=ycf[:, cc, :], in0=src, scalar1=wcol, scalar2=pv[:, PV_CB + cc:PV_CB + cc + 1],
                            op0=ALU.mult, op1=ALU.add), reads=["ub", pvk], writes=[("ycf", cc)])
                    else:
                        P.op("dve", lambda e, cc=cc, wcol=wcol, src=src: e.scalar_tensor_tensor(
                            out=ycf[:, cc, :], in0=src, scalar=wcol, in1=ycf[:, cc, :], op0=ALU.mult, op1=ALU.add),
                            reads=["ub", pvk, ("ycf", cc)], writes=[("ycf", cc)])
            k = 0
            allycf = [("ycf", cc) for cc in range(4)]
            for tt in range(NT):
                sl = slice(tt * TT, (tt + 1) * TT)
                P.op("act", lambda e, sl=sl: e.activation(out=sqf, in_=ycf[:, :, sl], func=AF.Square), reads=allycf, writes=["sqf"])
                b1, b1k = pb()
                b2, b2k = pb()

                def mm1(e, sl=sl, b1=b1):
                    for cc in range(4):
                        ins = e.matmul(b1[:, :], lhsT=onesF[:], rhs=ycf[:, cc, sl], start=(cc == 0), stop=(cc == 3))
                    return ins

                def mm2(e, b2=b2):
                    for cc in range(4):
                        ins = e.matmul(b2[:, :], lhsT=onesF[:], rhs=sqf[:, cc, :], start=(cc == 0), stop=(cc == 3))
                    return ins
                P.op("pe", mm1, reads=allycf + ["onesF"], writes=[b1k])
                P.op("pe", mm2, reads=["sqf", "onesF"], writes=[b2k])
                P.op("dve", lambda e, b1=b1: e.tensor_scalar(out=mf, in0=b1[:, :], scalar1=1.0 / 512, scalar2=None, op0=ALU.mult),
                     reads=[b1k], writes=["mf"])
                P.op("dve", lambda e: e.tensor_tensor(out=vf, in0=mf, in1=mf, op=ALU.mult), reads=["mf"], writes=["vf"])
                P.op("dve", lambda e, b2=b2: e.scalar_tensor_tensor(out=vf, in0=b2[:, :], scalar=1.0 / 512, in1=vf,
                                                                    op0=ALU.mult, op1=ALU.subtract),
                     reads=[b2k, "vf"], writes=["vf"])
                P.op("act", lambda e: e.activation(out=vf, in_=vf, func=AF.Ln, bias=EPS), reads=["vf"], writes=["vf"])
                P.op("act", lambda e: e.activation(out=vf, in_=vf, func=AF.Exp, scale=-0.5), reads=["vf"], writes=["vf"])
                for cc in range(4):
                    t_ = tf[k % 2]
                    tk = ("ctf", k % 2)
                    k += 1
                    P.op("dve", lambda e, cc=cc, sl=sl, t_=t_: e.tensor_tensor(out=t_, in0=ycf[:, cc, sl], in1=mf, op=ALU.subtract),
                         reads=allycf + ["mf"], writes=[tk])
                    P.op("dve", lambda e, t_=t_: e.tensor_tensor(out=t_, in0=t_, in1=vf, op=ALU.mult), reads=[tk, "vf"], writes=[tk])
                    P.op("act", lambda e, cc=cc, sl=sl, t_=t_: e.activation(
                        out=ycb[:, cc, sl], in_=t_, func=AF.Silu, scale=pv[:, PV_LG + cc:PV_LG + cc + 1],
                        bias=pv[:, PV_LB + cc:PV_LB + cc + 1]), reads=[tk, pvk], writes=[("ycb", cc)])
            for cc in range(4):
                P.dma("sp", lambda e, cc=cc: e.dma_start(out=S["ycT"][:, cc, :], in_=ycb[:, cc, :]), reads=[("ycb", cc)], writes=[("ycTs", cc)])

        def ph_merge(pv, pvk, gat, gatk, w_gate, w_br, w_out):
            fresh()
            wo = A.bf(8, D)
            hh = A.bf(8, 1024)
            yy = A.bf(3, 4, 1024)
            zT = A.bf(8, 1024)
            wg = [A.bf(3, 8, 128) for _ in range(2)]
            wb = [A.bf(3, 4, 128) for _ in range(2)]
            gsb = [A.f32(TT) for _ in range(2)]
            zacc = [A.f32(TT) for _ in range(2)]
            prod = [A.f32(TT) for _ in range(2)]
            P.dma("pool", lambda e: e.dma_start(out=wo, in_=w_out.rearrange("(c p) n -> p c n", p=128)), writes=["wo"])
            ysrc = [S["yaT"].rearrange("(c p) t -> p c t", p=128), S["ybT"].rearrange("(c p) t -> p c t", p=128), S["ycT"]]
            ykeys = [[("yaTs", hp, h) for hp in range(4) for h in range(2)],
                     [("ybTs", h, g) for h in range(4) for g in range(4)],
                     [("ycTs", cc) for cc in range(4)]]
            cnt = 0
            k = 0
            for half in range(2):
                hs = slice(half * 1024, (half + 1) * 1024)
                P.dma("sp", lambda e, hs=hs: e.dma_start(out=hh, in_=S["h2T"][:, :, hs]),
                      reads=[("h2Ts", c) for c in range(DC)], writes=["hh"])
                for i in range(3):
                    P.dma("sp", lambda e, i=i, hs=hs: e.dma_start(out=yy[:, i, :, :], in_=ysrc[i][:, :, hs]),
                          reads=ykeys[i], writes=[("yy", i)])
                for j in range(DC):
                    s = cnt % 2
                    cnt += 1
                    for i in range(3):
                        col = i * 1024 + j * 128
                        P.dma("pool", lambda e, s=s, i=i, col=col: e.dma_start(
                            out=wg[s][:, i, :, :], in_=w_gate[:, col:col + 128].rearrange("(c p) n -> p c n", p=128)),
                            writes=[("wg", s, i)])
                        P.dma("pool", lambda e, s=s, i=i, j=j: e.dma_start(
                            out=wb[s][:, i, :, :], in_=w_br[i * 512:(i + 1) * 512, j * 128:(j + 1) * 128].rearrange("(c p) n -> p c n", p=128)),
                            writes=[("wb", s, i)])
                    for t2 in range(2):
                        sl = slice(t2 * TT, (t2 + 1) * TT)
                        za = zacc[(2 * j + t2) % 2]
                        zk = ("zacc", (2 * j + t2) % 2)
                        for i in range(3):
                            u = k % 2
                            k += 1
                            bg, bgk = pb()
                            by, byk = pb()

                            def mg(e, s=s, i=i, sl=sl, bg=bg):
                                for kc in range(DC):
                                    ins = e.matmul(bg[:, :], lhsT=wg[s][:, i, kc, :], rhs=hh[:, kc, sl], start=(kc == 0), stop=(kc == DC - 1))
                                return ins

                            def my(e, s=s, i=i, sl=sl, by=by):
                                for kc in range(4):
                                    ins = e.matmul(by[:, :], lhsT=wb[s][:, i, kc, :], rhs=yy[:, i, kc, sl], start=(kc == 0), stop=(kc == 3))
                                return ins
                            P.op("pe", mg, reads=[("wg", s, i), "hh"], writes=[bgk])
                            P.op("pe", my, reads=[("wb", s, i), ("yy", i)], writes=[byk])
                            P.op("act", lambda e, u=u, bg=bg, i=i, j=j: e.activation(
                                out=gsb[u], in_=bg[:, :], func=AF.Sigmoid, bias=pv[:, PV_BG + i * 8 + j:PV_BG + i * 8 + j + 1]),
                                reads=[bgk, pvk], writes=[("gsb", u)])
                            if i == 0:
                                P.op("dve", lambda e, u=u, by=by, za=za: e.tensor_tensor(out=za, in0=by[:, :], in1=gsb[u], op=ALU.mult),
                                     reads=[byk, ("gsb", u)], writes=[zk])
                            else:
                                P.op("dve", lambda e, u=u, by=by: e.tensor_tensor(out=prod[u], in0=by[:, :], in1=gsb[u], op=ALU.mult),
                                     reads=[byk, ("gsb", u)], writes=[("prod", u)])
                                if i == 1:
                                    P.op("pool", lambda e, u=u, za=za: e.tensor_tensor(out=za, in0=za, in1=prod[u], op=ALU.add),
                                         reads=[zk, ("prod", u)], writes=[zk])
                                else:
                                    P.op("pool", lambda e, u=u, za=za, j=j, sl=sl: e.tensor_tensor(out=zT[:, j, sl], in0=za, in1=prod[u], op=ALU.add),
                                         reads=[zk, ("prod", u)], writes=[("zT", t2)])
                for dj in range(DC):
                    for t2 in range(2):
                        sl = slice(t2 * TT, (t2 + 1) * TT)
                        xs = slice(half * 1024 + t2 * TT, half * 1024 + (t2 + 1) * TT)
                        tt = half * 2 + t2
                        bd, bdk = pb()

                        def mo(e, dj=dj, sl=sl, bd=bd):
                            for kc in range(DC):
                                ins = e.matmul(bd[:, :], lhsT=wo[:, kc, dj * 128:(dj + 1) * 128], rhs=zT[:, kc, sl], start=(kc == 0), stop=(kc == DC - 1))
                            return ins
                        P.op("pe", mo, reads=["wo", ("zT", t2)], writes=[bdk])
                        P.op("dve", lambda e, dj=dj, xs=xs, bd=bd: e.scalar_tensor_tensor(
                            out=xT[:, dj, xs], in0=bd[:, :], scalar=gat[:, 1, dj:dj + 1], in1=xT[:, dj, xs], op0=ALU.mult, op1=ALU.add),
                            reads=[bdk, gatk, ("xT", tt, dj // 4)], writes=[("xT", tt, dj // 4)])

        def ph_out():
            fresh()
            ot = [A.f32(D) for _ in range(2)]
            for i in range(16):
                s = i % 2
                for hb in range(2):
                    bk, bkey = pb()

                    def tr(e, i=i, hb=hb, bk=bk):
                        for c4 in range(4):
                            c = hb * 4 + c4
                            ins = e.transpose(out=bk[:, c4 * 128:(c4 + 1) * 128], in_=xT[:, c, i * 128:(i + 1) * 128], identity=identF[:])
                        return ins
                    P.op("pe", tr, reads=[("xT", i // 4, hb), "identF"], writes=[bkey])
                    if hb == 0:
                        P.op("dve", lambda e, s=s, bk=bk: e.tensor_copy(out=ot[s][:, 0:512], in_=bk[:, :]), reads=[bkey], writes=[("ot", s, 0)])
                    else:
                        P.op("act", lambda e, s=s, bk=bk: e.activation(out=ot[s][:, 512:1024], in_=bk[:, :], func=AF.Identity),
                             reads=[bkey], writes=[("ot", s, 1)])
                P.dma("sp", lambda e, i=i, s=s: e.dma_start(out=O["out"][i * 128:(i + 1) * 128, :], in_=ot[s]),
                      reads=[("ot", s, 0), ("ot", s, 1)], writes=[("outd", i)])

        finals = []
        if stage == 1:
            ph_load_x()
        else:
            ph_load_xT()
            derive(pvA, "pvA", modA, "modA", sclA, gatA, "A")
            s2 = _DBG.get("s2", 9)
            ph_qu(pvA, "pvA", modA, "modA", sclA, "sclA", I["w_inA"])
            lam_init = ph_lambda(la)
            if s2 >= 2:
                ph_dil(pvA, "pvA")
            if s2 >= 3:
                ph_diff(pvA, "pvA", lam_init)
            if s2 >= 4:
                ph_conv(pvA, "pvA")
            if s2 >= 5:
                ph_merge(pvA, "pvA", gatA, "gatA", I["w_gateA"], I["w_brA"], I["w_outA"])
            if s2 >= 6:
                ph_ffn_full(2, pvA, "pvA", modA, "modA", sclA, "sclA", gatA, "gatA", I["w_fiA"], I["w_foA"])
        if lb is not None:
            up = _DBG.get("upto", 9) if _DBG.get("s2", 9) >= 7 else 0
            if up >= 1:
                ph_adaln(pvB, "pvB", I["w_adaB"], modB, "modB")
                derive(pvB, "pvB", modB, "modB", sclB, gatB, "B")
            if up >= 2:
                ph_ffn_full(0, pvB, "pvB", modB, "modB", sclB, "sclB", gatB, "gatB", I["w_fiB"], I["w_foB"])
            if up >= 3:
                ph_kv(pvB, "pvB", modB, "modB", sclB, "sclB", I["w_inB"])
            ph_store_x()
            P.dma("sp", lambda e: e.dma_start(out=O["modB"][:, :], in_=modB[:]), reads=["modB"], writes=["modBout"])
        else:
            ph_out()
        P.barrier()
        P.op("pool", lambda e: e.memset(neglam[:, 3:4], 0.0), writes=["__tail"])
        P.emit(["__tail"])
    return nc


_CACHE = {}
_DBG = {}


def _prog(stage):
    if stage not in _CACHE:
        _CACHE[stage] = build(stage)
    return _CACHE[stage]


def _mixer_inputs(inp, l, prev_res, dbias, fbias):
    maps = []
    for c in range(NCORES):
        own = prev_res[c]
        prv = prev_res[c - 1] if c % 2 == 1 else prev_res[c]
        maps.append({
            "xT_in": own["xT_out"], "modA": own["modB"], "pvA": _host_pv(inp, l, c),
            "w_inA": inp["w_in"][l], "w_gateA": inp["w_gate"][l], "w_brA": inp["w_branch"][l].reshape(1536, D),
            "w_outA": inp["w_out"][l], "w_fiA": inp["w_ffn_in"][l, 1], "w_foA": inp["w_ffn_out"][l, 1],
            "lamA": inp["lambda_vec"][l].reshape(1, 256), "subgA": inp["subln_g"][l].reshape(1, 128),
            "dbias": dbias, "fbias": fbias, "ut_p": prv["ut"],
            "kaT_o": own["kaT"], "kbT_o": own["kbT"], "va_o": own["va"], "vb_o": own["vb"],
            "kaT_p": prv["kaT"], "kbT_p": prv["kbT"], "va_p": prv["va"], "vb_p": prv["vb"],
        })
    return maps


def _ffn_kv_inputs(inp, l, c):
    return {"pvB": _host_pv(inp, l, c), "w_adaB": inp["w_ada"][l], "w_fiB": inp["w_ffn_in"][l, 0],
            "w_foB": inp["w_ffn_out"][l, 0], "w_inB": inp["w_in"][l]}


def kernel(**inputs):
    inp = {k: np.ascontiguousarray(np.asarray(v, dtype=np.float32)) for k, v in inputs.items()}
    dbias, fbias = _host_bias_tables(inp["rel_bias"])
    cores = list(range(NCORES))
    m1 = []
    for c in cores:
        b, hf = c // 2, c % 2
        d = {"x": np.ascontiguousarray(inp["x"][b, hf * T:(hf + 1) * T, :])}
        d.update(_ffn_kv_inputs(inp, 0, c))
        m1.append(d)
    r1 = run_bass_kernel_spmd(_prog(1), m1, core_ids=cores).results
    m2 = _mixer_inputs(inp, 0, r1, dbias, fbias)
    for c in cores:
        m2[c].update(_ffn_kv_inputs(inp, 1, c))
    r2 = run_bass_kernel_spmd(_prog(2), m2, core_ids=cores).results
    m3 = _mixer_inputs(inp, 1, r2, dbias, fbias)
    r3 = run_bass_kernel_spmd(_prog(3), m3, core_ids=cores).results
    out = np.empty((4, 2 * T, D), np.float32)
    for c in cores:
        out[c // 2, (c % 2) * T:(c % 2 + 1) * T, :] = r3[c]["out"]
    return out
```

```python
import numpy as np
import concourse.bass as bass
import concourse.mybir as mybir
from concourse.bass_utils import run_bass_kernel_spmd

F32 = mybir.dt.float32
BF16 = mybir.dt.bfloat16
AF = mybir.ActivationFunctionType
ALU = mybir.AluOpType

D = 1024
DC = 8
T = 2048
TT = 512
NT = T // TT
DFF = 2816
FC = 22
NCORES = 8


class _Op:
    __slots__ = ("eng", "fn", "deps", "is_dma", "sig", "tok", "idx")


class Prog:
    ENGS = ("pe", "act", "dve", "pool", "sp")
    NDMA = 6

    def __init__(self, nc):
        self.nc = nc
        self.ops = []
        self.last_w = {}
        self.readers = {}
        self.last_c = {}
        self.dma_hist = {}
        self.pending = {}

    def _add(self, eng, fn, reads, writes, is_dma):
        op = _Op()
        op.eng, op.fn, op.is_dma, op.sig, op.tok = eng, fn, is_dma, False, None
        op.idx = len(self.ops)
        deps = {}
        for r in reads:
            w = self.last_w.get(r)
            if w is not None:
                deps[w.idx] = (w, True)
        for wkey in writes:
            w = self.last_w.get(wkey)
            if w is not None and w.idx not in deps:
                deps[w.idx] = (w, False)
            for rd in self.readers.get(wkey, ()):
                if rd.idx not in deps:
                    deps[rd.idx] = (rd, False)
        keep = []
        for d in self.pending.pop(eng, ()):
            if d.eng == eng and not d.is_dma and eng == "pe":
                continue
            if d.idx not in deps:
                keep.append(d)
                d.sig = True
        for d, raw in deps.values():
            if d.eng == eng and not d.is_dma and not is_dma:
                if eng == "pe" or not raw:
                    continue
            keep.append(d)
            d.sig = True
        op.deps = keep
        for r in reads:
            self.readers.setdefault(r, []).append(op)
        for wkey in writes:
            self.last_w[wkey] = op
            self.readers[wkey] = []
        self.ops.append(op)
        if is_dma:
            self.dma_hist.setdefault(eng, []).append(op)
        else:
            self.last_c[eng] = op
        return op

    def barrier(self):
        B = list(self.last_c.values())
        for q, h in self.dma_hist.items():
            B.extend(h[-self.NDMA:])
        for e in self.ENGS:
            self.pending[e] = list(self.pending.get(e, ())) + B

    def op(self, eng, fn, reads=(), writes=()):
        reads, writes = tuple(reads), tuple(writes)
        extra = tuple(r for r in reads if (r == "psb" or (isinstance(r, tuple) and r[0] == "ps")) and r not in writes)
        return self._add(eng, fn, reads, writes + extra, False)

    def dma(self, eng, fn, reads=(), writes=()):
        return self._add(eng, fn, tuple(reads), tuple(writes), True)

    def emit(self, final_keys):
        nc = self.nc
        import contextlib
        with contextlib.ExitStack() as st:
            esem = {e: st.enter_context(nc.semaphore("s_" + e)) for e in self.ENGS}
            dsem = {e: [st.enter_context(nc.semaphore("d_%s%d" % (e, i))) for i in range(self.NDMA)]
                    for e in ("sp", "pool", "act")}
            ecnt = {e: 0 for e in self.ENGS}
            dcnt = {e: [0] * self.NDMA for e in dsem}
            drr = {e: 0 for e in dsem}
            finals = [self.last_w[k] for k in final_keys]
            for f in finals:
                f.sig = True
            prewait = {}
            for op in self.ops:
                if op.is_dma:
                    k = drr[op.eng] % self.NDMA
                    drr[op.eng] += 1
                    prewait[op.idx] = (dsem[op.eng][k], dcnt[op.eng][k])
                    dcnt[op.eng][k] += 16
                    op.tok = (dsem[op.eng][k], dcnt[op.eng][k])
                elif op.sig:
                    ecnt[op.eng] += 1
                    op.tok = (esem[op.eng], ecnt[op.eng])
            assert max(ecnt.values()) < 60000, ecnt
            per = {e: [o for o in self.ops if o.eng == e] for e in self.ENGS}
            block = st.enter_context(nc.Block())

            def run(eng_obj, ename, tail):
                waited = {}

                def w(sem, val):
                    if val <= 0:
                        return
                    key = id(sem)
                    if waited.get(key, 0) >= val:
                        return
                    waited[key] = val
                    eng_obj.wait_ge(sem, val)
                for op in per[ename]:
                    for d in op.deps:
                        w(*d.tok)
                    if op.is_dma:
                        w(*prewait[op.idx])
                    ins = op.fn(eng_obj)
                    if op.tok is not None:
                        ins.then_inc(op.tok[0], 16 if op.is_dma else 1)
                if tail:
                    for f in finals:
                        w(*f.tok)

            @block.tensor
            def _(e):
                run(e, "pe", False)

            @block.scalar
            def _(e):
                run(e, "act", False)

            @block.vector
            def _(e):
                run(e, "dve", False)

            @block.gpsimd
            def _(e):
                run(e, "pool", False)

            @block.sync
            def _(e):
                run(e, "sp", True)


def _rs(ap, shape):
    shape = list(shape)
    if len(shape) == 1:
        return ap
    names = "abcd"[:len(shape)]
    kw = {names[i]: shape[i] for i in range(len(shape))}
    return ap.rearrange("p (%s) -> p %s" % (" ".join(names), " ".join(names)), **kw)


class Arena:
    def __init__(self, ap_f32, nwords):
        self.base, self.n, self.off = ap_f32, nwords, 0

    def reset(self):
        self.off = 0

    def f32(self, *shape):
        n = int(np.prod(shape))
        a = self.base[:, self.off:self.off + n]
        self.off += n
        assert self.off <= self.n, (self.off, self.n)
        return _rs(a, shape)

    def bf(self, *shape):
        n = int(np.prod(shape))
        w = (n + 1) // 2
        a = self.base[:, self.off:self.off + w].bitcast(BF16)
        self.off += w
        assert self.off <= self.n, (self.off, self.n)
        return _rs(a[:, 0:n], shape)


PV_NG = 0
PV_BADA = 24
PV_GAIN = 96
PV_CW = 100
PV_CB = 224
PV_LG = 228
PV_LB = 232
PV_BG = 236
PV_CS = 260
PV_PF = 268
PV_PB = 269
NPV = 272

LW = 2688
EPS = 1e-6
DIL = ((128, 1), (512, 4), (2048, 16))


def _t5_bucket_np(dist):
    dist = np.asarray(dist, np.int64)
    dd = np.maximum(dist.astype(np.float32), np.float32(1.0))
    large = 16 + (np.log(dd / np.float32(16.0)) / np.float32(np.log(2048.0 / 16.0)) * np.float32(16.0)).astype(np.int32)
    large = np.minimum(large, 31)
    return np.where(dist < 16, dist, large).astype(np.int64)


def _host_bias_tables(rel_bias):
    NEG = np.float32(-1e30)
    k = np.arange(128)[:, None]
    dbias = np.empty((128, 8, 3, 256), np.float32)
    for p, (win, d) in enumerate(DIL):
        j = np.arange(256)[None, :]
        rel = np.where(j < 128, j + 128 - k, j - 128 - k)
        valid = (rel >= 0) & (rel <= 128)
        b = _t5_bucket_np(np.maximum(rel, 0) * d)
        for h in range(8):
            dbias[:, h, p, :] = np.where(valid, rel_bias[b, h], NEG)
    j = np.arange(LW)[None, :]
    dist = j - 384 - k
    b = _t5_bucket_np(np.maximum(dist, 0))
    fbias = np.empty((128, 4, LW), np.float32)
    for h in range(4):
        fbias[:, h, :] = np.where(dist >= 0, rel_bias[b, 8 + h], NEG)
    return dbias, fbias


def _host_pv(inp, l, core):
    b = core // 2
    pv = np.zeros((128, NPV), np.float32)
    pv[:, PV_NG:PV_NG + 24] = inp["norm_g"][l].reshape(3, 8, 128).transpose(2, 0, 1).reshape(128, 24)
    pv[:, PV_BADA:PV_BADA + 72] = inp["b_ada"][l].reshape(9, 8, 128).transpose(2, 0, 1).reshape(128, 72)
    g = inp["qk_gain"][l]
    pv[:, PV_GAIN + 0] = np.concatenate([g[0], g[0]])
    pv[:, PV_GAIN + 1] = np.concatenate([g[1], g[1]])
    pv[:, PV_GAIN + 2] = np.concatenate([g[2], g[3]])
    pv[:, PV_GAIN + 3] = np.concatenate([g[4], g[5]])
    pv[:, PV_CW:PV_CW + 124] = inp["conv_w"][l].reshape(31, 4, 128).transpose(2, 1, 0).reshape(128, 124)
    pv[:, PV_CB:PV_CB + 4] = inp["conv_b"][l].reshape(4, 128).T
    pv[:, PV_LG:PV_LG + 4] = inp["conv_ln_g"][l].reshape(4, 128).T
    pv[:, PV_LB:PV_LB + 4] = inp["conv_ln_b"][l].reshape(4, 128).T
    pv[:, PV_BG:PV_BG + 24] = inp["b_gate"][l].reshape(3, 8, 128).transpose(2, 0, 1).reshape(128, 24)
    pv[:, PV_CS:PV_CS + 8] = inp["c"][b].reshape(8, 128).T
    pv[:, PV_PF] = 1.0 if core % 2 == 1 else 0.0
    pv[:, PV_PB] = 0.0 if core % 2 == 1 else -30000.0
    return pv


def build(stage, dbg=False):
    import contextlib
    nc = bass.Bass("TRN2", target_bir_lowering=False)
    la = {1: None, 2: 0, 3: 1}[stage]
    lb = {1: 0, 2: 1, 3: None}[stage]

    def din(name, shape, dt=F32):
        return nc.dram_tensor(name, list(shape), dt, kind="ExternalInput").ap()

    def dout(name, shape, dt=F32):
        return nc.dram_tensor(name, list(shape), dt, kind="ExternalOutput").ap()

    def dscr(name, shape, dt=F32):
        return nc.dram_tensor(name, list(shape), dt, kind=("ExternalOutput" if dbg else "Internal")).ap()

    I = {}
    if stage == 1:
        I["x"] = din("x", [T, D])
    else:
        I["xT_in"] = din("xT_in", [128, DC, T])
        I["modA"] = din("modA", [128, 72])
        I["pvA"] = din("pvA", [128, NPV])
        for n, s in (("w_inA", [D, 4096]), ("w_gateA", [D, 3072]), ("w_brA", [1536, D]), ("w_outA", [D, D]),
                     ("w_fiA", [D, 2 * DFF]), ("w_foA", [DFF, D]), ("lamA", [1, 256]), ("subgA", [1, 128]),
                     ("dbias", [128, 8, 3, 256]), ("fbias", [128, 4, LW]), ("ut_p", [128, 4, 32])):
            I[n] = din(n, s)
        for n in ("kaT_o", "kbT_o", "kaT_p", "kbT_p"):
            I[n] = din(n, [128, 4, T], BF16)
        for n in ("va_o", "vb_o", "va_p", "vb_p"):
            I[n] = din(n, [T, 512], BF16)
    if lb is not None:
        I["pvB"] = din("pvB", [128, NPV])
        for n, s in (("w_adaB", [D, 9 * D]), ("w_fiB", [D, 2 * DFF]), ("w_foB", [DFF, D]), ("w_inB", [D, 4096])):
            I[n] = din(n, s)
    O = {}
    if stage < 3:
        O["xT_out"] = dout("xT_out", [128, DC, T])
        O["modB"] = dout("modB", [128, 72])
        O["kaT"] = dout("kaT", [128, 4, T], BF16)
        O["kbT"] = dout("kbT", [128, 4, T], BF16)
        O["va"] = dout("va", [T, 512], BF16)
        O["vb"] = dout("vb", [T, 512], BF16)
        O["ut"] = dout("ut", [128, 4, 32])
    else:
        O["out"] = dout("out", [T, D])
    S = {}
    if la is not None:
        S["h2T"] = dscr("h2T_s", [128, DC, T], BF16)
        S["qaT"] = dscr("qaT_s", [128, 4, T], BF16)
        S["qbT"] = dscr("qbT_s", [128, 4, T], BF16)
        S["uT"] = dscr("uT_s", [128, 4, 32 + T])
        S["yaT"] = dscr("yaT_s", [512, T], BF16)
        S["ybT"] = dscr("ybT_s", [512, T], BF16)
        S["ycT"] = dscr("ycT_s", [128, 4, T], BF16)
        S["vaf"] = dscr("vaf_s", [2 * T, 512], BF16)
        S["vbf"] = dscr("vbf_s", [2 * T, 512], BF16)

    with contextlib.ExitStack() as st:
        def sb(name, shape, dt):
            return st.enter_context(nc.sbuf_tensor(name, shape, dt))
        xT = sb("xT", [128, DC, T], F32)
        identF = sb("identF", [128, 128], F32)
        onesF = sb("onesF", [128, 128], F32)
        identB = sb("identB", [128, 128], BF16)
        onesB = sb("onesB", [128, 128], BF16)
        blkB = sb("blkB", [128, 128], BF16)
        pvA = sb("pvA_t", [128, NPV], F32)
        pvB = sb("pvB_t", [128, NPV], F32)
        modA = sb("modA_t", [128, 72], F32)
        modB = sb("modB_t", [128, 72], F32)
        sclA = sb("sclA", [128, 3, 8], F32)
        gatA = sb("gatA", [128, 3, 8], F32)
        sclB = sb("sclB", [128, 3, 8], F32)
        gatB = sb("gatB", [128, 3, 8], F32)
        neglam = sb("neglam", [128, 4], F32)
        NA = 33600
        arena_t = sb("arena", [128, NA], F32)
        A = Arena(arena_t[:, :], NA)
        ps = [st.enter_context(nc.psum_tensor("ps%d" % i, [128, 512], F32)) for i in range(7)]
        psb = st.enter_context(nc.psum_tensor("psb", [128, 1024], BF16))
        P = Prog(nc)
        rr = [0]

        def pb(lo=0, hi=7):
            i = lo + rr[0] % (hi - lo)
            rr[0] += 1
            return ps[i], ("ps", i)

        P.op("pool", lambda e: e.memset(identF[:], 0.0), writes=["identF"])
        P.op("pool", lambda e: e.affine_select(out=identF[:], in_=identF[:], pattern=[[-1, 128]],
                                                compare_op=ALU.not_equal, fill=1.0, base=0, channel_multiplier=1),
             reads=["identF"], writes=["identF"])
        P.op("pool", lambda e: e.memset(onesF[:], 1.0), writes=["onesF"])
        P.op("pool", lambda e: e.memset(onesB[:], 1.0), writes=["onesB"])
        P.op("pool", lambda e: e.memset(blkB[:], 0.0), writes=["blkB"])
        P.op("pool", lambda e: e.memset(blkB[0:64, 0:64], 1.0), reads=["blkB"], writes=["blkB"])
        P.op("pool", lambda e: e.memset(blkB[64:128, 64:128], 1.0), reads=["blkB"], writes=["blkB"])
        P.op("dve", lambda e: e.tensor_copy(out=identB[:], in_=identF[:]), reads=["identF"], writes=["identB"])
        if la is not None:
            P.dma("sp", lambda e: e.dma_start(out=pvA[:], in_=I["pvA"][:, :]), writes=["pvA"])
            P.dma("sp", lambda e: e.dma_start(out=modA[:], in_=I["modA"][:, :]), writes=["modA"])
        if lb is not None:
            P.dma("sp", lambda e: e.dma_start(out=pvB[:], in_=I["pvB"][:, :]), writes=["pvB"])

        def derive(pv, pvk, mod, modk, scl, gat, tag):
            for n in range(3):
                P.op("dve", lambda e, n=n: e.scalar_tensor_tensor(
                    out=scl[:, n, :], in0=mod[:, (3 * n + 1) * 8:(3 * n + 2) * 8], scalar=1.0,
                    in1=pv[:, PV_NG + n * 8:PV_NG + n * 8 + 8], op0=ALU.add, op1=ALU.mult),
                    reads=[modk, pvk], writes=["scl" + tag])
                P.op("dve", lambda e, n=n: e.tensor_scalar(
                    out=gat[:, n, :], in0=mod[:, (3 * n + 2) * 8:(3 * n + 3) * 8],
                    scalar1=(1.0 if n == 1 else 0.5), scalar2=None, op0=ALU.mult),
                    reads=[modk], writes=["gat" + tag])

        def fresh(mark=0):
            P.barrier()
            A.off = mark

        def ph_load_x():
            A.reset()
            xin = [A.f32(1024) for _ in range(2)]
            for i in range(16):
                s = i % 2
                P.dma("sp", lambda e, i=i, s=s: e.dma_start(out=xin[s], in_=I["x"][i * 128:(i + 1) * 128, :]),
                      writes=[("xin", s)])
                for hb in range(2):
                    bk, bkey = pb()

                    def tr(e, s=s, hb=hb, bk=bk):
                        for c4 in range(4):
                            c = hb * 4 + c4
                            ins = e.transpose(out=bk[:, c4 * 128:(c4 + 1) * 128],
                                              in_=xin[s][:, c * 128:(c + 1) * 128], identity=identF[:])
                        return ins
                    P.op("pe", tr, reads=[("xin", s), "identF"], writes=[bkey])
                    dst = xT[:, hb * 4:(hb + 1) * 4, i * 128:(i + 1) * 128]
                    if hb == 0:
                        P.op("dve", lambda e, dst=dst, bk=bk: e.tensor_copy(out=dst, in_=_rs(bk[:, :], [4, 128])),
                             reads=[bkey], writes=[("xT", i // 4, hb)])
                    else:
                        P.op("act", lambda e, dst=dst, bk=bk: e.activation(out=dst, in_=_rs(bk[:, :], [4, 128]),
                                                                          func=AF.Identity),
                             reads=[bkey], writes=[("xT", i // 4, hb)])

        def xkeys(tt):
            return [("xT", tt, 0), ("xT", tt, 1)]

        def ph_load_xT():
            for c in range(DC):
                P.dma("sp", lambda e, c=c: e.dma_start(out=xT[:, c, :], in_=I["xT_in"][:, c, :]),
                      writes=[("xT", tt, hb) for tt in range(NT) for hb in range(2)])

        def ph_adaln(pv, pvk, w_ada, mod, modk):
            A.reset()
            P.barrier()
            csb = A.bf(8)
            wa = [A.bf(8, 1024) for _ in range(2)]
            P.op("act", lambda e: e.activation(out=csb, in_=pv[:, PV_CS:PV_CS + 8], func=AF.Silu),
                 reads=[pvk], writes=["csb"])
            bk, bkey = ps[6], ("ps", 6)
            for j in range(9):
                s = j % 2
                P.dma("pool", lambda e, j=j, s=s: e.dma_start(
                    out=wa[s], in_=w_ada[:, j * 1024:(j + 1) * 1024].rearrange("(c p) n -> p c n", p=128)),
                    writes=[("wa", s)])

                def mm(e, j=j, s=s):
                    for cb in range(8):
                        for kc in range(8):
                            ins = e.matmul(bk[:, j * 8 + cb:j * 8 + cb + 1], lhsT=wa[s][:, kc, cb * 128:(cb + 1) * 128],
                                           rhs=csb[:, kc:kc + 1], start=(kc == 0), stop=(kc == 7))
                    return ins
                P.op("pe", mm, reads=[("wa", s), "csb"], writes=[bkey])
            P.op("dve", lambda e: e.tensor_tensor(out=mod[:, :], in0=bk[:, 0:72], in1=pv[:, PV_BADA:PV_BADA + 72],
                                                  op=ALU.add), reads=[bkey, pvk], writes=[modk])

        def ph_norm(n, pv, pvk, mod, modk, scl, sclk, hT):
            sqb = A.bf(8, TT)
            rs = [A.f32(TT) for _ in range(2)]
            tmpf = [A.f32(TT) for _ in range(2)]
            k = 0
            for tt in range(NT):
                sl = slice(tt * TT, (tt + 1) * TT)
                P.op("act", lambda e, sl=sl: e.activation(out=sqb, in_=xT[:, :, sl], func=AF.Square),
                     reads=xkeys(tt), writes=["sqb"])
                bk, bkey = pb()

                def mm(e, bk=bk):
                    for c in range(DC):
                        ins = e.matmul(bk[:, :], lhsT=onesB[:], rhs=sqb[:, c, :], start=(c == 0), stop=(c == DC - 1))
                    return ins
                P.op("pe", mm, reads=["sqb", "onesB"], writes=[bkey])
                r = rs[tt % 2]
                rk = ("rs", tt % 2)
                P.op("act", lambda e, r=r, bk=bk: e.activation(out=r, in_=bk[:, :], func=AF.Ln, scale=1.0 / D, bias=EPS),
                     reads=[bkey], writes=[rk])
                P.op("act", lambda e, r=r: e.activation(out=r, in_=r, func=AF.Exp, scale=-0.5), reads=[rk], writes=[rk])
                for c in range(DC):
                    tf = tmpf[k % 2]
                    tk = ("tmpf", k % 2)
                    k += 1
                    P.op("dve", lambda e, c=c, sl=sl, tf=tf, r=r: e.tensor_tensor(out=tf, in0=xT[:, c, sl], in1=r, op=ALU.mult),
                         reads=xkeys(tt) + [rk], writes=[tk])
                    P.op("act", lambda e, c=c, sl=sl, tf=tf: e.activation(
                        out=hT[:, c, sl], in_=tf, func=AF.Identity, scale=scl[:, n, c:c + 1],
                        bias=mod[:, 3 * n * 8 + c:3 * n * 8 + c + 1]),
                        reads=[tk, sclk, modk], writes=[("h", tt)])

        def ph_ffn(w_up, w_dn, gat, gatk, n, hT):
            groups = [(0, 6), (6, 6), (12, 6), (18, 4)]
            actT = A.bf(6, T)
            wup = [A.bf(2, 8, 256) for _ in range(2)]
            wdn = [A.bf(6, D) for _ in range(2)]
            sgf = [A.f32(TT) for _ in range(2)]
            k = 0
            npair = 0
            for gi, (c0, gn) in enumerate(groups):
                ws = gi % 2
                P.dma("pool", lambda e, c0=c0, gn=gn, ws=ws: e.dma_start(
                    out=wdn[ws][:, 0:gn, :], in_=w_dn[c0 * 128:(c0 + gn) * 128, :].rearrange("(i p) n -> p i n", p=128)),
                    writes=[("wdn", ws)])
                for pi in range(gn // 2):
                    cpair = c0 + 2 * pi
                    us = npair % 2
                    npair += 1
                    for gu in range(2):
                        col = gu * DFF + cpair * 128
                        P.dma("pool", lambda e, us=us, gu=gu, col=col: e.dma_start(
                            out=wup[us][:, gu, :, :], in_=w_up[:, col:col + 256].rearrange("(c p) n -> p c n", p=128)),
                            writes=[("wup", us, gu)])
                    for ci in range(2):
                        il = 2 * pi + ci
                        for tt in range(NT):
                            sl = slice(tt * TT, (tt + 1) * TT)
                            bg, bgk = pb()
                            bu, buk = pb()

                            def mmg(e, us=us, ci=ci, sl=sl, bg=bg, gu=0):
                                for kc in range(DC):
                                    ins = e.matmul(bg[:, :], lhsT=wup[us][:, gu, kc, ci * 128:(ci + 1) * 128], rhs=hT[:, kc, sl],
                                                   start=(kc == 0), stop=(kc == DC - 1))
                                return ins
                            P.op("pe", mmg, reads=[("wup", us, 0), ("h", tt)], writes=[bgk])
                            P.op("pe", lambda e, us=us, ci=ci, sl=sl, bu=bu: mmg(e, us, ci, sl, bu, 1),
                                 reads=[("wup", us, 1), ("h", tt)], writes=[buk])
                            sg = sgf[k % 2]
                            sk = ("sgf", k % 2)
                            k += 1
                            P.op("act", lambda e, sg=sg, bg=bg: e.activation(out=sg, in_=bg[:, :], func=AF.Silu),
                                 reads=[bgk], writes=[sk])
                            P.op("dve", lambda e, sg=sg, bu=bu, il=il, sl=sl: e.tensor_tensor(
                                out=actT[:, il, sl], in0=bu[:, :], in1=sg, op=ALU.mult),
                                reads=[buk, sk], writes=[("actT", tt)])
                for dc in range(DC):
                    for tt in range(NT):
                        sl = slice(tt * TT, (tt + 1) * TT)
                        bd, bdk = pb()

                        def mmd(e, dc=dc, sl=sl, bd=bd, gn=gn, ws=ws):
                            for i in range(gn):
                                ins = e.matmul(bd[:, :], lhsT=wdn[ws][:, i, dc * 128:(dc + 1) * 128], rhs=actT[:, i, sl],
                                               start=(i == 0), stop=(i == gn - 1))
                            return ins
                        P.op("pe", mmd, reads=[("wdn", ws), ("actT", tt)], writes=[bdk])
                        P.op("dve", lambda e, dc=dc, sl=sl, bd=bd: e.scalar_tensor_tensor(
                            out=xT[:, dc, sl], in0=bd[:, :], scalar=gat[:, n, dc:dc + 1], in1=xT[:, dc, sl],
                            op0=ALU.mult, op1=ALU.add),
                            reads=[bdk, gatk, ("xT", tt, dc // 4)], writes=[("xT", tt, dc // 4)])

        def proj_norm(w_in, colbase, pv, pvk, gaincol, hT, dst):
            wq = A.bf(8, 512)
            kst = [A.bf(T) for _ in range(2)]
            sq = [A.bf(TT) for _ in range(2)]
            qf = [A.f32(TT) for _ in range(2)]
            rs = [A.f32(TT) for _ in range(2)]
            P.dma("pool", lambda e: e.dma_start(out=wq, in_=w_in[:, colbase:colbase + 512].rearrange("(c p) n -> p c n", p=128)),
                  writes=["wq"])
            k = 0
            for ch in range(4):
                ks = kst[ch % 2]
                kk = ("kst", ch % 2)
                for tt in range(NT):
                    sl = slice(tt * TT, (tt + 1) * TT)
                    u = k % 2
                    k += 1
                    bq, bqk = pb()

                    def mm(e, ch=ch, sl=sl, bq=bq):
                        for kc in range(DC):
                            ins = e.matmul(bq[:, :], lhsT=wq[:, kc, ch * 128:(ch + 1) * 128], rhs=hT[:, kc, sl],
                                           start=(kc == 0), stop=(kc == DC - 1))
                        return ins
                    P.op("pe", mm, reads=["wq", ("h", tt)], writes=[bqk])
                    P.op("act", lambda e, u=u, bq=bq: e.activation(out=sq[u], in_=bq[:, :], func=AF.Square),
                         reads=[bqk], writes=[("sq", u)])
                    P.op("dve", lambda e, u=u, bq=bq: e.tensor_copy(out=qf[u], in_=bq[:, :]), reads=[bqk], writes=[("qf", u)])
                    bs, bsk = pb()
                    P.op("pe", lambda e, u=u, bs=bs: e.matmul(bs[:, :], lhsT=blkB[:], rhs=sq[u], start=True, stop=True),
                         reads=[("sq", u), "blkB"], writes=[bsk])
                    P.op("act", lambda e, u=u, bs=bs: e.activation(out=rs[u], in_=bs[:, :], func=AF.Ln, scale=1.0 / 64, bias=EPS),
                         reads=[bsk], writes=[("prs", u)])
                    P.op("act", lambda e, u=u: e.activation(out=rs[u], in_=rs[u], func=AF.Exp, scale=-0.5),
                         reads=[("prs", u)], writes=[("prs", u)])
                    P.op("dve", lambda e, u=u, ks=ks, sl=sl: e.scalar_tensor_tensor(
                        out=ks[:, sl], in0=qf[u], scalar=pv[:, PV_GAIN + gaincol:PV_GAIN + gaincol + 1], in1=rs[u],
                        op0=ALU.mult, op1=ALU.mult), reads=[("qf", u), ("prs", u), pvk], writes=[kk])
                P.dma("sp", lambda e, ch=ch, ks=ks: e.dma_start(out=dst(ch), in_=ks), reads=[kk], writes=[("dst", colbase, ch)])

        def v_proj(w_in, colbase, hT, dst):
            wv = A.bf(8, 512)
            vst = [A.bf(512) for _ in range(2)]
            P.dma("pool", lambda e: e.dma_start(out=wv, in_=w_in[:, colbase:colbase + 512].rearrange("(c p) n -> p c n", p=128)),
                  writes=["wv"])
            for i in range(16):
                u = i % 2
                bv, bvk = pb()

                def mm(e, i=i, bv=bv):
                    for kc in range(DC):
                        ins = e.matmul(bv[:, :], lhsT=hT[:, kc, i * 128:(i + 1) * 128], rhs=wv[:, kc, :],
                                       start=(kc == 0), stop=(kc == DC - 1))
                    return ins
                P.op("pe", mm, reads=["wv", ("h", i // 4)], writes=[bvk])
                P.op("act", lambda e, u=u, bv=bv: e.activation(out=vst[u], in_=bv[:, :], func=AF.Identity),
                     reads=[bvk], writes=[("vst", u)])
                P.dma("sp", lambda e, i=i, u=u: e.dma_start(out=dst[i * 128:(i + 1) * 128, :], in_=vst[u]),
                      reads=[("vst", u)], writes=[("vdst", colbase, i)])

        def glu(w_in, hT, tts, sink):
            wu = A.bf(8, 1024)
            sgf = [A.f32(TT) for _ in range(2)]
            uf = [A.f32(TT) for _ in range(2)]
            P.dma("pool", lambda e: e.dma_start(out=wu, in_=w_in[:, 3072:4096].rearrange("(c p) n -> p c n", p=128)),
                  writes=["wu"])
            k = 0
            for ch in range(4):
                for tt in tts:
                    sl = slice(tt * TT, (tt + 1) * TT)
                    u = k % 2
                    k += 1
                    b1, b1k = pb()
                    b2, b2k = pb()

                    def mm(e, col, bk, sl=sl):
                        for kc in range(DC):
                            ins = e.matmul(bk[:, :], lhsT=wu[:, kc, col:col + 128], rhs=hT[:, kc, sl],
                                           start=(kc == 0), stop=(kc == DC - 1))
                        return ins
                    P.op("pe", lambda e, ch=ch, b1=b1, mm=mm: mm(e, ch * 128, b1), reads=["wu", ("h", tt)], writes=[b1k])
                    P.op("pe", lambda e, ch=ch, b2=b2, mm=mm: mm(e, 512 + ch * 128, b2), reads=["wu", ("h", tt)], writes=[b2k])
                    P.op("act", lambda e, u=u, b2=b2: e.activation(out=sgf[u], in_=b2[:, :], func=AF.Sigmoid),
                         reads=[b2k], writes=[("gsg", u)])
                    P.op("dve", lambda e, u=u, b1=b1: e.tensor_tensor(out=uf[u], in0=b1[:, :], in1=sgf[u], op=ALU.mult),
                         reads=[b1k, ("gsg", u)], writes=[("guf", u)])
                    sink(ch, tt, uf[u], ("guf", u))

        def ph_kv(pv, pvk, mod, modk, scl, sclk, w_in):
            A.reset()
            P.barrier()
            hT = A.bf(8, T)
            mark = A.off
            kvl = _DBG.get("kv", 9)
            ph_norm(1, pv, pvk, mod, modk, scl, sclk, hT)
            fresh(mark)
            if kvl >= 1:
                proj_norm(w_in, 512, pv, pvk, 1, hT, lambda ch: O["kaT"][:, ch, :])
                fresh(mark)
            if kvl >= 2:
                proj_norm(w_in, 2048, pv, pvk, 3, hT, lambda ch: O["kbT"][:, ch, :])
                fresh(mark)
            if kvl >= 3:
                v_proj(w_in, 1024, hT, O["va"])
                fresh(mark)
            if kvl >= 4:
                v_proj(w_in, 2560, hT, O["vb"])
                fresh(mark)
            if kvl < 5:
                return

            def sink(ch, tt, uf, key):
                P.dma("sp", lambda e, ch=ch, uf=uf: e.dma_start(out=O["ut"][:, ch, :], in_=uf[:, TT - 32:TT]),
                      reads=[key], writes=[("utout", ch)])
            glu(w_in, hT, [NT - 1], sink)

        def ph_store_x():
            for c in range(DC):
                P.dma("sp", lambda e, c=c: e.dma_start(out=O["xT_out"][:, c, :], in_=xT[:, c, :]),
                      reads=[("xT", tt, c // 4) for tt in range(NT)], writes=[("xout", c)])

        def ph_ffn_full(n, pv, pvk, mod, modk, scl, sclk, gat, gatk, w_up, w_dn):
            fresh()
            hT = A.bf(8, T)
            mark = A.off
            ph_norm(n, pv, pvk, mod, modk, scl, sclk, hT)
            fresh(mark)
            ph_ffn(w_up, w_dn, gat, gatk, n, hT)

        def ph_qu(pv, pvk, mod, modk, scl, sclk, w_in):
            fresh()
            hT = A.bf(8, T)
            mark = A.off
            ph_norm(1, pv, pvk, mod, modk, scl, sclk, hT)
            for c in range(DC):
                P.dma("sp", lambda e, c=c: e.dma_start(out=S["h2T"][:, c, :], in_=hT[:, c, :]),
                      reads=[("h", tt) for tt in range(NT)], writes=[("h2Ts", c)])
            fresh(mark)
            proj_norm(w_in, 0, pv, pvk, 0, hT, lambda ch: S["qaT"][:, ch, :])
            fresh(mark)
            proj_norm(w_in, 1536, pv, pvk, 2, hT, lambda ch: S["qbT"][:, ch, :])
            fresh(mark)
            utp = A.f32(4, 32)
            P.dma("sp", lambda e: e.dma_start(out=utp, in_=I["ut_p"][:, :, :]), writes=["utp"])
            P.op("dve", lambda e: e.tensor_scalar(out=utp, in0=utp, scalar1=pv[:, PV_PF:PV_PF + 1], scalar2=None, op0=ALU.mult),
                 reads=["utp", pvk], writes=["utp"])
            P.dma("sp", lambda e: e.dma_start(out=S["uT"][:, :, 0:32], in_=utp), reads=["utp"], writes=["uTs"])

            def sink(ch, tt, uf, key):
                P.dma("sp", lambda e, ch=ch, tt=tt, uf=uf: e.dma_start(out=S["uT"][:, ch, 32 + tt * TT:32 + (tt + 1) * TT], in_=uf),
                      reads=[key], writes=[("uTs", ch, tt)])
            glu(w_in, hT, list(range(NT)), sink)
            for nm, src_p, src_o in (("vaf", "va_p", "va_o"), ("vbf", "vb_p", "vb_o")):
                P.dma("sp", lambda e, nm=nm, src_p=src_p: e.dma_start(out=S[nm][0:T, :], in_=I[src_p][:, :]), writes=[(nm, 0)])
                P.dma("sp", lambda e, nm=nm, src_o=src_o: e.dma_start(out=S[nm][T:2 * T, :], in_=I[src_o][:, :]), writes=[(nm, 1)])

        def ph_dil(pv, pvk):
            fresh()
            qh = A.bf(T)
            kh = A.bf(2 * T)
            vbuf = [A.bf(32, 256) for _ in range(2)]
            pT = [A.bf(2, 256) for _ in range(2)]
            tmp = [A.f32(2, 256) for _ in range(2)]
            acc = A.f32(2, T)
            db = A.f32(2, 3, 256)
            rlow = A.f32(T)
            yn = A.bf(T)
            for s in range(2):
                P.op("pool", lambda e, s=s: e.memset(vbuf[s], 1.0), writes=[("vbuf", s, r) for r in range(16)])
            unit = 0
            allacc = [("acc", b) for b in range(16)]
            for hp in range(4):
                P.dma("sp", lambda e, hp=hp: e.dma_start(out=qh, in_=S["qaT"][:, hp, :]), writes=["qh"])
                P.dma("sp", lambda e, hp=hp: e.dma_start(out=kh[:, 0:T], in_=I["kaT_p"][:, hp, :]), writes=[("kh", 0)])
                P.dma("sp", lambda e, hp=hp: e.dma_start(out=kh[:, T:2 * T], in_=I["kaT_o"][:, hp, :]), writes=[("kh", 1)])
                P.dma("sp", lambda e, hp=hp: e.dma_start(out=db, in_=I["dbias"][:, 2 * hp:2 * hp + 2, :, :]), writes=["db"])
                pend = []

                def flush(n):
                    while len(pend) > n:
                        pend.pop(0)()
                for p, (win, d) in enumerate(DIL):
                    vs = (hp * 3 + p) % 2
                    vb = vbuf[vs]
                    nb = 16 // d
                    vview = S["vaf"].rearrange("(i d) c -> i d c", d=d)
                    i0_ = T // d - 128
                    for r in range(d):
                        for h in range(2):
                            src = vview[i0_:i0_ + 128 * (nb + 1), r, hp * 128 + h * 64:hp * 128 + h * 64 + 64].rearrange(
                                "(j p) c -> p j c", p=128)
                            P.dma("sp", lambda e, vb=vb, r=r, h=h, nb=nb, src=src: e.dma_start(
                                out=vb[:, r * (nb + 1):(r + 1) * (nb + 1), h * 128:h * 128 + 64], in_=src),
                                reads=[("vaf", 0), ("vaf", 1)], writes=[("vbuf", vs, r)])
                    khv = _rs(kh, [2 * T // d, d])
                    qhv = _rs(qh, [T // d, d])
                    for r in range(d):
                        for m in range(nb):
                            u = unit % 2
                            unit += 1
                            bS = [pb(), pb()]
                            bO, bOk = pb()

                            def st_(e, r=r, m=m, d=d, bS=bS, khv=khv, qhv=qhv):
                                for h in range(2):
                                    for jj in range(2):
                                        ki = T // d + 128 * (m - 1 + jj)
                                        ins = e.matmul(bS[h][0][:, jj * 128:(jj + 1) * 128],
                                                       lhsT=khv[h * 64:(h + 1) * 64, ki:ki + 128, r],
                                                       rhs=qhv[h * 64:(h + 1) * 64, 128 * m:128 * m + 128, r],
                                                       start=True, stop=True)
                                return ins
                            P.op("pe", st_, reads=["qh", ("kh", 0), ("kh", 1)], writes=[bS[0][1], bS[1][1]])
                            for h in range(2):
                                P.op("dve", lambda e, h=h, u=u, p=p, bS=bS: e.scalar_tensor_tensor(
                                    out=tmp[u][:, h, :], in0=bS[h][0][:, 0:256], scalar=0.125, in1=db[:, h, p, :],
                                    op0=ALU.mult, op1=ALU.add), reads=[bS[h][1], "db"], writes=[("dtmp", u, h)])
                                if m == 0:
                                    P.op("act", lambda e, h=h, u=u: e.activation(
                                        out=pT[u][:, h, 0:128], in_=tmp[u][:, h, 0:128], func=AF.Exp, bias=pv[:, PV_PB:PV_PB + 1]),
                                        reads=[("dtmp", u, h), pvk], writes=[("dpT", u, h)])
                                    P.op("act", lambda e, h=h, u=u: e.activation(
                                        out=pT[u][:, h, 128:256], in_=tmp[u][:, h, 128:256], func=AF.Exp),
                                        reads=[("dtmp", u, h)], writes=[("dpT", u, h)])
                                else:
                                    P.op("act", lambda e, h=h, u=u: e.activation(out=pT[u][:, h, :], in_=tmp[u][:, h, :], func=AF.Exp),
                                         reads=[("dtmp", u, h)], writes=[("dpT", u, h)])

                            def back(r=r, m=m, nb=nb, u=u, vb=vb, vs=vs, bO=bO, bOk=bOk, d=d, p=p):
                                def pv_(e):
                                    for h in range(2):
                                        for jj in range(2):
                                            ins = e.matmul(bO[:, h * 128:(h + 1) * 128],
                                                           lhsT=vb[:, r * (nb + 1) + m + jj, h * 128:(h + 1) * 128],
                                                           rhs=pT[u][:, h, jj * 128:(jj + 1) * 128], start=(jj == 0), stop=(jj == 1))
                                    return ins
                                P.op("pe", pv_, reads=[("dpT", u, 0), ("dpT", u, 1), ("vbuf", vs, r)], writes=[bOk])
                                av = acc.rearrange("p h (i d) -> p h i d", d=d)[:, :, 128 * m:128 * m + 128, r]
                                ak = [("acc", b) for b in range(d * m, d * (m + 1))]
                                if p == 0:
                                    P.op("dve", lambda e: e.tensor_copy(out=av, in_=_rs(bO[:, 0:256], [2, 128])),
                                         reads=[bOk], writes=ak)
                                else:
                                    P.op("dve", lambda e: e.tensor_tensor(out=av, in0=_rs(bO[:, 0:256], [2, 128]), in1=av, op=ALU.add),
                                         reads=[bOk] + ak, writes=ak)
                            pend.append(back)
                            flush(1)
                flush(0)
                for h in range(2):
                    P.op("dve", lambda e, h=h: e.reciprocal(out=acc[64:128, h, :], in_=acc[64:128, h, :]), reads=allacc, writes=allacc)
                    P.op("dve", lambda e, h=h: e.tensor_copy(out=rlow[0:64, :], in_=acc[64:128, h, :]), reads=allacc, writes=["rlow"])
                    P.op("dve", lambda e, h=h: e.tensor_tensor(out=yn[0:64, :], in0=acc[0:64, h, :], in1=rlow[0:64, :], op=ALU.mult),
                         reads=allacc + ["rlow"], writes=["yn"])
                    P.dma("sp", lambda e, hp=hp, h=h: e.dma_start(out=S["yaT"][(hp * 2 + h) * 64:(hp * 2 + h + 1) * 64, :], in_=yn[0:64, :]),
                          reads=["yn"], writes=[("yaTs", hp, h)])

        def ph_lambda(l):
            import math
            lam_init = 0.8 - 0.6 * math.exp(-0.3 * l)
            fresh()
            lv = A.f32(256)
            t = A.f32(128)
            s2 = A.f32(2)
            P.dma("sp", lambda e: e.dma_start(out=lv, in_=I["lamA"][0:1, :].partition_broadcast(128)), writes=["lv"])
            P.op("dve", lambda e: e.tensor_tensor(out=_rs(t, [2, 64]), in0=_rs(lv, [2, 2, 64])[:, :, 0, :],
                                                  in1=_rs(lv, [2, 2, 64])[:, :, 1, :], op=ALU.mult), reads=["lv"], writes=["lvt"])
            P.op("dve", lambda e: e.tensor_reduce(out=s2, in_=_rs(t, [2, 64]), axis=mybir.AxisListType.X, op=ALU.add),
                 reads=["lvt"], writes=["lvs"])
            P.op("act", lambda e: e.activation(out=s2, in_=s2, func=AF.Exp), reads=["lvs"], writes=["lvs"])
            P.op("dve", lambda e: e.tensor_tensor(out=neglam[:, 1:2], in0=s2[:, 1:2], in1=s2[:, 0:1], op=ALU.subtract),
                 reads=["lvs"], writes=["neglam1"])
            P.op("dve", lambda e: e.tensor_scalar(out=neglam[:, 0:1], in0=neglam[:, 1:2], scalar1=-lam_init, scalar2=None, op0=ALU.add),
                 reads=["neglam1"], writes=["neglam"])
            return lam_init

        def ph_diff(pv, pvk, lam_init):
            fresh()
            qh = A.bf(T)
            kh = A.bf(2 * T)
            vaug = A.bf(32, 130)
            W = A.f32(LW)
            NS = 4
            tmp = [A.f32(TT) for _ in range(NS)]
            pT = [A.bf(TT) for _ in range(NS)]
            sbank = [(ps[0], ("ps", 0)), (ps[1], ("ps", 1)), (ps[6], ("ps", 6)), (ps[5], ("ps", 5))]

            def aslot(c, qb):
                a = c * 4 + qb
                return ps[2 + a // 3], ("ps", 2 + a // 3), (a % 3) * 160
            o1 = A.f32(128)
            of = A.f32(128)
            sm = A.f32(8)
            ybt = [A.bf(128) for _ in range(4)]
            ybst = A.bf(TT)
            gsub = A.f32(128)
            P.op("pool", lambda e: e.memset(vaug, 1.0), writes=["vaug"])
            P.dma("sp", lambda e: e.dma_start(out=gsub, in_=I["subgA"][0:1, :].partition_broadcast(128)), writes=["gsub"])
            P.op("dve", lambda e: e.tensor_scalar(out=gsub, in0=gsub, scalar1=(1.0 - lam_init), scalar2=None, op0=ALU.mult),
                 reads=["gsub"], writes=["gsub"])
            cnt = 0
            for h in range(4):
                P.dma("sp", lambda e, h=h: e.dma_start(out=qh, in_=S["qbT"][:, h, :]), writes=["qh"])
                P.dma("sp", lambda e, h=h: e.dma_start(out=kh[:, 0:T], in_=I["kbT_p"][:, h, :]), writes=[("kh", 0)])
                P.dma("sp", lambda e, h=h: e.dma_start(out=kh[:, T:2 * T], in_=I["kbT_o"][:, h, :]), writes=[("kh", 1)])
                P.dma("sp", lambda e, h=h: e.dma_start(
                    out=vaug[:, :, 0:128], in_=S["vbf"][:, h * 128:(h + 1) * 128].rearrange("(j p) c -> p j c", p=128)),
                    reads=[("vbf", 0), ("vbf", 1)], writes=["vaug"])
                P.dma("sp", lambda e, h=h: e.dma_start(out=W, in_=I["fbias"][:, h, :]), writes=["W"])
                pend = []
                finb = []

                def flush(n):
                    while len(pend) > n:
                        pend.pop(0)()
                since = 0
                for g in range(4):
                    first = {}
                    for c in range(2):
                        nkb = 16 + 4 * g + 4
                        for kbi in range(nkb):
                            u = cnt % NS
                            cnt += 1
                            bS, bSk = sbank[u]
                            P.op("pe", lambda e, c=c, kbi=kbi, g=g, bS=bS: e.matmul(
                                bS[:, :], lhsT=kh[c * 64:(c + 1) * 64, kbi * 128:(kbi + 1) * 128],
                                rhs=qh[c * 64:(c + 1) * 64, g * TT:(g + 1) * TT], start=True, stop=True),
                                reads=["qh", ("kh", 0), ("kh", 1)], writes=[bSk])
                            delta = (T + TT * g) - 128 * kbi
                            off = delta + 384 if delta < 1792 else 2176
                            P.op("dve", lambda e, u=u, off=off, bS=bS: e.scalar_tensor_tensor(
                                out=tmp[u], in0=bS[:, :], scalar=0.125, in1=W[:, off:off + TT], op0=ALU.mult, op1=ALU.add),
                                reads=[bSk, "W"], writes=[("ftmp", u)])
                            if kbi < 16:
                                P.op("act", lambda e, u=u: e.activation(out=pT[u], in_=tmp[u], func=AF.Exp, bias=pv[:, PV_PB:PV_PB + 1]),
                                     reads=[("ftmp", u), pvk], writes=[("fpT", u)])
                            else:
                                P.op("act", lambda e, u=u: e.activation(out=pT[u], in_=tmp[u], func=AF.Exp),
                                     reads=[("ftmp", u)], writes=[("fpT", u)])
                            plan = []
                            for qb in range(4):
                                if kbi >= 16 and (4 * g + qb) < (kbi - 16):
                                    continue
                                bkx, bkk, col = aslot(c, qb)
                                stf = first.get(bkk, True)
                                first[bkk] = False
                                last = (kbi == 16 + 4 * g + qb)
                                plan.append((qb, stf, last, bkx, bkk, col))

                            def back(plan=plan, c=c, u=u, kbi=kbi):
                                def pv_(e):
                                    for qb, stf, last, bkx, bkk, col in plan:
                                        ins = e.matmul(bkx[:, col:col + 129], lhsT=pT[u][:, qb * 128:(qb + 1) * 128],
                                                       rhs=vaug[:, kbi, 0:129], start=stf, stop=last, skip_group_check=True)
                                    return ins
                                P.op("pe", pv_, reads=[("fpT", u), "vaug"], writes=sorted(set(x[4] for x in plan)))
                            pend.append(back)
                            flush(NS - 1)
                            since += 1
                            if finb and since >= 3:
                                finb.pop(0)()
                    flush(0)
                    if finb:
                        finb.pop(0)()
                    for qb in range(4):
                        b1, b1k, col = aslot(0, qb)
                        b2, b2k, col2 = aslot(1, qb)
                        yb = ybt[qb]
                        P.op("dve", lambda e, b1=b1, col=col: e.reciprocal(out=sm[:, 0:1], in_=b1[:, col + 128:col + 129]),
                             reads=[b1k], writes=["sm0"])
                        P.op("dve", lambda e, b2=b2, col2=col2: e.reciprocal(out=sm[:, 1:2], in_=b2[:, col2 + 128:col2 + 129]),
                             reads=[b2k], writes=["sm1"])
                        P.op("dve", lambda e: e.tensor_tensor(out=sm[:, 2:3], in0=sm[:, 1:2], in1=neglam[:, 0:1], op=ALU.mult),
                             reads=["sm1", "neglam"], writes=["sm2"])
                        P.op("act", lambda e, b1=b1, col=col: e.activation(out=o1, in_=b1[:, col:col + 128], func=AF.Identity, scale=sm[:, 0:1]),
                             reads=[b1k, "sm0"], writes=["o1"])
                        P.op("dve", lambda e, b2=b2, col2=col2: e.scalar_tensor_tensor(
                            out=of, in0=b2[:, col2:col2 + 128], scalar=sm[:, 2:3], in1=o1, op0=ALU.mult, op1=ALU.add),
                            reads=[b2k, "sm2", "o1"], writes=["of"])
                        P.op("act", lambda e: e.activation(out=o1, in_=of, func=AF.Square, accum_out=sm[:, 3:4]),
                             reads=["of", "o1"], writes=["o1", "sm3"])
                        P.op("act", lambda e: e.activation(out=sm[:, 4:5], in_=sm[:, 3:4], func=AF.Ln, scale=1.0 / 128, bias=EPS),
                             reads=["sm3"], writes=["sm4"])
                        P.op("act", lambda e: e.activation(out=sm[:, 4:5], in_=sm[:, 4:5], func=AF.Exp, scale=-0.5),
                             reads=["sm4"], writes=["sm4"])
                        P.op("dve", lambda e, yb=yb: e.scalar_tensor_tensor(
                            out=yb, in0=of, scalar=sm[:, 4:5], in1=gsub, op0=ALU.mult, op1=ALU.mult),
                            reads=["of", "sm4", "gsub"], writes=[("ybt", qb)])

                    def fin_b(h=h, g=g):
                        def tr(e):
                            for qb in range(4):
                                ins = e.transpose(out=psb[:, qb * 128:(qb + 1) * 128], in_=ybt[qb], identity=identB[:])
                            return ins
                        P.op("pe", tr, reads=[("ybt", qb) for qb in range(4)] + ["identB"], writes=["psb"])
                        P.op("act", lambda e: e.activation(out=ybst, in_=psb[:, 0:TT], func=AF.Identity), reads=["psb"], writes=["ybst"])
                        P.dma("sp", lambda e: e.dma_start(out=S["ybT"][h * 128:(h + 1) * 128, g * TT:(g + 1) * TT], in_=ybst),
                              reads=["ybst"], writes=[("ybTs", h, g)])
                    finb.append(fin_b)
                    since = 0
                while finb:
                    finb.pop(0)()

        def ph_conv(pv, pvk):
            fresh()
            ubb = A.bf(4, 32 + T)
            dg = A.bf(124, 128)
            ycf = A.f32(4, T)
            ycb = A.bf(4, T)
            sqf = A.f32(4, TT)
            mf = A.f32(TT)
            vf = A.f32(TT)
            tf = [A.f32(TT) for _ in range(2)]
            HW_ = (32 + T) // 2
            for cc in range(4):
                for hh in range(2):
                    P.dma("pool", lambda e, cc=cc, hh=hh: e.dma_start(out=ubb[:, cc, hh * HW_:(hh + 1) * HW_],
                                                                      in_=S["uT"][:, cc, hh * HW_:(hh + 1) * HW_]),
                          reads=["uTs"] + [("uTs", cc, tt) for tt in range(NT)], writes=[("ubb", cc)])
            for idx in range(124):
                P.op("dve", lambda e, idx=idx: e.tensor_scalar(out=dg[:, idx, :], in0=identB[:], scalar1=pv[:, PV_CW + idx:PV_CW + idx + 1],
                                                               scalar2=None, op0=ALU.mult), reads=["identB", pvk], writes=[("dg", idx // 31)])
            for cc in range(4):
                for tt in range(NT):
                    bk, bkey = pb()

                    def mmc(e, cc=cc, tt=tt, bk=bk):
                        for j in range(31):
                            ins = e.matmul(bk[:, :], lhsT=dg[:, cc * 31 + j, :], rhs=ubb[:, cc, 2 + j + tt * TT:2 + j + (tt + 1) * TT],
                                           start=(j == 0), stop=(j == 30))
                        return ins
                    P.op("pe", mmc, reads=[("dg", cc), ("ubb", cc)], writes=[bkey])
                    P.op("act", lambda e, cc=cc, tt=tt, bk=bk: e.activation(
                        out=ycf[:, cc, tt * TT:(tt + 1) * TT], in_=bk[:, :], func=AF.Identity, bias=pv[:, PV_CB + cc:PV_CB + cc + 1]),
                        reads=[bkey, pvk], writes=[("ycf", cc)])
            k = 0
            allycf = [("ycf", cc) for cc in range(4)]
            for tt in range(NT):
                sl = slice(tt * TT, (tt + 1) * TT)
                P.op("act", lambda e, sl=sl: e.activation(out=sqf, in_=ycf[:, :, sl], func=AF.Square), reads=allycf, writes=["sqf"])
                b1, b1k = pb()
                b2, b2k = pb()

                def mm1(e, sl=sl, b1=b1):
                    for cc in range(4):
                        ins = e.matmul(b1[:, :], lhsT=onesF[:], rhs=ycf[:, cc, sl], start=(cc == 0), stop=(cc == 3))
                    return ins

                def mm2(e, b2=b2):
                    for cc in range(4):
                        ins = e.matmul(b2[:, :], lhsT=onesF[:], rhs=sqf[:, cc, :], start=(cc == 0), stop=(cc == 3))
                    return ins
                P.op("pe", mm1, reads=allycf + ["onesF"], writes=[b1k])
                P.op("pe", mm2, reads=["sqf", "onesF"], writes=[b2k])
                P.op("dve", lambda e, b1=b1: e.tensor_scalar(out=mf, in0=b1[:, :], scalar1=1.0 / 512, scalar2=None, op0=ALU.mult),
                     reads=[b1k], writes=["mf"])
                P.op("dve", lambda e: e.tensor_tensor(out=vf, in0=mf, in1=mf, op=ALU.mult), reads=["mf"], writes=["vf"])
                P.op("dve", lambda e, b2=b2: e.scalar_tensor_tensor(out=vf, in0=b2[:, :], scalar=1.0 / 512, in1=vf,
                                                                    op0=ALU.mult, op1=ALU.subtract),
                     reads=[b2k, "vf"], writes=["vf"])
                P.op("act", lambda e: e.activation(out=vf, in_=vf, func=AF.Ln, bias=EPS), reads=["vf"], writes=["vf"])
                P.op("act", lambda e: e.activation(out=vf, in_=vf, func=AF.Exp, scale=-0.5), reads=["vf"], writes=["vf"])
                for cc in range(4):
                    t_ = tf[k % 2]
                    tk = ("ctf", k % 2)
                    k += 1
                    P.op("dve", lambda e, cc=cc, sl=sl, t_=t_: e.tensor_tensor(out=t_, in0=ycf[:, cc, sl], in1=mf, op=ALU.subtract),
                         reads=allycf + ["mf"], writes=[tk])
                    P.op("dve", lambda e, t_=t_: e.tensor_tensor(out=t_, in0=t_, in1=vf, op=ALU.mult), reads=[tk, "vf"], writes=[tk])
                    P.op("act", lambda e, cc=cc, sl=sl, t_=t_: e.activation(
                        out=ycb[:, cc, sl], in_=t_, func=AF.Silu, scale=pv[:, PV_LG + cc:PV_LG + cc + 1],
                        bias=pv[:, PV_LB + cc:PV_LB + cc + 1]), reads=[tk, pvk], writes=[("ycb", cc)])
            for cc in range(4):
                P.dma("sp", lambda e, cc=cc: e.dma_start(out=S["ycT"][:, cc, :], in_=ycb[:, cc, :]), reads=[("ycb", cc)], writes=[("ycTs", cc)])

        def ph_merge(pv, pvk, gat, gatk, w_gate, w_br, w_out):
            fresh()
            wo = A.bf(8, D)
            hh = A.bf(8, 1024)
            yy = A.bf(3, 4, 1024)
            zT = A.bf(8, 1024)
            wg = [A.bf(3, 8, 128) for _ in range(2)]
            wb = [A.bf(3, 4, 128) for _ in range(2)]
            gsb = [A.f32(TT) for _ in range(2)]
            zacc = [A.f32(TT) for _ in range(2)]
            prod = [A.f32(TT) for _ in range(2)]
            P.dma("pool", lambda e: e.dma_start(out=wo, in_=w_out.rearrange("(c p) n -> p c n", p=128)), writes=["wo"])
            ysrc = [S["yaT"].rearrange("(c p) t -> p c t", p=128), S["ybT"].rearrange("(c p) t -> p c t", p=128), S["ycT"]]
            ykeys = [[("yaTs", hp, h) for hp in range(4) for h in range(2)],
                     [("ybTs", h, g) for h in range(4) for g in range(4)],
                     [("ycTs", cc) for cc in range(4)]]
            cnt = 0
            k = 0
            for half in range(2):
                hs = slice(half * 1024, (half + 1) * 1024)
                P.dma("sp", lambda e, hs=hs: e.dma_start(out=hh, in_=S["h2T"][:, :, hs]),
                      reads=[("h2Ts", c) for c in range(DC)], writes=["hh"])
                for i in range(3):
                    P.dma("sp", lambda e, i=i, hs=hs: e.dma_start(out=yy[:, i, :, :], in_=ysrc[i][:, :, hs]),
                          reads=ykeys[i], writes=[("yy", i)])
                for j in range(DC):
                    s = cnt % 2
                    cnt += 1
                    for i in range(3):
                        col = i * 1024 + j * 128
                        P.dma("pool", lambda e, s=s, i=i, col=col: e.dma_start(
                            out=wg[s][:, i, :, :], in_=w_gate[:, col:col + 128].rearrange("(c p) n -> p c n", p=128)),
                            writes=[("wg", s, i)])
                        P.dma("pool", lambda e, s=s, i=i, j=j: e.dma_start(
                            out=wb[s][:, i, :, :], in_=w_br[i * 512:(i + 1) * 512, j * 128:(j + 1) * 128].rearrange("(c p) n -> p c n", p=128)),
                            writes=[("wb", s, i)])
                    for t2 in range(2):
                        sl = slice(t2 * TT, (t2 + 1) * TT)
                        za = zacc[(2 * j + t2) % 2]
                        zk = ("zacc", (2 * j + t2) % 2)
                        for i in range(3):
                            u = k % 2
                            k += 1
                            bg, bgk = pb()
                            by, byk = pb()

                            def mg(e, s=s, i=i, sl=sl, bg=bg):
                                for kc in range(DC):
                                    ins = e.matmul(bg[:, :], lhsT=wg[s][:, i, kc, :], rhs=hh[:, kc, sl], start=(kc == 0), stop=(kc == DC - 1))
                                return ins

                            def my(e, s=s, i=i, sl=sl, by=by):
                                for kc in range(4):
                                    ins = e.matmul(by[:, :], lhsT=wb[s][:, i, kc, :], rhs=yy[:, i, kc, sl], start=(kc == 0), stop=(kc == 3))
                                return ins
                            P.op("pe", mg, reads=[("wg", s, i), "hh"], writes=[bgk])
                            P.op("pe", my, reads=[("wb", s, i), ("yy", i)], writes=[byk])
                            P.op("act", lambda e, u=u, bg=bg, i=i, j=j: e.activation(
                                out=gsb[u], in_=bg[:, :], func=AF.Sigmoid, bias=pv[:, PV_BG + i * 8 + j:PV_BG + i * 8 + j + 1]),
                                reads=[bgk, pvk], writes=[("gsb", u)])
                            if i == 0:
                                P.op("dve", lambda e, u=u, by=by, za=za: e.tensor_tensor(out=za, in0=by[:, :], in1=gsb[u], op=ALU.mult),
                                     reads=[byk, ("gsb", u)], writes=[zk])
                            else:
                                P.op("dve", lambda e, u=u, by=by: e.tensor_tensor(out=prod[u], in0=by[:, :], in1=gsb[u], op=ALU.mult),
                                     reads=[byk, ("gsb", u)], writes=[("prod", u)])
                                if i == 1:
                                    P.op("pool", lambda e, u=u, za=za: e.tensor_tensor(out=za, in0=za, in1=prod[u], op=ALU.add),
                                         reads=[zk, ("prod", u)], writes=[zk])
                                else:
                                    P.op("pool", lambda e, u=u, za=za, j=j, sl=sl: e.tensor_tensor(out=zT[:, j, sl], in0=za, in1=prod[u], op=ALU.add),
                                         reads=[zk, ("prod", u)], writes=[("zT", t2)])
                for dj in range(DC):
                    for t2 in range(2):
                        sl = slice(t2 * TT, (t2 + 1) * TT)
                        xs = slice(half * 1024 + t2 * TT, half * 1024 + (t2 + 1) * TT)
                        tt = half * 2 + t2
                        bd, bdk = pb()

                        def mo(e, dj=dj, sl=sl, bd=bd):
                            for kc in range(DC):
                                ins = e.matmul(bd[:, :], lhsT=wo[:, kc, dj * 128:(dj + 1) * 128], rhs=zT[:, kc, sl], start=(kc == 0), stop=(kc == DC - 1))
                            return ins
                        P.op("pe", mo, reads=["wo", ("zT", t2)], writes=[bdk])
                        P.op("dve", lambda e, dj=dj, xs=xs, bd=bd: e.scalar_tensor_tensor(
                            out=xT[:, dj, xs], in0=bd[:, :], scalar=gat[:, 1, dj:dj + 1], in1=xT[:, dj, xs], op0=ALU.mult, op1=ALU.add),
                            reads=[bdk, gatk, ("xT", tt, dj // 4)], writes=[("xT", tt, dj // 4)])

        def ph_out():
            fresh()
            ot = [A.f32(D) for _ in range(2)]
            for i in range(16):
                s = i % 2
                for hb in range(2):
                    bk, bkey = pb()

                    def tr(e, i=i, hb=hb, bk=bk):
                        for c4 in range(4):
                            c = hb * 4 + c4
                            ins = e.transpose(out=bk[:, c4 * 128:(c4 + 1) * 128], in_=xT[:, c, i * 128:(i + 1) * 128], identity=identF[:])
                        return ins
                    P.op("pe", tr, reads=[("xT", i // 4, hb), "identF"], writes=[bkey])
                    if hb == 0:
                        P.op("dve", lambda e, s=s, bk=bk: e.tensor_copy(out=ot[s][:, 0:512], in_=bk[:, :]), reads=[bkey], writes=[("ot", s, 0)])
                    else:
                        P.op("act", lambda e, s=s, bk=bk: e.activation(out=ot[s][:, 512:1024], in_=bk[:, :], func=AF.Identity),
                             reads=[bkey], writes=[("ot", s, 1)])
                P.dma("sp", lambda e, i=i, s=s: e.dma_start(out=O["out"][i * 128:(i + 1) * 128, :], in_=ot[s]),
                      reads=[("ot", s, 0), ("ot", s, 1)], writes=[("outd", i)])

        finals = []
        if stage == 1:
            ph_load_x()
        else:
            ph_load_xT()
            derive(pvA, "pvA", modA, "modA", sclA, gatA, "A")
            s2 = _DBG.get("s2", 9)
            ph_qu(pvA, "pvA", modA, "modA", sclA, "sclA", I["w_inA"])
            lam_init = ph_lambda(la)
            if s2 >= 2:
                ph_dil(pvA, "pvA")
            if s2 >= 3:
                ph_diff(pvA, "pvA", lam_init)
            if s2 >= 4:
                ph_conv(pvA, "pvA")
            if s2 >= 5:
                ph_merge(pvA, "pvA", gatA, "gatA", I["w_gateA"], I["w_brA"], I["w_outA"])
            if s2 >= 6:
                ph_ffn_full(2, pvA, "pvA", modA, "modA", sclA, "sclA", gatA, "gatA", I["w_fiA"], I["w_foA"])
        if lb is not None:
            up = _DBG.get("upto", 9) if _DBG.get("s2", 9) >= 7 else 0
            if up >= 1:
                ph_adaln(pvB, "pvB", I["w_adaB"], modB, "modB")
                derive(pvB, "pvB", modB, "modB", sclB, gatB, "B")
            if up >= 2:
                ph_ffn_full(0, pvB, "pvB", modB, "modB", sclB, "sclB", gatB, "gatB", I["w_fiB"], I["w_foB"])
            if up >= 3:
                ph_kv(pvB, "pvB", modB, "modB", sclB, "sclB", I["w_inB"])
            ph_store_x()
            P.dma("sp", lambda e: e.dma_start(out=O["modB"][:, :], in_=modB[:]), reads=["modB"], writes=["modBout"])
        else:
            ph_out()
        P.barrier()
        P.op("pool", lambda e: e.memset(neglam[:, 3:4], 0.0), writes=["__tail"])
        P.emit(["__tail"])
    return nc


_CACHE = {}
_DBG = {}


def _prog(stage):
    if stage not in _CACHE:
        _CACHE[stage] = build(stage)
    return _CACHE[stage]


def _mixer_inputs(inp, l, prev_res, dbias, fbias):
    maps = []
    for c in range(NCORES):
        own = prev_res[c]
        prv = prev_res[c - 1] if c % 2 == 1 else prev_res[c]
        maps.append({
            "xT_in": own["xT_out"], "modA": own["modB"], "pvA": _host_pv(inp, l, c),
            "w_inA": inp["w_in"][l], "w_gateA": inp["w_gate"][l], "w_brA": inp["w_branch"][l].reshape(1536, D),
            "w_outA": inp["w_out"][l], "w_fiA": inp["w_ffn_in"][l, 1], "w_foA": inp["w_ffn_out"][l, 1],
            "lamA": inp["lambda_vec"][l].reshape(1, 256), "subgA": inp["subln_g"][l].reshape(1, 128),
            "dbias": dbias, "fbias": fbias, "ut_p": prv["ut"],
            "kaT_o": own["kaT"], "kbT_o": own["kbT"], "va_o": own["va"], "vb_o": own["vb"],
            "kaT_p": prv["kaT"], "kbT_p": prv["kbT"], "va_p": prv["va"], "vb_p": prv["vb"],
        })
    return maps


def _ffn_kv_inputs(inp, l, c):
    return {"pvB": _host_pv(inp, l, c), "w_adaB": inp["w_ada"][l], "w_fiB": inp["w_ffn_in"][l, 0],
            "w_foB": inp["w_ffn_out"][l, 0], "w_inB": inp["w_in"][l]}


def kernel(**inputs):
    inp = {k: np.ascontiguousarray(np.asarray(v, dtype=np.float32)) for k, v in inputs.items()}
    dbias, fbias = _host_bias_tables(inp["rel_bias"])
    cores = list(range(NCORES))
    m1 = []
    for c in cores:
        b, hf = c // 2, c % 2
        d = {"x": np.ascontiguousarray(inp["x"][b, hf * T:(hf + 1) * T, :])}
        d.update(_ffn_kv_inputs(inp, 0, c))
        m1.append(d)
    r1 = run_bass_kernel_spmd(_prog(1), m1, core_ids=cores).results
    m2 = _mixer_inputs(inp, 0, r1, dbias, fbias)
    for c in cores:
        m2[c].update(_ffn_kv_inputs(inp, 1, c))
    r2 = run_bass_kernel_spmd(_prog(2), m2, core_ids=cores).results
    m3 = _mixer_inputs(inp, 1, r2, dbias, fbias)
    r3 = run_bass_kernel_spmd(_prog(3), m3, core_ids=cores).results
    out = np.empty((4, 2 * T, D), np.float32)
    for c in cores:
        out[c // 2, (c % 2) * T:(c % 2 + 1) * T, :] = r3[c]["out"]
    return out
```

```python
import numpy as np
import concourse.bass as bass
import concourse.mybir as mybir
from concourse.bass_utils import run_bass_kernel_spmd

F32 = mybir.dt.float32
BF16 = mybir.dt.bfloat16
AF = mybir.ActivationFunctionType
ALU = mybir.AluOpType

D = 1024
DC = 8
T = 2048
TT = 512
NT = T // TT
DFF = 2816
FC = 22
NCORES = 8


class _Op:
    __slots__ = ("eng", "fn", "deps", "is_dma", "sig", "tok", "idx")


class Prog:
    ENGS = ("pe", "act", "dve", "pool", "sp")
    NDMA = 6

    def __init__(self, nc):
        self.nc = nc
        self.ops = []
        self.last_w = {}
        self.readers = {}
        self.last_c = {}
        self.dma_hist = {}
        self.pending = {}

    def _add(self, eng, fn, reads, writes, is_dma):
        op = _Op()
        op.eng, op.fn, op.is_dma, op.sig, op.tok = eng, fn, is_dma, False, None
        op.idx = len(self.ops)
        deps = {}
        for r in reads:
            w = self.last_w.get(r)
            if w is not None:
                deps[w.idx] = (w, True)
        for wkey in writes:
            w = self.last_w.get(wkey)
            if w is not None and w.idx not in deps:
                deps[w.idx] = (w, False)
            for rd in self.readers.get(wkey, ()):
                if rd.idx not in deps:
                    deps[rd.idx] = (rd, False)
        keep = []
        for d in self.pending.pop(eng, ()):
            if d.eng == eng and not d.is_dma and eng == "pe":
                continue
            if d.idx not in deps:
                keep.append(d)
                d.sig = True
        for d, raw in deps.values():
            if d.eng == eng and not d.is_dma and not is_dma:
                if eng == "pe" or not raw:
                    continue
            keep.append(d)
            d.sig = True
        op.deps = keep
        for r in reads:
            self.readers.setdefault(r, []).append(op)
        for wkey in writes:
            self.last_w[wkey] = op
            self.readers[wkey] = []
        self.ops.append(op)
        if is_dma:
            self.dma_hist.setdefault(eng, []).append(op)
        else:
            self.last_c[eng] = op
        return op

    def barrier(self):
        B = list(self.last_c.values())
        for q, h in self.dma_hist.items():
            B.extend(h[-self.NDMA:])
        for e in self.ENGS:
            self.pending[e] = list(self.pending.get(e, ())) + B

    def op(self, eng, fn, reads=(), writes=()):
        reads, writes = tuple(reads), tuple(writes)
        extra = tuple(r for r in reads if (r == "psb" or (isinstance(r, tuple) and r[0] == "ps")) and r not in writes)
        return self._add(eng, fn, reads, writes + extra, False)

    def dma(self, eng, fn, reads=(), writes=()):
        return self._add(eng, fn, tuple(reads), tuple(writes), True)

    def emit(self, final_keys):
        nc = self.nc
        import contextlib
        with contextlib.ExitStack() as st:
            esem = {e: st.enter_context(nc.semaphore("s_" + e)) for e in self.ENGS}
            dsem = {e: [st.enter_context(nc.semaphore("d_%s%d" % (e, i))) for i in range(self.NDMA)]
                    for e in ("sp", "pool", "act")}
            ecnt = {e: 0 for e in self.ENGS}
            dcnt = {e: [0] * self.NDMA for e in dsem}
            drr = {e: 0 for e in dsem}
            finals = [self.last_w[k] for k in final_keys]
            for f in finals:
                f.sig = True
            prewait = {}
            for op in self.ops:
                if op.is_dma:
                    k = drr[op.eng] % self.NDMA
                    drr[op.eng] += 1
                    prewait[op.idx] = (dsem[op.eng][k], dcnt[op.eng][k])
                    dcnt[op.eng][k] += 16
                    op.tok = (dsem[op.eng][k], dcnt[op.eng][k])
                elif op.sig:
                    ecnt[op.eng] += 1
                    op.tok = (esem[op.eng], ecnt[op.eng])
            assert max(ecnt.values()) < 60000, ecnt
            per = {e: [o for o in self.ops if o.eng == e] for e in self.ENGS}
            block = st.enter_context(nc.Block())

            def run(eng_obj, ename, tail):
                waited = {}

                def w(sem, val):
                    if val <= 0:
                        return
                    key = id(sem)
                    if waited.get(key, 0) >= val:
                        return
                    waited[key] = val
                    eng_obj.wait_ge(sem, val)
                for op in per[ename]:
                    for d in op.deps:
                        w(*d.tok)
                    if op.is_dma:
                        w(*prewait[op.idx])
                    ins = op.fn(eng_obj)
                    if op.tok is not None:
                        ins.then_inc(op.tok[0], 16 if op.is_dma else 1)
                if tail:
                    for f in finals:
                        w(*f.tok)

            @block.tensor
            def _(e):
                run(e, "pe", False)

            @block.scalar
            def _(e):
                run(e, "act", False)

            @block.vector
            def _(e):
                run(e, "dve", False)

            @block.gpsimd
            def _(e):
                run(e, "pool", False)

            @block.sync
            def _(e):
                run(e, "sp", True)


def _rs(ap, shape):
    shape = list(shape)
    if len(shape) == 1:
        return ap
    names = "abcd"[:len(shape)]
    kw = {names[i]: shape[i] for i in range(len(shape))}
    return ap.rearrange("p (%s) -> p %s" % (" ".join(names), " ".join(names)), **kw)


class Arena:
    def __init__(self, ap_f32, nwords):
        self.base, self.n, self.off = ap_f32, nwords, 0

    def reset(self):
        self.off = 0

    def f32(self, *shape):
        n = int(np.prod(shape))
        a = self.base[:, self.off:self.off + n]
        self.off += n
        assert self.off <= self.n, (self.off, self.n)
        return _rs(a, shape)

    def bf(self, *shape):
        n = int(np.prod(shape))
        w = (n + 1) // 2
        a = self.base[:, self.off:self.off + w].bitcast(BF16)
        self.off += w
        assert self.off <= self.n, (self.off, self.n)
        return _rs(a[:, 0:n], shape)


PV_NG = 0
PV_BADA = 24
PV_GAIN = 96
PV_CW = 100
PV_CB = 224
PV_LG = 228
PV_LB = 232
PV_BG = 236
PV_CS = 260
PV_PF = 268
PV_PB = 269
NPV = 272

LW = 2688
EPS = 1e-6
DIL = ((128, 1), (512, 4), (2048, 16))


def _t5_bucket_np(dist):
    dist = np.asarray(dist, np.int64)
    dd = np.maximum(dist.astype(np.float32), np.float32(1.0))
    large = 16 + (np.log(dd / np.float32(16.0)) / np.float32(np.log(2048.0 / 16.0)) * np.float32(16.0)).astype(np.int32)
    large = np.minimum(large, 31)
    return np.where(dist < 16, dist, large).astype(np.int64)


def _host_bias_tables(rel_bias):
    NEG = np.float32(-1e30)
    k = np.arange(128)[:, None]
    dbias = np.empty((128, 8, 3, 256), np.float32)
    for p, (win, d) in enumerate(DIL):
        j = np.arange(256)[None, :]
        rel = np.where(j < 128, j + 128 - k, j - 128 - k)
        valid = (rel >= 0) & (rel <= 128)
        b = _t5_bucket_np(np.maximum(rel, 0) * d)
        for h in range(8):
            dbias[:, h, p, :] = np.where(valid, rel_bias[b, h], NEG)
    j = np.arange(LW)[None, :]
    dist = j - 384 - k
    b = _t5_bucket_np(np.maximum(dist, 0))
    fbias = np.empty((128, 4, LW), np.float32)
    for h in range(4):
        fbias[:, h, :] = np.where(dist >= 0, rel_bias[b, 8 + h], NEG)
    return dbias, fbias


def _host_pv(inp, l, core):
    b = core // 2
    pv = np.zeros((128, NPV), np.float32)
    pv[:, PV_NG:PV_NG + 24] = inp["norm_g"][l].reshape(3, 8, 128).transpose(2, 0, 1).reshape(128, 24)
    pv[:, PV_BADA:PV_BADA + 72] = inp["b_ada"][l].reshape(9, 8, 128).transpose(2, 0, 1).reshape(128, 72)
    g = inp["qk_gain"][l]
    pv[:, PV_GAIN + 0] = np.concatenate([g[0], g[0]])
    pv[:, PV_GAIN + 1] = np.concatenate([g[1], g[1]])
    pv[:, PV_GAIN + 2] = np.concatenate([g[2], g[3]])
    pv[:, PV_GAIN + 3] = np.concatenate([g[4], g[5]])
    pv[:, PV_CW:PV_CW + 124] = inp["conv_w"][l].reshape(31, 4, 128).transpose(2, 1, 0).reshape(128, 124)
    pv[:, PV_CB:PV_CB + 4] = inp["conv_b"][l].reshape(4, 128).T
    pv[:, PV_LG:PV_LG + 4] = inp["conv_ln_g"][l].reshape(4, 128).T
    pv[:, PV_LB:PV_LB + 4] = inp["conv_ln_b"][l].reshape(4, 128).T
    pv[:, PV_BG:PV_BG + 24] = inp["b_gate"][l].reshape(3, 8, 128).transpose(2, 0, 1).reshape(128, 24)
    pv[:, PV_CS:PV_CS + 8] = inp["c"][b].reshape(8, 128).T
    pv[:, PV_PF] = 1.0 if core % 2 == 1 else 0.0
    pv[:, PV_PB] = 0.0 if core % 2 == 1 else -30000.0
    return pv


def build(stage, dbg=False):
    import contextlib
    nc = bass.Bass("TRN2", target_bir_lowering=False)
    la = {1: None, 2: 0, 3: 1}[stage]
    lb = {1: 0, 2: 1, 3: None}[stage]

    def din(name, shape, dt=F32):
        return nc.dram_tensor(name, list(shape), dt, kind="ExternalInput").ap()

    def dout(name, shape, dt=F32):
        return nc.dram_tensor(name, list(shape), dt, kind="ExternalOutput").ap()

    def dscr(name, shape, dt=F32):
        return nc.dram_tensor(name, list(shape), dt, kind=("ExternalOutput" if dbg else "Internal")).ap()

    I = {}
    if stage == 1:
        I["x"] = din("x", [T, D])
    else:
        I["xT_in"] = din("xT_in", [128, DC, T])
        I["modA"] = din("modA", [128, 72])
        I["pvA"] = din("pvA", [128, NPV])
        for n, s in (("w_inA", [D, 4096]), ("w_gateA", [D, 3072]), ("w_brA", [1536, D]), ("w_outA", [D, D]),
                     ("w_fiA", [D, 2 * DFF]), ("w_foA", [DFF, D]), ("lamA", [1, 256]), ("subgA", [1, 128]),
                     ("dbias", [128, 8, 3, 256]), ("fbias", [128, 4, LW]), ("ut_p", [128, 4, 32])):
            I[n] = din(n, s)
        for n in ("kaT_o", "kbT_o", "kaT_p", "kbT_p"):
            I[n] = din(n, [128, 4, T], BF16)
        for n in ("va_o", "vb_o", "va_p", "vb_p"):
            I[n] = din(n, [T, 512], BF16)
    if lb is not None:
        I["pvB"] = din("pvB", [128, NPV])
        for n, s in (("w_adaB", [D, 9 * D]), ("w_fiB", [D, 2 * DFF]), ("w_foB", [DFF, D]), ("w_inB", [D, 4096])):
            I[n] = din(n, s)
    O = {}
    if stage < 3:
        O["xT_out"] = dout("xT_out", [128, DC, T])
        O["modB"] = dout("modB", [128, 72])
        O["kaT"] = dout("kaT", [128, 4, T], BF16)
        O["kbT"] = dout("kbT", [128, 4, T], BF16)
        O["va"] = dout("va", [T, 512], BF16)
        O["vb"] = dout("vb", [T, 512], BF16)
        O["ut"] = dout("ut", [128, 4, 32])
    else:
        O["out"] = dout("out", [T, D])
    S = {}
    if la is not None:
        S["h2T"] = dscr("h2T_s", [128, DC, T], BF16)
        S["qaT"] = dscr("qaT_s", [128, 4, T], BF16)
        S["qbT"] = dscr("qbT_s", [128, 4, T], BF16)
        S["uT"] = dscr("uT_s", [128, 4, 32 + T])
        S["yaT"] = dscr("yaT_s", [512, T], BF16)
        S["ybT"] = dscr("ybT_s", [512, T], BF16)
        S["ycT"] = dscr("ycT_s", [128, 4, T], BF16)
        S["vaf"] = dscr("vaf_s", [2 * T, 512], BF16)
        S["vbf"] = dscr("vbf_s", [2 * T, 512], BF16)

    with contextlib.ExitStack() as st:
        def sb(name, shape, dt):
            return st.enter_context(nc.sbuf_tensor(name, shape, dt))
        xT = sb("xT", [128, DC, T], F32)
        identF = sb("identF", [128, 128], F32)
        onesF = sb("onesF", [128, 128], F32)
        identB = sb("identB", [128, 128], BF16)
        onesB = sb("onesB", [128, 128], BF16)
        blkB = sb("blkB", [128, 128], BF16)
        pvA = sb("pvA_t", [128, NPV], F32)
        pvB = sb("pvB_t", [128, NPV], F32)
        modA = sb("modA_t", [128, 72], F32)
        modB = sb("modB_t", [128, 72], F32)
        sclA = sb("sclA", [128, 3, 8], F32)
        gatA = sb("gatA", [128, 3, 8], F32)
        sclB = sb("sclB", [128, 3, 8], F32)
        gatB = sb("gatB", [128, 3, 8], F32)
        neglam = sb("neglam", [128, 4], F32)
        NA = 33600
        arena_t = sb("arena", [128, NA], F32)
        A = Arena(arena_t[:, :], NA)
        ps = [st.enter_context(nc.psum_tensor("ps%d" % i, [128, 512], F32)) for i in range(7)]
        psb = st.enter_context(nc.psum_tensor("psb", [128, 1024], BF16))
        P = Prog(nc)
        rr = [0]

        def pb(lo=0, hi=7):
            i = lo + rr[0] % (hi - lo)
            rr[0] += 1
            return ps[i], ("ps", i)

        P.op("pool", lambda e: e.memset(identF[:], 0.0), writes=["identF"])
        P.op("pool", lambda e: e.affine_select(out=identF[:], in_=identF[:], pattern=[[-1, 128]],
                                                compare_op=ALU.not_equal, fill=1.0, base=0, channel_multiplier=1),
             reads=["identF"], writes=["identF"])
        P.op("pool", lambda e: e.memset(onesF[:], 1.0), writes=["onesF"])
        P.op("pool", lambda e: e.memset(onesB[:], 1.0), writes=["onesB"])
        P.op("pool", lambda e: e.memset(blkB[:], 0.0), writes=["blkB"])
        P.op("pool", lambda e: e.memset(blkB[0:64, 0:64], 1.0), reads=["blkB"], writes=["blkB"])
        P.op("pool", lambda e: e.memset(blkB[64:128, 64:128], 1.0), reads=["blkB"], writes=["blkB"])
        P.op("dve", lambda e: e.tensor_copy(out=identB[:], in_=identF[:]), reads=["identF"], writes=["identB"])
        if la is not None:
            P.dma("sp", lambda e: e.dma_start(out=pvA[:], in_=I["pvA"][:, :]), writes=["pvA"])
            P.dma("sp", lambda e: e.dma_start(out=modA[:], in_=I["modA"][:, :]), writes=["modA"])
        if lb is not None:
            P.dma("sp", lambda e: e.dma_start(out=pvB[:], in_=I["pvB"][:, :]), writes=["pvB"])

        def derive(pv, pvk, mod, modk, scl, gat, tag):
            for n in range(3):
                P.op("dve", lambda e, n=n: e.scalar_tensor_tensor(
                    out=scl[:, n, :], in0=mod[:, (3 * n + 1) * 8:(3 * n + 2) * 8], scalar=1.0,
                    in1=pv[:, PV_NG + n * 8:PV_NG + n * 8 + 8], op0=ALU.add, op1=ALU.mult),
                    reads=[modk, pvk], writes=["scl" + tag])
                P.op("dve", lambda e, n=n: e.tensor_scalar(
                    out=gat[:, n, :], in0=mod[:, (3 * n + 2) * 8:(3 * n + 3) * 8],
                    scalar1=(1.0 if n == 1 else 0.5), scalar2=None, op0=ALU.mult),
                    reads=[modk], writes=["gat" + tag])

        def fresh(mark=0):
            P.barrier()
            A.off = mark

        def ph_load_x():
            A.reset()
            xin = [A.f32(1024) for _ in range(2)]
            for i in range(16):
                s = i % 2
                P.dma("sp", lambda e, i=i, s=s: e.dma_start(out=xin[s], in_=I["x"][i * 128:(i + 1) * 128, :]),
                      writes=[("xin", s)])
                for hb in range(2):
                    bk, bkey = pb()

                    def tr(e, s=s, hb=hb, bk=bk):
                        for c4 in range(4):
                            c = hb * 4 + c4
                            ins = e.transpose(out=bk[:, c4 * 128:(c4 + 1) * 128],
                                              in_=xin[s][:, c * 128:(c + 1) * 128], identity=identF[:])
                        return ins
                    P.op("pe", tr, reads=[("xin", s), "identF"], writes=[bkey])
                    dst = xT[:, hb * 4:(hb + 1) * 4, i * 128:(i + 1) * 128]
                    if hb == 0:
                        P.op("dve", lambda e, dst=dst, bk=bk: e.tensor_copy(out=dst, in_=_rs(bk[:, :], [4, 128])),
                             reads=[bkey], writes=[("xT", i // 4, hb)])
                    else:
                        P.op("act", lambda e, dst=dst, bk=bk: e.activation(out=dst, in_=_rs(bk[:, :], [4, 128]),
                                                                          func=AF.Identity),
                             reads=[bkey], writes=[("xT", i // 4, hb)])

        def xkeys(tt):
            return [("xT", tt, 0), ("xT", tt, 1)]

        def ph_load_xT():
            for c in range(DC):
                P.dma("sp", lambda e, c=c: e.dma_start(out=xT[:, c, :], in_=I["xT_in"][:, c, :]),
                      writes=[("xT", tt, hb) for tt in range(NT) for hb in range(2)])

        def ph_adaln(pv, pvk, w_ada, mod, modk):
            A.reset()
            P.barrier()
            csb = A.bf(8)
            wa = [A.bf(8, 1024) for _ in range(2)]
            P.op("act", lambda e: e.activation(out=csb, in_=pv[:, PV_CS:PV_CS + 8], func=AF.Silu),
                 reads=[pvk], writes=["csb"])
            bk, bkey = ps[6], ("ps", 6)
            for j in range(9):
                s = j % 2
                P.dma("pool", lambda e, j=j, s=s: e.dma_start(
                    out=wa[s], in_=w_ada[:, j * 1024:(j + 1) * 1024].rearrange("(c p) n -> p c n", p=128)),
                    writes=[("wa", s)])

                def mm(e, j=j, s=s):
                    for cb in range(8):
                        for kc in range(8):
                            ins = e.matmul(bk[:, j * 8 + cb:j * 8 + cb + 1], lhsT=wa[s][:, kc, cb * 128:(cb + 1) * 128],
                                           rhs=csb[:, kc:kc + 1], start=(kc == 0), stop=(kc == 7))
                    return ins
                P.op("pe", mm, reads=[("wa", s), "csb"], writes=[bkey])
            P.op("dve", lambda e: e.tensor_tensor(out=mod[:, :], in0=bk[:, 0:72], in1=pv[:, PV_BADA:PV_BADA + 72],
                                                  op=ALU.add), reads=[bkey, pvk], writes=[modk])

        def ph_norm(n, pv, pvk, mod, modk, scl, sclk, hT):
            sqb = A.bf(8, TT)
            rs = [A.f32(TT) for _ in range(2)]
            tmpf = [A.f32(TT) for _ in range(2)]
            k = 0
            for tt in range(NT):
                sl = slice(tt * TT, (tt + 1) * TT)
                P.op("act", lambda e, sl=sl: e.activation(out=sqb, in_=xT[:, :, sl], func=AF.Square),
                     reads=xkeys(tt), writes=["sqb"])
                bk, bkey = pb()

                def mm(e, bk=bk):
                    for c in range(DC):
                        ins = e.matmul(bk[:, :], lhsT=onesB[:], rhs=sqb[:, c, :], start=(c == 0), stop=(c == DC - 1))
                    return ins
                P.op("pe", mm, reads=["sqb", "onesB"], writes=[bkey])
                r = rs[tt % 2]
                rk = ("rs", tt % 2)
                P.op("act", lambda e, r=r, bk=bk: e.activation(out=r, in_=bk[:, :], func=AF.Ln, scale=1.0 / D, bias=EPS),
                     reads=[bkey], writes=[rk])
                P.op("act", lambda e, r=r: e.activation(out=r, in_=r, func=AF.Exp, scale=-0.5), reads=[rk], writes=[rk])
                for c in range(DC):
                    tf = tmpf[k % 2]
                    tk = ("tmpf", k % 2)
                    k += 1
                    P.op("dve", lambda e, c=c, sl=sl, tf=tf, r=r: e.tensor_tensor(out=tf, in0=xT[:, c, sl], in1=r, op=ALU.mult),
                         reads=xkeys(tt) + [rk], writes=[tk])
                    P.op("act", lambda e, c=c, sl=sl, tf=tf: e.activation(
                        out=hT[:, c, sl], in_=tf, func=AF.Identity, scale=scl[:, n, c:c + 1],
                        bias=mod[:, 3 * n * 8 + c:3 * n * 8 + c + 1]),
                        reads=[tk, sclk, modk], writes=[("h", tt)])

        def ph_ffn(w_up, w_dn, gat, gatk, n, hT):
            groups = [(0, 6), (6, 6), (12, 6), (18, 4)]
            actT = A.bf(6, T)
            wup = [A.bf(2, 8, 256) for _ in range(2)]
            wdn = [A.bf(6, D) for _ in range(2)]
            sgf = [A.f32(TT) for _ in range(2)]
            k = 0
            npair = 0
            for gi, (c0, gn) in enumerate(groups):
                ws = gi % 2
                P.dma("pool", lambda e, c0=c0, gn=gn, ws=ws: e.dma_start(
                    out=wdn[ws][:, 0:gn, :], in_=w_dn[c0 * 128:(c0 + gn) * 128, :].rearrange("(i p) n -> p i n", p=128)),
                    writes=[("wdn", ws)])
                for pi in range(gn // 2):
                    cpair = c0 + 2 * pi
                    us = npair % 2
                    npair += 1
                    for gu in range(2):
                        col = gu * DFF + cpair * 128
                        P.dma("pool", lambda e, us=us, gu=gu, col=col: e.dma_start(
                            out=wup[us][:, gu, :, :], in_=w_up[:, col:col + 256].rearrange("(c p) n -> p c n", p=128)),
                            writes=[("wup", us, gu)])
                    for ci in range(2):
                        il = 2 * pi + ci
                        for tt in range(NT):
                            sl = slice(tt * TT, (tt + 1) * TT)
                            bg, bgk = pb()
                            bu, buk = pb()

                            def mmg(e, us=us, ci=ci, sl=sl, bg=bg, gu=0):
                                for kc in range(DC):
                                    ins = e.matmul(bg[:, :], lhsT=wup[us][:, gu, kc, ci * 128:(ci + 1) * 128], rhs=hT[:, kc, sl],
                                                   start=(kc == 0), stop=(kc == DC - 1))
                                return ins
                            P.op("pe", mmg, reads=[("wup", us, 0), ("h", tt)], writes=[bgk])
                            P.op("pe", lambda e, us=us, ci=ci, sl=sl, bu=bu: mmg(e, us, ci, sl, bu, 1),
                                 reads=[("wup", us, 1), ("h", tt)], writes=[buk])
                            sg = sgf[k % 2]
                            sk = ("sgf", k % 2)
                            k += 1
                            P.op("act", lambda e, sg=sg, bg=bg: e.activation(out=sg, in_=bg[:, :], func=AF.Silu),
                                 reads=[bgk], writes=[sk])
                            P.op("dve", lambda e, sg=sg, bu=bu, il=il, sl=sl: e.tensor_tensor(
                                out=actT[:, il, sl], in0=bu[:, :], in1=sg, op=ALU.mult),
                                reads=[buk, sk], writes=[("actT", tt)])
                for dc in range(DC):
                    for tt in range(NT):
                        sl = slice(tt * TT, (tt + 1) * TT)
                        bd, bdk = pb()

                        def mmd(e, dc=dc, sl=sl, bd=bd, gn=gn, ws=ws):
                            for i in range(gn):
                                ins = e.matmul(bd[:, :], lhsT=wdn[ws][:, i, dc * 128:(dc + 1) * 128], rhs=actT[:, i, sl],
                                               start=(i == 0), stop=(i == gn - 1))
                            return ins
                        P.op("pe", mmd, reads=[("wdn", ws), ("actT", tt)], writes=[bdk])
                        P.op("dve", lambda e, dc=dc, sl=sl, bd=bd: e.scalar_tensor_tensor(
                            out=xT[:, dc, sl], in0=bd[:, :], scalar=gat[:, n, dc:dc + 1], in1=xT[:, dc, sl],
                            op0=ALU.mult, op1=ALU.add),
                            reads=[bdk, gatk, ("xT", tt, dc // 4)], writes=[("xT", tt, dc // 4)])

        def proj_norm(w_in, colbase, pv, pvk, gaincol, hT, dst):
            wq = A.bf(8, 512)
            kst = [A.bf(T) for _ in range(2)]
            sq = [A.bf(TT) for _ in range(2)]
            qf = [A.f32(TT) for _ in range(2)]
            rs = [A.f32(TT) for _ in range(2)]
            P.dma("pool", lambda e: e.dma_start(out=wq, in_=w_in[:, colbase:colbase + 512].rearrange("(c p) n -> p c n", p=128)),
                  writes=["wq"])
            k = 0
            pend = []
            for ch in range(4):
                ks = kst[ch % 2]
                kk = ("kst", ch % 2)
                for tt in range(NT):
                    sl = slice(tt * TT, (tt + 1) * TT)
                    u = k % 2
                    k += 1
                    bq, bqk = pb()

                    def mm(e, ch=ch, sl=sl, bq=bq):
                        for kc in range(DC):
                            ins = e.matmul(bq[:, :], lhsT=wq[:, kc, ch * 128:(ch + 1) * 128], rhs=hT[:, kc, sl],
                                           start=(kc == 0), stop=(kc == DC - 1))
                        return ins
                    P.op("pe", mm, reads=["wq", ("h", tt)], writes=[bqk])
                    P.op("act", lambda e, u=u, bq=bq: e.activation(out=sq[u], in_=bq[:, :], func=AF.Square),
                         reads=[bqk], writes=[("sq", u)])
                    P.op("dve", lambda e, u=u, bq=bq: e.tensor_copy(out=qf[u], in_=bq[:, :]), reads=[bqk], writes=[("qf", u)])

                    def back(u=u, ks=ks, kk=kk, sl=sl, ch=ch, tt=tt):
                        bs, bsk = pb()
                        P.op("pe", lambda e: e.matmul(bs[:, :], lhsT=blkB[:], rhs=sq[u], start=True, stop=True),
                             reads=[("sq", u), "blkB"], writes=[bsk])
                        P.op("act", lambda e: e.activation(out=rs[u], in_=bs[:, :], func=AF.Ln, scale=1.0 / 64, bias=EPS),
                             reads=[bsk], writes=[("prs", u)])
                        P.op("act", lambda e: e.activation(out=rs[u], in_=rs[u], func=AF.Exp, scale=-0.5),
                             reads=[("prs", u)], writes=[("prs", u)])
                        P.op("dve", lambda e: e.scalar_tensor_tensor(
                            out=ks[:, sl], in0=qf[u], scalar=pv[:, PV_GAIN + gaincol:PV_GAIN + gaincol + 1], in1=rs[u],
                            op0=ALU.mult, op1=ALU.mult), reads=[("qf", u), ("prs", u), pvk], writes=[kk])
                        if tt == NT - 1:
                            P.dma("sp", lambda e: e.dma_start(out=dst(ch), in_=ks), reads=[kk], writes=[("dst", colbase, ch)])
                    pend.append(back)
                    while len(pend) > 1:
                        pend.pop(0)()
            while pend:
                pend.pop(0)()

        def v_proj(w_in, colbase, hT, dst):
            wv = A.bf(8, 512)
            vst = [A.bf(512) for _ in range(2)]
            P.dma("pool", lambda e: e.dma_start(out=wv, in_=w_in[:, colbase:colbase + 512].rearrange("(c p) n -> p c n", p=128)),
                  writes=["wv"])
            for i in range(16):
                u = i % 2
                bv, bvk = pb()

                def mm(e, i=i, bv=bv):
                    for kc in range(DC):
                        ins = e.matmul(bv[:, :], lhsT=hT[:, kc, i * 128:(i + 1) * 128], rhs=wv[:, kc, :],
                                       start=(kc == 0), stop=(kc == DC - 1))
                    return ins
                P.op("pe", mm, reads=["wv", ("h", i // 4)], writes=[bvk])
                P.op("act", lambda e, u=u, bv=bv: e.activation(out=vst[u], in_=bv[:, :], func=AF.Identity),
                     reads=[bvk], writes=[("vst", u)])
                P.dma("sp", lambda e, i=i, u=u: e.dma_start(out=dst[i * 128:(i + 1) * 128, :], in_=vst[u]),
                      reads=[("vst", u)], writes=[("vdst", colbase, i)])

        def glu(w_in, hT, tts, sink):
            wu = A.bf(8, 1024)
            sgf = [A.f32(TT) for _ in range(2)]
            uf = [A.f32(TT) for _ in range(2)]
            P.dma("pool", lambda e: e.dma_start(out=wu, in_=w_in[:, 3072:4096].rearrange("(c p) n -> p c n", p=128)),
                  writes=["wu"])
            k = 0
            for ch in range(4):
                for tt in tts:
                    sl = slice(tt * TT, (tt + 1) * TT)
                    u = k % 2
                    k += 1
                    b1, b1k = pb()
                    b2, b2k = pb()

                    def mm(e, col, bk, sl=sl):
                        for kc in range(DC):
                            ins = e.matmul(bk[:, :], lhsT=wu[:, kc, col:col + 128], rhs=hT[:, kc, sl],
                                           start=(kc == 0), stop=(kc == DC - 1))
                        return ins
                    P.op("pe", lambda e, ch=ch, b1=b1, mm=mm: mm(e, ch * 128, b1), reads=["wu", ("h", tt)], writes=[b1k])
                    P.op("pe", lambda e, ch=ch, b2=b2, mm=mm: mm(e, 512 + ch * 128, b2), reads=["wu", ("h", tt)], writes=[b2k])
                    P.op("act", lambda e, u=u, b2=b2: e.activation(out=sgf[u], in_=b2[:, :], func=AF.Sigmoid),
                         reads=[b2k], writes=[("gsg", u)])
                    P.op("dve", lambda e, u=u, b1=b1: e.tensor_tensor(out=uf[u], in0=b1[:, :], in1=sgf[u], op=ALU.mult),
                         reads=[b1k, ("gsg", u)], writes=[("guf", u)])
                    sink(ch, tt, uf[u], ("guf", u))

        def ph_kv(pv, pvk, mod, modk, scl, sclk, w_in):
            A.reset()
            P.barrier()
            hT = A.bf(8, T)
            mark = A.off
            kvl = _DBG.get("kv", 9)
            ph_norm(1, pv, pvk, mod, modk, scl, sclk, hT)
            fresh(mark)
            if kvl >= 1:
                proj_norm(w_in, 512, pv, pvk, 1, hT, lambda ch: O["kaT"][:, ch, :])
                fresh(mark)
            if kvl >= 2:
                proj_norm(w_in, 2048, pv, pvk, 3, hT, lambda ch: O["kbT"][:, ch, :])
                fresh(mark)
            if kvl >= 3:
                v_proj(w_in, 1024, hT, O["va"])
                fresh(mark)
            if kvl >= 4:
                v_proj(w_in, 2560, hT, O["vb"])
                fresh(mark)
            if kvl < 5:
                return

            def sink(ch, tt, uf, key):
                P.dma("sp", lambda e, ch=ch, uf=uf: e.dma_start(out=O["ut"][:, ch, :], in_=uf[:, TT - 32:TT]),
                      reads=[key], writes=[("utout", ch)])
            glu(w_in, hT, [NT - 1], sink)

        def ph_store_x():
            for c in range(DC):
                P.dma("sp", lambda e, c=c: e.dma_start(out=O["xT_out"][:, c, :], in_=xT[:, c, :]),
                      reads=[("xT", tt, c // 4) for tt in range(NT)], writes=[("xout", c)])

        def ph_ffn_full(n, pv, pvk, mod, modk, scl, sclk, gat, gatk, w_up, w_dn):
            fresh()
            hT = A.bf(8, T)
            mark = A.off
            ph_norm(n, pv, pvk, mod, modk, scl, sclk, hT)
            fresh(mark)
            ph_ffn(w_up, w_dn, gat, gatk, n, hT)

        def ph_qu(pv, pvk, mod, modk, scl, sclk, w_in):
            fresh()
            hT = A.bf(8, T)
            mark = A.off
            ph_norm(1, pv, pvk, mod, modk, scl, sclk, hT)
            for c in range(DC):
                P.dma("sp", lambda e, c=c: e.dma_start(out=S["h2T"][:, c, :], in_=hT[:, c, :]),
                      reads=[("h", tt) for tt in range(NT)], writes=[("h2Ts", c)])
            fresh(mark)
            proj_norm(w_in, 0, pv, pvk, 0, hT, lambda ch: S["qaT"][:, ch, :])
            fresh(mark)
            proj_norm(w_in, 1536, pv, pvk, 2, hT, lambda ch: S["qbT"][:, ch, :])
            fresh(mark)
            utp = A.f32(4, 32)
            P.dma("sp", lambda e: e.dma_start(out=utp, in_=I["ut_p"][:, :, :]), writes=["utp"])
            P.op("dve", lambda e: e.tensor_scalar(out=utp, in0=utp, scalar1=pv[:, PV_PF:PV_PF + 1], scalar2=None, op0=ALU.mult),
                 reads=["utp", pvk], writes=["utp"])
            P.dma("sp", lambda e: e.dma_start(out=S["uT"][:, :, 0:32], in_=utp), reads=["utp"], writes=["uTs"])

            def sink(ch, tt, uf, key):
                P.dma("sp", lambda e, ch=ch, tt=tt, uf=uf: e.dma_start(out=S["uT"][:, ch, 32 + tt * TT:32 + (tt + 1) * TT], in_=uf),
                      reads=[key], writes=[("uTs", ch, tt)])
            glu(w_in, hT, list(range(NT)), sink)
            for nm, src_p, src_o in (("vaf", "va_p", "va_o"), ("vbf", "vb_p", "vb_o")):
                P.dma("sp", lambda e, nm=nm, src_p=src_p: e.dma_start(out=S[nm][0:T, :], in_=I[src_p][:, :]), writes=[(nm, 0)])
                P.dma("sp", lambda e, nm=nm, src_o=src_o: e.dma_start(out=S[nm][T:2 * T, :], in_=I[src_o][:, :]), writes=[(nm, 1)])

        def ph_dil(pv, pvk):
            fresh()
            qh2 = [A.bf(T) for _ in range(2)]
            kh2 = [A.bf(2 * T) for _ in range(2)]
            vbuf = [A.bf(32, 256) for _ in range(2)]
            pT = [A.bf(2, 256) for _ in range(2)]
            tmp = [A.f32(2, 256) for _ in range(2)]
            acc = A.f32(2, T)
            db2 = [A.f32(2, 3, 256) for _ in range(2)]
            dbb2 = [A.bf(2, 3, 256) for _ in range(2)]
            rlow = A.f32(T)
            yn = A.bf(T)
            for s in range(2):
                P.op("pool", lambda e, s=s: e.memset(vbuf[s], 1.0), writes=[("vbuf", s, r) for r in range(16)])
            unit = 0
            allacc = [("acc", b) for b in range(16)]
            def load_hp(hp):
                b_ = hp % 2
                P.dma("sp", lambda e: e.dma_start(out=qh2[b_], in_=S["qaT"][:, hp, :]), writes=[("qh", b_)])
                P.dma("sp", lambda e: e.dma_start(out=kh2[b_][:, 0:T], in_=I["kaT_p"][:, hp, :]), writes=[("kh", b_, 0)])
                P.dma("sp", lambda e: e.dma_start(out=kh2[b_][:, T:2 * T], in_=I["kaT_o"][:, hp, :]), writes=[("kh", b_, 1)])
                P.dma("sp", lambda e: e.dma_start(out=db2[b_], in_=I["dbias"][:, 2 * hp:2 * hp + 2, :, :]), writes=[("db", b_)])
                P.op("dve", lambda e: e.tensor_scalar(out=dbb2[b_], in0=db2[b_], scalar1=8.0, scalar2=None, op0=ALU.mult),
                     reads=[("db", b_)], writes=[("dbb", b_)])
            load_hp(0)
            for hp in range(4):
                if hp + 1 < 4:
                    load_hp(hp + 1)
                hb_ = hp % 2
                qh, kh, dbb = qh2[hb_], kh2[hb_], dbb2[hb_]
                pend = []

                def flush(n):
                    while len(pend) > n:
                        pend.pop(0)()
                for p, (win, d) in enumerate(DIL):
                    vs = (hp * 3 + p) % 2
                    vb = vbuf[vs]
                    nb = 16 // d
                    vview = S["vaf"].rearrange("(i d) c -> i d c", d=d)
                    i0_ = T // d - 128
                    for r in range(d):
                        for h in range(2):
                            src = vview[i0_:i0_ + 128 * (nb + 1), r, hp * 128 + h * 64:hp * 128 + h * 64 + 64].rearrange(
                                "(j p) c -> p j c", p=128)
                            P.dma("sp", lambda e, vb=vb, r=r, h=h, nb=nb, src=src: e.dma_start(
                                out=vb[:, r * (nb + 1):(r + 1) * (nb + 1), h * 128:h * 128 + 64], in_=src),
                                reads=[("vaf", 0), ("vaf", 1)], writes=[("vbuf", vs, r)])
                    khv = _rs(kh, [2 * T // d, d])
                    qhv = _rs(qh, [T // d, d])
                    for r in range(d):
                        for m in range(nb):
                            u = unit % 2
                            unit += 1
                            bS = [pb(), pb()]
                            bO, bOk = pb()

                            def st_(e, r=r, m=m, d=d, bS=bS, khv=khv, qhv=qhv, dbb=dbb, p=p):
                                for h in range(2):
                                    for jj in range(2):
                                        ki = T // d + 128 * (m - 1 + jj)
                                        ins = e.matmul(bS[h][0][:, jj * 128:(jj + 1) * 128],
                                                       lhsT=khv[h * 64:(h + 1) * 64, ki:ki + 128, r],
                                                       rhs=qhv[h * 64:(h + 1) * 64, 128 * m:128 * m + 128, r],
                                                       start=(jj == 0), stop=False, skip_group_check=True)
                                for h in range(2):
                                    for jj in range(2):
                                        ins = e.matmul(bS[h][0][:, jj * 128:(jj + 1) * 128], lhsT=identB[:],
                                                       rhs=dbb[:, h, p, jj * 128:(jj + 1) * 128],
                                                       start=False, stop=True, skip_group_check=True)
                                return ins
                            P.op("pe", st_, reads=[("qh", hb_), ("kh", hb_, 0), ("kh", hb_, 1), ("dbb", hb_), "identB"],
                                 writes=[bS[0][1], bS[1][1]])
                            for h in range(2):
                                if m == 0:
                                    P.op("act", lambda e, h=h, u=u, bS=bS: e.activation(
                                        out=pT[u][:, h, 0:128], in_=bS[h][0][:, 0:128], func=AF.Exp, scale=0.125, bias=pv[:, PV_PB:PV_PB + 1]),
                                        reads=[bS[h][1], pvk], writes=[("dpT", u, h)])
                                    P.op("act", lambda e, h=h, u=u, bS=bS: e.activation(
                                        out=pT[u][:, h, 128:256], in_=bS[h][0][:, 128:256], func=AF.Exp, scale=0.125),
                                        reads=[bS[h][1]], writes=[("dpT", u, h)])
                                else:
                                    P.op("act", lambda e, h=h, u=u, bS=bS: e.activation(out=pT[u][:, h, :], in_=bS[h][0][:, 0:256], func=AF.Exp, scale=0.125),
                                         reads=[bS[h][1]], writes=[("dpT", u, h)])

                            def back(r=r, m=m, nb=nb, u=u, vb=vb, vs=vs, bO=bO, bOk=bOk, d=d, p=p):
                                def pv_(e):
                                    for h in range(2):
                                        for jj in range(2):
                                            ins = e.matmul(bO[:, h * 128:(h + 1) * 128],
                                                           lhsT=vb[:, r * (nb + 1) + m + jj, h * 128:(h + 1) * 128],
                                                           rhs=pT[u][:, h, jj * 128:(jj + 1) * 128], start=(jj == 0), stop=(jj == 1))
                                    return ins
                                P.op("pe", pv_, reads=[("dpT", u, 0), ("dpT", u, 1), ("vbuf", vs, r)], writes=[bOk])
                                av = acc.rearrange("p h (i d) -> p h i d", d=d)[:, :, 128 * m:128 * m + 128, r]
                                ak = [("acc", b) for b in range(d * m, d * (m + 1))]
                                if p == 0:
                                    P.op("dve", lambda e: e.tensor_copy(out=av, in_=_rs(bO[:, 0:256], [2, 128])),
                                         reads=[bOk], writes=ak)
                                else:
                                    P.op("dve", lambda e: e.tensor_tensor(out=av, in0=_rs(bO[:, 0:256], [2, 128]), in1=av, op=ALU.add),
                                         reads=[bOk] + ak, writes=ak)
                            pend.append(back)
                            flush(1)
                flush(0)
                for h in range(2):
                    P.op("dve", lambda e, h=h: e.reciprocal(out=acc[64:128, h, :], in_=acc[64:128, h, :]), reads=allacc, writes=allacc)
                    P.op("dve", lambda e, h=h: e.tensor_copy(out=rlow[0:64, :], in_=acc[64:128, h, :]), reads=allacc, writes=["rlow"])
                    P.op("dve", lambda e, h=h: e.tensor_tensor(out=yn[0:64, :], in0=acc[0:64, h, :], in1=rlow[0:64, :], op=ALU.mult),
                         reads=allacc + ["rlow"], writes=["yn"])
                    P.dma("sp", lambda e, hp=hp, h=h: e.dma_start(out=S["yaT"][(hp * 2 + h) * 64:(hp * 2 + h + 1) * 64, :], in_=yn[0:64, :]),
                          reads=["yn"], writes=[("yaTs", hp, h)])

        def ph_lambda(l):
            import math
            lam_init = 0.8 - 0.6 * math.exp(-0.3 * l)
            fresh()
            lv = A.f32(256)
            t = A.f32(128)
            s2 = A.f32(2)
            P.dma("sp", lambda e: e.dma_start(out=lv, in_=I["lamA"][0:1, :].partition_broadcast(128)), writes=["lv"])
            P.op("dve", lambda e: e.tensor_tensor(out=_rs(t, [2, 64]), in0=_rs(lv, [2, 2, 64])[:, :, 0, :],
                                                  in1=_rs(lv, [2, 2, 64])[:, :, 1, :], op=ALU.mult), reads=["lv"], writes=["lvt"])
            P.op("dve", lambda e: e.tensor_reduce(out=s2, in_=_rs(t, [2, 64]), axis=mybir.AxisListType.X, op=ALU.add),
                 reads=["lvt"], writes=["lvs"])
            P.op("act", lambda e: e.activation(out=s2, in_=s2, func=AF.Exp), reads=["lvs"], writes=["lvs"])
            P.op("dve", lambda e: e.tensor_tensor(out=neglam[:, 1:2], in0=s2[:, 1:2], in1=s2[:, 0:1], op=ALU.subtract),
                 reads=["lvs"], writes=["neglam1"])
            P.op("dve", lambda e: e.tensor_scalar(out=neglam[:, 0:1], in0=neglam[:, 1:2], scalar1=-lam_init, scalar2=None, op0=ALU.add),
                 reads=["neglam1"], writes=["neglam"])
            return lam_init

        def ph_diff(pv, pvk, lam_init):
            fresh()
            qh2 = [A.bf(T) for _ in range(2)]
            kh2 = [A.bf(2 * T) for _ in range(2)]
            vaug2 = [A.bf(32, 130) for _ in range(2)]
            W2 = [A.f32(LW) for _ in range(2)]
            NS = 4
            tmp = [A.f32(TT) for _ in range(NS)]
            pT = [A.bf(TT) for _ in range(NS)]
            sbank = [(ps[0], ("ps", 0)), (ps[1], ("ps", 1)), (ps[6], ("ps", 6)), (ps[5], ("ps", 5))]

            def aslot(c, qb):
                a = c * 4 + qb
                return ps[2 + a // 3], ("ps", 2 + a // 3), (a % 3) * 160
            o1 = A.f32(128)
            of = A.f32(128)
            sm = A.f32(8)
            ybt = [A.bf(128) for _ in range(4)]
            ybst = A.bf(TT)
            gsub = A.f32(128)
            for b_ in range(2):
                P.op("pool", lambda e, b_=b_: e.memset(vaug2[b_], 1.0), writes=[("vaug", b_)])
            P.dma("sp", lambda e: e.dma_start(out=gsub, in_=I["subgA"][0:1, :].partition_broadcast(128)), writes=["gsub"])

            def load_head(h):
                b_ = h % 2
                P.dma("sp", lambda e: e.dma_start(out=qh2[b_], in_=S["qbT"][:, h, :]), writes=[("qh", b_)])
                P.dma("sp", lambda e: e.dma_start(out=kh2[b_][:, 0:T], in_=I["kbT_p"][:, h, :]), writes=[("kh", b_, 0)])
                P.dma("sp", lambda e: e.dma_start(out=kh2[b_][:, T:2 * T], in_=I["kbT_o"][:, h, :]), writes=[("kh", b_, 1)])
                P.dma("sp", lambda e: e.dma_start(
                    out=vaug2[b_][:, :, 0:128], in_=S["vbf"][:, h * 128:(h + 1) * 128].rearrange("(j p) c -> p j c", p=128)),
                    reads=[("vbf", 0), ("vbf", 1)], writes=[("vaug", b_)])
                P.dma("sp", lambda e: e.dma_start(out=W2[b_], in_=I["fbias"][:, h, :]), writes=[("W", b_)])
            P.op("dve", lambda e: e.tensor_scalar(out=gsub, in0=gsub, scalar1=(1.0 - lam_init), scalar2=None, op0=ALU.mult),
                 reads=["gsub"], writes=["gsub"])
            cnt = 0
            load_head(0)
            for h in range(4):
                if h + 1 < 4:
                    load_head(h + 1)
                hb_ = h % 2
                qh, kh, vaug, W = qh2[hb_], kh2[hb_], vaug2[hb_], W2[hb_]
                pend = []
                finb = []

                def flush(n):
                    while len(pend) > n:
                        pend.pop(0)()
                since = 0
                for g in range(4):
                    first = {}
                    for c in range(2):
                        nkb = 16 + 4 * g + 4
                        for kbi in range(nkb):
                            u = cnt % NS
                            cnt += 1
                            bS, bSk = sbank[u]
                            P.op("pe", lambda e, c=c, kbi=kbi, g=g, bS=bS, kh=kh, qh=qh: e.matmul(
                                bS[:, :], lhsT=kh[c * 64:(c + 1) * 64, kbi * 128:(kbi + 1) * 128],
                                rhs=qh[c * 64:(c + 1) * 64, g * TT:(g + 1) * TT], start=True, stop=True),
                                reads=[("qh", hb_), ("kh", hb_, 0), ("kh", hb_, 1)], writes=[bSk])
                            delta = (T + TT * g) - 128 * kbi
                            off = delta + 384 if delta < 1792 else 2176
                            P.op("dve", lambda e, u=u, off=off, bS=bS, W=W: e.scalar_tensor_tensor(
                                out=tmp[u], in0=bS[:, :], scalar=0.125, in1=W[:, off:off + TT], op0=ALU.mult, op1=ALU.add),
                                reads=[bSk, ("W", hb_)], writes=[("ftmp", u)])
                            if kbi < 16:
                                P.op("act", lambda e, u=u: e.activation(out=pT[u], in_=tmp[u], func=AF.Exp, bias=pv[:, PV_PB:PV_PB + 1]),
                                     reads=[("ftmp", u), pvk], writes=[("fpT", u)])
                            else:
                                P.op("act", lambda e, u=u: e.activation(out=pT[u], in_=tmp[u], func=AF.Exp),
                                     reads=[("ftmp", u)], writes=[("fpT", u)])
                            plan = []
                            for qb in range(4):
                                if kbi >= 16 and (4 * g + qb) < (kbi - 16):
                                    continue
                                bkx, bkk, col = aslot(c, qb)
                                stf = first.get(bkk, True)
                                first[bkk] = False
                                last = (kbi == 16 + 4 * g + qb)
                                plan.append((qb, stf, last, bkx, bkk, col))

                            def back(plan=plan, c=c, u=u, kbi=kbi, vaug=vaug, hb_=hb_):
                                def pv_(e):
                                    for qb, stf, last, bkx, bkk, col in plan:
                                        ins = e.matmul(bkx[:, col:col + 129], lhsT=pT[u][:, qb * 128:(qb + 1) * 128],
                                                       rhs=vaug[:, kbi, 0:129], start=stf, stop=last, skip_group_check=True)
                                    return ins
                                P.op("pe", pv_, reads=[("fpT", u), ("vaug", hb_)], writes=sorted(set(x[4] for x in plan)))
                            pend.append(back)
                            flush(NS - 1)
                            since += 1
                            if finb and since >= 3:
                                finb.pop(0)()
                    flush(0)
                    if finb:
                        finb.pop(0)()
                    for qb in range(4):
                        b1, b1k, col = aslot(0, qb)
                        b2, b2k, col2 = aslot(1, qb)
                        yb = ybt[qb]
                        P.op("dve", lambda e, b1=b1, col=col: e.reciprocal(out=sm[:, 0:1], in_=b1[:, col + 128:col + 129]),
                             reads=[b1k], writes=["sm0"])
                        P.op("dve", lambda e, b2=b2, col2=col2: e.reciprocal(out=sm[:, 1:2], in_=b2[:, col2 + 128:col2 + 129]),
                             reads=[b2k], writes=["sm1"])
                        P.op("dve", lambda e: e.tensor_tensor(out=sm[:, 2:3], in0=sm[:, 1:2], in1=neglam[:, 0:1], op=ALU.mult),
                             reads=["sm1", "neglam"], writes=["sm2"])
                        P.op("act", lambda e, b1=b1, col=col: e.activation(out=o1, in_=b1[:, col:col + 128], func=AF.Identity, scale=sm[:, 0:1]),
                             reads=[b1k, "sm0"], writes=["o1"])
                        P.op("dve", lambda e, b2=b2, col2=col2: e.scalar_tensor_tensor(
                            out=of, in0=b2[:, col2:col2 + 128], scalar=sm[:, 2:3], in1=o1, op0=ALU.mult, op1=ALU.add),
                            reads=[b2k, "sm2", "o1"], writes=["of"])
                        P.op("act", lambda e: e.activation(out=o1, in_=of, func=AF.Square, accum_out=sm[:, 3:4]),
                             reads=["of", "o1"], writes=["o1", "sm3"])
                        P.op("act", lambda e: e.activation(out=sm[:, 4:5], in_=sm[:, 3:4], func=AF.Ln, scale=1.0 / 128, bias=EPS),
                             reads=["sm3"], writes=["sm4"])
                        P.op("act", lambda e: e.activation(out=sm[:, 4:5], in_=sm[:, 4:5], func=AF.Exp, scale=-0.5),
                             reads=["sm4"], writes=["sm4"])
                        P.op("dve", lambda e, yb=yb: e.scalar_tensor_tensor(
                            out=yb, in0=of, scalar=sm[:, 4:5], in1=gsub, op0=ALU.mult, op1=ALU.mult),
                            reads=["of", "sm4", "gsub"], writes=[("ybt", qb)])

                    def fin_b(h=h, g=g):
                        def tr(e):
                            for qb in range(4):
                                ins = e.transpose(out=psb[:, qb * 128:(qb + 1) * 128], in_=ybt[qb], identity=identB[:])
                            return ins
                        P.op("pe", tr, reads=[("ybt", qb) for qb in range(4)] + ["identB"], writes=["psb"])
                        P.op("act", lambda e: e.activation(out=ybst, in_=psb[:, 0:TT], func=AF.Identity), reads=["psb"], writes=["ybst"])
                        P.dma("sp", lambda e: e.dma_start(out=S["ybT"][h * 128:(h + 1) * 128, g * TT:(g + 1) * TT], in_=ybst),
                              reads=["ybst"], writes=[("ybTs", h, g)])
                    finb.append(fin_b)
                    since = 0
                while finb:
                    finb.pop(0)()

        def ph_conv(pv, pvk):
            fresh()
            ubb = A.bf(4, 32 + T)
            dg = A.bf(124, 128)
            ycf = A.f32(4, T)
            ycb = A.bf(4, T)
            sqf = A.f32(4, TT)
            mf = A.f32(TT)
            vf = A.f32(TT)
            tf = [A.f32(TT) for _ in range(2)]
            HW_ = (32 + T) // 2
            for cc in range(4):
                for hh in range(2):
                    P.dma("pool", lambda e, cc=cc, hh=hh: e.dma_start(out=ubb[:, cc, hh * HW_:(hh + 1) * HW_],
                                                                      in_=S["uT"][:, cc, hh * HW_:(hh + 1) * HW_]),
                          reads=["uTs"] + [("uTs", cc, tt) for tt in range(NT)], writes=[("ubb", cc)])
            for idx in range(124):
                P.op("dve", lambda e, idx=idx: e.tensor_scalar(out=dg[:, idx, :], in0=identB[:], scalar1=pv[:, PV_CW + idx:PV_CW + idx + 1],
                                                               scalar2=None, op0=ALU.mult), reads=["identB", pvk], writes=[("dg", idx // 31)])
            for cc in range(4):
                for tt in range(NT):
                    bk, bkey = pb()

                    def mmc(e, cc=cc, tt=tt, bk=bk):
                        for j in range(31):
                            ins = e.matmul(bk[:, :], lhsT=dg[:, cc * 31 + j, :], rhs=ubb[:, cc, 2 + j + tt * TT:2 + j + (tt + 1) * TT],
                                           start=(j == 0), stop=(j == 30))
                        return ins
                    P.op("pe", mmc, reads=[("dg", cc), ("ubb", cc)], writes=[bkey])
                    P.op("act", lambda e, cc=cc, tt=tt, bk=bk: e.activation(
                        out=ycf[:, cc, tt * TT:(tt + 1) * TT], in_=bk[:, :], func=AF.Identity, bias=pv[:, PV_CB + cc:PV_CB + cc + 1]),
                        reads=[bkey, pvk], writes=[("ycf", cc)])
            k = 0
            allycf = [("ycf", cc) for cc in range(4)]
            for tt in range(NT):
                sl = slice(tt * TT, (tt + 1) * TT)
                P.op("act", lambda e, sl=sl: e.activation(out=sqf, in_=ycf[:, :, sl], func=AF.Square), reads=allycf, writes=["sqf"])
                b1, b1k = pb()
                b2, b2k = pb()

                def mm1(e, sl=sl, b1=b1):
                    for cc in range(4):
                        ins = e.matmul(b1[:, :], lhsT=onesF[:], rhs=ycf[:, cc, sl], start=(cc == 0), stop=(cc == 3))
                    return ins

                def mm2(e, b2=b2):
                    for cc in range(4):
                        ins = e.matmul(b2[:, :], lhsT=onesF[:], rhs=sqf[:, cc, :], start=(cc == 0), stop=(cc == 3))
                    return ins
                P.op("pe", mm1, reads=allycf + ["onesF"], writes=[b1k])
                P.op("pe", mm2, reads=["sqf", "onesF"], writes=[b2k])
                P.op("dve", lambda e, b1=b1: e.tensor_scalar(out=mf, in0=b1[:, :], scalar1=1.0 / 512, scalar2=None, op0=ALU.mult),
                     reads=[b1k], writes=["mf"])
                P.op("dve", lambda e: e.tensor_tensor(out=vf, in0=mf, in1=mf, op=ALU.mult), reads=["mf"], writes=["vf"])
                P.op("dve", lambda e, b2=b2: e.scalar_tensor_tensor(out=vf, in0=b2[:, :], scalar=1.0 / 512, in1=vf,
                                                                    op0=ALU.mult, op1=ALU.subtract),
                     reads=[b2k, "vf"], writes=["vf"])
                P.op("act", lambda e: e.activation(out=vf, in_=vf, func=AF.Ln, bias=EPS), reads=["vf"], writes=["vf"])
                P.op("act", lambda e: e.activation(out=vf, in_=vf, func=AF.Exp, scale=-0.5), reads=["vf"], writes=["vf"])
                for cc in range(4):
                    t_ = tf[k % 2]
                    tk = ("ctf", k % 2)
                    k += 1
                    P.op("dve", lambda e, cc=cc, sl=sl, t_=t_: e.tensor_tensor(out=t_, in0=ycf[:, cc, sl], in1=mf, op=ALU.subtract),
                         reads=allycf + ["mf"], writes=[tk])
                    P.op("dve", lambda e, t_=t_: e.tensor_tensor(out=t_, in0=t_, in1=vf, op=ALU.mult), reads=[tk, "vf"], writes=[tk])
                    P.op("act", lambda e, cc=cc, sl=sl, t_=t_: e.activation(
                        out=ycb[:, cc, sl], in_=t_, func=AF.Silu, scale=pv[:, PV_LG + cc:PV_LG + cc + 1],
                        bias=pv[:, PV_LB + cc:PV_LB + cc + 1]), reads=[tk, pvk], writes=[("ycb", cc)])
            for cc in range(4):
                P.dma("sp", lambda e, cc=cc: e.dma_start(out=S["ycT"][:, cc, :], in_=ycb[:, cc, :]), reads=[("ycb", cc)], writes=[("ycTs", cc)])

        def ph_merge(pv, pvk, gat, gatk, w_gate, w_br, w_out):
            fresh()
            wo = A.bf(8, D)
            hh = A.bf(8, 1024)
            yy = A.bf(3, 4, 1024)
            zT = A.bf(8, 1024)
            wg = [A.bf(3, 8, 128) for _ in range(3)]
            wb = [A.bf(3, 4, 128) for _ in range(3)]
            gsb = [A.f32(TT) for _ in range(2)]
            zacc = [A.f32(TT) for _ in range(2)]
            prod = [A.f32(TT) for _ in range(2)]
            P.dma("pool", lambda e: e.dma_start(out=wo, in_=w_out.rearrange("(c p) n -> p c n", p=128)), writes=["wo"])
            ysrc = [S["yaT"].rearrange("(c p) t -> p c t", p=128), S["ybT"].rearrange("(c p) t -> p c t", p=128), S["ycT"]]
            ykeys = [[("yaTs", hp, h) for hp in range(4) for h in range(2)],
                     [("ybTs", h, g) for h in range(4) for g in range(4)],
                     [("ycTs", cc) for cc in range(4)]]
            cnt = 0
            k = 0
            for half in range(2):
                hs = slice(half * 1024, (half + 1) * 1024)
                P.dma("sp", lambda e, hs=hs: e.dma_start(out=hh, in_=S["h2T"][:, :, hs]),
                      reads=[("h2Ts", c) for c in range(DC)], writes=["hh"])
                for i in range(3):
                    P.dma("sp", lambda e, i=i, hs=hs: e.dma_start(out=yy[:, i, :, :], in_=ysrc[i][:, :, hs]),
                          reads=ykeys[i], writes=[("yy", i)])
                for j in range(DC):
                    s = cnt % 3
                    cnt += 1
                    for i in range(3):
                        col = i * 1024 + j * 128
                        P.dma("pool", lambda e, s=s, i=i, col=col: e.dma_start(
                            out=wg[s][:, i, :, :], in_=w_gate[:, col:col + 128].rearrange("(c p) n -> p c n", p=128)),
                            writes=[("wg", s, i)])
                        P.dma("pool", lambda e, s=s, i=i, j=j: e.dma_start(
                            out=wb[s][:, i, :, :], in_=w_br[i * 512:(i + 1) * 512, j * 128:(j + 1) * 128].rearrange("(c p) n -> p c n", p=128)),
                            writes=[("wb", s, i)])
                    for t2 in range(2):
                        sl = slice(t2 * TT, (t2 + 1) * TT)
                        za = zacc[(2 * j + t2) % 2]
                        zk = ("zacc", (2 * j + t2) % 2)
                        for i in range(3):
                            u = k % 2
                            k += 1
                            bg, bgk = pb()
                            by, byk = pb()

                            def mg(e, s=s, i=i, sl=sl, bg=bg):
                                for kc in range(DC):
                                    ins = e.matmul(bg[:, :], lhsT=wg[s][:, i, kc, :], rhs=hh[:, kc, sl], start=(kc == 0), stop=(kc == DC - 1))
                                return ins

                            def my(e, s=s, i=i, sl=sl, by=by):
                                for kc in range(4):
                                    ins = e.matmul(by[:, :], lhsT=wb[s][:, i, kc, :], rhs=yy[:, i, kc, sl], start=(kc == 0), stop=(kc == 3))
                                return ins
                            P.op("pe", mg, reads=[("wg", s, i), "hh"], writes=[bgk])
                            P.op("pe", my, reads=[("wb", s, i), ("yy", i)], writes=[byk])
                            P.op("act", lambda e, u=u, bg=bg, i=i, j=j: e.activation(
                                out=gsb[u], in_=bg[:, :], func=AF.Sigmoid, bias=pv[:, PV_BG + i * 8 + j:PV_BG + i * 8 + j + 1]),
                                reads=[bgk, pvk], writes=[("gsb", u)])
                            if i == 0:
                                P.op("dve", lambda e, u=u, by=by, za=za: e.tensor_tensor(out=za, in0=by[:, :], in1=gsb[u], op=ALU.mult),
                                     reads=[byk, ("gsb", u)], writes=[zk])
                            else:
                                P.op("dve", lambda e, u=u, by=by: e.tensor_tensor(out=prod[u], in0=by[:, :], in1=gsb[u], op=ALU.mult),
                                     reads=[byk, ("gsb", u)], writes=[("prod", u)])
                                if i == 1:
                                    P.op("pool", lambda e, u=u, za=za: e.tensor_tensor(out=za, in0=za, in1=prod[u], op=ALU.add),
                                         reads=[zk, ("prod", u)], writes=[zk])
                                else:
                                    P.op("pool", lambda e, u=u, za=za, j=j, sl=sl: e.tensor_tensor(out=zT[:, j, sl], in0=za, in1=prod[u], op=ALU.add),
                                         reads=[zk, ("prod", u)], writes=[("zT", t2)])
                for dj in range(DC):
                    for t2 in range(2):
                        sl = slice(t2 * TT, (t2 + 1) * TT)
                        xs = slice(half * 1024 + t2 * TT, half * 1024 + (t2 + 1) * TT)
                        tt = half * 2 + t2
                        bd, bdk = pb()

                        def mo(e, dj=dj, sl=sl, bd=bd):
                            for kc in range(DC):
                                ins = e.matmul(bd[:, :], lhsT=wo[:, kc, dj * 128:(dj + 1) * 128], rhs=zT[:, kc, sl], start=(kc == 0), stop=(kc == DC - 1))
                            return ins
                        P.op("pe", mo, reads=["wo", ("zT", t2)], writes=[bdk])
                        P.op("dve", lambda e, dj=dj, xs=xs, bd=bd: e.scalar_tensor_tensor(
                            out=xT[:, dj, xs], in0=bd[:, :], scalar=gat[:, 1, dj:dj + 1], in1=xT[:, dj, xs], op0=ALU.mult, op1=ALU.add),
                            reads=[bdk, gatk, ("xT", tt, dj // 4)], writes=[("xT", tt, dj // 4)])

        def ph_out():
            fresh()
            ot = [A.f32(D) for _ in range(2)]
            for i in range(16):
                s = i % 2
                for hb in range(2):
                    bk, bkey = pb()

                    def tr(e, i=i, hb=hb, bk=bk):
                        for c4 in range(4):
                            c = hb * 4 + c4
                            ins = e.transpose(out=bk[:, c4 * 128:(c4 + 1) * 128], in_=xT[:, c, i * 128:(i + 1) * 128], identity=identF[:])
                        return ins
                    P.op("pe", tr, reads=[("xT", i // 4, hb), "identF"], writes=[bkey])
                    if hb == 0:
                        P.op("dve", lambda e, s=s, bk=bk: e.tensor_copy(out=ot[s][:, 0:512], in_=bk[:, :]), reads=[bkey], writes=[("ot", s, 0)])
                    else:
                        P.op("act", lambda e, s=s, bk=bk: e.activation(out=ot[s][:, 512:1024], in_=bk[:, :], func=AF.Identity),
                             reads=[bkey], writes=[("ot", s, 1)])
                P.dma("sp", lambda e, i=i, s=s: e.dma_start(out=O["out"][i * 128:(i + 1) * 128, :], in_=ot[s]),
                      reads=[("ot", s, 0), ("ot", s, 1)], writes=[("outd", i)])

        finals = []
        if stage == 1:
            ph_load_x()
        else:
            ph_load_xT()
            derive(pvA, "pvA", modA, "modA", sclA, gatA, "A")
            s2 = _DBG.get("s2", 9)
            ph_qu(pvA, "pvA", modA, "modA", sclA, "sclA", I["w_inA"])
            lam_init = ph_lambda(la)
            if s2 >= 2:
                ph_dil(pvA, "pvA")
            if s2 >= 3:
                ph_diff(pvA, "pvA", lam_init)
            if s2 >= 4:
                ph_conv(pvA, "pvA")
            if s2 >= 5:
                ph_merge(pvA, "pvA", gatA, "gatA", I["w_gateA"], I["w_brA"], I["w_outA"])
            if s2 >= 6:
                ph_ffn_full(2, pvA, "pvA", modA, "modA", sclA, "sclA", gatA, "gatA", I["w_fiA"], I["w_foA"])
        if lb is not None:
            up = _DBG.get("upto", 9) if _DBG.get("s2", 9) >= 7 else 0
            if up >= 1:
                ph_adaln(pvB, "pvB", I["w_adaB"], modB, "modB")
                derive(pvB, "pvB", modB, "modB", sclB, gatB, "B")
            if up >= 2:
                ph_ffn_full(0, pvB, "pvB", modB, "modB", sclB, "sclB", gatB, "gatB", I["w_fiB"], I["w_foB"])
            if up >= 3:
                ph_kv(pvB, "pvB", modB, "modB", sclB, "sclB", I["w_inB"])
            ph_store_x()
            P.dma("sp", lambda e: e.dma_start(out=O["modB"][:, :], in_=modB[:]), reads=["modB"], writes=["modBout"])
        else:
            ph_out()
        P.barrier()
        P.op("pool", lambda e: e.memset(neglam[:, 3:4], 0.0), writes=["__tail"])
        P.emit(["__tail"])
    return nc


_CACHE = {}
_DBG = {}


def _prog(stage):
    if stage not in _CACHE:
        _CACHE[stage] = build(stage)
    return _CACHE[stage]


def _mixer_inputs(inp, l, prev_res, dbias, fbias):
    maps = []
    for c in range(NCORES):
        own = prev_res[c]
        prv = prev_res[c - 1] if c % 2 == 1 else prev_res[c]
        maps.append({
            "xT_in": own["xT_out"], "modA": own["modB"], "pvA": _host_pv(inp, l, c),
            "w_inA": inp["w_in"][l], "w_gateA": inp["w_gate"][l], "w_brA": inp["w_branch"][l].reshape(1536, D),
            "w_outA": inp["w_out"][l], "w_fiA": inp["w_ffn_in"][l, 1], "w_foA": inp["w_ffn_out"][l, 1],
            "lamA": inp["lambda_vec"][l].reshape(1, 256), "subgA": inp["subln_g"][l].reshape(1, 128),
            "dbias": dbias, "fbias": fbias, "ut_p": prv["ut"],
            "kaT_o": own["kaT"], "kbT_o": own["kbT"], "va_o": own["va"], "vb_o": own["vb"],
            "kaT_p": prv["kaT"], "kbT_p": prv["kbT"], "va_p": prv["va"], "vb_p": prv["vb"],
        })
    return maps


def _ffn_kv_inputs(inp, l, c):
    return {"pvB": _host_pv(inp, l, c), "w_adaB": inp["w_ada"][l], "w_fiB": inp["w_ffn_in"][l, 0],
            "w_foB": inp["w_ffn_out"][l, 0], "w_inB": inp["w_in"][l]}


def kernel(**inputs):
    inp = {k: np.ascontiguousarray(np.asarray(v, dtype=np.float32)) for k, v in inputs.items()}
    dbias, fbias = _host_bias_tables(inp["rel_bias"])
    cores = list(range(NCORES))
    m1 = []
    for c in cores:
        b, hf = c // 2, c % 2
        d = {"x": np.ascontiguousarray(inp["x"][b, hf * T:(hf + 1) * T, :])}
        d.update(_ffn_kv_inputs(inp, 0, c))
        m1.append(d)
    r1 = run_bass_kernel_spmd(_prog(1), m1, core_ids=cores).results
    m2 = _mixer_inputs(inp, 0, r1, dbias, fbias)
    for c in cores:
        m2[c].update(_ffn_kv_inputs(inp, 1, c))
    r2 = run_bass_kernel_spmd(_prog(2), m2, core_ids=cores).results
    m3 = _mixer_inputs(inp, 1, r2, dbias, fbias)
    r3 = run_bass_kernel_spmd(_prog(3), m3, core_ids=cores).results
    out = np.empty((4, 2 * T, D), np.float32)
    for c in cores:
        out[c // 2, (c % 2) * T:(c % 2 + 1) * T, :] = r3[c]["out"]
    return out
```

```python
import numpy as np
import concourse.bass as bass
import concourse.mybir as mybir
from concourse.bass_utils import run_bass_kernel_spmd

F32 = mybir.dt.float32
BF16 = mybir.dt.bfloat16
AF = mybir.ActivationFunctionType
ALU = mybir.AluOpType

D = 1024
DC = 8
T = 2048
TT = 512
NT = T // TT
DFF = 2816
FC = 22
NCORES = 8


class _Op:
    __slots__ = ("eng", "fn", "deps", "is_dma", "sig", "tok", "idx")


class Prog:
    ENGS = ("pe", "act", "dve", "pool", "sp")
    NDMA = 6

    def __init__(self, nc):
        self.nc = nc
        self.ops = []
        self.last_w = {}
        self.readers = {}
        self.last_c = {}
        self.dma_hist = {}
        self.pending = {}

    def _add(self, eng, fn, reads, writes, is_dma):
        op = _Op()
        op.eng, op.fn, op.is_dma, op.sig, op.tok = eng, fn, is_dma, False, None
        op.idx = len(self.ops)
        deps = {}
        for r in reads:
            w = self.last_w.get(r)
            if w is not None:
                deps[w.idx] = (w, True)
        for wkey in writes:
            w = self.last_w.get(wkey)
            if w is not None and w.idx not in deps:
                deps[w.idx] = (w, False)
            for rd in self.readers.get(wkey, ()):
                if rd.idx not in deps:
                    deps[rd.idx] = (rd, False)
        keep = []
        for d in self.pending.pop(eng, ()):
            if d.eng == eng and not d.is_dma and eng == "pe":
                continue
            if d.idx not in deps:
                keep.append(d)
                d.sig = True
        for d, raw in deps.values():
            if d.eng == eng and not d.is_dma and not is_dma:
                if eng == "pe" or not raw:
                    continue
            keep.append(d)
            d.sig = True
        op.deps = keep
        for r in reads:
            self.readers.setdefault(r, []).append(op)
        for wkey in writes:
            self.last_w[wkey] = op
            self.readers[wkey] = []
        self.ops.append(op)
        if is_dma:
            self.dma_hist.setdefault(eng, []).append(op)
        else:
            self.last_c[eng] = op
        return op

    def barrier(self):
        B = list(self.last_c.values())
        for q, h in self.dma_hist.items():
            B.extend(h[-self.NDMA:])
        for e in self.ENGS:
            self.pending[e] = list(self.pending.get(e, ())) + B

    def op(self, eng, fn, reads=(), writes=()):
        reads, writes = tuple(reads), tuple(writes)
        extra = tuple(r for r in reads if (r == "psb" or (isinstance(r, tuple) and r[0] == "ps")) and r not in writes)
        return self._add(eng, fn, reads, writes + extra, False)

    def dma(self, eng, fn, reads=(), writes=()):
        return self._add(eng, fn, tuple(reads), tuple(writes), True)

    def emit(self, final_keys):
        nc = self.nc
        import contextlib
        with contextlib.ExitStack() as st:
            esem = {e: st.enter_context(nc.semaphore("s_" + e)) for e in self.ENGS}
            dsem = {e: [st.enter_context(nc.semaphore("d_%s%d" % (e, i))) for i in range(self.NDMA)]
                    for e in ("sp", "pool", "act")}
            ecnt = {e: 0 for e in self.ENGS}
            dcnt = {e: [0] * self.NDMA for e in dsem}
            drr = {e: 0 for e in dsem}
            finals = [self.last_w[k] for k in final_keys]
            for f in finals:
                f.sig = True
            prewait = {}
            for op in self.ops:
                if op.is_dma:
                    k = drr[op.eng] % self.NDMA
                    drr[op.eng] += 1
                    prewait[op.idx] = (dsem[op.eng][k], dcnt[op.eng][k])
                    dcnt[op.eng][k] += 16
                    op.tok = (dsem[op.eng][k], dcnt[op.eng][k])
                elif op.sig:
                    ecnt[op.eng] += 1
                    op.tok = (esem[op.eng], ecnt[op.eng])
            assert max(ecnt.values()) < 60000, ecnt
            per = {e: [o for o in self.ops if o.eng == e] for e in self.ENGS}
            block = st.enter_context(nc.Block())

            def run(eng_obj, ename, tail):
                waited = {}

                def w(sem, val):
                    if val <= 0:
                        return
                    key = id(sem)
                    if waited.get(key, 0) >= val:
                        return
                    waited[key] = val
                    eng_obj.wait_ge(sem, val)
                for op in per[ename]:
                    for d in op.deps:
                        w(*d.tok)
                    if op.is_dma:
                        w(*prewait[op.idx])
                    ins = op.fn(eng_obj)
                    if op.tok is not None:
                        ins.then_inc(op.tok[0], 16 if op.is_dma else 1)
                if tail:
                    for f in finals:
                        w(*f.tok)

            @block.tensor
            def _(e):
                run(e, "pe", False)

            @block.scalar
            def _(e):
                run(e, "act", False)

            @block.vector
            def _(e):
                run(e, "dve", False)

            @block.gpsimd
            def _(e):
                run(e, "pool", False)

            @block.sync
            def _(e):
                run(e, "sp", True)


def _rs(ap, shape):
    shape = list(shape)
    if len(shape) == 1:
        return ap
    names = "abcd"[:len(shape)]
    kw = {names[i]: shape[i] for i in range(len(shape))}
    return ap.rearrange("p (%s) -> p %s" % (" ".join(names), " ".join(names)), **kw)


class Arena:
    def __init__(self, ap_f32, nwords):
        self.base, self.n, self.off = ap_f32, nwords, 0

    def reset(self):
        self.off = 0

    def f32(self, *shape):
        n = int(np.prod(shape))
        a = self.base[:, self.off:self.off + n]
        self.off += n
        assert self.off <= self.n, (self.off, self.n)
        return _rs(a, shape)

    def bf(self, *shape):
        n = int(np.prod(shape))
        w = (n + 1) // 2
        a = self.base[:, self.off:self.off + w].bitcast(BF16)
        self.off += w
        assert self.off <= self.n, (self.off, self.n)
        return _rs(a[:, 0:n], shape)


PV_NG = 0
PV_BADA = 24
PV_GAIN = 96
PV_CW = 100
PV_CB = 224
PV_LG = 228
PV_LB = 232
PV_BG = 236
PV_CS = 260
PV_PF = 268
PV_PB = 269
NPV = 272

LW = 2688
EPS = 1e-6
DIL = ((128, 1), (512, 4), (2048, 16))


def _t5_bucket_np(dist):
    dist = np.asarray(dist, np.int64)
    dd = np.maximum(dist.astype(np.float32), np.float32(1.0))
    large = 16 + (np.log(dd / np.float32(16.0)) / np.float32(np.log(2048.0 / 16.0)) * np.float32(16.0)).astype(np.int32)
    large = np.minimum(large, 31)
    return np.where(dist < 16, dist, large).astype(np.int64)


def _host_bias_tables(rel_bias):
    NEG = np.float32(-1e30)
    k = np.arange(128)[:, None]
    dbias = np.empty((128, 8, 3, 256), np.float32)
    for p, (win, d) in enumerate(DIL):
        j = np.arange(256)[None, :]
        rel = np.where(j < 128, j + 128 - k, j - 128 - k)
        valid = (rel >= 0) & (rel <= 128)
        b = _t5_bucket_np(np.maximum(rel, 0) * d)
        for h in range(8):
            dbias[:, h, p, :] = np.where(valid, rel_bias[b, h], NEG)
    j = np.arange(LW)[None, :]
    dist = j - 384 - k
    b = _t5_bucket_np(np.maximum(dist, 0))
    fbias = np.empty((128, 4, LW), np.float32)
    for h in range(4):
        fbias[:, h, :] = np.where(dist >= 0, rel_bias[b, 8 + h], NEG)
    return dbias, fbias


def _host_pv(inp, l, core):
    b = core // 2
    pv = np.zeros((128, NPV), np.float32)
    pv[:, PV_NG:PV_NG + 24] = inp["norm_g"][l].reshape(3, 8, 128).transpose(2, 0, 1).reshape(128, 24)
    pv[:, PV_BADA:PV_BADA + 72] = inp["b_ada"][l].reshape(9, 8, 128).transpose(2, 0, 1).reshape(128, 72)
    g = inp["qk_gain"][l]
    pv[:, PV_GAIN + 0] = np.concatenate([g[0], g[0]])
    pv[:, PV_GAIN + 1] = np.concatenate([g[1], g[1]])
    pv[:, PV_GAIN + 2] = np.concatenate([g[2], g[3]])
    pv[:, PV_GAIN + 3] = np.concatenate([g[4], g[5]])
    pv[:, PV_CW:PV_CW + 124] = inp["conv_w"][l].reshape(31, 4, 128).transpose(2, 1, 0).reshape(128, 124)
    pv[:, PV_CB:PV_CB + 4] = inp["conv_b"][l].reshape(4, 128).T
    pv[:, PV_LG:PV_LG + 4] = inp["conv_ln_g"][l].reshape(4, 128).T
    pv[:, PV_LB:PV_LB + 4] = inp["conv_ln_b"][l].reshape(4, 128).T
    pv[:, PV_BG:PV_BG + 24] = inp["b_gate"][l].reshape(3, 8, 128).transpose(2, 0, 1).reshape(128, 24)
    pv[:, PV_CS:PV_CS + 8] = inp["c"][b].reshape(8, 128).T
    pv[:, PV_PF] = 1.0 if core % 2 == 1 else 0.0
    pv[:, PV_PB] = 0.0 if core % 2 == 1 else -30000.0
    return pv


def build(stage, dbg=False):
    import contextlib
    nc = bass.Bass("TRN2", target_bir_lowering=False)
    la = {1: None, 2: 0, 3: 1}[stage]
    lb = {1: 0, 2: 1, 3: None}[stage]

    def din(name, shape, dt=F32):
        return nc.dram_tensor(name, list(shape), dt, kind="ExternalInput").ap()

    def dout(name, shape, dt=F32):
        return nc.dram_tensor(name, list(shape), dt, kind="ExternalOutput").ap()

    def dscr(name, shape, dt=F32):
        return nc.dram_tensor(name, list(shape), dt, kind=("ExternalOutput" if dbg else "Internal")).ap()

    I = {}
    if stage == 1:
        I["x"] = din("x", [T, D])
    else:
        I["xT_in"] = din("xT_in", [128, DC, T])
        I["modA"] = din("modA", [128, 72])
        I["pvA"] = din("pvA", [128, NPV])
        for n, s in (("w_inA", [D, 4096]), ("w_gateA", [D, 3072]), ("w_brA", [1536, D]), ("w_outA", [D, D]),
                     ("w_fiA", [D, 2 * DFF]), ("w_foA", [DFF, D]), ("lamA", [1, 256]), ("subgA", [1, 128]),
                     ("dbias", [128, 8, 3, 256]), ("fbias", [128, 4, LW]), ("ut_p", [128, 4, 32])):
            I[n] = din(n, s)
        for n in ("kaT_o", "kbT_o", "kaT_p", "kbT_p"):
            I[n] = din(n, [128, 4, T], BF16)
        for n in ("va_o", "vb_o", "va_p", "vb_p"):
            I[n] = din(n, [T, 512], BF16)
    if lb is not None:
        I["pvB"] = din("pvB", [128, NPV])
        for n, s in (("w_adaB", [D, 9 * D]), ("w_fiB", [D, 2 * DFF]), ("w_foB", [DFF, D]), ("w_inB", [D, 4096])):
            I[n] = din(n, s)
    O = {}
    if stage < 3:
        O["xT_out"] = dout("xT_out", [128, DC, T])
        O["modB"] = dout("modB", [128, 72])
        O["kaT"] = dout("kaT", [128, 4, T], BF16)
        O["kbT"] = dout("kbT", [128, 4, T], BF16)
        O["va"] = dout("va", [T, 512], BF16)
        O["vb"] = dout("vb", [T, 512], BF16)
        O["ut"] = dout("ut", [128, 4, 32])
    else:
        O["out"] = dout("out", [T, D])
    S = {}
    if la is not None:
        S["h2T"] = dscr("h2T_s", [128, DC, T], BF16)
        S["qaT"] = dscr("qaT_s", [128, 4, T], BF16)
        S["qbT"] = dscr("qbT_s", [128, 4, T], BF16)
        S["uT"] = dscr("uT_s", [128, 4, 32 + T])
        S["yaT"] = dscr("yaT_s", [512, T], BF16)
        S["ybT"] = dscr("ybT_s", [512, T], BF16)
        S["ycT"] = dscr("ycT_s", [128, 4, T], BF16)
        S["vaf"] = dscr("vaf_s", [2 * T, 512], BF16)
        S["vbf"] = dscr("vbf_s", [2 * T, 512], BF16)

    with contextlib.ExitStack() as st:
        def sb(name, shape, dt):
            return st.enter_context(nc.sbuf_tensor(name, shape, dt))
        xT = sb("xT", [128, DC, T], F32)
        identF = sb("identF", [128, 128], F32)
        onesF = sb("onesF", [128, 128], F32)
        identB = sb("identB", [128, 128], BF16)
        onesB = sb("onesB", [128, 128], BF16)
        blkB = sb("blkB", [128, 128], BF16)
        pvA = sb("pvA_t", [128, NPV], F32)
        pvB = sb("pvB_t", [128, NPV], F32)
        modA = sb("modA_t", [128, 72], F32)
        modB = sb("modB_t", [128, 72], F32)
        sclA = sb("sclA", [128, 3, 8], F32)
        gatA = sb("gatA", [128, 3, 8], F32)
        sclB = sb("sclB", [128, 3, 8], F32)
        gatB = sb("gatB", [128, 3, 8], F32)
        neglam = sb("neglam", [128, 4], F32)
        NA = 33600
        arena_t = sb("arena", [128, NA], F32)
        A = Arena(arena_t[:, :], NA)
        ps = [st.enter_context(nc.psum_tensor("ps%d" % i, [128, 512], F32)) for i in range(7)]
        psb = st.enter_context(nc.psum_tensor("psb", [128, 1024], BF16))
        P = Prog(nc)
        rr = [0]

        def pb(lo=0, hi=7):
            i = lo + rr[0] % (hi - lo)
            rr[0] += 1
            return ps[i], ("ps", i)

        P.op("pool", lambda e: e.memset(identF[:], 0.0), writes=["identF"])
        P.op("pool", lambda e: e.affine_select(out=identF[:], in_=identF[:], pattern=[[-1, 128]],
                                                compare_op=ALU.not_equal, fill=1.0, base=0, channel_multiplier=1),
             reads=["identF"], writes=["identF"])
        P.op("pool", lambda e: e.memset(onesF[:], 1.0), writes=["onesF"])
        P.op("pool", lambda e: e.memset(onesB[:], 1.0), writes=["onesB"])
        P.op("pool", lambda e: e.memset(blkB[:], 0.0), writes=["blkB"])
        P.op("pool", lambda e: e.memset(blkB[0:64, 0:64], 1.0), reads=["blkB"], writes=["blkB"])
        P.op("pool", lambda e: e.memset(blkB[64:128, 64:128], 1.0), reads=["blkB"], writes=["blkB"])
        P.op("dve", lambda e: e.tensor_copy(out=identB[:], in_=identF[:]), reads=["identF"], writes=["identB"])
        if la is not None:
            P.dma("sp", lambda e: e.dma_start(out=pvA[:], in_=I["pvA"][:, :]), writes=["pvA"])
            P.dma("sp", lambda e: e.dma_start(out=modA[:], in_=I["modA"][:, :]), writes=["modA"])
        if lb is not None:
            P.dma("sp", lambda e: e.dma_start(out=pvB[:], in_=I["pvB"][:, :]), writes=["pvB"])

        def derive(pv, pvk, mod, modk, scl, gat, tag):
            for n in range(3):
                P.op("dve", lambda e, n=n: e.scalar_tensor_tensor(
                    out=scl[:, n, :], in0=mod[:, (3 * n + 1) * 8:(3 * n + 2) * 8], scalar=1.0,
                    in1=pv[:, PV_NG + n * 8:PV_NG + n * 8 + 8], op0=ALU.add, op1=ALU.mult),
                    reads=[modk, pvk], writes=["scl" + tag])
                P.op("dve", lambda e, n=n: e.tensor_scalar(
                    out=gat[:, n, :], in0=mod[:, (3 * n + 2) * 8:(3 * n + 3) * 8],
                    scalar1=(1.0 if n == 1 else 0.5), scalar2=None, op0=ALU.mult),
                    reads=[modk], writes=["gat" + tag])

        def fresh(mark=0):
            P.barrier()
            A.off = mark

        def ph_load_x():
            A.reset()
            xin = [A.f32(1024) for _ in range(2)]
            for i in range(16):
                s = i % 2
                P.dma("sp", lambda e, i=i, s=s: e.dma_start(out=xin[s], in_=I["x"][i * 128:(i + 1) * 128, :]),
                      writes=[("xin", s)])
                for hb in range(2):
                    bk, bkey = pb()

                    def tr(e, s=s, hb=hb, bk=bk):
                        for c4 in range(4):
                            c = hb * 4 + c4
                            ins = e.transpose(out=bk[:, c4 * 128:(c4 + 1) * 128],
                                              in_=xin[s][:, c * 128:(c + 1) * 128], identity=identF[:])
                        return ins
                    P.op("pe", tr, reads=[("xin", s), "identF"], writes=[bkey])
                    dst = xT[:, hb * 4:(hb + 1) * 4, i * 128:(i + 1) * 128]
                    if hb == 0:
                        P.op("dve", lambda e, dst=dst, bk=bk: e.tensor_copy(out=dst, in_=_rs(bk[:, :], [4, 128])),
                             reads=[bkey], writes=[("xT", i // 4, hb)])
                    else:
                        P.op("act", lambda e, dst=dst, bk=bk: e.activation(out=dst, in_=_rs(bk[:, :], [4, 128]),
                                                                          func=AF.Identity),
                             reads=[bkey], writes=[("xT", i // 4, hb)])

        def xkeys(tt):
            return [("xT", tt, 0), ("xT", tt, 1)]

        def ph_load_xT():
            for c in range(DC):
                P.dma("sp", lambda e, c=c: e.dma_start(out=xT[:, c, :], in_=I["xT_in"][:, c, :]),
                      writes=[("xT", tt, hb) for tt in range(NT) for hb in range(2)])

        def ph_adaln(pv, pvk, w_ada, mod, modk):
            A.reset()
            P.barrier()
            csb = A.bf(8)
            wa = [A.bf(8, 1024) for _ in range(2)]
            P.op("act", lambda e: e.activation(out=csb, in_=pv[:, PV_CS:PV_CS + 8], func=AF.Silu),
                 reads=[pvk], writes=["csb"])
            bk, bkey = ps[6], ("ps", 6)
            for j in range(9):
                s = j % 2
                P.dma("pool", lambda e, j=j, s=s: e.dma_start(
                    out=wa[s], in_=w_ada[:, j * 1024:(j + 1) * 1024].rearrange("(c p) n -> p c n", p=128)),
                    writes=[("wa", s)])

                def mm(e, j=j, s=s):
                    for cb in range(8):
                        for kc in range(8):
                            ins = e.matmul(bk[:, j * 8 + cb:j * 8 + cb + 1], lhsT=wa[s][:, kc, cb * 128:(cb + 1) * 128],
                                           rhs=csb[:, kc:kc + 1], start=(kc == 0), stop=(kc == 7))
                    return ins
                P.op("pe", mm, reads=[("wa", s), "csb"], writes=[bkey])
            P.op("dve", lambda e: e.tensor_tensor(out=mod[:, :], in0=bk[:, 0:72], in1=pv[:, PV_BADA:PV_BADA + 72],
                                                  op=ALU.add), reads=[bkey, pvk], writes=[modk])

        def ph_norm(n, pv, pvk, mod, modk, scl, sclk, hT):
            sqb = A.bf(8, TT)
            rs = [A.f32(TT) for _ in range(2)]
            tmpf = [A.f32(TT) for _ in range(2)]
            k = 0
            for tt in range(NT):
                sl = slice(tt * TT, (tt + 1) * TT)
                P.op("act", lambda e, sl=sl: e.activation(out=sqb, in_=xT[:, :, sl], func=AF.Square),
                     reads=xkeys(tt), writes=["sqb"])
                bk, bkey = pb()

                def mm(e, bk=bk):
                    for c in range(DC):
                        ins = e.matmul(bk[:, :], lhsT=onesB[:], rhs=sqb[:, c, :], start=(c == 0), stop=(c == DC - 1))
                    return ins
                P.op("pe", mm, reads=["sqb", "onesB"], writes=[bkey])
                r = rs[tt % 2]
                rk = ("rs", tt % 2)
                P.op("act", lambda e, r=r, bk=bk: e.activation(out=r, in_=bk[:, :], func=AF.Ln, scale=1.0 / D, bias=EPS),
                     reads=[bkey], writes=[rk])
                P.op("act", lambda e, r=r: e.activation(out=r, in_=r, func=AF.Exp, scale=-0.5), reads=[rk], writes=[rk])
                for c in range(DC):
                    tf = tmpf[k % 2]
                    tk = ("tmpf", k % 2)
                    k += 1
                    P.op("dve", lambda e, c=c, sl=sl, tf=tf, r=r: e.tensor_tensor(out=tf, in0=xT[:, c, sl], in1=r, op=ALU.mult),
                         reads=xkeys(tt) + [rk], writes=[tk])
                    P.op("act", lambda e, c=c, sl=sl, tf=tf: e.activation(
                        out=hT[:, c, sl], in_=tf, func=AF.Identity, scale=scl[:, n, c:c + 1],
                        bias=mod[:, 3 * n * 8 + c:3 * n * 8 + c + 1]),
                        reads=[tk, sclk, modk], writes=[("h", tt)])

        def ph_ffn(w_up, w_dn, gat, gatk, n, hT):
            groups = [(0, 6), (6, 6), (12, 6), (18, 4)]
            actT = A.bf(6, T)
            wup = [A.bf(2, 8, 256) for _ in range(2)]
            wdn = [A.bf(6, D) for _ in range(2)]
            sgf = [A.f32(TT) for _ in range(2)]
            k = 0
            npair = 0
            for gi, (c0, gn) in enumerate(groups):
                ws = gi % 2
                P.dma("pool", lambda e, c0=c0, gn=gn, ws=ws: e.dma_start(
                    out=wdn[ws][:, 0:gn, :], in_=w_dn[c0 * 128:(c0 + gn) * 128, :].rearrange("(i p) n -> p i n", p=128)),
                    writes=[("wdn", ws)])
                for pi in range(gn // 2):
                    cpair = c0 + 2 * pi
                    us = npair % 2
                    npair += 1
                    for gu in range(2):
                        col = gu * DFF + cpair * 128
                        P.dma("pool", lambda e, us=us, gu=gu, col=col: e.dma_start(
                            out=wup[us][:, gu, :, :], in_=w_up[:, col:col + 256].rearrange("(c p) n -> p c n", p=128)),
                            writes=[("wup", us, gu)])
                    for ci in range(2):
                        il = 2 * pi + ci
                        for tt in range(NT):
                            sl = slice(tt * TT, (tt + 1) * TT)
                            bg, bgk = pb()
                            bu, buk = pb()

                            def mmg(e, us=us, ci=ci, sl=sl, bg=bg, gu=0):
                                for kc in range(DC):
                                    ins = e.matmul(bg[:, :], lhsT=wup[us][:, gu, kc, ci * 128:(ci + 1) * 128], rhs=hT[:, kc, sl],
                                                   start=(kc == 0), stop=(kc == DC - 1))
                                return ins
                            P.op("pe", mmg, reads=[("wup", us, 0), ("h", tt)], writes=[bgk])
                            P.op("pe", lambda e, us=us, ci=ci, sl=sl, bu=bu: mmg(e, us, ci, sl, bu, 1),
                                 reads=[("wup", us, 1), ("h", tt)], writes=[buk])
                            sg = sgf[k % 2]
                            sk = ("sgf", k % 2)
                            k += 1
                            P.op("act", lambda e, sg=sg, bg=bg: e.activation(out=sg, in_=bg[:, :], func=AF.Silu),
                                 reads=[bgk], writes=[sk])
                            P.op("dve", lambda e, sg=sg, bu=bu, il=il, sl=sl: e.tensor_tensor(
                                out=actT[:, il, sl], in0=bu[:, :], in1=sg, op=ALU.mult),
                                reads=[buk, sk], writes=[("actT", tt)])
                for dc in range(DC):
                    for tt in range(NT):
                        sl = slice(tt * TT, (tt + 1) * TT)
                        bd, bdk = pb()

                        def mmd(e, dc=dc, sl=sl, bd=bd, gn=gn, ws=ws):
                            for i in range(gn):
                                ins = e.matmul(bd[:, :], lhsT=wdn[ws][:, i, dc * 128:(dc + 1) * 128], rhs=actT[:, i, sl],
                                               start=(i == 0), stop=(i == gn - 1))
                            return ins
                        P.op("pe", mmd, reads=[("wdn", ws), ("actT", tt)], writes=[bdk])
                        P.op("dve", lambda e, dc=dc, sl=sl, bd=bd: e.scalar_tensor_tensor(
                            out=xT[:, dc, sl], in0=bd[:, :], scalar=gat[:, n, dc:dc + 1], in1=xT[:, dc, sl],
                            op0=ALU.mult, op1=ALU.add),
                            reads=[bdk, gatk, ("xT", tt, dc // 4)], writes=[("xT", tt, dc // 4)])

        def proj_norm(w_in, colbase, pv, pvk, gaincol, hT, dst):
            wq = A.bf(8, 512)
            kst = [A.bf(T) for _ in range(2)]
            sq = [A.bf(TT) for _ in range(2)]
            qf = [A.f32(TT) for _ in range(2)]
            rs = [A.f32(TT) for _ in range(2)]
            P.dma("pool", lambda e: e.dma_start(out=wq, in_=w_in[:, colbase:colbase + 512].rearrange("(c p) n -> p c n", p=128)),
                  writes=["wq"])
            k = 0
            pend = []
            for ch in range(4):
                ks = kst[ch % 2]
                kk = ("kst", ch % 2)
                for tt in range(NT):
                    sl = slice(tt * TT, (tt + 1) * TT)
                    u = k % 2
                    k += 1
                    bq, bqk = pb()

                    def mm(e, ch=ch, sl=sl, bq=bq):
                        for kc in range(DC):
                            ins = e.matmul(bq[:, :], lhsT=wq[:, kc, ch * 128:(ch + 1) * 128], rhs=hT[:, kc, sl],
                                           start=(kc == 0), stop=(kc == DC - 1))
                        return ins
                    P.op("pe", mm, reads=["wq", ("h", tt)], writes=[bqk])
                    P.op("act", lambda e, u=u, bq=bq: e.activation(out=sq[u], in_=bq[:, :], func=AF.Square),
                         reads=[bqk], writes=[("sq", u)])
                    P.op("dve", lambda e, u=u, bq=bq: e.tensor_copy(out=qf[u], in_=bq[:, :]), reads=[bqk], writes=[("qf", u)])

                    def back(u=u, ks=ks, kk=kk, sl=sl, ch=ch, tt=tt):
                        bs, bsk = pb()
                        P.op("pe", lambda e: e.matmul(bs[:, :], lhsT=blkB[:], rhs=sq[u], start=True, stop=True),
                             reads=[("sq", u), "blkB"], writes=[bsk])
                        P.op("act", lambda e: e.activation(out=rs[u], in_=bs[:, :], func=AF.Ln, scale=1.0 / 64, bias=EPS),
                             reads=[bsk], writes=[("prs", u)])
                        P.op("act", lambda e: e.activation(out=rs[u], in_=rs[u], func=AF.Exp, scale=-0.5),
                             reads=[("prs", u)], writes=[("prs", u)])
                        P.op("dve", lambda e: e.scalar_tensor_tensor(
                            out=ks[:, sl], in0=qf[u], scalar=pv[:, PV_GAIN + gaincol:PV_GAIN + gaincol + 1], in1=rs[u],
                            op0=ALU.mult, op1=ALU.mult), reads=[("qf", u), ("prs", u), pvk], writes=[kk])
                        if tt == NT - 1:
                            P.dma("sp", lambda e: e.dma_start(out=dst(ch), in_=ks), reads=[kk], writes=[("dst", colbase, ch)])
                    pend.append(back)
                    while len(pend) > 1:
                        pend.pop(0)()
            while pend:
                pend.pop(0)()

        def v_proj(w_in, colbase, hT, dst):
            wv = A.bf(8, 512)
            vst = [A.bf(512) for _ in range(2)]
            P.dma("pool", lambda e: e.dma_start(out=wv, in_=w_in[:, colbase:colbase + 512].rearrange("(c p) n -> p c n", p=128)),
                  writes=["wv"])
            for i in range(16):
                u = i % 2
                bv, bvk = pb()

                def mm(e, i=i, bv=bv):
                    for kc in range(DC):
                        ins = e.matmul(bv[:, :], lhsT=hT[:, kc, i * 128:(i + 1) * 128], rhs=wv[:, kc, :],
                                       start=(kc == 0), stop=(kc == DC - 1))
                    return ins
                P.op("pe", mm, reads=["wv", ("h", i // 4)], writes=[bvk])
                P.op("act", lambda e, u=u, bv=bv: e.activation(out=vst[u], in_=bv[:, :], func=AF.Identity),
                     reads=[bvk], writes=[("vst", u)])
                P.dma("sp", lambda e, i=i, u=u: e.dma_start(out=dst[i * 128:(i + 1) * 128, :], in_=vst[u]),
                      reads=[("vst", u)], writes=[("vdst", colbase, i)])

        def glu(w_in, hT, tts, sink):
            wu = A.bf(8, 1024)
            sgf = [A.f32(TT) for _ in range(2)]
            uf = [A.f32(TT) for _ in range(2)]
            P.dma("pool", lambda e: e.dma_start(out=wu, in_=w_in[:, 3072:4096].rearrange("(c p) n -> p c n", p=128)),
                  writes=["wu"])
            k = 0
            for ch in range(4):
                for tt in tts:
                    sl = slice(tt * TT, (tt + 1) * TT)
                    u = k % 2
                    k += 1
                    b1, b1k = pb()
                    b2, b2k = pb()

                    def mm(e, col, bk, sl=sl):
                        for kc in range(DC):
                            ins = e.matmul(bk[:, :], lhsT=wu[:, kc, col:col + 128], rhs=hT[:, kc, sl],
                                           start=(kc == 0), stop=(kc == DC - 1))
                        return ins
                    P.op("pe", lambda e, ch=ch, b1=b1, mm=mm: mm(e, ch * 128, b1), reads=["wu", ("h", tt)], writes=[b1k])
                    P.op("pe", lambda e, ch=ch, b2=b2, mm=mm: mm(e, 512 + ch * 128, b2), reads=["wu", ("h", tt)], writes=[b2k])
                    P.op("act", lambda e, u=u, b2=b2: e.activation(out=sgf[u], in_=b2[:, :], func=AF.Sigmoid),
                         reads=[b2k], writes=[("gsg", u)])
                    P.op("dve", lambda e, u=u, b1=b1: e.tensor_tensor(out=uf[u], in0=b1[:, :], in1=sgf[u], op=ALU.mult),
                         reads=[b1k, ("gsg", u)], writes=[("guf", u)])
                    sink(ch, tt, uf[u], ("guf", u))

        def ph_kv(pv, pvk, mod, modk, scl, sclk, w_in):
            A.reset()
            P.barrier()
            hT = A.bf(8, T)
            mark = A.off
            kvl = _DBG.get("kv", 9)
            ph_norm(1, pv, pvk, mod, modk, scl, sclk, hT)
            mark = A.off
            if kvl >= 1:
                proj_norm(w_in, 512, pv, pvk, 1, hT, lambda ch: O["kaT"][:, ch, :])
                fresh(mark)
            if kvl >= 2:
                proj_norm(w_in, 2048, pv, pvk, 3, hT, lambda ch: O["kbT"][:, ch, :])
                fresh(mark)
            if kvl >= 3:
                v_proj(w_in, 1024, hT, O["va"])
                fresh(mark)
            if kvl >= 4:
                v_proj(w_in, 2560, hT, O["vb"])
                fresh(mark)
            if kvl < 5:
                return

            def sink(ch, tt, uf, key):
                P.dma("sp", lambda e, ch=ch, uf=uf: e.dma_start(out=O["ut"][:, ch, :], in_=uf[:, TT - 32:TT]),
                      reads=[key], writes=[("utout", ch)])
            glu(w_in, hT, [NT - 1], sink)

        def ph_store_x():
            for c in range(DC):
                P.dma("sp", lambda e, c=c: e.dma_start(out=O["xT_out"][:, c, :], in_=xT[:, c, :]),
                      reads=[("xT", tt, c // 4) for tt in range(NT)], writes=[("xout", c)])

        def ph_ffn_full(n, pv, pvk, mod, modk, scl, sclk, gat, gatk, w_up, w_dn):
            fresh()
            hT = A.bf(8, T)
            mark = A.off
            ph_norm(n, pv, pvk, mod, modk, scl, sclk, hT)
            ph_ffn(w_up, w_dn, gat, gatk, n, hT)

        def ph_qu(pv, pvk, mod, modk, scl, sclk, w_in):
            fresh()
            hT = A.bf(8, T)
            mark = A.off
            ph_norm(1, pv, pvk, mod, modk, scl, sclk, hT)
            for c in range(DC):
                P.dma("sp", lambda e, c=c: e.dma_start(out=S["h2T"][:, c, :], in_=hT[:, c, :]),
                      reads=[("h", tt) for tt in range(NT)], writes=[("h2Ts", c)])
            mark = A.off
            proj_norm(w_in, 0, pv, pvk, 0, hT, lambda ch: S["qaT"][:, ch, :])
            fresh(mark)
            proj_norm(w_in, 1536, pv, pvk, 2, hT, lambda ch: S["qbT"][:, ch, :])
            fresh(mark)
            utp = A.f32(4, 32)
            P.dma("sp", lambda e: e.dma_start(out=utp, in_=I["ut_p"][:, :, :]), writes=["utp"])
            P.op("dve", lambda e: e.tensor_scalar(out=utp, in0=utp, scalar1=pv[:, PV_PF:PV_PF + 1], scalar2=None, op0=ALU.mult),
                 reads=["utp", pvk], writes=["utp"])
            P.dma("sp", lambda e: e.dma_start(out=S["uT"][:, :, 0:32], in_=utp), reads=["utp"], writes=["uTs"])

            def sink(ch, tt, uf, key):
                P.dma("sp", lambda e, ch=ch, tt=tt, uf=uf: e.dma_start(out=S["uT"][:, ch, 32 + tt * TT:32 + (tt + 1) * TT], in_=uf),
                      reads=[key], writes=[("uTs", ch, tt)])
            glu(w_in, hT, list(range(NT)), sink)
            for nm, src_p, src_o in (("vaf", "va_p", "va_o"), ("vbf", "vb_p", "vb_o")):
                P.dma("sp", lambda e, nm=nm, src_p=src_p: e.dma_start(out=S[nm][0:T, :], in_=I[src_p][:, :]), writes=[(nm, 0)])
                P.dma("sp", lambda e, nm=nm, src_o=src_o: e.dma_start(out=S[nm][T:2 * T, :], in_=I[src_o][:, :]), writes=[(nm, 1)])

        def ph_dil(pv, pvk):
            fresh()
            qh2 = [A.bf(T) for _ in range(2)]
            kh2 = [A.bf(2 * T) for _ in range(2)]
            vbuf = [A.bf(32, 256) for _ in range(2)]
            pT = [A.bf(2, 256) for _ in range(2)]
            tmp = [A.f32(2, 256) for _ in range(2)]
            acc = A.f32(2, T)
            db2 = [A.f32(2, 3, 256) for _ in range(2)]
            dbb2 = [A.bf(2, 3, 256) for _ in range(2)]
            rlow = A.f32(T)
            yn = A.bf(T)
            for s in range(2):
                P.op("pool", lambda e, s=s: e.memset(vbuf[s], 1.0), writes=[("vbuf", s, r) for r in range(16)])
            unit = 0
            allacc = [("acc", b) for b in range(16)]
            def load_hp(hp):
                b_ = hp % 2
                P.dma("sp", lambda e: e.dma_start(out=qh2[b_], in_=S["qaT"][:, hp, :]), writes=[("qh", b_)])
                P.dma("sp", lambda e: e.dma_start(out=kh2[b_][:, 0:T], in_=I["kaT_p"][:, hp, :]), writes=[("kh", b_, 0)])
                P.dma("sp", lambda e: e.dma_start(out=kh2[b_][:, T:2 * T], in_=I["kaT_o"][:, hp, :]), writes=[("kh", b_, 1)])
                P.dma("sp", lambda e: e.dma_start(out=db2[b_], in_=I["dbias"][:, 2 * hp:2 * hp + 2, :, :]), writes=[("db", b_)])
                P.op("dve", lambda e: e.tensor_scalar(out=dbb2[b_], in0=db2[b_], scalar1=8.0, scalar2=None, op0=ALU.mult),
                     reads=[("db", b_)], writes=[("dbb", b_)])
            load_hp(0)
            for hp in range(4):
                if hp + 1 < 4:
                    load_hp(hp + 1)
                hb_ = hp % 2
                qh, kh, dbb = qh2[hb_], kh2[hb_], dbb2[hb_]
                pend = []

                def flush(n):
                    while len(pend) > n:
                        pend.pop(0)()
                for p, (win, d) in enumerate(DIL):
                    vs = (hp * 3 + p) % 2
                    vb = vbuf[vs]
                    nb = 16 // d
                    vview = S["vaf"].rearrange("(i d) c -> i d c", d=d)
                    i0_ = T // d - 128
                    for r in range(d):
                        for h in range(2):
                            src = vview[i0_:i0_ + 128 * (nb + 1), r, hp * 128 + h * 64:hp * 128 + h * 64 + 64].rearrange(
                                "(j p) c -> p j c", p=128)
                            P.dma("sp", lambda e, vb=vb, r=r, h=h, nb=nb, src=src: e.dma_start(
                                out=vb[:, r * (nb + 1):(r + 1) * (nb + 1), h * 128:h * 128 + 64], in_=src),
                                reads=[("vaf", 0), ("vaf", 1)], writes=[("vbuf", vs, r)])
                    khv = _rs(kh, [2 * T // d, d])
                    qhv = _rs(qh, [T // d, d])
                    for r in range(d):
                        for m in range(nb):
                            u = unit % 2
                            unit += 1
                            bS = [pb(), pb()]
                            bO, bOk = pb()

                            def st_(e, r=r, m=m, d=d, bS=bS, khv=khv, qhv=qhv, dbb=dbb, p=p):
                                for h in range(2):
                                    for jj in range(2):
                                        ki = T // d + 128 * (m - 1 + jj)
                                        ins = e.matmul(bS[h][0][:, jj * 128:(jj + 1) * 128],
                                                       lhsT=khv[h * 64:(h + 1) * 64, ki:ki + 128, r],
                                                       rhs=qhv[h * 64:(h + 1) * 64, 128 * m:128 * m + 128, r],
                                                       start=(jj == 0), stop=False, skip_group_check=True)
                                for h in range(2):
                                    for jj in range(2):
                                        ins = e.matmul(bS[h][0][:, jj * 128:(jj + 1) * 128], lhsT=identB[:],
                                                       rhs=dbb[:, h, p, jj * 128:(jj + 1) * 128],
                                                       start=False, stop=True, skip_group_check=True)
                                return ins
                            P.op("pe", st_, reads=[("qh", hb_), ("kh", hb_, 0), ("kh", hb_, 1), ("dbb", hb_), "identB"],
                                 writes=[bS[0][1], bS[1][1]])
                            for h in range(2):
                                if m == 0:
                                    P.op("act", lambda e, h=h, u=u, bS=bS: e.activation(
                                        out=pT[u][:, h, 0:128], in_=bS[h][0][:, 0:128], func=AF.Exp, scale=0.125, bias=pv[:, PV_PB:PV_PB + 1]),
                                        reads=[bS[h][1], pvk], writes=[("dpT", u, h)])
                                    P.op("act", lambda e, h=h, u=u, bS=bS: e.activation(
                                        out=pT[u][:, h, 128:256], in_=bS[h][0][:, 128:256], func=AF.Exp, scale=0.125),
                                        reads=[bS[h][1]], writes=[("dpT", u, h)])
                                else:
                                    P.op("act", lambda e, h=h, u=u, bS=bS: e.activation(out=pT[u][:, h, :], in_=bS[h][0][:, 0:256], func=AF.Exp, scale=0.125),
                                         reads=[bS[h][1]], writes=[("dpT", u, h)])

                            def back(r=r, m=m, nb=nb, u=u, vb=vb, vs=vs, bO=bO, bOk=bOk, d=d, p=p):
                                def pv_(e):
                                    for h in range(2):
                                        for jj in range(2):
                                            ins = e.matmul(bO[:, h * 128:(h + 1) * 128],
                                                           lhsT=vb[:, r * (nb + 1) + m + jj, h * 128:(h + 1) * 128],
                                                           rhs=pT[u][:, h, jj * 128:(jj + 1) * 128], start=(jj == 0), stop=(jj == 1))
                                    return ins
                                P.op("pe", pv_, reads=[("dpT", u, 0), ("dpT", u, 1), ("vbuf", vs, r)], writes=[bOk])
                                av = acc.rearrange("p h (i d) -> p h i d", d=d)[:, :, 128 * m:128 * m + 128, r]
                                ak = [("acc", b) for b in range(d * m, d * (m + 1))]
                                if p == 0:
                                    P.op("dve", lambda e: e.tensor_copy(out=av, in_=_rs(bO[:, 0:256], [2, 128])),
                                         reads=[bOk], writes=ak)
                                else:
                                    P.op("dve", lambda e: e.tensor_tensor(out=av, in0=_rs(bO[:, 0:256], [2, 128]), in1=av, op=ALU.add),
                                         reads=[bOk] + ak, writes=ak)
                            pend.append(back)
                            flush(1)
                flush(0)
                for h in range(2):
                    P.op("dve", lambda e, h=h: e.reciprocal(out=acc[64:128, h, :], in_=acc[64:128, h, :]), reads=allacc, writes=allacc)
                    P.op("dve", lambda e, h=h: e.tensor_copy(out=rlow[0:64, :], in_=acc[64:128, h, :]), reads=allacc, writes=["rlow"])
                    P.op("dve", lambda e, h=h: e.tensor_tensor(out=yn[0:64, :], in0=acc[0:64, h, :], in1=rlow[0:64, :], op=ALU.mult),
                         reads=allacc + ["rlow"], writes=["yn"])
                    P.dma("sp", lambda e, hp=hp, h=h: e.dma_start(out=S["yaT"][(hp * 2 + h) * 64:(hp * 2 + h + 1) * 64, :], in_=yn[0:64, :]),
                          reads=["yn"], writes=[("yaTs", hp, h)])

        def ph_lambda(l):
            import math
            lam_init = 0.8 - 0.6 * math.exp(-0.3 * l)
            fresh()
            lv = A.f32(256)
            t = A.f32(128)
            s2 = A.f32(2)
            P.dma("sp", lambda e: e.dma_start(out=lv, in_=I["lamA"][0:1, :].partition_broadcast(128)), writes=["lv"])
            P.op("dve", lambda e: e.tensor_tensor(out=_rs(t, [2, 64]), in0=_rs(lv, [2, 2, 64])[:, :, 0, :],
                                                  in1=_rs(lv, [2, 2, 64])[:, :, 1, :], op=ALU.mult), reads=["lv"], writes=["lvt"])
            P.op("dve", lambda e: e.tensor_reduce(out=s2, in_=_rs(t, [2, 64]), axis=mybir.AxisListType.X, op=ALU.add),
                 reads=["lvt"], writes=["lvs"])
            P.op("act", lambda e: e.activation(out=s2, in_=s2, func=AF.Exp), reads=["lvs"], writes=["lvs"])
            P.op("dve", lambda e: e.tensor_tensor(out=neglam[:, 1:2], in0=s2[:, 1:2], in1=s2[:, 0:1], op=ALU.subtract),
                 reads=["lvs"], writes=["neglam1"])
            P.op("dve", lambda e: e.tensor_scalar(out=neglam[:, 0:1], in0=neglam[:, 1:2], scalar1=-lam_init, scalar2=None, op0=ALU.add),
                 reads=["neglam1"], writes=["neglam"])
            return lam_init

        def ph_diff(pv, pvk, lam_init):
            fresh()
            qh2 = [A.bf(T) for _ in range(2)]
            kh2 = [A.bf(2 * T) for _ in range(2)]
            vaug2 = [A.bf(32, 130) for _ in range(2)]
            W2 = [A.f32(LW) for _ in range(2)]
            NS = 4
            tmp = [A.f32(TT) for _ in range(NS)]
            pT = [A.bf(TT) for _ in range(NS)]
            sbank = [(ps[0], ("ps", 0)), (ps[1], ("ps", 1)), (ps[6], ("ps", 6)), (ps[5], ("ps", 5))]

            def aslot(c, qb):
                a = c * 4 + qb
                return ps[2 + a // 3], ("ps", 2 + a // 3), (a % 3) * 160
            o1 = A.f32(128)
            of = A.f32(128)
            sm = A.f32(8)
            ybt = [A.bf(128) for _ in range(4)]
            ybst = A.bf(TT)
            gsub = A.f32(128)
            for b_ in range(2):
                P.op("pool", lambda e, b_=b_: e.memset(vaug2[b_], 1.0), writes=[("vaug", b_)])
            P.dma("sp", lambda e: e.dma_start(out=gsub, in_=I["subgA"][0:1, :].partition_broadcast(128)), writes=["gsub"])

            def load_head(h):
                b_ = h % 2
                P.dma("sp", lambda e: e.dma_start(out=qh2[b_], in_=S["qbT"][:, h, :]), writes=[("qh", b_)])
                P.dma("sp", lambda e: e.dma_start(out=kh2[b_][:, 0:T], in_=I["kbT_p"][:, h, :]), writes=[("kh", b_, 0)])
                P.dma("sp", lambda e: e.dma_start(out=kh2[b_][:, T:2 * T], in_=I["kbT_o"][:, h, :]), writes=[("kh", b_, 1)])
                P.dma("sp", lambda e: e.dma_start(
                    out=vaug2[b_][:, :, 0:128], in_=S["vbf"][:, h * 128:(h + 1) * 128].rearrange("(j p) c -> p j c", p=128)),
                    reads=[("vbf", 0), ("vbf", 1)], writes=[("vaug", b_)])
                P.dma("sp", lambda e: e.dma_start(out=W2[b_], in_=I["fbias"][:, h, :]), writes=[("W", b_)])
            P.op("dve", lambda e: e.tensor_scalar(out=gsub, in0=gsub, scalar1=(1.0 - lam_init), scalar2=None, op0=ALU.mult),
                 reads=["gsub"], writes=["gsub"])
            cnt = 0
            load_head(0)
            for h in range(4):
                if h + 1 < 4:
                    load_head(h + 1)
                hb_ = h % 2
                qh, kh, vaug, W = qh2[hb_], kh2[hb_], vaug2[hb_], W2[hb_]
                pend = []
                finb = []

                def flush(n):
                    while len(pend) > n:
                        pend.pop(0)()
                since = 0
                for g in range(4):
                    first = {}
                    for c in range(2):
                        nkb = 16 + 4 * g + 4
                        for kbi in range(nkb):
                            u = cnt % NS
                            cnt += 1
                            bS, bSk = sbank[u]
                            P.op("pe", lambda e, c=c, kbi=kbi, g=g, bS=bS, kh=kh, qh=qh: e.matmul(
                                bS[:, :], lhsT=kh[c * 64:(c + 1) * 64, kbi * 128:(kbi + 1) * 128],
                                rhs=qh[c * 64:(c + 1) * 64, g * TT:(g + 1) * TT], start=True, stop=True),
                                reads=[("qh", hb_), ("kh", hb_, 0), ("kh", hb_, 1)], writes=[bSk])
                            delta = (T + TT * g) - 128 * kbi
                            off = delta + 384 if delta < 1792 else 2176
                            P.op("dve", lambda e, u=u, off=off, bS=bS, W=W: e.scalar_tensor_tensor(
                                out=tmp[u], in0=bS[:, :], scalar=0.125, in1=W[:, off:off + TT], op0=ALU.mult, op1=ALU.add),
                                reads=[bSk, ("W", hb_)], writes=[("ftmp", u)])
                            if kbi < 16:
                                P.op("act", lambda e, u=u: e.activation(out=pT[u], in_=tmp[u], func=AF.Exp, bias=pv[:, PV_PB:PV_PB + 1]),
                                     reads=[("ftmp", u), pvk], writes=[("fpT", u)])
                            else:
                                P.op("act", lambda e, u=u: e.activation(out=pT[u], in_=tmp[u], func=AF.Exp),
                                     reads=[("ftmp", u)], writes=[("fpT", u)])
                            plan = []
                            for qb in range(4):
                                if kbi >= 16 and (4 * g + qb) < (kbi - 16):
                                    continue
                                bkx, bkk, col = aslot(c, qb)
                                stf = first.get(bkk, True)
                                first[bkk] = False
                                last = (kbi == 16 + 4 * g + qb)
                                plan.append((qb, stf, last, bkx, bkk, col))

                            def back(plan=plan, c=c, u=u, kbi=kbi, vaug=vaug, hb_=hb_):
                                def pv_(e):
                                    for qb, stf, last, bkx, bkk, col in plan:
                                        ins = e.matmul(bkx[:, col:col + 129], lhsT=pT[u][:, qb * 128:(qb + 1) * 128],
                                                       rhs=vaug[:, kbi, 0:129], start=stf, stop=last, skip_group_check=True)
                                    return ins
                                P.op("pe", pv_, reads=[("fpT", u), ("vaug", hb_)], writes=sorted(set(x[4] for x in plan)))
                            pend.append(back)
                            flush(NS - 1)
                            since += 1
                            if finb and since >= 3:
                                finb.pop(0)()
                    flush(0)
                    if finb:
                        finb.pop(0)()
                    for qb in range(4):
                        b1, b1k, col = aslot(0, qb)
                        b2, b2k, col2 = aslot(1, qb)
                        yb = ybt[qb]
                        P.op("dve", lambda e, b1=b1, col=col: e.reciprocal(out=sm[:, 0:1], in_=b1[:, col + 128:col + 129]),
                             reads=[b1k], writes=["sm0"])
                        P.op("dve", lambda e, b2=b2, col2=col2: e.reciprocal(out=sm[:, 1:2], in_=b2[:, col2 + 128:col2 + 129]),
                             reads=[b2k], writes=["sm1"])
                        P.op("dve", lambda e: e.tensor_tensor(out=sm[:, 2:3], in0=sm[:, 1:2], in1=neglam[:, 0:1], op=ALU.mult),
                             reads=["sm1", "neglam"], writes=["sm2"])
                        P.op("act", lambda e, b1=b1, col=col: e.activation(out=o1, in_=b1[:, col:col + 128], func=AF.Identity, scale=sm[:, 0:1]),
                             reads=[b1k, "sm0"], writes=["o1"])
                        P.op("dve", lambda e, b2=b2, col2=col2: e.scalar_tensor_tensor(
                            out=of, in0=b2[:, col2:col2 + 128], scalar=sm[:, 2:3], in1=o1, op0=ALU.mult, op1=ALU.add),
                            reads=[b2k, "sm2", "o1"], writes=["of"])
                        P.op("act", lambda e: e.activation(out=o1, in_=of, func=AF.Square, accum_out=sm[:, 3:4]),
                             reads=["of", "o1"], writes=["o1", "sm3"])
                        P.op("act", lambda e: e.activation(out=sm[:, 4:5], in_=sm[:, 3:4], func=AF.Ln, scale=1.0 / 128, bias=EPS),
                             reads=["sm3"], writes=["sm4"])
                        P.op("act", lambda e: e.activation(out=sm[:, 4:5], in_=sm[:, 4:5], func=AF.Exp, scale=-0.5),
                             reads=["sm4"], writes=["sm4"])
                        P.op("dve", lambda e, yb=yb: e.scalar_tensor_tensor(
                            out=yb, in0=of, scalar=sm[:, 4:5], in1=gsub, op0=ALU.mult, op1=ALU.mult),
                            reads=["of", "sm4", "gsub"], writes=[("ybt", qb)])

                    def fin_b(h=h, g=g):
                        def tr(e):
                            for qb in range(4):
                                ins = e.transpose(out=psb[:, qb * 128:(qb + 1) * 128], in_=ybt[qb], identity=identB[:])
                            return ins
                        P.op("pe", tr, reads=[("ybt", qb) for qb in range(4)] + ["identB"], writes=["psb"])
                        P.op("act", lambda e: e.activation(out=ybst, in_=psb[:, 0:TT], func=AF.Identity), reads=["psb"], writes=["ybst"])
                        P.dma("sp", lambda e: e.dma_start(out=S["ybT"][h * 128:(h + 1) * 128, g * TT:(g + 1) * TT], in_=ybst),
                              reads=["ybst"], writes=[("ybTs", h, g)])
                    finb.append(fin_b)
                    since = 0
                while finb:
                    finb.pop(0)()

        def ph_conv(pv, pvk):
            fresh()
            ubb = A.bf(4, 32 + T)
            dg = A.bf(124, 128)
            ycf = A.f32(4, T)
            ycb = A.bf(4, T)
            sqf = A.f32(4, TT)
            mf = A.f32(TT)
            vf = A.f32(TT)
            tf = [A.f32(TT) for _ in range(2)]
            HW_ = (32 + T) // 2
            for cc in range(4):
                for hh in range(2):
                    P.dma("pool", lambda e, cc=cc, hh=hh: e.dma_start(out=ubb[:, cc, hh * HW_:(hh + 1) * HW_],
                                                                      in_=S["uT"][:, cc, hh * HW_:(hh + 1) * HW_]),
                          reads=["uTs"] + [("uTs", cc, tt) for tt in range(NT)], writes=[("ubb", cc)])
            for idx in range(124):
                P.op("dve", lambda e, idx=idx: e.tensor_scalar(out=dg[:, idx, :], in0=identB[:], scalar1=pv[:, PV_CW + idx:PV_CW + idx + 1],
                                                               scalar2=None, op0=ALU.mult), reads=["identB", pvk], writes=[("dg", idx // 31)])
            for cc in range(4):
                for tt in range(NT):
                    bk, bkey = pb()

                    def mmc(e, cc=cc, tt=tt, bk=bk):
                        for j in range(31):
                            ins = e.matmul(bk[:, :], lhsT=dg[:, cc * 31 + j, :], rhs=ubb[:, cc, 2 + j + tt * TT:2 + j + (tt + 1) * TT],
                                           start=(j == 0), stop=(j == 30))
                        return ins
                    P.op("pe", mmc, reads=[("dg", cc), ("ubb", cc)], writes=[bkey])
                    P.op("act", lambda e, cc=cc, tt=tt, bk=bk: e.activation(
                        out=ycf[:, cc, tt * TT:(tt + 1) * TT], in_=bk[:, :], func=AF.Identity, bias=pv[:, PV_CB + cc:PV_CB + cc + 1]),
                        reads=[bkey, pvk], writes=[("ycf", cc)])
            k = 0
            allycf = [("ycf", cc) for cc in range(4)]
            for tt in range(NT):
                sl = slice(tt * TT, (tt + 1) * TT)
                P.op("act", lambda e, sl=sl: e.activation(out=sqf, in_=ycf[:, :, sl], func=AF.Square), reads=allycf, writes=["sqf"])
                b1, b1k = pb()
                b2, b2k = pb()

                def mm1(e, sl=sl, b1=b1):
                    for cc in range(4):
                        ins = e.matmul(b1[:, :], lhsT=onesF[:], rhs=ycf[:, cc, sl], start=(cc == 0), stop=(cc == 3))
                    return ins

                def mm2(e, b2=b2):
                    for cc in range(4):
                        ins = e.matmul(b2[:, :], lhsT=onesF[:], rhs=sqf[:, cc, :], start=(cc == 0), stop=(cc == 3))
                    return ins
                P.op("pe", mm1, reads=allycf + ["onesF"], writes=[b1k])
                P.op("pe", mm2, reads=["sqf", "onesF"], writes=[b2k])
                P.op("dve", lambda e, b1=b1: e.tensor_scalar(out=mf, in0=b1[:, :], scalar1=1.0 / 512, scalar2=None, op0=ALU.mult),
                     reads=[b1k], writes=["mf"])
                P.op("dve", lambda e: e.tensor_tensor(out=vf, in0=mf, in1=mf, op=ALU.mult), reads=["mf"], writes=["vf"])
                P.op("dve", lambda e, b2=b2: e.scalar_tensor_tensor(out=vf, in0=b2[:, :], scalar=1.0 / 512, in1=vf,
                                                                    op0=ALU.mult, op1=ALU.subtract),
                     reads=[b2k, "vf"], writes=["vf"])
                P.op("act", lambda e: e.activation(out=vf, in_=vf, func=AF.Ln, bias=EPS), reads=["vf"], writes=["vf"])
                P.op("act", lambda e: e.activation(out=vf, in_=vf, func=AF.Exp, scale=-0.5), reads=["vf"], writes=["vf"])
                for cc in range(4):
                    t_ = tf[k % 2]
                    tk = ("ctf", k % 2)
                    k += 1
                    P.op("dve", lambda e, cc=cc, sl=sl, t_=t_: e.tensor_tensor(out=t_, in0=ycf[:, cc, sl], in1=mf, op=ALU.subtract),
                         reads=allycf + ["mf"], writes=[tk])
                    P.op("dve", lambda e, t_=t_: e.tensor_tensor(out=t_, in0=t_, in1=vf, op=ALU.mult), reads=[tk, "vf"], writes=[tk])
                    P.op("act", lambda e, cc=cc, sl=sl, t_=t_: e.activation(
                        out=ycb[:, cc, sl], in_=t_, func=AF.Silu, scale=pv[:, PV_LG + cc:PV_LG + cc + 1],
                        bias=pv[:, PV_LB + cc:PV_LB + cc + 1]), reads=[tk, pvk], writes=[("ycb", cc)])
            for cc in range(4):
                P.dma("sp", lambda e, cc=cc: e.dma_start(out=S["ycT"][:, cc, :], in_=ycb[:, cc, :]), reads=[("ycb", cc)], writes=[("ycTs", cc)])

        def ph_merge(pv, pvk, gat, gatk, w_gate, w_br, w_out):
            fresh()
            wo = A.bf(8, D)
            hh = A.bf(8, 1024)
            yy = A.bf(3, 4, 1024)
            zT = A.bf(8, 1024)
            wg = [A.bf(3, 8, 128) for _ in range(3)]
            wb = [A.bf(3, 4, 128) for _ in range(3)]
            gsb = [A.f32(TT) for _ in range(2)]
            zacc = [A.f32(TT) for _ in range(2)]
            prod = [A.f32(TT) for _ in range(2)]
            P.dma("pool", lambda e: e.dma_start(out=wo, in_=w_out.rearrange("(c p) n -> p c n", p=128)), writes=["wo"])
            ysrc = [S["yaT"].rearrange("(c p) t -> p c t", p=128), S["ybT"].rearrange("(c p) t -> p c t", p=128), S["ycT"]]
            ykeys = [[("yaTs", hp, h) for hp in range(4) for h in range(2)],
                     [("ybTs", h, g) for h in range(4) for g in range(4)],
                     [("ycTs", cc) for cc in range(4)]]
            cnt = 0
            k = 0
            for half in range(2):
                hs = slice(half * 1024, (half + 1) * 1024)
                P.dma("sp", lambda e, hs=hs: e.dma_start(out=hh, in_=S["h2T"][:, :, hs]),
                      reads=[("h2Ts", c) for c in range(DC)], writes=["hh"])
                for i in range(3):
                    P.dma("sp", lambda e, i=i, hs=hs: e.dma_start(out=yy[:, i, :, :], in_=ysrc[i][:, :, hs]),
                          reads=ykeys[i], writes=[("yy", i)])
                for j in range(DC):
                    s = cnt % 3
                    cnt += 1
                    for i in range(3):
                        col = i * 1024 + j * 128
                        P.dma("pool", lambda e, s=s, i=i, col=col: e.dma_start(
                            out=wg[s][:, i, :, :], in_=w_gate[:, col:col + 128].rearrange("(c p) n -> p c n", p=128)),
                            writes=[("wg", s, i)])
                        P.dma("pool", lambda e, s=s, i=i, j=j: e.dma_start(
                            out=wb[s][:, i, :, :], in_=w_br[i * 512:(i + 1) * 512, j * 128:(j + 1) * 128].rearrange("(c p) n -> p c n", p=128)),
                            writes=[("wb", s, i)])
                    for t2 in range(2):
                        sl = slice(t2 * TT, (t2 + 1) * TT)
                        za = zacc[(2 * j + t2) % 2]
                        zk = ("zacc", (2 * j + t2) % 2)
                        for i in range(3):
                            u = k % 2
                            k += 1
                            bg, bgk = pb()
                            by, byk = pb()

                            def mg(e, s=s, i=i, sl=sl, bg=bg):
                                for kc in range(DC):
                                    ins = e.matmul(bg[:, :], lhsT=wg[s][:, i, kc, :], rhs=hh[:, kc, sl], start=(kc == 0), stop=(kc == DC - 1))
                                return ins

                            def my(e, s=s, i=i, sl=sl, by=by):
                                for kc in range(4):
                                    ins = e.matmul(by[:, :], lhsT=wb[s][:, i, kc, :], rhs=yy[:, i, kc, sl], start=(kc == 0), stop=(kc == 3))
                                return ins
                            P.op("pe", mg, reads=[("wg", s, i), "hh"], writes=[bgk])
                            P.op("pe", my, reads=[("wb", s, i), ("yy", i)], writes=[byk])
                            P.op("act", lambda e, u=u, bg=bg, i=i, j=j: e.activation(
                                out=gsb[u], in_=bg[:, :], func=AF.Sigmoid, bias=pv[:, PV_BG + i * 8 + j:PV_BG + i * 8 + j + 1]),
                                reads=[bgk, pvk], writes=[("gsb", u)])
                            if i == 0:
                                P.op("dve", lambda e, u=u, by=by, za=za: e.tensor_tensor(out=za, in0=by[:, :], in1=gsb[u], op=ALU.mult),
                                     reads=[byk, ("gsb", u)], writes=[zk])
                            else:
                                P.op("dve", lambda e, u=u, by=by: e.tensor_tensor(out=prod[u], in0=by[:, :], in1=gsb[u], op=ALU.mult),
                                     reads=[byk, ("gsb", u)], writes=[("prod", u)])
                                if i == 1:
                                    P.op("pool", lambda e, u=u, za=za: e.tensor_tensor(out=za, in0=za, in1=prod[u], op=ALU.add),
                                         reads=[zk, ("prod", u)], writes=[zk])
                                else:
                                    P.op("pool", lambda e, u=u, za=za, j=j, sl=sl: e.tensor_tensor(out=zT[:, j, sl], in0=za, in1=prod[u], op=ALU.add),
                                         reads=[zk, ("prod", u)], writes=[("zT", t2)])
                for dj in range(DC):
                    for t2 in range(2):
                        sl = slice(t2 * TT, (t2 + 1) * TT)
                        xs = slice(half * 1024 + t2 * TT, half * 1024 + (t2 + 1) * TT)
                        tt = half * 2 + t2
                        bd, bdk = pb()

                        def mo(e, dj=dj, sl=sl, bd=bd):
                            for kc in range(DC):
                                ins = e.matmul(bd[:, :], lhsT=wo[:, kc, dj * 128:(dj + 1) * 128], rhs=zT[:, kc, sl], start=(kc == 0), stop=(kc == DC - 1))
                            return ins
                        P.op("pe", mo, reads=["wo", ("zT", t2)], writes=[bdk])
                        P.op("dve", lambda e, dj=dj, xs=xs, bd=bd: e.scalar_tensor_tensor(
                            out=xT[:, dj, xs], in0=bd[:, :], scalar=gat[:, 1, dj:dj + 1], in1=xT[:, dj, xs], op0=ALU.mult, op1=ALU.add),
                            reads=[bdk, gatk, ("xT", tt, dj // 4)], writes=[("xT", tt, dj // 4)])

        def ph_out():
            fresh()
            ot = [A.f32(D) for _ in range(2)]
            for i in range(16):
                s = i % 2
                for hb in range(2):
                    bk, bkey = pb()

                    def tr(e, i=i, hb=hb, bk=bk):
                        for c4 in range(4):
                            c = hb * 4 + c4
                            ins = e.transpose(out=bk[:, c4 * 128:(c4 + 1) * 128], in_=xT[:, c, i * 128:(i + 1) * 128], identity=identF[:])
                        return ins
                    P.op("pe", tr, reads=[("xT", i // 4, hb), "identF"], writes=[bkey])
                    if hb == 0:
                        P.op("dve", lambda e, s=s, bk=bk: e.tensor_copy(out=ot[s][:, 0:512], in_=bk[:, :]), reads=[bkey], writes=[("ot", s, 0)])
                    else:
                        P.op("act", lambda e, s=s, bk=bk: e.activation(out=ot[s][:, 512:1024], in_=bk[:, :], func=AF.Identity),
                             reads=[bkey], writes=[("ot", s, 1)])
                P.dma("sp", lambda e, i=i, s=s: e.dma_start(out=O["out"][i * 128:(i + 1) * 128, :], in_=ot[s]),
                      reads=[("ot", s, 0), ("ot", s, 1)], writes=[("outd", i)])

        finals = []
        if stage == 1:
            ph_load_x()
        else:
            ph_load_xT()
            derive(pvA, "pvA", modA, "modA", sclA, gatA, "A")
            s2 = _DBG.get("s2", 9)
            ph_qu(pvA, "pvA", modA, "modA", sclA, "sclA", I["w_inA"])
            lam_init = ph_lambda(la)
            if s2 >= 2:
                ph_dil(pvA, "pvA")
            if s2 >= 3:
                ph_diff(pvA, "pvA", lam_init)
            if s2 >= 4:
                ph_conv(pvA, "pvA")
            if s2 >= 5:
                ph_merge(pvA, "pvA", gatA, "gatA", I["w_gateA"], I["w_brA"], I["w_outA"])
            if s2 >= 6:
                ph_ffn_full(2, pvA, "pvA", modA, "modA", sclA, "sclA", gatA, "gatA", I["w_fiA"], I["w_foA"])
        if lb is not None:
            up = _DBG.get("upto", 9) if _DBG.get("s2", 9) >= 7 else 0
            if up >= 1:
                ph_adaln(pvB, "pvB", I["w_adaB"], modB, "modB")
                derive(pvB, "pvB", modB, "modB", sclB, gatB, "B")
            if up >= 2:
                ph_ffn_full(0, pvB, "pvB", modB, "modB", sclB, "sclB", gatB, "gatB", I["w_fiB"], I["w_foB"])
            if up >= 3:
                ph_kv(pvB, "pvB", modB, "modB", sclB, "sclB", I["w_inB"])
            ph_store_x()
            P.dma("sp", lambda e: e.dma_start(out=O["modB"][:, :], in_=modB[:]), reads=["modB"], writes=["modBout"])
        else:
            ph_out()
        P.barrier()
        P.op("pool", lambda e: e.memset(neglam[:, 3:4], 0.0), writes=["__tail"])
        P.emit(["__tail"])
    return nc


_CACHE = {}
_DBG = {}


def _prog(stage):
    if stage not in _CACHE:
        _CACHE[stage] = build(stage)
    return _CACHE[stage]


def _mixer_inputs(inp, l, prev_res, dbias, fbias):
    maps = []
    for c in range(NCORES):
        own = prev_res[c]
        prv = prev_res[c - 1] if c % 2 == 1 else prev_res[c]
        maps.append({
            "xT_in": own["xT_out"], "modA": own["modB"], "pvA": _host_pv(inp, l, c),
            "w_inA": inp["w_in"][l], "w_gateA": inp["w_gate"][l], "w_brA": inp["w_branch"][l].reshape(1536, D),
            "w_outA": inp["w_out"][l], "w_fiA": inp["w_ffn_in"][l, 1], "w_foA": inp["w_ffn_out"][l, 1],
            "lamA": inp["lambda_vec"][l].reshape(1, 256), "subgA": inp["subln_g"][l].reshape(1, 128),
            "dbias": dbias, "fbias": fbias, "ut_p": prv["ut"],
            "kaT_o": own["kaT"], "kbT_o": own["kbT"], "va_o": own["va"], "vb_o": own["vb"],
            "kaT_p": prv["kaT"], "kbT_p": prv["kbT"], "va_p": prv["va"], "vb_p": prv["vb"],
        })
    return maps


def _ffn_kv_inputs(inp, l, c):
    return {"pvB": _host_pv(inp, l, c), "w_adaB": inp["w_ada"][l], "w_fiB": inp["w_ffn_in"][l, 0],
            "w_foB": inp["w_ffn_out"][l, 0], "w_inB": inp["w_in"][l]}


def kernel(**inputs):
    inp = {k: np.ascontiguousarray(np.asarray(v, dtype=np.float32)) for k, v in inputs.items()}
    dbias, fbias = _host_bias_tables(inp["rel_bias"])
    cores = list(range(NCORES))
    m1 = []
    for c in cores:
        b, hf = c // 2, c % 2
        d = {"x": np.ascontiguousarray(inp["x"][b, hf * T:(hf + 1) * T, :])}
        d.update(_ffn_kv_inputs(inp, 0, c))
        m1.append(d)
    r1 = run_bass_kernel_spmd(_prog(1), m1, core_ids=cores).results
    m2 = _mixer_inputs(inp, 0, r1, dbias, fbias)
    for c in cores:
        m2[c].update(_ffn_kv_inputs(inp, 1, c))
    r2 = run_bass_kernel_spmd(_prog(2), m2, core_ids=cores).results
    m3 = _mixer_inputs(inp, 1, r2, dbias, fbias)
    r3 = run_bass_kernel_spmd(_prog(3), m3, core_ids=cores).results
    out = np.empty((4, 2 * T, D), np.float32)
    for c in cores:
        out[c // 2, (c % 2) * T:(c % 2 + 1) * T, :] = r3[c]["out"]
    return out
```

```python
import numpy as np
import concourse.bass as bass
import concourse.mybir as mybir
from concourse.bass_utils import run_bass_kernel_spmd

F32 = mybir.dt.float32
BF16 = mybir.dt.bfloat16
AF = mybir.ActivationFunctionType
ALU = mybir.AluOpType

D = 1024
DC = 8
T = 2048
TT = 512
NT = T // TT
DFF = 2816
FC = 22
NCORES = 8


class _Op:
    __slots__ = ("eng", "fn", "deps", "is_dma", "sig", "tok", "idx")


class Prog:
    ENGS = ("pe", "act", "dve", "pool", "sp")
    NDMA = 6

    def __init__(self, nc):
        self.nc = nc
        self.ops = []
        self.last_w = {}
        self.readers = {}
        self.last_c = {}
        self.dma_hist = {}
        self.pending = {}

    def _add(self, eng, fn, reads, writes, is_dma):
        op = _Op()
        op.eng, op.fn, op.is_dma, op.sig, op.tok = eng, fn, is_dma, False, None
        op.idx = len(self.ops)
        deps = {}
        for r in reads:
            w = self.last_w.get(r)
            if w is not None:
                deps[w.idx] = (w, True)
        for wkey in writes:
            w = self.last_w.get(wkey)
            if w is not None and w.idx not in deps:
                deps[w.idx] = (w, False)
            for rd in self.readers.get(wkey, ()):
                if rd.idx not in deps:
                    deps[rd.idx] = (rd, False)
        keep = []
        for d in self.pending.pop(eng, ()):
            if d.eng == eng and not d.is_dma and eng == "pe":
                continue
            if d.idx not in deps:
                keep.append(d)
                d.sig = True
        for d, raw in deps.values():
            if d.eng == eng and not d.is_dma and not is_dma:
                if eng == "pe" or not raw:
                    continue
            keep.append(d)
            d.sig = True
        op.deps = keep
        for r in reads:
            self.readers.setdefault(r, []).append(op)
        for wkey in writes:
            self.last_w[wkey] = op
            self.readers[wkey] = []
        self.ops.append(op)
        if is_dma:
            self.dma_hist.setdefault(eng, []).append(op)
        else:
            self.last_c[eng] = op
        return op

    def barrier(self):
        B = list(self.last_c.values())
        for q, h in self.dma_hist.items():
            B.extend(h[-self.NDMA:])
        for e in self.ENGS:
            self.pending[e] = list(self.pending.get(e, ())) + B

    def op(self, eng, fn, reads=(), writes=()):
        reads, writes = tuple(reads), tuple(writes)
        extra = tuple(r for r in reads if (r == "psb" or (isinstance(r, tuple) and r[0] == "ps")) and r not in writes)
        return self._add(eng, fn, reads, writes + extra, False)

    def dma(self, eng, fn, reads=(), writes=()):
        return self._add(eng, fn, tuple(reads), tuple(writes), True)

    def emit(self, final_keys):
        nc = self.nc
        import contextlib
        with contextlib.ExitStack() as st:
            esem = {e: st.enter_context(nc.semaphore("s_" + e)) for e in self.ENGS}
            dsem = {e: [st.enter_context(nc.semaphore("d_%s%d" % (e, i))) for i in range(self.NDMA)]
                    for e in ("sp", "pool", "act")}
            ecnt = {e: 0 for e in self.ENGS}
            dcnt = {e: [0] * self.NDMA for e in dsem}
            drr = {e: 0 for e in dsem}
            finals = [self.last_w[k] for k in final_keys]
            for f in finals:
                f.sig = True
            prewait = {}
            for op in self.ops:
                if op.is_dma:
                    k = drr[op.eng] % self.NDMA
                    drr[op.eng] += 1
                    prewait[op.idx] = (dsem[op.eng][k], dcnt[op.eng][k])
                    dcnt[op.eng][k] += 16
                    op.tok = (dsem[op.eng][k], dcnt[op.eng][k])
                elif op.sig:
                    ecnt[op.eng] += 1
                    op.tok = (esem[op.eng], ecnt[op.eng])
            assert max(ecnt.values()) < 60000, ecnt
            per = {e: [o for o in self.ops if o.eng == e] for e in self.ENGS}
            block = st.enter_context(nc.Block())

            def run(eng_obj, ename, tail):
                waited = {}

                def w(sem, val):
                    if val <= 0:
                        return
                    key = id(sem)
                    if waited.get(key, 0) >= val:
                        return
                    waited[key] = val
                    eng_obj.wait_ge(sem, val)
                for op in per[ename]:
                    for d in op.deps:
                        w(*d.tok)
                    if op.is_dma:
                        w(*prewait[op.idx])
                    ins = op.fn(eng_obj)
                    if op.tok is not None:
                        ins.then_inc(op.tok[0], 16 if op.is_dma else 1)
                if tail:
                    for f in finals:
                        w(*f.tok)

            @block.tensor
            def _(e):
                run(e, "pe", False)

            @block.scalar
            def _(e):
                run(e, "act", False)

            @block.vector
            def _(e):
                run(e, "dve", False)

            @block.gpsimd
            def _(e):
                run(e, "pool", False)

            @block.sync
            def _(e):
                run(e, "sp", True)


def _rs(ap, shape):
    shape = list(shape)
    if len(shape) == 1:
        return ap
    names = "abcd"[:len(shape)]
    kw = {names[i]: shape[i] for i in range(len(shape))}
    return ap.rearrange("p (%s) -> p %s" % (" ".join(names), " ".join(names)), **kw)


class Arena:
    def __init__(self, ap_f32, nwords):
        self.base, self.n, self.off = ap_f32, nwords, 0

    def reset(self):
        self.off = 0

    def f32(self, *shape):
        n = int(np.prod(shape))
        a = self.base[:, self.off:self.off + n]
        self.off += n
        assert self.off <= self.n, (self.off, self.n)
        return _rs(a, shape)

    def bf(self, *shape):
        n = int(np.prod(shape))
        w = (n + 1) // 2
        a = self.base[:, self.off:self.off + w].bitcast(BF16)
        self.off += w
        assert self.off <= self.n, (self.off, self.n)
        return _rs(a[:, 0:n], shape)


PV_NG = 0
PV_BADA = 24
PV_GAIN = 96
PV_CW = 100
PV_CB = 224
PV_LG = 228
PV_LB = 232
PV_BG = 236
PV_CS = 260
PV_PF = 268
PV_PB = 269
NPV = 272

LW = 2688
EPS = 1e-6
DIL = ((128, 1), (512, 4), (2048, 16))


def _t5_bucket_np(dist):
    dist = np.asarray(dist, np.int64)
    dd = np.maximum(dist.astype(np.float32), np.float32(1.0))
    large = 16 + (np.log(dd / np.float32(16.0)) / np.float32(np.log(2048.0 / 16.0)) * np.float32(16.0)).astype(np.int32)
    large = np.minimum(large, 31)
    return np.where(dist < 16, dist, large).astype(np.int64)


def _host_bias_tables(rel_bias):
    NEG = np.float32(-1e30)
    k = np.arange(128)[:, None]
    dbias = np.empty((128, 8, 3, 256), np.float32)
    for p, (win, d) in enumerate(DIL):
        j = np.arange(256)[None, :]
        rel = np.where(j < 128, j + 128 - k, j - 128 - k)
        valid = (rel >= 0) & (rel <= 128)
        b = _t5_bucket_np(np.maximum(rel, 0) * d)
        for h in range(8):
            dbias[:, h, p, :] = np.where(valid, rel_bias[b, h], NEG)
    j = np.arange(LW)[None, :]
    dist = j - 384 - k
    b = _t5_bucket_np(np.maximum(dist, 0))
    fbias = np.empty((128, 4, LW), np.float32)
    for h in range(4):
        fbias[:, h, :] = np.where(dist >= 0, rel_bias[b, 8 + h], NEG)
    return dbias, fbias


def _host_pv(inp, l, core):
    b = core // 2
    pv = np.zeros((128, NPV), np.float32)
    pv[:, PV_NG:PV_NG + 24] = inp["norm_g"][l].reshape(3, 8, 128).transpose(2, 0, 1).reshape(128, 24)
    pv[:, PV_BADA:PV_BADA + 72] = inp["b_ada"][l].reshape(9, 8, 128).transpose(2, 0, 1).reshape(128, 72)
    g = inp["qk_gain"][l]
    pv[:, PV_GAIN + 0] = np.concatenate([g[0], g[0]])
    pv[:, PV_GAIN + 1] = np.concatenate([g[1], g[1]])
    pv[:, PV_GAIN + 2] = np.concatenate([g[2], g[3]])
    pv[:, PV_GAIN + 3] = np.concatenate([g[4], g[5]])
    pv[:, PV_CW:PV_CW + 124] = inp["conv_w"][l].reshape(31, 4, 128).transpose(2, 1, 0).reshape(128, 124)
    pv[:, PV_CB:PV_CB + 4] = inp["conv_b"][l].reshape(4, 128).T
    pv[:, PV_LG:PV_LG + 4] = inp["conv_ln_g"][l].reshape(4, 128).T
    pv[:, PV_LB:PV_LB + 4] = inp["conv_ln_b"][l].reshape(4, 128).T
    pv[:, PV_BG:PV_BG + 24] = inp["b_gate"][l].reshape(3, 8, 128).transpose(2, 0, 1).reshape(128, 24)
    pv[:, PV_CS:PV_CS + 8] = inp["c"][b].reshape(8, 128).T
    pv[:, PV_PF] = 1.0 if core % 2 == 1 else 0.0
    pv[:, PV_PB] = 0.0 if core % 2 == 1 else -30000.0
    return pv


def build(stage, dbg=False):
    import contextlib
    nc = bass.Bass("TRN2", target_bir_lowering=False)
    la = {1: None, 2: 0, 3: 1}[stage]
    lb = {1: 0, 2: 1, 3: None}[stage]

    def din(name, shape, dt=F32):
        return nc.dram_tensor(name, list(shape), dt, kind="ExternalInput").ap()

    def dout(name, shape, dt=F32):
        return nc.dram_tensor(name, list(shape), dt, kind="ExternalOutput").ap()

    def dscr(name, shape, dt=F32):
        return nc.dram_tensor(name, list(shape), dt, kind=("ExternalOutput" if dbg else "Internal")).ap()

    I = {}
    if stage == 1:
        I["x"] = din("x", [T, D])
    else:
        I["xT_in"] = din("xT_in", [128, DC, T])
        I["modA"] = din("modA", [128, 72])
        I["pvA"] = din("pvA", [128, NPV])
        for n, s in (("w_inA", [D, 4096]), ("w_gateA", [D, 3072]), ("w_brA", [1536, D]), ("w_outA", [D, D]),
                     ("w_fiA", [D, 2 * DFF]), ("w_foA", [DFF, D]), ("lamA", [1, 256]), ("subgA", [1, 128]),
                     ("dbias", [128, 8, 3, 256]), ("fbias", [128, 4, LW]), ("ut_p", [128, 4, 32])):
            I[n] = din(n, s)
        for n in ("kaT_o", "kbT_o", "kaT_p", "kbT_p"):
            I[n] = din(n, [128, 4, T], BF16)
        for n in ("va_o", "vb_o", "va_p", "vb_p"):
            I[n] = din(n, [T, 512], BF16)
    if lb is not None:
        I["pvB"] = din("pvB", [128, NPV])
        for n, s in (("w_adaB", [D, 9 * D]), ("w_fiB", [D, 2 * DFF]), ("w_foB", [DFF, D]), ("w_inB", [D, 4096])):
            I[n] = din(n, s)
    O = {}
    if stage < 3:
        O["xT_out"] = dout("xT_out", [128, DC, T])
        O["modB"] = dout("modB", [128, 72])
        O["kaT"] = dout("kaT", [128, 4, T], BF16)
        O["kbT"] = dout("kbT", [128, 4, T], BF16)
        O["va"] = dout("va", [T, 512], BF16)
        O["vb"] = dout("vb", [T, 512], BF16)
        O["ut"] = dout("ut", [128, 4, 32])
    else:
        O["out"] = dout("out", [T, D])
    S = {}
    if la is not None:
        S["h2T"] = dscr("h2T_s", [128, DC, T], BF16)
        S["qaT"] = dscr("qaT_s", [128, 4, T], BF16)
        S["qbT"] = dscr("qbT_s", [128, 4, T], BF16)
        S["uT"] = dscr("uT_s", [128, 4, 32 + T])
        S["yaT"] = dscr("yaT_s", [512, T], BF16)
        S["ybT"] = dscr("ybT_s", [512, T], BF16)
        S["ycT"] = dscr("ycT_s", [128, 4, T], BF16)
        S["vaf"] = dscr("vaf_s", [2 * T, 512], BF16)
        S["vbf"] = dscr("vbf_s", [2 * T, 512], BF16)

    with contextlib.ExitStack() as st:
        def sb(name, shape, dt):
            return st.enter_context(nc.sbuf_tensor(name, shape, dt))
        xT = sb("xT", [128, DC, T], F32)
        identF = sb("identF", [128, 128], F32)
        onesF = sb("onesF", [128, 128], F32)
        identB = sb("identB", [128, 128], BF16)
        onesB = sb("onesB", [128, 128], BF16)
        blkB = sb("blkB", [128, 128], BF16)
        pvA = sb("pvA_t", [128, NPV], F32)
        pvB = sb("pvB_t", [128, NPV], F32)
        modA = sb("modA_t", [128, 72], F32)
        modB = sb("modB_t", [128, 72], F32)
        sclA = sb("sclA", [128, 3, 8], F32)
        gatA = sb("gatA", [128, 3, 8], F32)
        sclB = sb("sclB", [128, 3, 8], F32)
        gatB = sb("gatB", [128, 3, 8], F32)
        neglam = sb("neglam", [128, 4], F32)
        NA = 33600
        arena_t = sb("arena", [128, NA], F32)
        A = Arena(arena_t[:, :], NA)
        ps = [st.enter_context(nc.psum_tensor("ps%d" % i, [128, 512], F32)) for i in range(7)]
        psb = st.enter_context(nc.psum_tensor("psb", [128, 1024], BF16))
        P = Prog(nc)
        rr = [0]

        def pb(lo=0, hi=7):
            i = lo + rr[0] % (hi - lo)
            rr[0] += 1
            return ps[i], ("ps", i)

        P.op("pool", lambda e: e.memset(identF[:], 0.0), writes=["identF"])
        P.op("pool", lambda e: e.affine_select(out=identF[:], in_=identF[:], pattern=[[-1, 128]],
                                                compare_op=ALU.not_equal, fill=1.0, base=0, channel_multiplier=1),
             reads=["identF"], writes=["identF"])
        P.op("pool", lambda e: e.memset(onesF[:], 1.0), writes=["onesF"])
        P.op("pool", lambda e: e.memset(onesB[:], 1.0), writes=["onesB"])
        P.op("pool", lambda e: e.memset(blkB[:], 0.0), writes=["blkB"])
        P.op("pool", lambda e: e.memset(blkB[0:64, 0:64], 1.0), reads=["blkB"], writes=["blkB"])
        P.op("pool", lambda e: e.memset(blkB[64:128, 64:128], 1.0), reads=["blkB"], writes=["blkB"])
        P.op("dve", lambda e: e.tensor_copy(out=identB[:], in_=identF[:]), reads=["identF"], writes=["identB"])
        if la is not None:
            P.dma("sp", lambda e: e.dma_start(out=pvA[:], in_=I["pvA"][:, :]), writes=["pvA"])
            P.dma("sp", lambda e: e.dma_start(out=modA[:], in_=I["modA"][:, :]), writes=["modA"])
        if lb is not None:
            P.dma("sp", lambda e: e.dma_start(out=pvB[:], in_=I["pvB"][:, :]), writes=["pvB"])

        def derive(pv, pvk, mod, modk, scl, gat, tag):
            for n in range(3):
                P.op("dve", lambda e, n=n: e.scalar_tensor_tensor(
                    out=scl[:, n, :], in0=mod[:, (3 * n + 1) * 8:(3 * n + 2) * 8], scalar=1.0,
                    in1=pv[:, PV_NG + n * 8:PV_NG + n * 8 + 8], op0=ALU.add, op1=ALU.mult),
                    reads=[modk, pvk], writes=["scl" + tag])
                P.op("dve", lambda e, n=n: e.tensor_scalar(
                    out=gat[:, n, :], in0=mod[:, (3 * n + 2) * 8:(3 * n + 3) * 8],
                    scalar1=(1.0 if n == 1 else 0.5), scalar2=None, op0=ALU.mult),
                    reads=[modk], writes=["gat" + tag])

        def fresh(mark=0):
            P.barrier()
            A.off = mark

        def ph_load_x():
            A.reset()
            xin = [A.f32(1024) for _ in range(2)]
            for i in range(16):
                s = i % 2
                P.dma("sp", lambda e, i=i, s=s: e.dma_start(out=xin[s], in_=I["x"][i * 128:(i + 1) * 128, :]),
                      writes=[("xin", s)])
                for hb in range(2):
                    bk, bkey = pb()

                    def tr(e, s=s, hb=hb, bk=bk):
                        for c4 in range(4):
                            c = hb * 4 + c4
                            ins = e.transpose(out=bk[:, c4 * 128:(c4 + 1) * 128],
                                              in_=xin[s][:, c * 128:(c + 1) * 128], identity=identF[:])
                        return ins
                    P.op("pe", tr, reads=[("xin", s), "identF"], writes=[bkey])
                    dst = xT[:, hb * 4:(hb + 1) * 4, i * 128:(i + 1) * 128]
                    if hb == 0:
                        P.op("dve", lambda e, dst=dst, bk=bk: e.tensor_copy(out=dst, in_=_rs(bk[:, :], [4, 128])),
                             reads=[bkey], writes=[("xT", i // 4, hb)])
                    else:
                        P.op("act", lambda e, dst=dst, bk=bk: e.activation(out=dst, in_=_rs(bk[:, :], [4, 128]),
                                                                          func=AF.Identity),
                             reads=[bkey], writes=[("xT", i // 4, hb)])

        def xkeys(tt):
            return [("xT", tt, 0), ("xT", tt, 1)]

        def ph_load_xT():
            for c in range(DC):
                P.dma("sp", lambda e, c=c: e.dma_start(out=xT[:, c, :], in_=I["xT_in"][:, c, :]),
                      writes=[("xT", tt, hb) for tt in range(NT) for hb in range(2)])

        def ph_adaln(pv, pvk, w_ada, mod, modk):
            A.reset()
            P.barrier()
            csb = A.bf(8)
            wa = [A.bf(8, 1024) for _ in range(2)]
            P.op("act", lambda e: e.activation(out=csb, in_=pv[:, PV_CS:PV_CS + 8], func=AF.Silu),
                 reads=[pvk], writes=["csb"])
            bk, bkey = ps[6], ("ps", 6)
            for j in range(9):
                s = j % 2
                P.dma("pool", lambda e, j=j, s=s: e.dma_start(
                    out=wa[s], in_=w_ada[:, j * 1024:(j + 1) * 1024].rearrange("(c p) n -> p c n", p=128)),
                    writes=[("wa", s)])

                def mm(e, j=j, s=s):
                    for cb in range(8):
                        for kc in range(8):
                            ins = e.matmul(bk[:, j * 8 + cb:j * 8 + cb + 1], lhsT=wa[s][:, kc, cb * 128:(cb + 1) * 128],
                                           rhs=csb[:, kc:kc + 1], start=(kc == 0), stop=(kc == 7))
                    return ins
                P.op("pe", mm, reads=[("wa", s), "csb"], writes=[bkey])
            P.op("dve", lambda e: e.tensor_tensor(out=mod[:, :], in0=bk[:, 0:72], in1=pv[:, PV_BADA:PV_BADA + 72],
                                                  op=ALU.add), reads=[bkey, pvk], writes=[modk])

        def ph_norm(n, pv, pvk, mod, modk, scl, sclk, hT):
            sqb = A.bf(8, TT)
            rs = [A.f32(TT) for _ in range(2)]
            tmpf = [A.f32(TT) for _ in range(2)]
            k = 0
            for tt in range(NT):
                sl = slice(tt * TT, (tt + 1) * TT)
                P.op("act", lambda e, sl=sl: e.activation(out=sqb, in_=xT[:, :, sl], func=AF.Square),
                     reads=xkeys(tt), writes=["sqb"])
                bk, bkey = pb()

                def mm(e, bk=bk):
                    for c in range(DC):
                        ins = e.matmul(bk[:, :], lhsT=onesB[:], rhs=sqb[:, c, :], start=(c == 0), stop=(c == DC - 1))
                    return ins
                P.op("pe", mm, reads=["sqb", "onesB"], writes=[bkey])
                r = rs[tt % 2]
                rk = ("rs", tt % 2)
                P.op("act", lambda e, r=r, bk=bk: e.activation(out=r, in_=bk[:, :], func=AF.Ln, scale=1.0 / D, bias=EPS),
                     reads=[bkey], writes=[rk])
                P.op("act", lambda e, r=r: e.activation(out=r, in_=r, func=AF.Exp, scale=-0.5), reads=[rk], writes=[rk])
                for c in range(DC):
                    tf = tmpf[k % 2]
                    tk = ("tmpf", k % 2)
                    k += 1
                    P.op("dve", lambda e, c=c, sl=sl, tf=tf, r=r: e.tensor_tensor(out=tf, in0=xT[:, c, sl], in1=r, op=ALU.mult),
                         reads=xkeys(tt) + [rk], writes=[tk])
                    P.op("act", lambda e, c=c, sl=sl, tf=tf: e.activation(
                        out=hT[:, c, sl], in_=tf, func=AF.Identity, scale=scl[:, n, c:c + 1],
                        bias=mod[:, 3 * n * 8 + c:3 * n * 8 + c + 1]),
                        reads=[tk, sclk, modk], writes=[("h", tt)])

        def ph_ffn(w_up, w_dn, gat, gatk, n, hT):
            groups = [(0, 6), (6, 6), (12, 6), (18, 4)]
            actT = A.bf(6, T)
            wup = [A.bf(2, 8, 256) for _ in range(2)]
            wdn = [A.bf(6, D) for _ in range(2)]
            sgf = [A.f32(TT) for _ in range(2)]
            k = 0
            npair = 0
            for gi, (c0, gn) in enumerate(groups):
                ws = gi % 2
                P.dma("pool", lambda e, c0=c0, gn=gn, ws=ws: e.dma_start(
                    out=wdn[ws][:, 0:gn, :], in_=w_dn[c0 * 128:(c0 + gn) * 128, :].rearrange("(i p) n -> p i n", p=128)),
                    writes=[("wdn", ws)])
                for pi in range(gn // 2):
                    cpair = c0 + 2 * pi
                    us = npair % 2
                    npair += 1
                    for gu in range(2):
                        col = gu * DFF + cpair * 128
                        P.dma("pool", lambda e, us=us, gu=gu, col=col: e.dma_start(
                            out=wup[us][:, gu, :, :], in_=w_up[:, col:col + 256].rearrange("(c p) n -> p c n", p=128)),
                            writes=[("wup", us, gu)])
                    for ci in range(2):
                        il = 2 * pi + ci
                        for tt in range(NT):
                            sl = slice(tt * TT, (tt + 1) * TT)
                            bg, bgk = pb()
                            bu, buk = pb()

                            def mmg(e, us=us, ci=ci, sl=sl, bg=bg, gu=0):
                                for kc in range(DC):
                                    ins = e.matmul(bg[:, :], lhsT=wup[us][:, gu, kc, ci * 128:(ci + 1) * 128], rhs=hT[:, kc, sl],
                                                   start=(kc == 0), stop=(kc == DC - 1))
                                return ins
                            P.op("pe", mmg, reads=[("wup", us, 0), ("h", tt)], writes=[bgk])
                            P.op("pe", lambda e, us=us, ci=ci, sl=sl, bu=bu: mmg(e, us, ci, sl, bu, 1),
                                 reads=[("wup", us, 1), ("h", tt)], writes=[buk])
                            sg = sgf[k % 2]
                            sk = ("sgf", k % 2)
                            k += 1
                            P.op("act", lambda e, sg=sg, bg=bg: e.activation(out=sg, in_=bg[:, :], func=AF.Silu),
                                 reads=[bgk], writes=[sk])
                            P.op("dve", lambda e, sg=sg, bu=bu, il=il, sl=sl: e.tensor_tensor(
                                out=actT[:, il, sl], in0=bu[:, :], in1=sg, op=ALU.mult),
                                reads=[buk, sk], writes=[("actT", tt)])
                for dc in range(DC):
                    for tt in range(NT):
                        sl = slice(tt * TT, (tt + 1) * TT)
                        bd, bdk = pb()

                        def mmd(e, dc=dc, sl=sl, bd=bd, gn=gn, ws=ws):
                            for i in range(gn):
                                ins = e.matmul(bd[:, :], lhsT=wdn[ws][:, i, dc * 128:(dc + 1) * 128], rhs=actT[:, i, sl],
                                               start=(i == 0), stop=(i == gn - 1))
                            return ins
                        P.op("pe", mmd, reads=[("wdn", ws), ("actT", tt)], writes=[bdk])
                        P.op("dve", lambda e, dc=dc, sl=sl, bd=bd: e.scalar_tensor_tensor(
                            out=xT[:, dc, sl], in0=bd[:, :], scalar=gat[:, n, dc:dc + 1], in1=xT[:, dc, sl],
                            op0=ALU.mult, op1=ALU.add),
                            reads=[bdk, gatk, ("xT", tt, dc // 4)], writes=[("xT", tt, dc // 4)])

        def preload_w(w_in, colbase, ncols, key):
            wt = A.bf(8, ncols)
            P.dma("pool", lambda e: e.dma_start(out=wt, in_=w_in[:, colbase:colbase + ncols].rearrange("(c p) n -> p c n", p=128)),
                  writes=[key])
            return wt, key

        def proj_norm(w_in, colbase, pv, pvk, gaincol, hT, dst, pre=None):
            wq, wqk = pre if pre is not None else (A.bf(8, 512), "wq")
            kst = [A.bf(T) for _ in range(2)]
            sq = [A.bf(TT) for _ in range(2)]
            qf = [A.f32(TT) for _ in range(2)]
            rs = [A.f32(TT) for _ in range(2)]
            if pre is None:
                P.dma("pool", lambda e: e.dma_start(out=wq, in_=w_in[:, colbase:colbase + 512].rearrange("(c p) n -> p c n", p=128)),
                      writes=["wq"])
            k = 0
            pend = []
            for ch in range(4):
                ks = kst[ch % 2]
                kk = ("kst", ch % 2)
                for tt in range(NT):
                    sl = slice(tt * TT, (tt + 1) * TT)
                    u = k % 2
                    k += 1
                    bq, bqk = pb()

                    def mm(e, ch=ch, sl=sl, bq=bq):
                        for kc in range(DC):
                            ins = e.matmul(bq[:, :], lhsT=wq[:, kc, ch * 128:(ch + 1) * 128], rhs=hT[:, kc, sl],
                                           start=(kc == 0), stop=(kc == DC - 1))
                        return ins
                    P.op("pe", mm, reads=[wqk, ("h", tt)], writes=[bqk])
                    P.op("act", lambda e, u=u, bq=bq: e.activation(out=sq[u], in_=bq[:, :], func=AF.Square),
                         reads=[bqk], writes=[("sq", u)])
                    P.op("dve", lambda e, u=u, bq=bq: e.tensor_copy(out=qf[u], in_=bq[:, :]), reads=[bqk], writes=[("qf", u)])

                    def back(u=u, ks=ks, kk=kk, sl=sl, ch=ch, tt=tt):
                        bs, bsk = pb()
                        P.op("pe", lambda e: e.matmul(bs[:, :], lhsT=blkB[:], rhs=sq[u], start=True, stop=True),
                             reads=[("sq", u), "blkB"], writes=[bsk])
                        P.op("act", lambda e: e.activation(out=rs[u], in_=bs[:, :], func=AF.Ln, scale=1.0 / 64, bias=EPS),
                             reads=[bsk], writes=[("prs", u)])
                        P.op("act", lambda e: e.activation(out=rs[u], in_=rs[u], func=AF.Exp, scale=-0.5),
                             reads=[("prs", u)], writes=[("prs", u)])
                        P.op("dve", lambda e: e.scalar_tensor_tensor(
                            out=ks[:, sl], in0=qf[u], scalar=pv[:, PV_GAIN + gaincol:PV_GAIN + gaincol + 1], in1=rs[u],
                            op0=ALU.mult, op1=ALU.mult), reads=[("qf", u), ("prs", u), pvk], writes=[kk])
                        if tt == NT - 1:
                            P.dma("sp", lambda e: e.dma_start(out=dst(ch), in_=ks), reads=[kk], writes=[("dst", colbase, ch)])
                    pend.append(back)
                    while len(pend) > 1:
                        pend.pop(0)()
            while pend:
                pend.pop(0)()

        def v_proj(w_in, colbase, hT, dst, pre=None):
            wv, wvk = pre if pre is not None else (A.bf(8, 512), "wv")
            vst = [A.bf(512) for _ in range(2)]
            if pre is None:
                P.dma("pool", lambda e: e.dma_start(out=wv, in_=w_in[:, colbase:colbase + 512].rearrange("(c p) n -> p c n", p=128)),
                      writes=["wv"])
            for i in range(16):
                u = i % 2
                bv, bvk = pb()

                def mm(e, i=i, bv=bv):
                    for kc in range(DC):
                        ins = e.matmul(bv[:, :], lhsT=hT[:, kc, i * 128:(i + 1) * 128], rhs=wv[:, kc, :],
                                       start=(kc == 0), stop=(kc == DC - 1))
                    return ins
                P.op("pe", mm, reads=[wvk, ("h", i // 4)], writes=[bvk])
                P.op("act", lambda e, u=u, bv=bv: e.activation(out=vst[u], in_=bv[:, :], func=AF.Identity),
                     reads=[bvk], writes=[("vst", u)])
                P.dma("sp", lambda e, i=i, u=u: e.dma_start(out=dst[i * 128:(i + 1) * 128, :], in_=vst[u]),
                      reads=[("vst", u)], writes=[("vdst", colbase, i)])

        def glu(w_in, hT, tts, sink, pre=None):
            wu, wuk = pre if pre is not None else (A.bf(8, 1024), "wu")
            sgf = [A.f32(TT) for _ in range(2)]
            uf = [A.f32(TT) for _ in range(2)]
            if pre is None:
                P.dma("pool", lambda e: e.dma_start(out=wu, in_=w_in[:, 3072:4096].rearrange("(c p) n -> p c n", p=128)),
                      writes=["wu"])
            k = 0
            for ch in range(4):
                for tt in tts:
                    sl = slice(tt * TT, (tt + 1) * TT)
                    u = k % 2
                    k += 1
                    b1, b1k = pb()
                    b2, b2k = pb()

                    def mm(e, col, bk, sl=sl):
                        for kc in range(DC):
                            ins = e.matmul(bk[:, :], lhsT=wu[:, kc, col:col + 128], rhs=hT[:, kc, sl],
                                           start=(kc == 0), stop=(kc == DC - 1))
                        return ins
                    P.op("pe", lambda e, ch=ch, b1=b1, mm=mm: mm(e, ch * 128, b1), reads=[wuk, ("h", tt)], writes=[b1k])
                    P.op("pe", lambda e, ch=ch, b2=b2, mm=mm: mm(e, 512 + ch * 128, b2), reads=[wuk, ("h", tt)], writes=[b2k])
                    P.op("act", lambda e, u=u, b2=b2: e.activation(out=sgf[u], in_=b2[:, :], func=AF.Sigmoid),
                         reads=[b2k], writes=[("gsg", u)])
                    P.op("dve", lambda e, u=u, b1=b1: e.tensor_tensor(out=uf[u], in0=b1[:, :], in1=sgf[u], op=ALU.mult),
                         reads=[b1k, ("gsg", u)], writes=[("guf", u)])
                    sink(ch, tt, uf[u], ("guf", u))

        def ph_kv(pv, pvk, mod, modk, scl, sclk, w_in):
            A.reset()
            P.barrier()
            hT = A.bf(8, T)
            p_ka = preload_w(w_in, 512, 512, "w_ka")
            p_kb = preload_w(w_in, 2048, 512, "w_kb")
            p_va = preload_w(w_in, 1024, 512, "w_va")
            p_vb = preload_w(w_in, 2560, 512, "w_vb")
            p_u = preload_w(w_in, 3072, 1024, "w_u")
            ph_norm(1, pv, pvk, mod, modk, scl, sclk, hT)
            mark = A.off
            proj_norm(w_in, 512, pv, pvk, 1, hT, lambda ch: O["kaT"][:, ch, :], pre=p_ka)
            fresh(mark)
            proj_norm(w_in, 2048, pv, pvk, 3, hT, lambda ch: O["kbT"][:, ch, :], pre=p_kb)
            fresh(mark)
            v_proj(w_in, 1024, hT, O["va"], pre=p_va)
            fresh(mark)
            v_proj(w_in, 2560, hT, O["vb"], pre=p_vb)
            fresh(mark)

            def sink(ch, tt, uf, key):
                P.dma("sp", lambda e, ch=ch, uf=uf: e.dma_start(out=O["ut"][:, ch, :], in_=uf[:, TT - 32:TT]),
                      reads=[key], writes=[("utout", ch)])
            glu(w_in, hT, [NT - 1], sink, pre=p_u)

        def ph_store_x():
            for c in range(DC):
                P.dma("sp", lambda e, c=c: e.dma_start(out=O["xT_out"][:, c, :], in_=xT[:, c, :]),
                      reads=[("xT", tt, c // 4) for tt in range(NT)], writes=[("xout", c)])

        def ph_ffn_full(n, pv, pvk, mod, modk, scl, sclk, gat, gatk, w_up, w_dn):
            fresh()
            hT = A.bf(8, T)
            mark = A.off
            ph_norm(n, pv, pvk, mod, modk, scl, sclk, hT)
            ph_ffn(w_up, w_dn, gat, gatk, n, hT)

        def ph_qu(pv, pvk, mod, modk, scl, sclk, w_in):
            fresh()
            hT = A.bf(8, T)
            p_qa = preload_w(w_in, 0, 512, "w_qa")
            p_qb = preload_w(w_in, 1536, 512, "w_qb")
            p_u = preload_w(w_in, 3072, 1024, "w_u")
            ph_norm(1, pv, pvk, mod, modk, scl, sclk, hT)
            for c in range(DC):
                P.dma("sp", lambda e, c=c: e.dma_start(out=S["h2T"][:, c, :], in_=hT[:, c, :]),
                      reads=[("h", tt) for tt in range(NT)], writes=[("h2Ts", c)])
            mark = A.off
            proj_norm(w_in, 0, pv, pvk, 0, hT, lambda ch: S["qaT"][:, ch, :], pre=p_qa)
            fresh(mark)
            proj_norm(w_in, 1536, pv, pvk, 2, hT, lambda ch: S["qbT"][:, ch, :], pre=p_qb)
            fresh(mark)
            utp = A.f32(4, 32)
            P.dma("sp", lambda e: e.dma_start(out=utp, in_=I["ut_p"][:, :, :]), writes=["utp"])
            P.op("dve", lambda e: e.tensor_scalar(out=utp, in0=utp, scalar1=pv[:, PV_PF:PV_PF + 1], scalar2=None, op0=ALU.mult),
                 reads=["utp", pvk], writes=["utp"])
            P.dma("sp", lambda e: e.dma_start(out=S["uT"][:, :, 0:32], in_=utp), reads=["utp"], writes=["uTs"])

            def sink(ch, tt, uf, key):
                P.dma("sp", lambda e, ch=ch, tt=tt, uf=uf: e.dma_start(out=S["uT"][:, ch, 32 + tt * TT:32 + (tt + 1) * TT], in_=uf),
                      reads=[key], writes=[("uTs", ch, tt)])
            glu(w_in, hT, list(range(NT)), sink, pre=p_u)
            for nm, src_p, src_o in (("vaf", "va_p", "va_o"), ("vbf", "vb_p", "vb_o")):
                P.dma("sp", lambda e, nm=nm, src_p=src_p: e.dma_start(out=S[nm][0:T, :], in_=I[src_p][:, :]), writes=[(nm, 0)])
                P.dma("sp", lambda e, nm=nm, src_o=src_o: e.dma_start(out=S[nm][T:2 * T, :], in_=I[src_o][:, :]), writes=[(nm, 1)])

        def ph_dil(pv, pvk):
            fresh()
            qh2 = [A.bf(T) for _ in range(2)]
            kh2 = [A.bf(2 * T) for _ in range(2)]
            vbuf = [A.bf(32, 256) for _ in range(2)]
            pT = [A.bf(2, 256) for _ in range(2)]
            tmp = [A.f32(2, 256) for _ in range(2)]
            acc = A.f32(2, T)
            db2 = [A.f32(2, 3, 256) for _ in range(2)]
            dbb2 = [A.bf(2, 3, 256) for _ in range(2)]
            rlow = A.f32(T)
            yn = A.bf(T)
            for s in range(2):
                P.op("pool", lambda e, s=s: e.memset(vbuf[s], 1.0), writes=[("vbuf", s, r) for r in range(16)])
            unit = 0
            allacc = [("acc", b) for b in range(16)]
            def load_hp(hp):
                b_ = hp % 2
                P.dma("sp", lambda e: e.dma_start(out=qh2[b_], in_=S["qaT"][:, hp, :]), writes=[("qh", b_)])
                P.dma("sp", lambda e: e.dma_start(out=kh2[b_][:, 0:T], in_=I["kaT_p"][:, hp, :]), writes=[("kh", b_, 0)])
                P.dma("sp", lambda e: e.dma_start(out=kh2[b_][:, T:2 * T], in_=I["kaT_o"][:, hp, :]), writes=[("kh", b_, 1)])
                P.dma("sp", lambda e: e.dma_start(out=db2[b_], in_=I["dbias"][:, 2 * hp:2 * hp + 2, :, :]), writes=[("db", b_)])
                P.op("dve", lambda e: e.tensor_scalar(out=dbb2[b_], in0=db2[b_], scalar1=8.0, scalar2=None, op0=ALU.mult),
                     reads=[("db", b_)], writes=[("dbb", b_)])
            load_hp(0)
            for hp in range(4):
                if hp + 1 < 4:
                    load_hp(hp + 1)
                hb_ = hp % 2
                qh, kh, dbb = qh2[hb_], kh2[hb_], dbb2[hb_]
                pend = []

                def flush(n):
                    while len(pend) > n:
                        pend.pop(0)()
                for p, (win, d) in enumerate(DIL):
                    vs = (hp * 3 + p) % 2
                    vb = vbuf[vs]
                    nb = 16 // d
                    vview = S["vaf"].rearrange("(i d) c -> i d c", d=d)
                    i0_ = T // d - 128
                    for r in range(d):
                        for h in range(2):
                            src = vview[i0_:i0_ + 128 * (nb + 1), r, hp * 128 + h * 64:hp * 128 + h * 64 + 64].rearrange(
                                "(j p) c -> p j c", p=128)
                            P.dma("sp", lambda e, vb=vb, r=r, h=h, nb=nb, src=src: e.dma_start(
                                out=vb[:, r * (nb + 1):(r + 1) * (nb + 1), h * 128:h * 128 + 64], in_=src),
                                reads=[("vaf", 0), ("vaf", 1)], writes=[("vbuf", vs, r)])
                    khv = _rs(kh, [2 * T // d, d])
                    qhv = _rs(qh, [T // d, d])
                    for r in range(d):
                        for m in range(nb):
                            u = unit % 2
                            unit += 1
                            bS = [pb(), pb()]
                            bO, bOk = pb()

                            def st_(e, r=r, m=m, d=d, bS=bS, khv=khv, qhv=qhv, dbb=dbb, p=p):
                                for h in range(2):
                                    for jj in range(2):
                                        ki = T // d + 128 * (m - 1 + jj)
                                        ins = e.matmul(bS[h][0][:, jj * 128:(jj + 1) * 128],
                                                       lhsT=khv[h * 64:(h + 1) * 64, ki:ki + 128, r],
                                                       rhs=qhv[h * 64:(h + 1) * 64, 128 * m:128 * m + 128, r],
                                                       start=(jj == 0), stop=False, skip_group_check=True)
                                for h in range(2):
                                    for jj in range(2):
                                        ins = e.matmul(bS[h][0][:, jj * 128:(jj + 1) * 128], lhsT=identB[:],
                                                       rhs=dbb[:, h, p, jj * 128:(jj + 1) * 128],
                                                       start=False, stop=True, skip_group_check=True)
                                return ins
                            P.op("pe", st_, reads=[("qh", hb_), ("kh", hb_, 0), ("kh", hb_, 1), ("dbb", hb_), "identB"],
                                 writes=[bS[0][1], bS[1][1]])
                            for h in range(2):
                                if m == 0:
                                    P.op("act", lambda e, h=h, u=u, bS=bS: e.activation(
                                        out=pT[u][:, h, 0:128], in_=bS[h][0][:, 0:128], func=AF.Exp, scale=0.125, bias=pv[:, PV_PB:PV_PB + 1]),
                                        reads=[bS[h][1], pvk], writes=[("dpT", u, h)])
                                    P.op("act", lambda e, h=h, u=u, bS=bS: e.activation(
                                        out=pT[u][:, h, 128:256], in_=bS[h][0][:, 128:256], func=AF.Exp, scale=0.125),
                                        reads=[bS[h][1]], writes=[("dpT", u, h)])
                                else:
                                    P.op("act", lambda e, h=h, u=u, bS=bS: e.activation(out=pT[u][:, h, :], in_=bS[h][0][:, 0:256], func=AF.Exp, scale=0.125),
                                         reads=[bS[h][1]], writes=[("dpT", u, h)])

                            def back(r=r, m=m, nb=nb, u=u, vb=vb, vs=vs, bO=bO, bOk=bOk, d=d, p=p):
                                def pv_(e):
                                    for h in range(2):
                                        for jj in range(2):
                                            ins = e.matmul(bO[:, h * 128:(h + 1) * 128],
                                                           lhsT=vb[:, r * (nb + 1) + m + jj, h * 128:(h + 1) * 128],
                                                           rhs=pT[u][:, h, jj * 128:(jj + 1) * 128], start=(jj == 0), stop=(jj == 1))
                                    return ins
                                P.op("pe", pv_, reads=[("dpT", u, 0), ("dpT", u, 1), ("vbuf", vs, r)], writes=[bOk])
                                av = acc.rearrange("p h (i d) -> p h i d", d=d)[:, :, 128 * m:128 * m + 128, r]
                                ak = [("acc", b) for b in range(d * m, d * (m + 1))]
                                if p == 0:
                                    P.op("dve", lambda e: e.tensor_copy(out=av, in_=_rs(bO[:, 0:256], [2, 128])),
                                         reads=[bOk], writes=ak)
                                else:
                                    P.op("dve", lambda e: e.tensor_tensor(out=av, in0=_rs(bO[:, 0:256], [2, 128]), in1=av, op=ALU.add),
                                         reads=[bOk] + ak, writes=ak)
                            pend.append(back)
                            flush(1)
                flush(0)
                for h in range(2):
                    P.op("dve", lambda e, h=h: e.reciprocal(out=acc[64:128, h, :], in_=acc[64:128, h, :]), reads=allacc, writes=allacc)
                    P.op("dve", lambda e, h=h: e.tensor_copy(out=rlow[0:64, :], in_=acc[64:128, h, :]), reads=allacc, writes=["rlow"])
                    P.op("dve", lambda e, h=h: e.tensor_tensor(out=yn[0:64, :], in0=acc[0:64, h, :], in1=rlow[0:64, :], op=ALU.mult),
                         reads=allacc + ["rlow"], writes=["yn"])
                    P.dma("sp", lambda e, hp=hp, h=h: e.dma_start(out=S["yaT"][(hp * 2 + h) * 64:(hp * 2 + h + 1) * 64, :], in_=yn[0:64, :]),
                          reads=["yn"], writes=[("yaTs", hp, h)])

        def ph_lambda(l):
            import math
            lam_init = 0.8 - 0.6 * math.exp(-0.3 * l)
            fresh()
            lv = A.f32(256)
            t = A.f32(128)
            s2 = A.f32(2)
            P.dma("sp", lambda e: e.dma_start(out=lv, in_=I["lamA"][0:1, :].partition_broadcast(128)), writes=["lv"])
            P.op("dve", lambda e: e.tensor_tensor(out=_rs(t, [2, 64]), in0=_rs(lv, [2, 2, 64])[:, :, 0, :],
                                                  in1=_rs(lv, [2, 2, 64])[:, :, 1, :], op=ALU.mult), reads=["lv"], writes=["lvt"])
            P.op("dve", lambda e: e.tensor_reduce(out=s2, in_=_rs(t, [2, 64]), axis=mybir.AxisListType.X, op=ALU.add),
                 reads=["lvt"], writes=["lvs"])
            P.op("act", lambda e: e.activation(out=s2, in_=s2, func=AF.Exp), reads=["lvs"], writes=["lvs"])
            P.op("dve", lambda e: e.tensor_tensor(out=neglam[:, 1:2], in0=s2[:, 1:2], in1=s2[:, 0:1], op=ALU.subtract),
                 reads=["lvs"], writes=["neglam1"])
            P.op("dve", lambda e: e.tensor_scalar(out=neglam[:, 0:1], in0=neglam[:, 1:2], scalar1=-lam_init, scalar2=None, op0=ALU.add),
                 reads=["neglam1"], writes=["neglam"])
            return lam_init

        def ph_diff(pv, pvk, lam_init):
            fresh()
            qh2 = [A.bf(T) for _ in range(2)]
            kh2 = [A.bf(2 * T) for _ in range(2)]
            vaug2 = [A.bf(32, 130) for _ in range(2)]
            W2 = [A.f32(LW) for _ in range(2)]
            NS = 4
            tmp = [A.f32(TT) for _ in range(NS)]
            pT = [A.bf(TT) for _ in range(NS)]
            sbank = [(ps[0], ("ps", 0)), (ps[1], ("ps", 1)), (ps[6], ("ps", 6)), (ps[5], ("ps", 5))]

            def aslot(c, qb):
                a = c * 4 + qb
                return ps[2 + a // 3], ("ps", 2 + a // 3), (a % 3) * 160
            o1 = A.f32(128)
            of = A.f32(128)
            sm = A.f32(8)
            ybt = [A.bf(128) for _ in range(4)]
            ybst = A.bf(TT)
            gsub = A.f32(128)
            for b_ in range(2):
                P.op("pool", lambda e, b_=b_: e.memset(vaug2[b_], 1.0), writes=[("vaug", b_)])
            P.dma("sp", lambda e: e.dma_start(out=gsub, in_=I["subgA"][0:1, :].partition_broadcast(128)), writes=["gsub"])

            def load_head(h):
                b_ = h % 2
                P.dma("sp", lambda e: e.dma_start(out=qh2[b_], in_=S["qbT"][:, h, :]), writes=[("qh", b_)])
                P.dma("sp", lambda e: e.dma_start(out=kh2[b_][:, 0:T], in_=I["kbT_p"][:, h, :]), writes=[("kh", b_, 0)])
                P.dma("sp", lambda e: e.dma_start(out=kh2[b_][:, T:2 * T], in_=I["kbT_o"][:, h, :]), writes=[("kh", b_, 1)])
                P.dma("sp", lambda e: e.dma_start(
                    out=vaug2[b_][:, :, 0:128], in_=S["vbf"][:, h * 128:(h + 1) * 128].rearrange("(j p) c -> p j c", p=128)),
                    reads=[("vbf", 0), ("vbf", 1)], writes=[("vaug", b_)])
                P.dma("sp", lambda e: e.dma_start(out=W2[b_], in_=I["fbias"][:, h, :]), writes=[("W", b_)])
            P.op("dve", lambda e: e.tensor_scalar(out=gsub, in0=gsub, scalar1=(1.0 - lam_init), scalar2=None, op0=ALU.mult),
                 reads=["gsub"], writes=["gsub"])
            cnt = 0
            load_head(0)
            for h in range(4):
                if h + 1 < 4:
                    load_head(h + 1)
                hb_ = h % 2
                qh, kh, vaug, W = qh2[hb_], kh2[hb_], vaug2[hb_], W2[hb_]
                pend = []
                finb = []

                def flush(n):
                    while len(pend) > n:
                        pend.pop(0)()
                since = 0
                for g in range(4):
                    first = {}
                    for c in range(2):
                        nkb = 16 + 4 * g + 4
                        for kbi in range(nkb):
                            u = cnt % NS
                            cnt += 1
                            bS, bSk = sbank[u]
                            P.op("pe", lambda e, c=c, kbi=kbi, g=g, bS=bS, kh=kh, qh=qh: e.matmul(
                                bS[:, :], lhsT=kh[c * 64:(c + 1) * 64, kbi * 128:(kbi + 1) * 128],
                                rhs=qh[c * 64:(c + 1) * 64, g * TT:(g + 1) * TT], start=True, stop=True),
                                reads=[("qh", hb_), ("kh", hb_, 0), ("kh", hb_, 1)], writes=[bSk])
                            delta = (T + TT * g) - 128 * kbi
                            off = delta + 384 if delta < 1792 else 2176
                            P.op("dve", lambda e, u=u, off=off, bS=bS, W=W: e.scalar_tensor_tensor(
                                out=tmp[u], in0=bS[:, :], scalar=0.125, in1=W[:, off:off + TT], op0=ALU.mult, op1=ALU.add),
                                reads=[bSk, ("W", hb_)], writes=[("ftmp", u)])
                            if kbi < 16:
                                P.op("act", lambda e, u=u: e.activation(out=pT[u], in_=tmp[u], func=AF.Exp, bias=pv[:, PV_PB:PV_PB + 1]),
                                     reads=[("ftmp", u), pvk], writes=[("fpT", u)])
                            else:
                                P.op("act", lambda e, u=u: e.activation(out=pT[u], in_=tmp[u], func=AF.Exp),
                                     reads=[("ftmp", u)], writes=[("fpT", u)])
                            plan = []
                            for qb in range(4):
                                if kbi >= 16 and (4 * g + qb) < (kbi - 16):
                                    continue
                                bkx, bkk, col = aslot(c, qb)
                                stf = first.get(bkk, True)
                                first[bkk] = False
                                last = (kbi == 16 + 4 * g + qb)
                                plan.append((qb, stf, last, bkx, bkk, col))

                            def back(plan=plan, c=c, u=u, kbi=kbi, vaug=vaug, hb_=hb_):
                                def pv_(e):
                                    for qb, stf, last, bkx, bkk, col in plan:
                                        ins = e.matmul(bkx[:, col:col + 129], lhsT=pT[u][:, qb * 128:(qb + 1) * 128],
                                                       rhs=vaug[:, kbi, 0:129], start=stf, stop=last, skip_group_check=True)
                                    return ins
                                P.op("pe", pv_, reads=[("fpT", u), ("vaug", hb_)], writes=sorted(set(x[4] for x in plan)))
                            pend.append(back)
                            flush(NS - 1)
                            since += 1
                            if finb and since >= 3:
                                finb.pop(0)()
                    flush(0)
                    if finb:
                        finb.pop(0)()
                    for qb in range(4):
                        b1, b1k, col = aslot(0, qb)
                        b2, b2k, col2 = aslot(1, qb)
                        yb = ybt[qb]
                        P.op("dve", lambda e, b1=b1, col=col: e.reciprocal(out=sm[:, 0:1], in_=b1[:, col + 128:col + 129]),
                             reads=[b1k], writes=["sm0"])
                        P.op("dve", lambda e, b2=b2, col2=col2: e.reciprocal(out=sm[:, 1:2], in_=b2[:, col2 + 128:col2 + 129]),
                             reads=[b2k], writes=["sm1"])
                        P.op("dve", lambda e: e.tensor_tensor(out=sm[:, 2:3], in0=sm[:, 1:2], in1=neglam[:, 0:1], op=ALU.mult),
                             reads=["sm1", "neglam"], writes=["sm2"])
                        P.op("act", lambda e, b1=b1, col=col: e.activation(out=o1, in_=b1[:, col:col + 128], func=AF.Identity, scale=sm[:, 0:1]),
                             reads=[b1k, "sm0"], writes=["o1"])
                        P.op("dve", lambda e, b2=b2, col2=col2: e.scalar_tensor_tensor(
                            out=of, in0=b2[:, col2:col2 + 128], scalar=sm[:, 2:3], in1=o1, op0=ALU.mult, op1=ALU.add),
                            reads=[b2k, "sm2", "o1"], writes=["of"])
                        P.op("act", lambda e: e.activation(out=o1, in_=of, func=AF.Square, accum_out=sm[:, 3:4]),
                             reads=["of", "o1"], writes=["o1", "sm3"])
                        P.op("act", lambda e: e.activation(out=sm[:, 4:5], in_=sm[:, 3:4], func=AF.Ln, scale=1.0 / 128, bias=EPS),
                             reads=["sm3"], writes=["sm4"])
                        P.op("act", lambda e: e.activation(out=sm[:, 4:5], in_=sm[:, 4:5], func=AF.Exp, scale=-0.5),
                             reads=["sm4"], writes=["sm4"])
                        P.op("dve", lambda e, yb=yb: e.scalar_tensor_tensor(
                            out=yb, in0=of, scalar=sm[:, 4:5], in1=gsub, op0=ALU.mult, op1=ALU.mult),
                            reads=["of", "sm4", "gsub"], writes=[("ybt", qb)])

                    def fin_b(h=h, g=g):
                        def tr(e):
                            for qb in range(4):
                                ins = e.transpose(out=psb[:, qb * 128:(qb + 1) * 128], in_=ybt[qb], identity=identB[:])
                            return ins
                        P.op("pe", tr, reads=[("ybt", qb) for qb in range(4)] + ["identB"], writes=["psb"])
                        P.op("act", lambda e: e.activation(out=ybst, in_=psb[:, 0:TT], func=AF.Identity), reads=["psb"], writes=["ybst"])
                        P.dma("sp", lambda e: e.dma_start(out=S["ybT"][h * 128:(h + 1) * 128, g * TT:(g + 1) * TT], in_=ybst),
                              reads=["ybst"], writes=[("ybTs", h, g)])
                    finb.append(fin_b)
                    since = 0
                while finb:
                    finb.pop(0)()

        def ph_conv(pv, pvk):
            fresh()
            ubb = A.bf(4, 32 + T)
            dg = A.bf(124, 128)
            ycf = A.f32(4, T)
            ycb = A.bf(4, T)
            sqf = A.f32(4, TT)
            mf = A.f32(TT)
            vf = A.f32(TT)
            tf = [A.f32(TT) for _ in range(2)]
            HW_ = (32 + T) // 2
            for cc in range(4):
                for hh in range(2):
                    P.dma("pool", lambda e, cc=cc, hh=hh: e.dma_start(out=ubb[:, cc, hh * HW_:(hh + 1) * HW_],
                                                                      in_=S["uT"][:, cc, hh * HW_:(hh + 1) * HW_]),
                          reads=["uTs"] + [("uTs", cc, tt) for tt in range(NT)], writes=[("ubb", cc)])
            for idx in range(124):
                P.op("dve", lambda e, idx=idx: e.tensor_scalar(out=dg[:, idx, :], in0=identB[:], scalar1=pv[:, PV_CW + idx:PV_CW + idx + 1],
                                                               scalar2=None, op0=ALU.mult), reads=["identB", pvk], writes=[("dg", idx // 31)])
            for cc in range(4):
                for tt in range(NT):
                    bk, bkey = pb()

                    def mmc(e, cc=cc, tt=tt, bk=bk):
                        for j in range(31):
                            ins = e.matmul(bk[:, :], lhsT=dg[:, cc * 31 + j, :], rhs=ubb[:, cc, 2 + j + tt * TT:2 + j + (tt + 1) * TT],
                                           start=(j == 0), stop=(j == 30))
                        return ins
                    P.op("pe", mmc, reads=[("dg", cc), ("ubb", cc)], writes=[bkey])
                    P.op("act", lambda e, cc=cc, tt=tt, bk=bk: e.activation(
                        out=ycf[:, cc, tt * TT:(tt + 1) * TT], in_=bk[:, :], func=AF.Identity, bias=pv[:, PV_CB + cc:PV_CB + cc + 1]),
                        reads=[bkey, pvk], writes=[("ycf", cc)])
            k = 0
            allycf = [("ycf", cc) for cc in range(4)]
            for tt in range(NT):
                sl = slice(tt * TT, (tt + 1) * TT)
                P.op("act", lambda e, sl=sl: e.activation(out=sqf, in_=ycf[:, :, sl], func=AF.Square), reads=allycf, writes=["sqf"])
                b1, b1k = pb()
                b2, b2k = pb()

                def mm1(e, sl=sl, b1=b1):
                    for cc in range(4):
                        ins = e.matmul(b1[:, :], lhsT=onesF[:], rhs=ycf[:, cc, sl], start=(cc == 0), stop=(cc == 3))
                    return ins

                def mm2(e, b2=b2):
                    for cc in range(4):
                        ins = e.matmul(b2[:, :], lhsT=onesF[:], rhs=sqf[:, cc, :], start=(cc == 0), stop=(cc == 3))
                    return ins
                P.op("pe", mm1, reads=allycf + ["onesF"], writes=[b1k])
                P.op("pe", mm2, reads=["sqf", "onesF"], writes=[b2k])
                P.op("dve", lambda e, b1=b1: e.tensor_scalar(out=mf, in0=b1[:, :], scalar1=1.0 / 512, scalar2=None, op0=ALU.mult),
                     reads=[b1k], writes=["mf"])
                P.op("dve", lambda e: e.tensor_tensor(out=vf, in0=mf, in1=mf, op=ALU.mult), reads=["mf"], writes=["vf"])
                P.op("dve", lambda e, b2=b2: e.scalar_tensor_tensor(out=vf, in0=b2[:, :], scalar=1.0 / 512, in1=vf,
                                                                    op0=ALU.mult, op1=ALU.subtract),
                     reads=[b2k, "vf"], writes=["vf"])
                P.op("act", lambda e: e.activation(out=vf, in_=vf, func=AF.Ln, bias=EPS), reads=["vf"], writes=["vf"])
                P.op("act", lambda e: e.activation(out=vf, in_=vf, func=AF.Exp, scale=-0.5), reads=["vf"], writes=["vf"])
                for cc in range(4):
                    t_ = tf[k % 2]
                    tk = ("ctf", k % 2)
                    k += 1
                    P.op("dve", lambda e, cc=cc, sl=sl, t_=t_: e.tensor_tensor(out=t_, in0=ycf[:, cc, sl], in1=mf, op=ALU.subtract),
                         reads=allycf + ["mf"], writes=[tk])
                    P.op("dve", lambda e, t_=t_: e.tensor_tensor(out=t_, in0=t_, in1=vf, op=ALU.mult), reads=[tk, "vf"], writes=[tk])
                    P.op("act", lambda e, cc=cc, sl=sl, t_=t_: e.activation(
                        out=ycb[:, cc, sl], in_=t_, func=AF.Silu, scale=pv[:, PV_LG + cc:PV_LG + cc + 1],
                        bias=pv[:, PV_LB + cc:PV_LB + cc + 1]), reads=[tk, pvk], writes=[("ycb", cc)])
            for cc in range(4):
                P.dma("sp", lambda e, cc=cc: e.dma_start(out=S["ycT"][:, cc, :], in_=ycb[:, cc, :]), reads=[("ycb", cc)], writes=[("ycTs", cc)])

        def ph_merge(pv, pvk, gat, gatk, w_gate, w_br, w_out):
            fresh()
            wo = A.bf(8, D)
            hh = A.bf(8, 1024)
            yy = A.bf(3, 4, 1024)
            zT = A.bf(8, 1024)
            wg = [A.bf(3, 8, 128) for _ in range(3)]
            wb = [A.bf(3, 4, 128) for _ in range(3)]
            gsb = [A.f32(TT) for _ in range(2)]
            zacc = [A.f32(TT) for _ in range(2)]
            prod = [A.f32(TT) for _ in range(2)]
            P.dma("pool", lambda e: e.dma_start(out=wo, in_=w_out.rearrange("(c p) n -> p c n", p=128)), writes=["wo"])
            ysrc = [S["yaT"].rearrange("(c p) t -> p c t", p=128), S["ybT"].rearrange("(c p) t -> p c t", p=128), S["ycT"]]
            ykeys = [[("yaTs", hp, h) for hp in range(4) for h in range(2)],
                     [("ybTs", h, g) for h in range(4) for g in range(4)],
                     [("ycTs", cc) for cc in range(4)]]
            cnt = 0
            k = 0
            for half in range(2):
                hs = slice(half * 1024, (half + 1) * 1024)
                P.dma("sp", lambda e, hs=hs: e.dma_start(out=hh, in_=S["h2T"][:, :, hs]),
                      reads=[("h2Ts", c) for c in range(DC)], writes=["hh"])
                for i in range(3):
                    P.dma("sp", lambda e, i=i, hs=hs: e.dma_start(out=yy[:, i, :, :], in_=ysrc[i][:, :, hs]),
                          reads=ykeys[i], writes=[("yy", i)])
                for j in range(DC):
                    s = cnt % 3
                    cnt += 1
                    for i in range(3):
                        col = i * 1024 + j * 128
                        P.dma("pool", lambda e, s=s, i=i, col=col: e.dma_start(
                            out=wg[s][:, i, :, :], in_=w_gate[:, col:col + 128].rearrange("(c p) n -> p c n", p=128)),
                            writes=[("wg", s, i)])
                        P.dma("pool", lambda e, s=s, i=i, j=j: e.dma_start(
                            out=wb[s][:, i, :, :], in_=w_br[i * 512:(i + 1) * 512, j * 128:(j + 1) * 128].rearrange("(c p) n -> p c n", p=128)),
                            writes=[("wb", s, i)])
                    for t2 in range(2):
                        sl = slice(t2 * TT, (t2 + 1) * TT)
                        za = zacc[(2 * j + t2) % 2]
                        zk = ("zacc", (2 * j + t2) % 2)
                        for i in range(3):
                            u = k % 2
                            k += 1
                            bg, bgk = pb()
                            by, byk = pb()

                            def mg(e, s=s, i=i, sl=sl, bg=bg):
                                for kc in range(DC):
                                    ins = e.matmul(bg[:, :], lhsT=wg[s][:, i, kc, :], rhs=hh[:, kc, sl], start=(kc == 0), stop=(kc == DC - 1))
                                return ins

                            def my(e, s=s, i=i, sl=sl, by=by):
                                for kc in range(4):
                                    ins = e.matmul(by[:, :], lhsT=wb[s][:, i, kc, :], rhs=yy[:, i, kc, sl], start=(kc == 0), stop=(kc == 3))
                                return ins
                            P.op("pe", mg, reads=[("wg", s, i), "hh"], writes=[bgk])
                            P.op("pe", my, reads=[("wb", s, i), ("yy", i)], writes=[byk])
                            P.op("act", lambda e, u=u, bg=bg, i=i, j=j: e.activation(
                                out=gsb[u], in_=bg[:, :], func=AF.Sigmoid, bias=pv[:, PV_BG + i * 8 + j:PV_BG + i * 8 + j + 1]),
                                reads=[bgk, pvk], writes=[("gsb", u)])
                            if i == 0:
                                P.op("dve", lambda e, u=u, by=by, za=za: e.tensor_tensor(out=za, in0=by[:, :], in1=gsb[u], op=ALU.mult),
                                     reads=[byk, ("gsb", u)], writes=[zk])
                            else:
                                P.op("dve", lambda e, u=u, by=by: e.tensor_tensor(out=prod[u], in0=by[:, :], in1=gsb[u], op=ALU.mult),
                                     reads=[byk, ("gsb", u)], writes=[("prod", u)])
                                if i == 1:
                                    P.op("pool", lambda e, u=u, za=za: e.tensor_tensor(out=za, in0=za, in1=prod[u], op=ALU.add),
                                         reads=[zk, ("prod", u)], writes=[zk])
                                else:
                                    P.op("pool", lambda e, u=u, za=za, j=j, sl=sl: e.tensor_tensor(out=zT[:, j, sl], in0=za, in1=prod[u], op=ALU.add),
                                         reads=[zk, ("prod", u)], writes=[("zT", t2)])
                for dj in range(DC):
                    for t2 in range(2):
                        sl = slice(t2 * TT, (t2 + 1) * TT)
                        xs = slice(half * 1024 + t2 * TT, half * 1024 + (t2 + 1) * TT)
                        tt = half * 2 + t2
                        bd, bdk = pb()

                        def mo(e, dj=dj, sl=sl, bd=bd):
                            for kc in range(DC):
                                ins = e.matmul(bd[:, :], lhsT=wo[:, kc, dj * 128:(dj + 1) * 128], rhs=zT[:, kc, sl], start=(kc == 0), stop=(kc == DC - 1))
                            return ins
                        P.op("pe", mo, reads=["wo", ("zT", t2)], writes=[bdk])
                        P.op("dve", lambda e, dj=dj, xs=xs, bd=bd: e.scalar_tensor_tensor(
                            out=xT[:, dj, xs], in0=bd[:, :], scalar=gat[:, 1, dj:dj + 1], in1=xT[:, dj, xs], op0=ALU.mult, op1=ALU.add),
                            reads=[bdk, gatk, ("xT", tt, dj // 4)], writes=[("xT", tt, dj // 4)])

        def ph_out():
            fresh()
            ot = [A.f32(D) for _ in range(2)]
            for i in range(16):
                s = i % 2
                for hb in range(2):
                    bk, bkey = pb()

                    def tr(e, i=i, hb=hb, bk=bk):
                        for c4 in range(4):
                            c = hb * 4 + c4
                            ins = e.transpose(out=bk[:, c4 * 128:(c4 + 1) * 128], in_=xT[:, c, i * 128:(i + 1) * 128], identity=identF[:])
                        return ins
                    P.op("pe", tr, reads=[("xT", i // 4, hb), "identF"], writes=[bkey])
                    if hb == 0:
                        P.op("dve", lambda e, s=s, bk=bk: e.tensor_copy(out=ot[s][:, 0:512], in_=bk[:, :]), reads=[bkey], writes=[("ot", s, 0)])
                    else:
                        P.op("act", lambda e, s=s, bk=bk: e.activation(out=ot[s][:, 512:1024], in_=bk[:, :], func=AF.Identity),
                             reads=[bkey], writes=[("ot", s, 1)])
                P.dma("sp", lambda e, i=i, s=s: e.dma_start(out=O["out"][i * 128:(i + 1) * 128, :], in_=ot[s]),
                      reads=[("ot", s, 0), ("ot", s, 1)], writes=[("outd", i)])

        finals = []
        if stage == 1:
            ph_load_x()
        else:
            ph_load_xT()
            derive(pvA, "pvA", modA, "modA", sclA, gatA, "A")
            s2 = _DBG.get("s2", 9)
            ph_qu(pvA, "pvA", modA, "modA", sclA, "sclA", I["w_inA"])
            lam_init = ph_lambda(la)
            if s2 >= 2:
                ph_dil(pvA, "pvA")
            if s2 >= 3:
                ph_diff(pvA, "pvA", lam_init)
            if s2 >= 4:
                ph_conv(pvA, "pvA")
            if s2 >= 5:
                ph_merge(pvA, "pvA", gatA, "gatA", I["w_gateA"], I["w_brA"], I["w_outA"])
            if s2 >= 6:
                ph_ffn_full(2, pvA, "pvA", modA, "modA", sclA, "sclA", gatA, "gatA", I["w_fiA"], I["w_foA"])
        if lb is not None:
            up = _DBG.get("upto", 9) if _DBG.get("s2", 9) >= 7 else 0
            if up >= 1:
                ph_adaln(pvB, "pvB", I["w_adaB"], modB, "modB")
                derive(pvB, "pvB", modB, "modB", sclB, gatB, "B")
            if up >= 2:
                ph_ffn_full(0, pvB, "pvB", modB, "modB", sclB, "sclB", gatB, "gatB", I["w_fiB"], I["w_foB"])
            if up >= 3:
                ph_kv(pvB, "pvB", modB, "modB", sclB, "sclB", I["w_inB"])
            ph_store_x()
            P.dma("sp", lambda e: e.dma_start(out=O["modB"][:, :], in_=modB[:]), reads=["modB"], writes=["modBout"])
        else:
            ph_out()
        P.barrier()
        P.op("pool", lambda e: e.memset(neglam[:, 3:4], 0.0), writes=["__tail"])
        P.emit(["__tail"])
    return nc


_CACHE = {}
_DBG = {}


def _prog(stage):
    if stage not in _CACHE:
        _CACHE[stage] = build(stage)
    return _CACHE[stage]


def _mixer_inputs(inp, l, prev_res, dbias, fbias):
    maps = []
    for c in range(NCORES):
        own = prev_res[c]
        prv = prev_res[c - 1] if c % 2 == 1 else prev_res[c]
        maps.append({
            "xT_in": own["xT_out"], "modA": own["modB"], "pvA": _host_pv(inp, l, c),
            "w_inA": inp["w_in"][l], "w_gateA": inp["w_gate"][l], "w_brA": inp["w_branch"][l].reshape(1536, D),
            "w_outA": inp["w_out"][l], "w_fiA": inp["w_ffn_in"][l, 1], "w_foA": inp["w_ffn_out"][l, 1],
            "lamA": inp["lambda_vec"][l].reshape(1, 256), "subgA": inp["subln_g"][l].reshape(1, 128),
            "dbias": dbias, "fbias": fbias, "ut_p": prv["ut"],
            "kaT_o": own["kaT"], "kbT_o": own["kbT"], "va_o": own["va"], "vb_o": own["vb"],
            "kaT_p": prv["kaT"], "kbT_p": prv["kbT"], "va_p": prv["va"], "vb_p": prv["vb"],
        })
    return maps


def _ffn_kv_inputs(inp, l, c):
    return {"pvB": _host_pv(inp, l, c), "w_adaB": inp["w_ada"][l], "w_fiB": inp["w_ffn_in"][l, 0],
            "w_foB": inp["w_ffn_out"][l, 0], "w_inB": inp["w_in"][l]}


def kernel(**inputs):
    inp = {k: np.ascontiguousarray(np.asarray(v, dtype=np.float32)) for k, v in inputs.items()}
    dbias, fbias = _host_bias_tables(inp["rel_bias"])
    cores = list(range(NCORES))
    m1 = []
    for c in cores:
        b, hf = c // 2, c % 2
        d = {"x": np.ascontiguousarray(inp["x"][b, hf * T:(hf + 1) * T, :])}
        d.update(_ffn_kv_inputs(inp, 0, c))
        m1.append(d)
    r1 = run_bass_kernel_spmd(_prog(1), m1, core_ids=cores).results
    m2 = _mixer_inputs(inp, 0, r1, dbias, fbias)
    for c in cores:
        m2[c].update(_ffn_kv_inputs(inp, 1, c))
    r2 = run_bass_kernel_spmd(_prog(2), m2, core_ids=cores).results
    m3 = _mixer_inputs(inp, 1, r2, dbias, fbias)
    r3 = run_bass_kernel_spmd(_prog(3), m3, core_ids=cores).results
    out = np.empty((4, 2 * T, D), np.float32)
    for c in cores:
        out[c // 2, (c % 2) * T:(c % 2 + 1) * T, :] = r3[c]["out"]
    return out
```

```python
import numpy as np
import concourse.bass as bass
import concourse.mybir as mybir
from concourse.bass_utils import run_bass_kernel_spmd

F32 = mybir.dt.float32
BF16 = mybir.dt.bfloat16
AF = mybir.ActivationFunctionType
ALU = mybir.AluOpType

D = 1024
DC = 8
T = 2048
TT = 512
NT = T // TT
DFF = 2816
FC = 22
NCORES = 8


class _Op:
    __slots__ = ("eng", "fn", "deps", "is_dma", "sig", "tok", "idx")


class Prog:
    ENGS = ("pe", "act", "dve", "pool", "sp")
    NDMA = 6

    def __init__(self, nc):
        self.nc = nc
        self.ops = []
        self.last_w = {}
        self.readers = {}
        self.last_c = {}
        self.dma_hist = {}
        self.pending = {}

    def _add(self, eng, fn, reads, writes, is_dma):
        op = _Op()
        op.eng, op.fn, op.is_dma, op.sig, op.tok = eng, fn, is_dma, False, None
        op.idx = len(self.ops)
        deps = {}
        for r in reads:
            w = self.last_w.get(r)
            if w is not None:
                deps[w.idx] = (w, True)
        for wkey in writes:
            w = self.last_w.get(wkey)
            if w is not None and w.idx not in deps:
                deps[w.idx] = (w, False)
            for rd in self.readers.get(wkey, ()):
                if rd.idx not in deps:
                    deps[rd.idx] = (rd, False)
        keep = []
        for d in self.pending.pop(eng, ()):
            if d.eng == eng and not d.is_dma and eng == "pe":
                continue
            if d.idx not in deps:
                keep.append(d)
                d.sig = True
        for d, raw in deps.values():
            if d.eng == eng and not d.is_dma and not is_dma:
                if eng == "pe" or not raw:
                    continue
            keep.append(d)
            d.sig = True
        op.deps = keep
        for r in reads:
            self.readers.setdefault(r, []).append(op)
        for wkey in writes:
            self.last_w[wkey] = op
            self.readers[wkey] = []
        self.ops.append(op)
        if is_dma:
            self.dma_hist.setdefault(eng, []).append(op)
        else:
            self.last_c[eng] = op
        return op

    def barrier(self):
        B = list(self.last_c.values())
        for q, h in self.dma_hist.items():
            B.extend(h[-self.NDMA:])
        for e in self.ENGS:
            self.pending[e] = list(self.pending.get(e, ())) + B

    def op(self, eng, fn, reads=(), writes=()):
        reads, writes = tuple(reads), tuple(writes)
        extra = tuple(r for r in reads if (r == "psb" or (isinstance(r, tuple) and r[0] == "ps")) and r not in writes)
        return self._add(eng, fn, reads, writes + extra, False)

    def dma(self, eng, fn, reads=(), writes=()):
        return self._add(eng, fn, tuple(reads), tuple(writes), True)

    def emit(self, final_keys):
        nc = self.nc
        import contextlib
        with contextlib.ExitStack() as st:
            esem = {e: st.enter_context(nc.semaphore("s_" + e)) for e in self.ENGS}
            dsem = {e: [st.enter_context(nc.semaphore("d_%s%d" % (e, i))) for i in range(self.NDMA)]
                    for e in ("sp", "pool", "act")}
            ecnt = {e: 0 for e in self.ENGS}
            dcnt = {e: [0] * self.NDMA for e in dsem}
            drr = {e: 0 for e in dsem}
            finals = [self.last_w[k] for k in final_keys]
            for f in finals:
                f.sig = True
            prewait = {}
            for op in self.ops:
                if op.is_dma:
                    k = drr[op.eng] % self.NDMA
                    drr[op.eng] += 1
                    prewait[op.idx] = (dsem[op.eng][k], dcnt[op.eng][k])
                    dcnt[op.eng][k] += 16
                    op.tok = (dsem[op.eng][k], dcnt[op.eng][k])
                elif op.sig:
                    ecnt[op.eng] += 1
                    op.tok = (esem[op.eng], ecnt[op.eng])
            assert max(ecnt.values()) < 60000, ecnt
            per = {e: [o for o in self.ops if o.eng == e] for e in self.ENGS}
            block = st.enter_context(nc.Block())

            def run(eng_obj, ename, tail):
                waited = {}

                def w(sem, val):
                    if val <= 0:
                        return
                    key = id(sem)
                    if waited.get(key, 0) >= val:
                        return
                    waited[key] = val
                    eng_obj.wait_ge(sem, val)
                for op in per[ename]:
                    for d in op.deps:
                        w(*d.tok)
                    if op.is_dma:
                        w(*prewait[op.idx])
                    ins = op.fn(eng_obj)
                    if op.tok is not None:
                        ins.then_inc(op.tok[0], 16 if op.is_dma else 1)
                if tail:
                    for f in finals:
                        w(*f.tok)

            @block.tensor
            def _(e):
                run(e, "pe", False)

            @block.scalar
            def _(e):
                run(e, "act", False)

            @block.vector
            def _(e):
                run(e, "dve", False)

            @block.gpsimd
            def _(e):
                run(e, "pool", False)

            @block.sync
            def _(e):
                run(e, "sp", True)


def _rs(ap, shape):
    shape = list(shape)
    if len(shape) == 1:
        return ap
    names = "abcd"[:len(shape)]
    kw = {names[i]: shape[i] for i in range(len(shape))}
    return ap.rearrange("p (%s) -> p %s" % (" ".join(names), " ".join(names)), **kw)


class Arena:
    def __init__(self, ap_f32, nwords):
        self.base, self.n, self.off = ap_f32, nwords, 0

    def reset(self):
        self.off = 0

    def f32(self, *shape):
        n = int(np.prod(shape))
        a = self.base[:, self.off:self.off + n]
        self.off += n
        assert self.off <= self.n, (self.off, self.n)
        return _rs(a, shape)

    def bf(self, *shape):
        n = int(np.prod(shape))
        w = (n + 1) // 2
        a = self.base[:, self.off:self.off + w].bitcast(BF16)
        self.off += w
        assert self.off <= self.n, (self.off, self.n)
        return _rs(a[:, 0:n], shape)


PV_NG = 0
PV_BADA = 24
PV_GAIN = 96
PV_CW = 100
PV_CB = 224
PV_LG = 228
PV_LB = 232
PV_BG = 236
PV_CS = 260
PV_PF = 268
PV_PB = 269
NPV = 272

LW = 2688
EPS = 1e-6
DIL = ((128, 1), (512, 4), (2048, 16))


def _t5_bucket_np(dist):
    dist = np.asarray(dist, np.int64)
    dd = np.maximum(dist.astype(np.float32), np.float32(1.0))
    large = 16 + (np.log(dd / np.float32(16.0)) / np.float32(np.log(2048.0 / 16.0)) * np.float32(16.0)).astype(np.int32)
    large = np.minimum(large, 31)
    return np.where(dist < 16, dist, large).astype(np.int64)


def _host_bias_tables(rel_bias):
    NEG = np.float32(-1e30)
    k = np.arange(128)[:, None]
    dbias = np.empty((128, 8, 3, 256), np.float32)
    for p, (win, d) in enumerate(DIL):
        j = np.arange(256)[None, :]
        rel = np.where(j < 128, j + 128 - k, j - 128 - k)
        valid = (rel >= 0) & (rel <= 128)
        b = _t5_bucket_np(np.maximum(rel, 0) * d)
        for h in range(8):
            dbias[:, h, p, :] = np.where(valid, rel_bias[b, h], NEG)
    j = np.arange(LW)[None, :]
    dist = j - 384 - k
    b = _t5_bucket_np(np.maximum(dist, 0))
    fbias = np.empty((128, 4, LW), np.float32)
    for h in range(4):
        fbias[:, h, :] = np.where(dist >= 0, rel_bias[b, 8 + h], NEG)
    return dbias, fbias


def _host_pv(inp, l, core):
    b = core // 2
    pv = np.zeros((128, NPV), np.float32)
    pv[:, PV_NG:PV_NG + 24] = inp["norm_g"][l].reshape(3, 8, 128).transpose(2, 0, 1).reshape(128, 24)
    pv[:, PV_BADA:PV_BADA + 72] = inp["b_ada"][l].reshape(9, 8, 128).transpose(2, 0, 1).reshape(128, 72)
    g = inp["qk_gain"][l]
    pv[:, PV_GAIN + 0] = np.concatenate([g[0], g[0]])
    pv[:, PV_GAIN + 1] = np.concatenate([g[1], g[1]])
    pv[:, PV_GAIN + 2] = np.concatenate([g[2], g[3]])
    pv[:, PV_GAIN + 3] = np.concatenate([g[4], g[5]])
    pv[:, PV_CW:PV_CW + 124] = inp["conv_w"][l].reshape(31, 4, 128).transpose(2, 1, 0).reshape(128, 124)
    pv[:, PV_CB:PV_CB + 4] = inp["conv_b"][l].reshape(4, 128).T
    pv[:, PV_LG:PV_LG + 4] = inp["conv_ln_g"][l].reshape(4, 128).T
    pv[:, PV_LB:PV_LB + 4] = inp["conv_ln_b"][l].reshape(4, 128).T
    pv[:, PV_BG:PV_BG + 24] = inp["b_gate"][l].reshape(3, 8, 128).transpose(2, 0, 1).reshape(128, 24)
    pv[:, PV_CS:PV_CS + 8] = inp["c"][b].reshape(8, 128).T
    pv[:, PV_PF] = 1.0 if core % 2 == 1 else 0.0
    pv[:, PV_PB] = 0.0 if core % 2 == 1 else -30000.0
    return pv


def build(stage, dbg=False):
    import contextlib
    nc = bass.Bass("TRN2", target_bir_lowering=False)
    la = {1: None, 2: 0, 3: 1}[stage]
    lb = {1: 0, 2: 1, 3: None}[stage]

    def din(name, shape, dt=F32):
        return nc.dram_tensor(name, list(shape), dt, kind="ExternalInput").ap()

    def dout(name, shape, dt=F32):
        return nc.dram_tensor(name, list(shape), dt, kind="ExternalOutput").ap()

    def dscr(name, shape, dt=F32):
        return nc.dram_tensor(name, list(shape), dt, kind=("ExternalOutput" if dbg else "Internal")).ap()

    I = {}
    if stage == 1:
        I["x"] = din("x", [T, D])
    else:
        I["xT_in"] = din("xT_in", [128, DC, T])
        I["modA"] = din("modA", [128, 72])
        I["pvA"] = din("pvA", [128, NPV])
        for n, s in (("w_inA", [D, 4096]), ("w_gateA", [D, 3072]), ("w_brA", [1536, D]), ("w_outA", [D, D]),
                     ("w_fiA", [D, 2 * DFF]), ("w_foA", [DFF, D]), ("lamA", [1, 256]), ("subgA", [1, 128]),
                     ("dbias", [128, 8, 3, 256]), ("fbias", [128, 4, LW]), ("ut_p", [128, 4, 32])):
            I[n] = din(n, s)
        for n in ("kaT_o", "kbT_o", "kaT_p", "kbT_p"):
            I[n] = din(n, [128, 4, T], BF16)
        for n in ("va_o", "vb_o", "va_p", "vb_p"):
            I[n] = din(n, [T, 512], BF16)
    if lb is not None:
        I["pvB"] = din("pvB", [128, NPV])
        for n, s in (("w_adaB", [D, 9 * D]), ("w_fiB", [D, 2 * DFF]), ("w_foB", [DFF, D]), ("w_inB", [D, 4096])):
            I[n] = din(n, s)
    O = {}
    if stage < 3:
        O["xT_out"] = dout("xT_out", [128, DC, T])
        O["modB"] = dout("modB", [128, 72])
        O["kaT"] = dout("kaT", [128, 4, T], BF16)
        O["kbT"] = dout("kbT", [128, 4, T], BF16)
        O["va"] = dout("va", [T, 512], BF16)
        O["vb"] = dout("vb", [T, 512], BF16)
        O["ut"] = dout("ut", [128, 4, 32])
    else:
        O["out"] = dout("out", [T, D])
    S = {}
    if la is not None:
        S["h2T"] = dscr("h2T_s", [128, DC, T], BF16)
        S["qaT"] = dscr("qaT_s", [128, 4, T], BF16)
        S["qbT"] = dscr("qbT_s", [128, 4, T], BF16)
        S["uT"] = dscr("uT_s", [128, 4, 32 + T])
        S["yaT"] = dscr("yaT_s", [512, T], BF16)
        S["ybT"] = dscr("ybT_s", [512, T], BF16)
        S["ycT"] = dscr("ycT_s", [128, 4, T], BF16)
        S["vaf"] = dscr("vaf_s", [2 * T, 512], BF16)
        S["vbf"] = dscr("vbf_s", [2 * T, 512], BF16)

    with contextlib.ExitStack() as st:
        def sb(name, shape, dt):
            return st.enter_context(nc.sbuf_tensor(name, shape, dt))
        xT = sb("xT", [128, DC, T], F32)
        identF = sb("identF", [128, 128], F32)
        onesF = sb("onesF", [128, 128], F32)
        identB = sb("identB", [128, 128], BF16)
        onesB = sb("onesB", [128, 128], BF16)
        blkB = sb("blkB", [128, 128], BF16)
        pvA = sb("pvA_t", [128, NPV], F32)
        pvB = sb("pvB_t", [128, NPV], F32)
        modA = sb("modA_t", [128, 72], F32)
        modB = sb("modB_t", [128, 72], F32)
        sclA = sb("sclA", [128, 3, 8], F32)
        gatA = sb("gatA", [128, 3, 8], F32)
        sclB = sb("sclB", [128, 3, 8], F32)
        gatB = sb("gatB", [128, 3, 8], F32)
        neglam = sb("neglam", [128, 4], F32)
        NA = 33600
        arena_t = sb("arena", [128, NA], F32)
        A = Arena(arena_t[:, :], NA)
        ps = [st.enter_context(nc.psum_tensor("ps%d" % i, [128, 512], F32)) for i in range(7)]
        psb = st.enter_context(nc.psum_tensor("psb", [128, 1024], BF16))
        P = Prog(nc)
        rr = [0]

        def pb(lo=0, hi=7):
            i = lo + rr[0] % (hi - lo)
            rr[0] += 1
            return ps[i], ("ps", i)

        P.op("pool", lambda e: e.memset(identF[:], 0.0), writes=["identF"])
        P.op("pool", lambda e: e.affine_select(out=identF[:], in_=identF[:], pattern=[[-1, 128]],
                                                compare_op=ALU.not_equal, fill=1.0, base=0, channel_multiplier=1),
             reads=["identF"], writes=["identF"])
        P.op("pool", lambda e: e.memset(onesF[:], 1.0), writes=["onesF"])
        P.op("pool", lambda e: e.memset(onesB[:], 1.0), writes=["onesB"])
        P.op("pool", lambda e: e.memset(blkB[:], 0.0), writes=["blkB"])
        P.op("pool", lambda e: e.memset(blkB[0:64, 0:64], 1.0), reads=["blkB"], writes=["blkB"])
        P.op("pool", lambda e: e.memset(blkB[64:128, 64:128], 1.0), reads=["blkB"], writes=["blkB"])
        P.op("dve", lambda e: e.tensor_copy(out=identB[:], in_=identF[:]), reads=["identF"], writes=["identB"])
        if la is not None:
            P.dma("sp", lambda e: e.dma_start(out=pvA[:], in_=I["pvA"][:, :]), writes=["pvA"])
            P.dma("sp", lambda e: e.dma_start(out=modA[:], in_=I["modA"][:, :]), writes=["modA"])
        if lb is not None:
            P.dma("sp", lambda e: e.dma_start(out=pvB[:], in_=I["pvB"][:, :]), writes=["pvB"])

        def derive(pv, pvk, mod, modk, scl, gat, tag):
            for n in range(3):
                P.op("dve", lambda e, n=n: e.scalar_tensor_tensor(
                    out=scl[:, n, :], in0=mod[:, (3 * n + 1) * 8:(3 * n + 2) * 8], scalar=1.0,
                    in1=pv[:, PV_NG + n * 8:PV_NG + n * 8 + 8], op0=ALU.add, op1=ALU.mult),
                    reads=[modk, pvk], writes=["scl" + tag])
                P.op("dve", lambda e, n=n: e.tensor_scalar(
                    out=gat[:, n, :], in0=mod[:, (3 * n + 2) * 8:(3 * n + 3) * 8],
                    scalar1=(1.0 if n == 1 else 0.5), scalar2=None, op0=ALU.mult),
                    reads=[modk], writes=["gat" + tag])

        def fresh(mark=0):
            P.barrier()
            A.off = mark

        def ph_load_x():
            A.reset()
            xin = [A.f32(1024) for _ in range(2)]
            for i in range(16):
                s = i % 2
                P.dma("sp", lambda e, i=i, s=s: e.dma_start(out=xin[s], in_=I["x"][i * 128:(i + 1) * 128, :]),
                      writes=[("xin", s)])
                for hb in range(2):
                    bk, bkey = pb()

                    def tr(e, s=s, hb=hb, bk=bk):
                        for c4 in range(4):
                            c = hb * 4 + c4
                            ins = e.transpose(out=bk[:, c4 * 128:(c4 + 1) * 128],
                                              in_=xin[s][:, c * 128:(c + 1) * 128], identity=identF[:])
                        return ins
                    P.op("pe", tr, reads=[("xin", s), "identF"], writes=[bkey])
                    dst = xT[:, hb * 4:(hb + 1) * 4, i * 128:(i + 1) * 128]
                    if hb == 0:
                        P.op("dve", lambda e, dst=dst, bk=bk: e.tensor_copy(out=dst, in_=_rs(bk[:, :], [4, 128])),
                             reads=[bkey], writes=[("xT", i // 4, hb)])
                    else:
                        P.op("act", lambda e, dst=dst, bk=bk: e.activation(out=dst, in_=_rs(bk[:, :], [4, 128]),
                                                                          func=AF.Identity),
                             reads=[bkey], writes=[("xT", i // 4, hb)])

        def xkeys(tt):
            return [("xT", tt, 0), ("xT", tt, 1)]

        def ph_load_xT():
            for c in range(DC):
                P.dma("sp", lambda e, c=c: e.dma_start(out=xT[:, c, :], in_=I["xT_in"][:, c, :]),
                      writes=[("xT", tt, hb) for tt in range(NT) for hb in range(2)])

        def ph_adaln(pv, pvk, w_ada, mod, modk):
            A.reset()
            P.barrier()
            csb = A.bf(8)
            wa = [A.bf(8, 1024) for _ in range(2)]
            P.op("act", lambda e: e.activation(out=csb, in_=pv[:, PV_CS:PV_CS + 8], func=AF.Silu),
                 reads=[pvk], writes=["csb"])
            bk, bkey = ps[6], ("ps", 6)
            for j in range(9):
                s = j % 2
                P.dma("pool", lambda e, j=j, s=s: e.dma_start(
                    out=wa[s], in_=w_ada[:, j * 1024:(j + 1) * 1024].rearrange("(c p) n -> p c n", p=128)),
                    writes=[("wa", s)])

                def mm(e, j=j, s=s):
                    for cb in range(8):
                        for kc in range(8):
                            ins = e.matmul(bk[:, j * 8 + cb:j * 8 + cb + 1], lhsT=wa[s][:, kc, cb * 128:(cb + 1) * 128],
                                           rhs=csb[:, kc:kc + 1], start=(kc == 0), stop=(kc == 7))
                    return ins
                P.op("pe", mm, reads=[("wa", s), "csb"], writes=[bkey])
            P.op("dve", lambda e: e.tensor_tensor(out=mod[:, :], in0=bk[:, 0:72], in1=pv[:, PV_BADA:PV_BADA + 72],
                                                  op=ALU.add), reads=[bkey, pvk], writes=[modk])

        def ph_norm(n, pv, pvk, mod, modk, scl, sclk, hT):
            sqb = A.bf(8, TT)
            rs = [A.f32(TT) for _ in range(2)]
            tmpf = [A.f32(TT) for _ in range(2)]
            k = 0
            for tt in range(NT):
                sl = slice(tt * TT, (tt + 1) * TT)
                P.op("act", lambda e, sl=sl: e.activation(out=sqb, in_=xT[:, :, sl], func=AF.Square),
                     reads=xkeys(tt), writes=["sqb"])
                bk, bkey = pb()

                def mm(e, bk=bk):
                    for c in range(DC):
                        ins = e.matmul(bk[:, :], lhsT=onesB[:], rhs=sqb[:, c, :], start=(c == 0), stop=(c == DC - 1))
                    return ins
                P.op("pe", mm, reads=["sqb", "onesB"], writes=[bkey])
                r = rs[tt % 2]
                rk = ("rs", tt % 2)
                P.op("act", lambda e, r=r, bk=bk: e.activation(out=r, in_=bk[:, :], func=AF.Ln, scale=1.0 / D, bias=EPS),
                     reads=[bkey], writes=[rk])
                P.op("act", lambda e, r=r: e.activation(out=r, in_=r, func=AF.Exp, scale=-0.5), reads=[rk], writes=[rk])
                for c in range(DC):
                    tf = tmpf[k % 2]
                    tk = ("tmpf", k % 2)
                    k += 1
                    P.op("dve", lambda e, c=c, sl=sl, tf=tf, r=r: e.tensor_tensor(out=tf, in0=xT[:, c, sl], in1=r, op=ALU.mult),
                         reads=xkeys(tt) + [rk], writes=[tk])
                    P.op("act", lambda e, c=c, sl=sl, tf=tf: e.activation(
                        out=hT[:, c, sl], in_=tf, func=AF.Identity, scale=scl[:, n, c:c + 1],
                        bias=mod[:, 3 * n * 8 + c:3 * n * 8 + c + 1]),
                        reads=[tk, sclk, modk], writes=[("h", tt)])

        def ph_ffn(w_up, w_dn, gat, gatk, n, hT):
            groups = [(0, 6), (6, 6), (12, 6), (18, 4)]
            actT = A.bf(6, T)
            wup = [A.bf(2, 8, 256) for _ in range(2)]
            wdn = [A.bf(6, D) for _ in range(2)]
            sgf = [A.f32(TT) for _ in range(2)]
            k = 0
            npair = 0
            for gi, (c0, gn) in enumerate(groups):
                ws = gi % 2
                P.dma("pool", lambda e, c0=c0, gn=gn, ws=ws: e.dma_start(
                    out=wdn[ws][:, 0:gn, :], in_=w_dn[c0 * 128:(c0 + gn) * 128, :].rearrange("(i p) n -> p i n", p=128)),
                    writes=[("wdn", ws)])
                for pi in range(gn // 2):
                    cpair = c0 + 2 * pi
                    us = npair % 2
                    npair += 1
                    for gu in range(2):
                        col = gu * DFF + cpair * 128
                        P.dma("pool", lambda e, us=us, gu=gu, col=col: e.dma_start(
                            out=wup[us][:, gu, :, :], in_=w_up[:, col:col + 256].rearrange("(c p) n -> p c n", p=128)),
                            writes=[("wup", us, gu)])
                    for ci in range(2):
                        il = 2 * pi + ci
                        for tt in range(NT):
                            sl = slice(tt * TT, (tt + 1) * TT)
                            bg, bgk = pb()
                            bu, buk = pb()

                            def mmg(e, us=us, ci=ci, sl=sl, bg=bg, gu=0):
                                for kc in range(DC):
                                    ins = e.matmul(bg[:, :], lhsT=wup[us][:, gu, kc, ci * 128:(ci + 1) * 128], rhs=hT[:, kc, sl],
                                                   start=(kc == 0), stop=(kc == DC - 1))
                                return ins
                            P.op("pe", mmg, reads=[("wup", us, 0), ("h", tt)], writes=[bgk])
                            P.op("pe", lambda e, us=us, ci=ci, sl=sl, bu=bu: mmg(e, us, ci, sl, bu, 1),
                                 reads=[("wup", us, 1), ("h", tt)], writes=[buk])
                            sg = sgf[k % 2]
                            sk = ("sgf", k % 2)
                            k += 1
                            P.op("act", lambda e, sg=sg, bg=bg: e.activation(out=sg, in_=bg[:, :], func=AF.Silu),
                                 reads=[bgk], writes=[sk])
                            P.op("dve", lambda e, sg=sg, bu=bu, il=il, sl=sl: e.tensor_tensor(
                                out=actT[:, il, sl], in0=bu[:, :], in1=sg, op=ALU.mult),
                                reads=[buk, sk], writes=[("actT", tt)])
                for dc in range(DC):
                    for tt in range(NT):
                        sl = slice(tt * TT, (tt + 1) * TT)
                        bd, bdk = pb()

                        def mmd(e, dc=dc, sl=sl, bd=bd, gn=gn, ws=ws):
                            for i in range(gn):
                                ins = e.matmul(bd[:, :], lhsT=wdn[ws][:, i, dc * 128:(dc + 1) * 128], rhs=actT[:, i, sl],
                                               start=(i == 0), stop=(i == gn - 1))
                            return ins
                        P.op("pe", mmd, reads=[("wdn", ws), ("actT", tt)], writes=[bdk])
                        P.op("dve", lambda e, dc=dc, sl=sl, bd=bd: e.scalar_tensor_tensor(
                            out=xT[:, dc, sl], in0=bd[:, :], scalar=gat[:, n, dc:dc + 1], in1=xT[:, dc, sl],
                            op0=ALU.mult, op1=ALU.add),
                            reads=[bdk, gatk, ("xT", tt, dc // 4)], writes=[("xT", tt, dc // 4)])

        def preload_w(w_in, colbase, ncols, key):
            wt = A.bf(8, ncols)
            P.dma("pool", lambda e: e.dma_start(out=wt, in_=w_in[:, colbase:colbase + ncols].rearrange("(c p) n -> p c n", p=128)),
                  writes=[key])
            return wt, key

        def proj_norm(w_in, colbase, pv, pvk, gaincol, hT, dst, pre=None):
            wq, wqk = pre if pre is not None else (A.bf(8, 512), "wq")
            kst = [A.bf(T) for _ in range(2)]
            sq = [A.bf(TT) for _ in range(2)]
            qf = [A.f32(TT) for _ in range(2)]
            rs = [A.f32(TT) for _ in range(2)]
            if pre is None:
                P.dma("pool", lambda e: e.dma_start(out=wq, in_=w_in[:, colbase:colbase + 512].rearrange("(c p) n -> p c n", p=128)),
                      writes=["wq"])
            k = 0
            pend = []
            for ch in range(4):
                ks = kst[ch % 2]
                kk = ("kst", ch % 2)
                for tt in range(NT):
                    sl = slice(tt * TT, (tt + 1) * TT)
                    u = k % 2
                    k += 1
                    bq, bqk = pb()

                    def mm(e, ch=ch, sl=sl, bq=bq):
                        for kc in range(DC):
                            ins = e.matmul(bq[:, :], lhsT=wq[:, kc, ch * 128:(ch + 1) * 128], rhs=hT[:, kc, sl],
                                           start=(kc == 0), stop=(kc == DC - 1))
                        return ins
                    P.op("pe", mm, reads=[wqk, ("h", tt)], writes=[bqk])
                    P.op("act", lambda e, u=u, bq=bq: e.activation(out=sq[u], in_=bq[:, :], func=AF.Square),
                         reads=[bqk], writes=[("sq", u)])
                    P.op("dve", lambda e, u=u, bq=bq: e.tensor_copy(out=qf[u], in_=bq[:, :]), reads=[bqk], writes=[("qf", u)])

                    def back(u=u, ks=ks, kk=kk, sl=sl, ch=ch, tt=tt):
                        bs, bsk = pb()
                        P.op("pe", lambda e: e.matmul(bs[:, :], lhsT=blkB[:], rhs=sq[u], start=True, stop=True),
                             reads=[("sq", u), "blkB"], writes=[bsk])
                        P.op("act", lambda e: e.activation(out=rs[u], in_=bs[:, :], func=AF.Ln, scale=1.0 / 64, bias=EPS),
                             reads=[bsk], writes=[("prs", u)])
                        P.op("act", lambda e: e.activation(out=rs[u], in_=rs[u], func=AF.Exp, scale=-0.5),
                             reads=[("prs", u)], writes=[("prs", u)])
                        P.op("dve", lambda e: e.scalar_tensor_tensor(
                            out=ks[:, sl], in0=qf[u], scalar=pv[:, PV_GAIN + gaincol:PV_GAIN + gaincol + 1], in1=rs[u],
                            op0=ALU.mult, op1=ALU.mult), reads=[("qf", u), ("prs", u), pvk], writes=[kk])
                        if tt == NT - 1:
                            P.dma("sp", lambda e: e.dma_start(out=dst(ch), in_=ks), reads=[kk], writes=[("dst", colbase, ch)])
                    pend.append(back)
                    while len(pend) > 1:
                        pend.pop(0)()
            while pend:
                pend.pop(0)()

        def v_proj(w_in, colbase, hT, dst, pre=None):
            wv, wvk = pre if pre is not None else (A.bf(8, 512), "wv")
            vst = [A.bf(512) for _ in range(2)]
            if pre is None:
                P.dma("pool", lambda e: e.dma_start(out=wv, in_=w_in[:, colbase:colbase + 512].rearrange("(c p) n -> p c n", p=128)),
                      writes=["wv"])
            for i in range(16):
                u = i % 2
                bv, bvk = pb()

                def mm(e, i=i, bv=bv):
                    for kc in range(DC):
                        ins = e.matmul(bv[:, :], lhsT=hT[:, kc, i * 128:(i + 1) * 128], rhs=wv[:, kc, :],
                                       start=(kc == 0), stop=(kc == DC - 1))
                    return ins
                P.op("pe", mm, reads=[wvk, ("h", i // 4)], writes=[bvk])
                P.op("act", lambda e, u=u, bv=bv: e.activation(out=vst[u], in_=bv[:, :], func=AF.Identity),
                     reads=[bvk], writes=[("vst", u)])
                P.dma("sp", lambda e, i=i, u=u: e.dma_start(out=dst[i * 128:(i + 1) * 128, :], in_=vst[u]),
                      reads=[("vst", u)], writes=[("vdst", colbase, i)])

        def glu(w_in, hT, tts, sink, pre=None):
            wu, wuk = pre if pre is not None else (A.bf(8, 1024), "wu")
            sgf = [A.f32(TT) for _ in range(2)]
            uf = [A.f32(TT) for _ in range(2)]
            if pre is None:
                P.dma("pool", lambda e: e.dma_start(out=wu, in_=w_in[:, 3072:4096].rearrange("(c p) n -> p c n", p=128)),
                      writes=["wu"])
            k = 0
            for ch in range(4):
                for tt in tts:
                    sl = slice(tt * TT, (tt + 1) * TT)
                    u = k % 2
                    k += 1
                    b1, b1k = pb()
                    b2, b2k = pb()

                    def mm(e, col, bk, sl=sl):
                        for kc in range(DC):
                            ins = e.matmul(bk[:, :], lhsT=wu[:, kc, col:col + 128], rhs=hT[:, kc, sl],
                                           start=(kc == 0), stop=(kc == DC - 1))
                        return ins
                    P.op("pe", lambda e, ch=ch, b1=b1, mm=mm: mm(e, ch * 128, b1), reads=[wuk, ("h", tt)], writes=[b1k])
                    P.op("pe", lambda e, ch=ch, b2=b2, mm=mm: mm(e, 512 + ch * 128, b2), reads=[wuk, ("h", tt)], writes=[b2k])
                    P.op("act", lambda e, u=u, b2=b2: e.activation(out=sgf[u], in_=b2[:, :], func=AF.Sigmoid),
                         reads=[b2k], writes=[("gsg", u)])
                    P.op("dve", lambda e, u=u, b1=b1: e.tensor_tensor(out=uf[u], in0=b1[:, :], in1=sgf[u], op=ALU.mult),
                         reads=[b1k, ("gsg", u)], writes=[("guf", u)])
                    sink(ch, tt, uf[u], ("guf", u))

        def ph_kv(pv, pvk, mod, modk, scl, sclk, w_in):
            A.reset()
            P.barrier()
            hT = A.bf(8, T)
            p_ka = preload_w(w_in, 512, 512, "w_ka")
            p_kb = preload_w(w_in, 2048, 512, "w_kb")
            p_va = preload_w(w_in, 1024, 512, "w_va")
            p_vb = preload_w(w_in, 2560, 512, "w_vb")
            p_u = preload_w(w_in, 3072, 1024, "w_u")
            ph_norm(1, pv, pvk, mod, modk, scl, sclk, hT)
            mark = A.off
            proj_norm(w_in, 512, pv, pvk, 1, hT, lambda ch: O["kaT"][:, ch, :], pre=p_ka)
            fresh(mark)
            proj_norm(w_in, 2048, pv, pvk, 3, hT, lambda ch: O["kbT"][:, ch, :], pre=p_kb)
            fresh(mark)
            v_proj(w_in, 1024, hT, O["va"], pre=p_va)
            fresh(mark)
            v_proj(w_in, 2560, hT, O["vb"], pre=p_vb)
            fresh(mark)

            def sink(ch, tt, uf, key):
                P.dma("sp", lambda e, ch=ch, uf=uf: e.dma_start(out=O["ut"][:, ch, :], in_=uf[:, TT - 32:TT]),
                      reads=[key], writes=[("utout", ch)])
            glu(w_in, hT, [NT - 1], sink, pre=p_u)

        def ph_store_x():
            for c in range(DC):
                P.dma("sp", lambda e, c=c: e.dma_start(out=O["xT_out"][:, c, :], in_=xT[:, c, :]),
                      reads=[("xT", tt, c // 4) for tt in range(NT)], writes=[("xout", c)])

        def ph_ffn_full(n, pv, pvk, mod, modk, scl, sclk, gat, gatk, w_up, w_dn):
            fresh()
            hT = A.bf(8, T)
            mark = A.off
            ph_norm(n, pv, pvk, mod, modk, scl, sclk, hT)
            ph_ffn(w_up, w_dn, gat, gatk, n, hT)

        def ph_qu(pv, pvk, mod, modk, scl, sclk, w_in):
            fresh()
            hT = A.bf(8, T)
            p_qa = preload_w(w_in, 0, 512, "w_qa")
            p_qb = preload_w(w_in, 1536, 512, "w_qb")
            p_u = preload_w(w_in, 3072, 1024, "w_u")
            ph_norm(1, pv, pvk, mod, modk, scl, sclk, hT)
            for c in range(DC):
                P.dma("sp", lambda e, c=c: e.dma_start(out=S["h2T"][:, c, :], in_=hT[:, c, :]),
                      reads=[("h", tt) for tt in range(NT)], writes=[("h2Ts", c)])
            mark = A.off
            proj_norm(w_in, 0, pv, pvk, 0, hT, lambda ch: S["qaT"][:, ch, :], pre=p_qa)
            fresh(mark)
            proj_norm(w_in, 1536, pv, pvk, 2, hT, lambda ch: S["qbT"][:, ch, :], pre=p_qb)
            fresh(mark)
            utp = A.f32(4, 32)
            P.dma("sp", lambda e: e.dma_start(out=utp, in_=I["ut_p"][:, :, :]), writes=["utp"])
            P.op("dve", lambda e: e.tensor_scalar(out=utp, in0=utp, scalar1=pv[:, PV_PF:PV_PF + 1], scalar2=None, op0=ALU.mult),
                 reads=["utp", pvk], writes=["utp"])
            P.dma("sp", lambda e: e.dma_start(out=S["uT"][:, :, 0:32], in_=utp), reads=["utp"], writes=["uTs"])

            def sink(ch, tt, uf, key):
                P.dma("sp", lambda e, ch=ch, tt=tt, uf=uf: e.dma_start(out=S["uT"][:, ch, 32 + tt * TT:32 + (tt + 1) * TT], in_=uf),
                      reads=[key], writes=[("uTs", ch, tt)])
            glu(w_in, hT, list(range(NT)), sink, pre=p_u)
            for nm, src_p, src_o in (("vaf", "va_p", "va_o"), ("vbf", "vb_p", "vb_o")):
                P.dma("sp", lambda e, nm=nm, src_p=src_p: e.dma_start(out=S[nm][0:T, :], in_=I[src_p][:, :]), writes=[(nm, 0)])
                P.dma("sp", lambda e, nm=nm, src_o=src_o: e.dma_start(out=S[nm][T:2 * T, :], in_=I[src_o][:, :]), writes=[(nm, 1)])

        def ph_dil(pv, pvk):
            fresh()
            qz2 = [[A.bf(T) for _ in range(2)] for _ in range(2)]
            kh2 = [A.bf(2 * T) for _ in range(2)]
            vbuf = [A.bf(32, 256) for _ in range(2)]
            pT = [A.bf(2, 256) for _ in range(2)]
            tmp = [A.f32(2, 256) for _ in range(2)]
            acc = A.f32(2, T)
            db2 = [A.f32(2, 3, 256) for _ in range(2)]
            dbb2 = [A.bf(2, 3, 256) for _ in range(2)]
            rlow = A.f32(T)
            yn = A.bf(T)
            for s in range(2):
                P.op("pool", lambda e, s=s: e.memset(vbuf[s], 1.0), writes=[("vbuf", s, r) for r in range(16)])
                for c_ in range(2):
                    P.op("pool", lambda e, s=s, c_=c_: e.memset(qz2[s][c_], 0.0), writes=[("qh", s)])
            unit = 0
            allacc = [("acc", b) for b in range(16)]
            def load_hp(hp):
                b_ = hp % 2
                for c_ in range(2):
                    P.dma("sp", lambda e, c_=c_: e.dma_start(out=qz2[b_][c_][c_ * 64:(c_ + 1) * 64, :],
                                                             in_=S["qaT"][c_ * 64:(c_ + 1) * 64, hp, :]), writes=[("qh", b_)])
                P.dma("sp", lambda e: e.dma_start(out=kh2[b_][:, 0:T], in_=I["kaT_p"][:, hp, :]), writes=[("kh", b_, 0)])
                P.dma("sp", lambda e: e.dma_start(out=kh2[b_][:, T:2 * T], in_=I["kaT_o"][:, hp, :]), writes=[("kh", b_, 1)])
                P.dma("sp", lambda e: e.dma_start(out=db2[b_], in_=I["dbias"][:, 2 * hp:2 * hp + 2, :, :]), writes=[("db", b_)])
                P.op("dve", lambda e: e.tensor_scalar(out=dbb2[b_], in0=db2[b_], scalar1=8.0, scalar2=None, op0=ALU.mult),
                     reads=[("db", b_)], writes=[("dbb", b_)])
            load_hp(0)
            for hp in range(4):
                if hp + 1 < 4:
                    load_hp(hp + 1)
                hb_ = hp % 2
                qz, kh, dbb = qz2[hb_], kh2[hb_], dbb2[hb_]
                pend = []

                def flush(n):
                    while len(pend) > n:
                        pend.pop(0)()
                for p, (win, d) in enumerate(DIL):
                    vs = (hp * 3 + p) % 2
                    vb = vbuf[vs]
                    nb = 16 // d
                    vview = S["vaf"].rearrange("(i d) c -> i d c", d=d)
                    i0_ = T // d - 128
                    for r in range(d):
                        for h in range(2):
                            src = vview[i0_:i0_ + 128 * (nb + 1), r, hp * 128 + h * 64:hp * 128 + h * 64 + 64].rearrange(
                                "(j p) c -> p j c", p=128)
                            P.dma("sp", lambda e, vb=vb, r=r, h=h, nb=nb, src=src: e.dma_start(
                                out=vb[:, r * (nb + 1):(r + 1) * (nb + 1), h * 128:h * 128 + 64], in_=src),
                                reads=[("vaf", 0), ("vaf", 1)], writes=[("vbuf", vs, r)])
                    khv = _rs(kh, [2 * T // d, d])
                    qhv = [_rs(qz[0], [T // d, d]), _rs(qz[1], [T // d, d])]
                    for r in range(d):
                        for m in range(nb):
                            u = unit % 2
                            unit += 1
                            bS = [pb(), pb()]
                            bO, bOk = pb()

                            def st_(e, r=r, m=m, d=d, bS=bS, khv=khv, qhv=qhv, dbb=dbb, p=p):
                                for h in range(2):
                                    for jj in range(2):
                                        ki = T // d + 128 * (m - 1 + jj)
                                        ins = e.matmul(bS[h][0][:, jj * 128:(jj + 1) * 128],
                                                       lhsT=khv[:, ki:ki + 128, r],
                                                       rhs=qhv[h][:, 128 * m:128 * m + 128, r],
                                                       start=(jj == 0), stop=False, skip_group_check=True)
                                for h in range(2):
                                    for jj in range(2):
                                        ins = e.matmul(bS[h][0][:, jj * 128:(jj + 1) * 128], lhsT=identB[:],
                                                       rhs=dbb[:, h, p, jj * 128:(jj + 1) * 128],
                                                       start=False, stop=True, skip_group_check=True)
                                return ins
                            P.op("pe", st_, reads=[("qh", hb_), ("kh", hb_, 0), ("kh", hb_, 1), ("dbb", hb_), "identB"],
                                 writes=[bS[0][1], bS[1][1]])
                            for h in range(2):
                                if m == 0:
                                    P.op("act", lambda e, h=h, u=u, bS=bS: e.activation(
                                        out=pT[u][:, h, 0:128], in_=bS[h][0][:, 0:128], func=AF.Exp, scale=0.125, bias=pv[:, PV_PB:PV_PB + 1]),
                                        reads=[bS[h][1], pvk], writes=[("dpT", u, h)])
                                    P.op("act", lambda e, h=h, u=u, bS=bS: e.activation(
                                        out=pT[u][:, h, 128:256], in_=bS[h][0][:, 128:256], func=AF.Exp, scale=0.125),
                                        reads=[bS[h][1]], writes=[("dpT", u, h)])
                                else:
                                    P.op("act", lambda e, h=h, u=u, bS=bS: e.activation(out=pT[u][:, h, :], in_=bS[h][0][:, 0:256], func=AF.Exp, scale=0.125),
                                         reads=[bS[h][1]], writes=[("dpT", u, h)])

                            def back(r=r, m=m, nb=nb, u=u, vb=vb, vs=vs, bO=bO, bOk=bOk, d=d, p=p):
                                def pv_(e):
                                    for h in range(2):
                                        for jj in range(2):
                                            ins = e.matmul(bO[:, h * 128:(h + 1) * 128],
                                                           lhsT=vb[:, r * (nb + 1) + m + jj, h * 128:(h + 1) * 128],
                                                           rhs=pT[u][:, h, jj * 128:(jj + 1) * 128], start=(jj == 0), stop=(jj == 1))
                                    return ins
                                P.op("pe", pv_, reads=[("dpT", u, 0), ("dpT", u, 1), ("vbuf", vs, r)], writes=[bOk])
                                av = acc.rearrange("p h (i d) -> p h i d", d=d)[:, :, 128 * m:128 * m + 128, r]
                                ak = [("acc", b) for b in range(d * m, d * (m + 1))]
                                if p == 0:
                                    P.op("dve", lambda e: e.tensor_copy(out=av, in_=_rs(bO[:, 0:256], [2, 128])),
                                         reads=[bOk], writes=ak)
                                else:
                                    P.op("dve", lambda e: e.tensor_tensor(out=av, in0=_rs(bO[:, 0:256], [2, 128]), in1=av, op=ALU.add),
                                         reads=[bOk] + ak, writes=ak)
                            pend.append(back)
                            flush(1)
                flush(0)
                for h in range(2):
                    P.op("dve", lambda e, h=h: e.reciprocal(out=acc[64:128, h, :], in_=acc[64:128, h, :]), reads=allacc, writes=allacc)
                    P.op("dve", lambda e, h=h: e.tensor_copy(out=rlow[0:64, :], in_=acc[64:128, h, :]), reads=allacc, writes=["rlow"])
                    P.op("dve", lambda e, h=h: e.tensor_tensor(out=yn[0:64, :], in0=acc[0:64, h, :], in1=rlow[0:64, :], op=ALU.mult),
                         reads=allacc + ["rlow"], writes=["yn"])
                    P.dma("sp", lambda e, hp=hp, h=h: e.dma_start(out=S["yaT"][(hp * 2 + h) * 64:(hp * 2 + h + 1) * 64, :], in_=yn[0:64, :]),
                          reads=["yn"], writes=[("yaTs", hp, h)])

        def ph_lambda(l):
            import math
            lam_init = 0.8 - 0.6 * math.exp(-0.3 * l)
            fresh()
            lv = A.f32(256)
            t = A.f32(128)
            s2 = A.f32(2)
            P.dma("sp", lambda e: e.dma_start(out=lv, in_=I["lamA"][0:1, :].partition_broadcast(128)), writes=["lv"])
            P.op("dve", lambda e: e.tensor_tensor(out=_rs(t, [2, 64]), in0=_rs(lv, [2, 2, 64])[:, :, 0, :],
                                                  in1=_rs(lv, [2, 2, 64])[:, :, 1, :], op=ALU.mult), reads=["lv"], writes=["lvt"])
            P.op("dve", lambda e: e.tensor_reduce(out=s2, in_=_rs(t, [2, 64]), axis=mybir.AxisListType.X, op=ALU.add),
                 reads=["lvt"], writes=["lvs"])
            P.op("act", lambda e: e.activation(out=s2, in_=s2, func=AF.Exp), reads=["lvs"], writes=["lvs"])
            P.op("dve", lambda e: e.tensor_tensor(out=neglam[:, 1:2], in0=s2[:, 1:2], in1=s2[:, 0:1], op=ALU.subtract),
                 reads=["lvs"], writes=["neglam1"])
            P.op("dve", lambda e: e.tensor_scalar(out=neglam[:, 0:1], in0=neglam[:, 1:2], scalar1=-lam_init, scalar2=None, op0=ALU.add),
                 reads=["neglam1"], writes=["neglam"])
            return lam_init

        def ph_diff(pv, pvk, lam_init):
            fresh()
            qz2 = [[A.bf(T) for _ in range(2)] for _ in range(2)]
            kh2 = [A.bf(2 * T) for _ in range(2)]
            vaug2 = [A.bf(32, 130) for _ in range(2)]
            W2 = [A.f32(LW) for _ in range(2)]
            NS = 4
            tmp = [A.f32(TT) for _ in range(NS)]
            pT = [A.bf(TT) for _ in range(NS)]
            sbank = [(ps[0], ("ps", 0)), (ps[1], ("ps", 1)), (ps[6], ("ps", 6)), (ps[5], ("ps", 5))]

            def aslot(c, qb):
                a = c * 4 + qb
                return ps[2 + a // 3], ("ps", 2 + a // 3), (a % 3) * 160
            o1 = A.f32(128)
            of = A.f32(128)
            sm = A.f32(8)
            ybt = [A.bf(128) for _ in range(4)]
            ybst = A.bf(TT)
            gsub = A.f32(128)
            for b_ in range(2):
                P.op("pool", lambda e, b_=b_: e.memset(vaug2[b_], 1.0), writes=[("vaug", b_)])
                for c_ in range(2):
                    P.op("pool", lambda e, b_=b_, c_=c_: e.memset(qz2[b_][c_], 0.0), writes=[("qh", b_)])
            P.dma("sp", lambda e: e.dma_start(out=gsub, in_=I["subgA"][0:1, :].partition_broadcast(128)), writes=["gsub"])

            def load_head(h):
                b_ = h % 2
                for c_ in range(2):
                    P.dma("sp", lambda e, c_=c_: e.dma_start(out=qz2[b_][c_][c_ * 64:(c_ + 1) * 64, :],
                                                             in_=S["qbT"][c_ * 64:(c_ + 1) * 64, h, :]), writes=[("qh", b_)])
                P.dma("sp", lambda e: e.dma_start(out=kh2[b_][:, 0:T], in_=I["kbT_p"][:, h, :]), writes=[("kh", b_, 0)])
                P.dma("sp", lambda e: e.dma_start(out=kh2[b_][:, T:2 * T], in_=I["kbT_o"][:, h, :]), writes=[("kh", b_, 1)])
                P.dma("sp", lambda e: e.dma_start(
                    out=vaug2[b_][:, :, 0:128], in_=S["vbf"][:, h * 128:(h + 1) * 128].rearrange("(j p) c -> p j c", p=128)),
                    reads=[("vbf", 0), ("vbf", 1)], writes=[("vaug", b_)])
                P.dma("sp", lambda e: e.dma_start(out=W2[b_], in_=I["fbias"][:, h, :]), writes=[("W", b_)])
            P.op("dve", lambda e: e.tensor_scalar(out=gsub, in0=gsub, scalar1=(1.0 - lam_init), scalar2=None, op0=ALU.mult),
                 reads=["gsub"], writes=["gsub"])
            cnt = 0
            load_head(0)
            for h in range(4):
                if h + 1 < 4:
                    load_head(h + 1)
                hb_ = h % 2
                qz, kh, vaug, W = qz2[hb_], kh2[hb_], vaug2[hb_], W2[hb_]
                pend = []
                finb = []

                def flush(n):
                    while len(pend) > n:
                        pend.pop(0)()
                since = 0
                for g in range(4):
                    first = {}
                    for c in range(2):
                        nkb = 16 + 4 * g + 4
                        for kbi in range(nkb):
                            u = cnt % NS
                            cnt += 1
                            bS, bSk = sbank[u]
                            P.op("pe", lambda e, c=c, kbi=kbi, g=g, bS=bS, kh=kh, qz=qz: e.matmul(
                                bS[:, :], lhsT=kh[:, kbi * 128:(kbi + 1) * 128],
                                rhs=qz[c][:, g * TT:(g + 1) * TT], start=True, stop=True),
                                reads=[("qh", hb_), ("kh", hb_, 0), ("kh", hb_, 1)], writes=[bSk])
                            delta = (T + TT * g) - 128 * kbi
                            off = delta + 384 if delta < 1792 else 2176
                            P.op("dve", lambda e, u=u, off=off, bS=bS, W=W: e.scalar_tensor_tensor(
                                out=tmp[u], in0=bS[:, :], scalar=0.125, in1=W[:, off:off + TT], op0=ALU.mult, op1=ALU.add),
                                reads=[bSk, ("W", hb_)], writes=[("ftmp", u)])
                            if kbi < 16:
                                P.op("act", lambda e, u=u: e.activation(out=pT[u], in_=tmp[u], func=AF.Exp, bias=pv[:, PV_PB:PV_PB + 1]),
                                     reads=[("ftmp", u), pvk], writes=[("fpT", u)])
                            else:
                                P.op("act", lambda e, u=u: e.activation(out=pT[u], in_=tmp[u], func=AF.Exp),
                                     reads=[("ftmp", u)], writes=[("fpT", u)])
                            plan = []
                            for qb in range(4):
                                if kbi >= 16 and (4 * g + qb) < (kbi - 16):
                                    continue
                                bkx, bkk, col = aslot(c, qb)
                                stf = first.get(bkk, True)
                                first[bkk] = False
                                last = (kbi == 16 + 4 * g + qb)
                                plan.append((qb, stf, last, bkx, bkk, col))

                            def back(plan=plan, c=c, u=u, kbi=kbi, vaug=vaug, hb_=hb_):
                                def pv_(e):
                                    for qb, stf, last, bkx, bkk, col in plan:
                                        ins = e.matmul(bkx[:, col:col + 129], lhsT=pT[u][:, qb * 128:(qb + 1) * 128],
                                                       rhs=vaug[:, kbi, 0:129], start=stf, stop=last, skip_group_check=True)
                                    return ins
                                P.op("pe", pv_, reads=[("fpT", u), ("vaug", hb_)], writes=sorted(set(x[4] for x in plan)))
                            pend.append(back)
                            flush(NS - 1)
                            since += 1
                            if finb and since >= 3:
                                finb.pop(0)()
                    flush(0)
                    if finb:
                        finb.pop(0)()
                    for qb in range(4):
                        b1, b1k, col = aslot(0, qb)
                        b2, b2k, col2 = aslot(1, qb)
                        yb = ybt[qb]
                        P.op("dve", lambda e, b1=b1, col=col: e.reciprocal(out=sm[:, 0:1], in_=b1[:, col + 128:col + 129]),
                             reads=[b1k], writes=["sm0"])
                        P.op("dve", lambda e, b2=b2, col2=col2: e.reciprocal(out=sm[:, 1:2], in_=b2[:, col2 + 128:col2 + 129]),
                             reads=[b2k], writes=["sm1"])
                        P.op("dve", lambda e: e.tensor_tensor(out=sm[:, 2:3], in0=sm[:, 1:2], in1=neglam[:, 0:1], op=ALU.mult),
                             reads=["sm1", "neglam"], writes=["sm2"])
                        P.op("act", lambda e, b1=b1, col=col: e.activation(out=o1, in_=b1[:, col:col + 128], func=AF.Identity, scale=sm[:, 0:1]),
                             reads=[b1k, "sm0"], writes=["o1"])
                        P.op("dve", lambda e, b2=b2, col2=col2: e.scalar_tensor_tensor(
                            out=of, in0=b2[:, col2:col2 + 128], scalar=sm[:, 2:3], in1=o1, op0=ALU.mult, op1=ALU.add),
                            reads=[b2k, "sm2", "o1"], writes=["of"])
                        P.op("act", lambda e: e.activation(out=o1, in_=of, func=AF.Square, accum_out=sm[:, 3:4]),
                             reads=["of", "o1"], writes=["o1", "sm3"])
                        P.op("act", lambda e: e.activation(out=sm[:, 4:5], in_=sm[:, 3:4], func=AF.Ln, scale=1.0 / 128, bias=EPS),
                             reads=["sm3"], writes=["sm4"])
                        P.op("act", lambda e: e.activation(out=sm[:, 4:5], in_=sm[:, 4:5], func=AF.Exp, scale=-0.5),
                             reads=["sm4"], writes=["sm4"])
                        P.op("dve", lambda e, yb=yb: e.scalar_tensor_tensor(
                            out=yb, in0=of, scalar=sm[:, 4:5], in1=gsub, op0=ALU.mult, op1=ALU.mult),
                            reads=["of", "sm4", "gsub"], writes=[("ybt", qb)])

                    def fin_b(h=h, g=g):
                        def tr(e):
                            for qb in range(4):
                                ins = e.transpose(out=psb[:, qb * 128:(qb + 1) * 128], in_=ybt[qb], identity=identB[:])
                            return ins
                        P.op("pe", tr, reads=[("ybt", qb) for qb in range(4)] + ["identB"], writes=["psb"])
                        P.op("act", lambda e: e.activation(out=ybst, in_=psb[:, 0:TT], func=AF.Identity), reads=["psb"], writes=["ybst"])
                        P.dma("sp", lambda e: e.dma_start(out=S["ybT"][h * 128:(h + 1) * 128, g * TT:(g + 1) * TT], in_=ybst),
                              reads=["ybst"], writes=[("ybTs", h, g)])
                    finb.append(fin_b)
                    since = 0
                while finb:
                    finb.pop(0)()

        def ph_conv(pv, pvk):
            fresh()
            ubb = A.bf(4, 32 + T)
            dg = A.bf(124, 128)
            ycf = A.f32(4, T)
            ycb = A.bf(4, T)
            sqf = A.f32(4, TT)
            mf = A.f32(TT)
            vf = A.f32(TT)
            tf = [A.f32(TT) for _ in range(2)]
            HW_ = (32 + T) // 2
            for cc in range(4):
                for hh in range(2):
                    P.dma("pool", lambda e, cc=cc, hh=hh: e.dma_start(out=ubb[:, cc, hh * HW_:(hh + 1) * HW_],
                                                                      in_=S["uT"][:, cc, hh * HW_:(hh + 1) * HW_]),
                          reads=["uTs"] + [("uTs", cc, tt) for tt in range(NT)], writes=[("ubb", cc)])
            for idx in range(124):
                P.op("dve", lambda e, idx=idx: e.tensor_scalar(out=dg[:, idx, :], in0=identB[:], scalar1=pv[:, PV_CW + idx:PV_CW + idx + 1],
                                                               scalar2=None, op0=ALU.mult), reads=["identB", pvk], writes=[("dg", idx // 31)])
            for cc in range(4):
                for tt in range(NT):
                    bk, bkey = pb()

                    def mmc(e, cc=cc, tt=tt, bk=bk):
                        for j in range(31):
                            ins = e.matmul(bk[:, :], lhsT=dg[:, cc * 31 + j, :], rhs=ubb[:, cc, 2 + j + tt * TT:2 + j + (tt + 1) * TT],
                                           start=(j == 0), stop=(j == 30))
                        return ins
                    P.op("pe", mmc, reads=[("dg", cc), ("ubb", cc)], writes=[bkey])
                    P.op("act", lambda e, cc=cc, tt=tt, bk=bk: e.activation(
                        out=ycf[:, cc, tt * TT:(tt + 1) * TT], in_=bk[:, :], func=AF.Identity, bias=pv[:, PV_CB + cc:PV_CB + cc + 1]),
                        reads=[bkey, pvk], writes=[("ycf", cc)])
            k = 0
            allycf = [("ycf", cc) for cc in range(4)]
            for tt in range(NT):
                sl = slice(tt * TT, (tt + 1) * TT)
                P.op("act", lambda e, sl=sl: e.activation(out=sqf, in_=ycf[:, :, sl], func=AF.Square), reads=allycf, writes=["sqf"])
                b1, b1k = pb()
                b2, b2k = pb()

                def mm1(e, sl=sl, b1=b1):
                    for cc in range(4):
                        ins = e.matmul(b1[:, :], lhsT=onesF[:], rhs=ycf[:, cc, sl], start=(cc == 0), stop=(cc == 3))
                    return ins

                def mm2(e, b2=b2):
                    for cc in range(4):
                        ins = e.matmul(b2[:, :], lhsT=onesF[:], rhs=sqf[:, cc, :], start=(cc == 0), stop=(cc == 3))
                    return ins
                P.op("pe", mm1, reads=allycf + ["onesF"], writes=[b1k])
                P.op("pe", mm2, reads=["sqf", "onesF"], writes=[b2k])
                P.op("dve", lambda e, b1=b1: e.tensor_scalar(out=mf, in0=b1[:, :], scalar1=1.0 / 512, scalar2=None, op0=ALU.mult),
                     reads=[b1k], writes=["mf"])
                P.op("dve", lambda e: e.tensor_tensor(out=vf, in0=mf, in1=mf, op=ALU.mult), reads=["mf"], writes=["vf"])
                P.op("dve", lambda e, b2=b2: e.scalar_tensor_tensor(out=vf, in0=b2[:, :], scalar=1.0 / 512, in1=vf,
                                                                    op0=ALU.mult, op1=ALU.subtract),
                     reads=[b2k, "vf"], writes=["vf"])
                P.op("act", lambda e: e.activation(out=vf, in_=vf, func=AF.Ln, bias=EPS), reads=["vf"], writes=["vf"])
                P.op("act", lambda e: e.activation(out=vf, in_=vf, func=AF.Exp, scale=-0.5), reads=["vf"], writes=["vf"])
                for cc in range(4):
                    t_ = tf[k % 2]
                    tk = ("ctf", k % 2)
                    k += 1
                    P.op("dve", lambda e, cc=cc, sl=sl, t_=t_: e.tensor_tensor(out=t_, in0=ycf[:, cc, sl], in1=mf, op=ALU.subtract),
                         reads=allycf + ["mf"], writes=[tk])
                    P.op("dve", lambda e, t_=t_: e.tensor_tensor(out=t_, in0=t_, in1=vf, op=ALU.mult), reads=[tk, "vf"], writes=[tk])
                    P.op("act", lambda e, cc=cc, sl=sl, t_=t_: e.activation(
                        out=ycb[:, cc, sl], in_=t_, func=AF.Silu, scale=pv[:, PV_LG + cc:PV_LG + cc + 1],
                        bias=pv[:, PV_LB + cc:PV_LB + cc + 1]), reads=[tk, pvk], writes=[("ycb", cc)])
            for cc in range(4):
                P.dma("sp", lambda e, cc=cc: e.dma_start(out=S["ycT"][:, cc, :], in_=ycb[:, cc, :]), reads=[("ycb", cc)], writes=[("ycTs", cc)])

        def ph_merge(pv, pvk, gat, gatk, w_gate, w_br, w_out):
            fresh()
            wo = A.bf(8, D)
            hh = A.bf(8, 1024)
            yy = A.bf(3, 4, 1024)
            zT = A.bf(8, 1024)
            wg = [A.bf(3, 8, 128) for _ in range(3)]
            wb = [A.bf(3, 4, 128) for _ in range(3)]
            gsb = [A.f32(TT) for _ in range(2)]
            zacc = [A.f32(TT) for _ in range(2)]
            prod = [A.f32(TT) for _ in range(2)]
            P.dma("pool", lambda e: e.dma_start(out=wo, in_=w_out.rearrange("(c p) n -> p c n", p=128)), writes=["wo"])
            ysrc = [S["yaT"].rearrange("(c p) t -> p c t", p=128), S["ybT"].rearrange("(c p) t -> p c t", p=128), S["ycT"]]
            ykeys = [[("yaTs", hp, h) for hp in range(4) for h in range(2)],
                     [("ybTs", h, g) for h in range(4) for g in range(4)],
                     [("ycTs", cc) for cc in range(4)]]
            cnt = 0
            k = 0
            for half in range(2):
                hs = slice(half * 1024, (half + 1) * 1024)
                P.dma("sp", lambda e, hs=hs: e.dma_start(out=hh, in_=S["h2T"][:, :, hs]),
                      reads=[("h2Ts", c) for c in range(DC)], writes=["hh"])
                for i in range(3):
                    P.dma("sp", lambda e, i=i, hs=hs: e.dma_start(out=yy[:, i, :, :], in_=ysrc[i][:, :, hs]),
                          reads=ykeys[i], writes=[("yy", i)])
                for j in range(DC):
                    s = cnt % 3
                    cnt += 1
                    for i in range(3):
                        col = i * 1024 + j * 128
                        P.dma("pool", lambda e, s=s, i=i, col=col: e.dma_start(
                            out=wg[s][:, i, :, :], in_=w_gate[:, col:col + 128].rearrange("(c p) n -> p c n", p=128)),
                            writes=[("wg", s, i)])
                        P.dma("pool", lambda e, s=s, i=i, j=j: e.dma_start(
                            out=wb[s][:, i, :, :], in_=w_br[i * 512:(i + 1) * 512, j * 128:(j + 1) * 128].rearrange("(c p) n -> p c n", p=128)),
                            writes=[("wb", s, i)])
                    for t2 in range(2):
                        sl = slice(t2 * TT, (t2 + 1) * TT)
                        za = zacc[(2 * j + t2) % 2]
                        zk = ("zacc", (2 * j + t2) % 2)
                        for i in range(3):
                            u = k % 2
                            k += 1
                            bg, bgk = pb()
                            by, byk = pb()

                            def mg(e, s=s, i=i, sl=sl, bg=bg):
                                for kc in range(DC):
                                    ins = e.matmul(bg[:, :], lhsT=wg[s][:, i, kc, :], rhs=hh[:, kc, sl], start=(kc == 0), stop=(kc == DC - 1))
                                return ins

                            def my(e, s=s, i=i, sl=sl, by=by):
                                for kc in range(4):
                                    ins = e.matmul(by[:, :], lhsT=wb[s][:, i, kc, :], rhs=yy[:, i, kc, sl], start=(kc == 0), stop=(kc == 3))
                                return ins
                            P.op("pe", mg, reads=[("wg", s, i), "hh"], writes=[bgk])
                            P.op("pe", my, reads=[("wb", s, i), ("yy", i)], writes=[byk])
                            P.op("act", lambda e, u=u, bg=bg, i=i, j=j: e.activation(
                                out=gsb[u], in_=bg[:, :], func=AF.Sigmoid, bias=pv[:, PV_BG + i * 8 + j:PV_BG + i * 8 + j + 1]),
                                reads=[bgk, pvk], writes=[("gsb", u)])
                            if i == 0:
                                P.op("dve", lambda e, u=u, by=by, za=za: e.tensor_tensor(out=za, in0=by[:, :], in1=gsb[u], op=ALU.mult),
                                     reads=[byk, ("gsb", u)], writes=[zk])
                            else:
                                P.op("dve", lambda e, u=u, by=by: e.tensor_tensor(out=prod[u], in0=by[:, :], in1=gsb[u], op=ALU.mult),
                                     reads=[byk, ("gsb", u)], writes=[("prod", u)])
                                if i == 1:
                                    P.op("pool", lambda e, u=u, za=za: e.tensor_tensor(out=za, in0=za, in1=prod[u], op=ALU.add),
                                         reads=[zk, ("prod", u)], writes=[zk])
                                else:
                                    P.op("pool", lambda e, u=u, za=za, j=j, sl=sl: e.tensor_tensor(out=zT[:, j, sl], in0=za, in1=prod[u], op=ALU.add),
                                         reads=[zk, ("prod", u)], writes=[("zT", t2)])
                for dj in range(DC):
                    for t2 in range(2):
                        sl = slice(t2 * TT, (t2 + 1) * TT)
                        xs = slice(half * 1024 + t2 * TT, half * 1024 + (t2 + 1) * TT)
                        tt = half * 2 + t2
                        bd, bdk = pb()

                        def mo(e, dj=dj, sl=sl, bd=bd):
                            for kc in range(DC):
                                ins = e.matmul(bd[:, :], lhsT=wo[:, kc, dj * 128:(dj + 1) * 128], rhs=zT[:, kc, sl], start=(kc == 0), stop=(kc == DC - 1))
                            return ins
                        P.op("pe", mo, reads=["wo", ("zT", t2)], writes=[bdk])
                        P.op("dve", lambda e, dj=dj, xs=xs, bd=bd: e.scalar_tensor_tensor(
                            out=xT[:, dj, xs], in0=bd[:, :], scalar=gat[:, 1, dj:dj + 1], in1=xT[:, dj, xs], op0=ALU.mult, op1=ALU.add),
                            reads=[bdk, gatk, ("xT", tt, dj // 4)], writes=[("xT", tt, dj // 4)])

        def ph_out():
            fresh()
            ot = [A.f32(D) for _ in range(2)]
            for i in range(16):
                s = i % 2
                for hb in range(2):
                    bk, bkey = pb()

                    def tr(e, i=i, hb=hb, bk=bk):
                        for c4 in range(4):
                            c = hb * 4 + c4
                            ins = e.transpose(out=bk[:, c4 * 128:(c4 + 1) * 128], in_=xT[:, c, i * 128:(i + 1) * 128], identity=identF[:])
                        return ins
                    P.op("pe", tr, reads=[("xT", i // 4, hb), "identF"], writes=[bkey])
                    if hb == 0:
                        P.op("dve", lambda e, s=s, bk=bk: e.tensor_copy(out=ot[s][:, 0:512], in_=bk[:, :]), reads=[bkey], writes=[("ot", s, 0)])
                    else:
                        P.op("act", lambda e, s=s, bk=bk: e.activation(out=ot[s][:, 512:1024], in_=bk[:, :], func=AF.Identity),
                             reads=[bkey], writes=[("ot", s, 1)])
                P.dma("sp", lambda e, i=i, s=s: e.dma_start(out=O["out"][i * 128:(i + 1) * 128, :], in_=ot[s]),
                      reads=[("ot", s, 0), ("ot", s, 1)], writes=[("outd", i)])

        finals = []
        if stage == 1:
            ph_load_x()
        else:
            ph_load_xT()
            derive(pvA, "pvA", modA, "modA", sclA, gatA, "A")
            s2 = _DBG.get("s2", 9)
            ph_qu(pvA, "pvA", modA, "modA", sclA, "sclA", I["w_inA"])
            lam_init = ph_lambda(la)
            if s2 >= 2:
                ph_dil(pvA, "pvA")
            if s2 >= 3:
                ph_diff(pvA, "pvA", lam_init)
            if s2 >= 4:
                ph_conv(pvA, "pvA")
            if s2 >= 5:
                ph_merge(pvA, "pvA", gatA, "gatA", I["w_gateA"], I["w_brA"], I["w_outA"])
            if s2 >= 6:
                ph_ffn_full(2, pvA, "pvA", modA, "modA", sclA, "sclA", gatA, "gatA", I["w_fiA"], I["w_foA"])
        if lb is not None:
            up = _DBG.get("upto", 9) if _DBG.get("s2", 9) >= 7 else 0
            if up >= 1:
                ph_adaln(pvB, "pvB", I["w_adaB"], modB, "modB")
                derive(pvB, "pvB", modB, "modB", sclB, gatB, "B")
            if up >= 2:
                ph_ffn_full(0, pvB, "pvB", modB, "modB", sclB, "sclB", gatB, "gatB", I["w_fiB"], I["w_foB"])
            if up >= 3:
                ph_kv(pvB, "pvB", modB, "modB", sclB, "sclB", I["w_inB"])
            ph_store_x()
            P.dma("sp", lambda e: e.dma_start(out=O["modB"][:, :], in_=modB[:]), reads=["modB"], writes=["modBout"])
        else:
            ph_out()
        P.barrier()
        P.op("pool", lambda e: e.memset(neglam[:, 3:4], 0.0), writes=["__tail"])
        P.emit(["__tail"])
    return nc


_CACHE = {}
_DBG = {}


def _prog(stage):
    if stage not in _CACHE:
        _CACHE[stage] = build(stage)
    return _CACHE[stage]


def _mixer_inputs(inp, l, prev_res, dbias, fbias):
    maps = []
    for c in range(NCORES):
        own = prev_res[c]
        prv = prev_res[c - 1] if c % 2 == 1 else prev_res[c]
        maps.append({
            "xT_in": own["xT_out"], "modA": own["modB"], "pvA": _host_pv(inp, l, c),
            "w_inA": inp["w_in"][l], "w_gateA": inp["w_gate"][l], "w_brA": inp["w_branch"][l].reshape(1536, D),
            "w_outA": inp["w_out"][l], "w_fiA": inp["w_ffn_in"][l, 1], "w_foA": inp["w_ffn_out"][l, 1],
            "lamA": inp["lambda_vec"][l].reshape(1, 256), "subgA": inp["subln_g"][l].reshape(1, 128),
            "dbias": dbias, "fbias": fbias, "ut_p": prv["ut"],
            "kaT_o": own["kaT"], "kbT_o": own["kbT"], "va_o": own["va"], "vb_o": own["vb"],
            "kaT_p": prv["kaT"], "kbT_p": prv["kbT"], "va_p": prv["va"], "vb_p": prv["vb"],
        })
    return maps


def _ffn_kv_inputs(inp, l, c):
    return {"pvB": _host_pv(inp, l, c), "w_adaB": inp["w_ada"][l], "w_fiB": inp["w_ffn_in"][l, 0],
            "w_foB": inp["w_ffn_out"][l, 0], "w_inB": inp["w_in"][l]}


def kernel(**inputs):
    inp = {k: np.ascontiguousarray(np.asarray(v, dtype=np.float32)) for k, v in inputs.items()}
    dbias, fbias = _host_bias_tables(inp["rel_bias"])
    cores = list(range(NCORES))
    m1 = []
    for c in cores:
        b, hf = c // 2, c % 2
        d = {"x": np.ascontiguousarray(inp["x"][b, hf * T:(hf + 1) * T, :])}
        d.update(_ffn_kv_inputs(inp, 0, c))
        m1.append(d)
    r1 = run_bass_kernel_spmd(_prog(1), m1, core_ids=cores).results
    m2 = _mixer_inputs(inp, 0, r1, dbias, fbias)
    for c in cores:
        m2[c].update(_ffn_kv_inputs(inp, 1, c))
    r2 = run_bass_kernel_spmd(_prog(2), m2, core_ids=cores).results
    m3 = _mixer_inputs(inp, 1, r2, dbias, fbias)
    r3 = run_bass_kernel_spmd(_prog(3), m3, core_ids=cores).results
    out = np.empty((4, 2 * T, D), np.float32)
    for c in cores:
        out[c // 2, (c % 2) * T:(c % 2 + 1) * T, :] = r3[c]["out"]
    return out
```

```python
import numpy as np
import concourse.bass as bass
import concourse.mybir as mybir
from concourse.bass_utils import run_bass_kernel_spmd

F32 = mybir.dt.float32
BF16 = mybir.dt.bfloat16
AF = mybir.ActivationFunctionType
ALU = mybir.AluOpType

D = 1024
DC = 8
T = 2048
TT = 512
NT = T // TT
DFF = 2816
FC = 22
NCORES = 8


class _Op:
    __slots__ = ("eng", "fn", "deps", "is_dma", "sig", "tok", "idx")


class Prog:
    ENGS = ("pe", "act", "dve", "pool", "sp")
    NDMA = 6

    def __init__(self, nc):
        self.nc = nc
        self.ops = []
        self.last_w = {}
        self.readers = {}
        self.last_c = {}
        self.dma_hist = {}
        self.pending = {}

    def _add(self, eng, fn, reads, writes, is_dma):
        op = _Op()
        op.eng, op.fn, op.is_dma, op.sig, op.tok = eng, fn, is_dma, False, None
        op.idx = len(self.ops)
        deps = {}
        for r in reads:
            w = self.last_w.get(r)
            if w is not None:
                deps[w.idx] = (w, True)
        for wkey in writes:
            w = self.last_w.get(wkey)
            if w is not None and w.idx not in deps:
                deps[w.idx] = (w, False)
            for rd in self.readers.get(wkey, ()):
                if rd.idx not in deps:
                    deps[rd.idx] = (rd, False)
        keep = []
        for d in self.pending.pop(eng, ()):
            if d.eng == eng and not d.is_dma and eng == "pe":
                continue
            if d.idx not in deps:
                keep.append(d)
                d.sig = True
        for d, raw in deps.values():
            if d.eng == eng and not d.is_dma and not is_dma:
                if eng == "pe":
                    continue
            keep.append(d)
            d.sig = True
        op.deps = keep
        for r in reads:
            self.readers.setdefault(r, []).append(op)
        for wkey in writes:
            self.last_w[wkey] = op
            self.readers[wkey] = []
        self.ops.append(op)
        if is_dma:
            self.dma_hist.setdefault(eng, []).append(op)
        else:
            self.last_c[eng] = op
        return op

    def barrier(self):
        B = list(self.last_c.values())
        for q, h in self.dma_hist.items():
            B.extend(h[-self.NDMA:])
        for e in self.ENGS:
            self.pending[e] = list(self.pending.get(e, ())) + B

    def op(self, eng, fn, reads=(), writes=()):
        reads, writes = tuple(reads), tuple(writes)
        extra = tuple(r for r in reads if (r == "psb" or (isinstance(r, tuple) and r[0] == "ps")) and r not in writes)
        return self._add(eng, fn, reads, writes + extra, False)

    def dma(self, eng, fn, reads=(), writes=()):
        return self._add(eng, fn, tuple(reads), tuple(writes), True)

    def emit(self, final_keys):
        nc = self.nc
        import contextlib
        with contextlib.ExitStack() as st:
            esem = {e: st.enter_context(nc.semaphore("s_" + e)) for e in self.ENGS}
            dsem = {e: [st.enter_context(nc.semaphore("d_%s%d" % (e, i))) for i in range(self.NDMA)]
                    for e in ("sp", "pool", "act")}
            ecnt = {e: 0 for e in self.ENGS}
            dcnt = {e: [0] * self.NDMA for e in dsem}
            drr = {e: 0 for e in dsem}
            finals = [self.last_w[k] for k in final_keys]
            for f in finals:
                f.sig = True
            prewait = {}
            for op in self.ops:
                if op.is_dma:
                    k = drr[op.eng] % self.NDMA
                    drr[op.eng] += 1
                    prewait[op.idx] = (dsem[op.eng][k], dcnt[op.eng][k])
                    dcnt[op.eng][k] += 16
                    op.tok = (dsem[op.eng][k], dcnt[op.eng][k])
                elif op.sig:
                    ecnt[op.eng] += 1
                    op.tok = (esem[op.eng], ecnt[op.eng])
            assert max(ecnt.values()) < 60000, ecnt
            per = {e: [o for o in self.ops if o.eng == e] for e in self.ENGS}
            block = st.enter_context(nc.Block())

            def run(eng_obj, ename, tail):
                waited = {}

                def w(sem, val):
                    if val <= 0:
                        return
                    key = id(sem)
                    if waited.get(key, 0) >= val:
                        return
                    waited[key] = val
                    eng_obj.wait_ge(sem, val)
                for op in per[ename]:
                    for d in op.deps:
                        w(*d.tok)
                    if op.is_dma:
                        w(*prewait[op.idx])
                    ins = op.fn(eng_obj)
                    if op.tok is not None:
                        ins.then_inc(op.tok[0], 16 if op.is_dma else 1)
                if tail:
                    for f in finals:
                        w(*f.tok)

            @block.tensor
            def _(e):
                run(e, "pe", False)

            @block.scalar
            def _(e):
                run(e, "act", False)

            @block.vector
            def _(e):
                run(e, "dve", False)

            @block.gpsimd
            def _(e):
                run(e, "pool", False)

            @block.sync
            def _(e):
                run(e, "sp", True)


def _rs(ap, shape):
    shape = list(shape)
    if len(shape) == 1:
        return ap
    names = "abcd"[:len(shape)]
    kw = {names[i]: shape[i] for i in range(len(shape))}
    return ap.rearrange("p (%s) -> p %s" % (" ".join(names), " ".join(names)), **kw)


class Arena:
    def __init__(self, ap_f32, nwords):
        self.base, self.n, self.off = ap_f32, nwords, 0

    def reset(self):
        self.off = 0

    def f32(self, *shape):
        n = int(np.prod(shape))
        a = self.base[:, self.off:self.off + n]
        self.off += n
        assert self.off <= self.n, (self.off, self.n)
        return _rs(a, shape)

    def bf(self, *shape):
        n = int(np.prod(shape))
        w = (n + 1) // 2
        a = self.base[:, self.off:self.off + w].bitcast(BF16)
        self.off += w
        assert self.off <= self.n, (self.off, self.n)
        return _rs(a[:, 0:n], shape)


PV_NG = 0
PV_BADA = 24
PV_GAIN = 96
PV_CW = 100
PV_CB = 224
PV_LG = 228
PV_LB = 232
PV_BG = 236
PV_CS = 260
PV_PF = 268
PV_PB = 269
NPV = 272

LW = 2688
EPS = 1e-6
DIL = ((128, 1), (512, 4), (2048, 16))


def _t5_bucket_np(dist):
    dist = np.asarray(dist, np.int64)
    dd = np.maximum(dist.astype(np.float32), np.float32(1.0))
    large = 16 + (np.log(dd / np.float32(16.0)) / np.float32(np.log(2048.0 / 16.0)) * np.float32(16.0)).astype(np.int32)
    large = np.minimum(large, 31)
    return np.where(dist < 16, dist, large).astype(np.int64)


def _host_bias_tables(rel_bias):
    NEG = np.float32(-1e30)
    k = np.arange(128)[:, None]
    dbias = np.empty((128, 8, 3, 256), np.float32)
    for p, (win, d) in enumerate(DIL):
        j = np.arange(256)[None, :]
        rel = np.where(j < 128, j + 128 - k, j - 128 - k)
        valid = (rel >= 0) & (rel <= 128)
        b = _t5_bucket_np(np.maximum(rel, 0) * d)
        for h in range(8):
            dbias[:, h, p, :] = np.where(valid, rel_bias[b, h], NEG)
    j = np.arange(LW)[None, :]
    dist = j - 384 - k
    b = _t5_bucket_np(np.maximum(dist, 0))
    fbias = np.empty((128, 4, LW), np.float32)
    for h in range(4):
        fbias[:, h, :] = np.where(dist >= 0, rel_bias[b, 8 + h], NEG)
    return dbias, fbias


def _host_pv(inp, l, core):
    b = core // 2
    pv = np.zeros((128, NPV), np.float32)
    pv[:, PV_NG:PV_NG + 24] = inp["norm_g"][l].reshape(3, 8, 128).transpose(2, 0, 1).reshape(128, 24)
    pv[:, PV_BADA:PV_BADA + 72] = inp["b_ada"][l].reshape(9, 8, 128).transpose(2, 0, 1).reshape(128, 72)
    g = inp["qk_gain"][l]
    pv[:, PV_GAIN + 0] = np.concatenate([g[0], g[0]])
    pv[:, PV_GAIN + 1] = np.concatenate([g[1], g[1]])
    pv[:, PV_GAIN + 2] = np.concatenate([g[2], g[3]])
    pv[:, PV_GAIN + 3] = np.concatenate([g[4], g[5]])
    pv[:, PV_CW:PV_CW + 124] = inp["conv_w"][l].reshape(31, 4, 128).transpose(2, 1, 0).reshape(128, 124)
    pv[:, PV_CB:PV_CB + 4] = inp["conv_b"][l].reshape(4, 128).T
    pv[:, PV_LG:PV_LG + 4] = inp["conv_ln_g"][l].reshape(4, 128).T
    pv[:, PV_LB:PV_LB + 4] = inp["conv_ln_b"][l].reshape(4, 128).T
    pv[:, PV_BG:PV_BG + 24] = inp["b_gate"][l].reshape(3, 8, 128).transpose(2, 0, 1).reshape(128, 24)
    pv[:, PV_CS:PV_CS + 8] = inp["c"][b].reshape(8, 128).T
    pv[:, PV_PF] = 1.0 if core % 2 == 1 else 0.0
    pv[:, PV_PB] = 0.0 if core % 2 == 1 else -30000.0
    return pv


def build(stage, dbg=False):
    import contextlib
    nc = bass.Bass("TRN2", target_bir_lowering=False)
    la = {1: None, 2: 0, 3: 1}[stage]
    lb = {1: 0, 2: 1, 3: None}[stage]

    def din(name, shape, dt=F32):
        return nc.dram_tensor(name, list(shape), dt, kind="ExternalInput").ap()

    def dout(name, shape, dt=F32):
        return nc.dram_tensor(name, list(shape), dt, kind="ExternalOutput").ap()

    def dscr(name, shape, dt=F32):
        return nc.dram_tensor(name, list(shape), dt, kind=("ExternalOutput" if dbg else "Internal")).ap()

    I = {}
    if stage == 1:
        I["x"] = din("x", [T, D])
    else:
        I["xT_in"] = din("xT_in", [128, DC, T])
        I["modA"] = din("modA", [128, 72])
        I["pvA"] = din("pvA", [128, NPV])
        for n, s in (("w_inA", [D, 4096]), ("w_gateA", [D, 3072]), ("w_brA", [1536, D]), ("w_outA", [D, D]),
                     ("w_fiA", [D, 2 * DFF]), ("w_foA", [DFF, D]), ("lamA", [1, 256]), ("subgA", [1, 128]),
                     ("dbias", [128, 8, 3, 256]), ("fbias", [128, 4, LW]), ("ut_p", [128, 4, 32])):
            I[n] = din(n, s)
        for n in ("kaT_o", "kbT_o", "kaT_p", "kbT_p"):
            I[n] = din(n, [128, 4, T], BF16)
        for n in ("va_o", "vb_o", "va_p", "vb_p"):
            I[n] = din(n, [T, 512], BF16)
    if lb is not None:
        I["pvB"] = din("pvB", [128, NPV])
        for n, s in (("w_adaB", [D, 9 * D]), ("w_fiB", [D, 2 * DFF]), ("w_foB", [DFF, D]), ("w_inB", [D, 4096])):
            I[n] = din(n, s)
    O = {}
    if stage < 3:
        O["xT_out"] = dout("xT_out", [128, DC, T])
        O["modB"] = dout("modB", [128, 72])
        O["kaT"] = dout("kaT", [128, 4, T], BF16)
        O["kbT"] = dout("kbT", [128, 4, T], BF16)
        O["va"] = dout("va", [T, 512], BF16)
        O["vb"] = dout("vb", [T, 512], BF16)
        O["ut"] = dout("ut", [128, 4, 32])
    else:
        O["out"] = dout("out", [T, D])
    S = {}
    if la is not None:
        S["h2T"] = dscr("h2T_s", [128, DC, T], BF16)
        S["qaT"] = dscr("qaT_s", [128, 4, T], BF16)
        S["qbT"] = dscr("qbT_s", [128, 4, T], BF16)
        S["uT"] = dscr("uT_s", [128, 4, 32 + T])
        S["yaT"] = dscr("yaT_s", [512, T], BF16)
        S["ybT"] = dscr("ybT_s", [512, T], BF16)
        S["ycT"] = dscr("ycT_s", [128, 4, T], BF16)
        S["vaf"] = dscr("vaf_s", [2 * T, 512], BF16)
        S["vbf"] = dscr("vbf_s", [2 * T, 512], BF16)

    with contextlib.ExitStack() as st:
        def sb(name, shape, dt):
            return st.enter_context(nc.sbuf_tensor(name, shape, dt))
        xT = sb("xT", [128, DC, T], F32)
        identF = sb("identF", [128, 128], F32)
        onesF = sb("onesF", [128, 128], F32)
        identB = sb("identB", [128, 128], BF16)
        onesB = sb("onesB", [128, 128], BF16)
        blkB = sb("blkB", [128, 128], BF16)
        pvA = sb("pvA_t", [128, NPV], F32)
        pvB = sb("pvB_t", [128, NPV], F32)
        modA = sb("modA_t", [128, 72], F32)
        modB = sb("modB_t", [128, 72], F32)
        sclA = sb("sclA", [128, 3, 8], F32)
        gatA = sb("gatA", [128, 3, 8], F32)
        sclB = sb("sclB", [128, 3, 8], F32)
        gatB = sb("gatB", [128, 3, 8], F32)
        neglam = sb("neglam", [128, 4], F32)
        NA = 33600
        arena_t = sb("arena", [128, NA], F32)
        A = Arena(arena_t[:, :], NA)
        ps = [st.enter_context(nc.psum_tensor("ps%d" % i, [128, 512], F32)) for i in range(7)]
        psb = st.enter_context(nc.psum_tensor("psb", [128, 1024], BF16))
        P = Prog(nc)
        rr = [0]

        def pb(lo=0, hi=7):
            i = lo + rr[0] % (hi - lo)
            rr[0] += 1
            return ps[i], ("ps", i)

        P.op("pool", lambda e: e.memset(identF[:], 0.0), writes=["identF"])
        P.op("pool", lambda e: e.affine_select(out=identF[:], in_=identF[:], pattern=[[-1, 128]],
                                                compare_op=ALU.not_equal, fill=1.0, base=0, channel_multiplier=1),
             reads=["identF"], writes=["identF"])
        P.op("pool", lambda e: e.memset(onesF[:], 1.0), writes=["onesF"])
        P.op("pool", lambda e: e.memset(onesB[:], 1.0), writes=["onesB"])
        P.op("pool", lambda e: e.memset(blkB[:], 0.0), writes=["blkB"])
        P.op("pool", lambda e: e.memset(blkB[0:64, 0:64], 1.0), reads=["blkB"], writes=["blkB"])
        P.op("pool", lambda e: e.memset(blkB[64:128, 64:128], 1.0), reads=["blkB"], writes=["blkB"])
        P.op("dve", lambda e: e.tensor_copy(out=identB[:], in_=identF[:]), reads=["identF"], writes=["identB"])
        if la is not None:
            P.dma("sp", lambda e: e.dma_start(out=pvA[:], in_=I["pvA"][:, :]), writes=["pvA"])
            P.dma("sp", lambda e: e.dma_start(out=modA[:], in_=I["modA"][:, :]), writes=["modA"])
        if lb is not None:
            P.dma("sp", lambda e: e.dma_start(out=pvB[:], in_=I["pvB"][:, :]), writes=["pvB"])

        def derive(pv, pvk, mod, modk, scl, gat, tag):
            for n in range(3):
                P.op("dve", lambda e, n=n: e.scalar_tensor_tensor(
                    out=scl[:, n, :], in0=mod[:, (3 * n + 1) * 8:(3 * n + 2) * 8], scalar=1.0,
                    in1=pv[:, PV_NG + n * 8:PV_NG + n * 8 + 8], op0=ALU.add, op1=ALU.mult),
                    reads=[modk, pvk], writes=["scl" + tag])
                P.op("dve", lambda e, n=n: e.tensor_scalar(
                    out=gat[:, n, :], in0=mod[:, (3 * n + 2) * 8:(3 * n + 3) * 8],
                    scalar1=(1.0 if n == 1 else 0.5), scalar2=None, op0=ALU.mult),
                    reads=[modk], writes=["gat" + tag])

        def fresh(mark=0):
            P.barrier()
            A.off = mark

        def ph_load_x():
            A.reset()
            xin = [A.f32(1024) for _ in range(2)]
            for i in range(16):
                s = i % 2
                P.dma("sp", lambda e, i=i, s=s: e.dma_start(out=xin[s], in_=I["x"][i * 128:(i + 1) * 128, :]),
                      writes=[("xin", s)])
                for hb in range(2):
                    bk, bkey = pb()

                    def tr(e, s=s, hb=hb, bk=bk):
                        for c4 in range(4):
                            c = hb * 4 + c4
                            ins = e.transpose(out=bk[:, c4 * 128:(c4 + 1) * 128],
                                              in_=xin[s][:, c * 128:(c + 1) * 128], identity=identF[:])
                        return ins
                    P.op("pe", tr, reads=[("xin", s), "identF"], writes=[bkey])
                    dst = xT[:, hb * 4:(hb + 1) * 4, i * 128:(i + 1) * 128]
                    if hb == 0:
                        P.op("dve", lambda e, dst=dst, bk=bk: e.tensor_copy(out=dst, in_=_rs(bk[:, :], [4, 128])),
                             reads=[bkey], writes=[("xT", i // 4, hb)])
                    else:
                        P.op("act", lambda e, dst=dst, bk=bk: e.activation(out=dst, in_=_rs(bk[:, :], [4, 128]),
                                                                          func=AF.Identity),
                             reads=[bkey], writes=[("xT", i // 4, hb)])

        def xkeys(tt):
            return [("xT", tt, 0), ("xT", tt, 1)]

        def ph_load_xT():
            for c in range(DC):
                P.dma("sp", lambda e, c=c: e.dma_start(out=xT[:, c, :], in_=I["xT_in"][:, c, :]),
                      writes=[("xT", tt, hb) for tt in range(NT) for hb in range(2)])

        def ph_adaln(pv, pvk, w_ada, mod, modk):
            A.reset()
            P.barrier()
            csb = A.bf(8)
            wa = [A.bf(8, 1024) for _ in range(2)]
            P.op("act", lambda e: e.activation(out=csb, in_=pv[:, PV_CS:PV_CS + 8], func=AF.Silu),
                 reads=[pvk], writes=["csb"])
            bk, bkey = ps[6], ("ps", 6)
            for j in range(9):
                s = j % 2
                P.dma("pool", lambda e, j=j, s=s: e.dma_start(
                    out=wa[s], in_=w_ada[:, j * 1024:(j + 1) * 1024].rearrange("(c p) n -> p c n", p=128)),
                    writes=[("wa", s)])

                def mm(e, j=j, s=s):
                    for cb in range(8):
                        for kc in range(8):
                            ins = e.matmul(bk[:, j * 8 + cb:j * 8 + cb + 1], lhsT=wa[s][:, kc, cb * 128:(cb + 1) * 128],
                                           rhs=csb[:, kc:kc + 1], start=(kc == 0), stop=(kc == 7))
                    return ins
                P.op("pe", mm, reads=[("wa", s), "csb"], writes=[bkey])
            P.op("dve", lambda e: e.tensor_tensor(out=mod[:, :], in0=bk[:, 0:72], in1=pv[:, PV_BADA:PV_BADA + 72],
                                                  op=ALU.add), reads=[bkey, pvk], writes=[modk])

        def ph_norm(n, pv, pvk, mod, modk, scl, sclk, hT):
            sqb = A.bf(8, TT)
            rs = [A.f32(TT) for _ in range(2)]
            tmpf = [A.f32(TT) for _ in range(2)]
            k = 0
            for tt in range(NT):
                sl = slice(tt * TT, (tt + 1) * TT)
                P.op("act", lambda e, sl=sl: e.activation(out=sqb, in_=xT[:, :, sl], func=AF.Square),
                     reads=xkeys(tt), writes=["sqb"])
                bk, bkey = pb()

                def mm(e, bk=bk):
                    for c in range(DC):
                        ins = e.matmul(bk[:, :], lhsT=onesB[:], rhs=sqb[:, c, :], start=(c == 0), stop=(c == DC - 1))
                    return ins
                P.op("pe", mm, reads=["sqb", "onesB"], writes=[bkey])
                r = rs[tt % 2]
                rk = ("rs", tt % 2)
                P.op("act", lambda e, r=r, bk=bk: e.activation(out=r, in_=bk[:, :], func=AF.Ln, scale=1.0 / D, bias=EPS),
                     reads=[bkey], writes=[rk])
                P.op("act", lambda e, r=r: e.activation(out=r, in_=r, func=AF.Exp, scale=-0.5), reads=[rk], writes=[rk])
                for c in range(DC):
                    tf = tmpf[k % 2]
                    tk = ("tmpf", k % 2)
                    k += 1
                    P.op("dve", lambda e, c=c, sl=sl, tf=tf, r=r: e.tensor_tensor(out=tf, in0=xT[:, c, sl], in1=r, op=ALU.mult),
                         reads=xkeys(tt) + [rk], writes=[tk])
                    P.op("act", lambda e, c=c, sl=sl, tf=tf: e.activation(
                        out=hT[:, c, sl], in_=tf, func=AF.Identity, scale=scl[:, n, c:c + 1],
                        bias=mod[:, 3 * n * 8 + c:3 * n * 8 + c + 1]),
                        reads=[tk, sclk, modk], writes=[("h", tt)])

        def ph_ffn(w_up, w_dn, gat, gatk, n, hT):
            groups = [(0, 6), (6, 6), (12, 6), (18, 4)]
            actT = A.bf(6, T)
            wup = [A.bf(2, 8, 256) for _ in range(2)]
            wdn = [A.bf(6, D) for _ in range(2)]
            sgf = [A.f32(TT) for _ in range(2)]
            k = 0
            npair = 0
            for gi, (c0, gn) in enumerate(groups):
                ws = gi % 2
                P.dma("pool", lambda e, c0=c0, gn=gn, ws=ws: e.dma_start(
                    out=wdn[ws][:, 0:gn, :], in_=w_dn[c0 * 128:(c0 + gn) * 128, :].rearrange("(i p) n -> p i n", p=128)),
                    writes=[("wdn", ws)])
                for pi in range(gn // 2):
                    cpair = c0 + 2 * pi
                    us = npair % 2
                    npair += 1
                    for gu in range(2):
                        col = gu * DFF + cpair * 128
                        P.dma("pool", lambda e, us=us, gu=gu, col=col: e.dma_start(
                            out=wup[us][:, gu, :, :], in_=w_up[:, col:col + 256].rearrange("(c p) n -> p c n", p=128)),
                            writes=[("wup", us, gu)])
                    for ci in range(2):
                        il = 2 * pi + ci
                        for tt in range(NT):
                            sl = slice(tt * TT, (tt + 1) * TT)
                            bg, bgk = pb()
                            bu, buk = pb()

                            def mmg(e, us=us, ci=ci, sl=sl, bg=bg, gu=0):
                                for kc in range(DC):
                                    ins = e.matmul(bg[:, :], lhsT=wup[us][:, gu, kc, ci * 128:(ci + 1) * 128], rhs=hT[:, kc, sl],
                                                   start=(kc == 0), stop=(kc == DC - 1))
                                return ins
                            P.op("pe", mmg, reads=[("wup", us, 0), ("h", tt)], writes=[bgk])
                            P.op("pe", lambda e, us=us, ci=ci, sl=sl, bu=bu: mmg(e, us, ci, sl, bu, 1),
                                 reads=[("wup", us, 1), ("h", tt)], writes=[buk])
                            sg = sgf[k % 2]
                            sk = ("sgf", k % 2)
                            k += 1
                            P.op("act", lambda e, sg=sg, bg=bg: e.activation(out=sg, in_=bg[:, :], func=AF.Silu),
                                 reads=[bgk], writes=[sk])
                            P.op("dve", lambda e, sg=sg, bu=bu, il=il, sl=sl: e.tensor_tensor(
                                out=actT[:, il, sl], in0=bu[:, :], in1=sg, op=ALU.mult),
                                reads=[buk, sk], writes=[("actT", tt)])
                for dc in range(DC):
                    for tt in range(NT):
                        sl = slice(tt * TT, (tt + 1) * TT)
                        bd, bdk = pb()

                        def mmd(e, dc=dc, sl=sl, bd=bd, gn=gn, ws=ws):
                            for i in range(gn):
                                ins = e.matmul(bd[:, :], lhsT=wdn[ws][:, i, dc * 128:(dc + 1) * 128], rhs=actT[:, i, sl],
                                               start=(i == 0), stop=(i == gn - 1))
                            return ins
                        P.op("pe", mmd, reads=[("wdn", ws), ("actT", tt)], writes=[bdk])
                        P.op("dve", lambda e, dc=dc, sl=sl, bd=bd: e.scalar_tensor_tensor(
                            out=xT[:, dc, sl], in0=bd[:, :], scalar=gat[:, n, dc:dc + 1], in1=xT[:, dc, sl],
                            op0=ALU.mult, op1=ALU.add),
                            reads=[bdk, gatk, ("xT", tt, dc // 4)], writes=[("xT", tt, dc // 4)])

        def preload_w(w_in, colbase, ncols, key):
            wt = A.bf(8, ncols)
            P.dma("pool", lambda e: e.dma_start(out=wt, in_=w_in[:, colbase:colbase + ncols].rearrange("(c p) n -> p c n", p=128)),
                  writes=[key])
            return wt, key

        def proj_norm(w_in, colbase, pv, pvk, gaincol, hT, dst, pre=None):
            wq, wqk = pre if pre is not None else (A.bf(8, 512), "wq")
            kst = [A.bf(T) for _ in range(2)]
            sq = [A.bf(TT) for _ in range(2)]
            qf = [A.f32(TT) for _ in range(2)]
            rs = [A.f32(TT) for _ in range(2)]
            if pre is None:
                P.dma("pool", lambda e: e.dma_start(out=wq, in_=w_in[:, colbase:colbase + 512].rearrange("(c p) n -> p c n", p=128)),
                      writes=["wq"])
            k = 0
            pend = []
            for ch in range(4):
                ks = kst[ch % 2]
                kk = ("kst", ch % 2)
                for tt in range(NT):
                    sl = slice(tt * TT, (tt + 1) * TT)
                    u = k % 2
                    k += 1
                    bq, bqk = pb()

                    def mm(e, ch=ch, sl=sl, bq=bq):
                        for kc in range(DC):
                            ins = e.matmul(bq[:, :], lhsT=wq[:, kc, ch * 128:(ch + 1) * 128], rhs=hT[:, kc, sl],
                                           start=(kc == 0), stop=(kc == DC - 1))
                        return ins
                    P.op("pe", mm, reads=[wqk, ("h", tt)], writes=[bqk])
                    P.op("act", lambda e, u=u, bq=bq: e.activation(out=sq[u], in_=bq[:, :], func=AF.Square),
                         reads=[bqk], writes=[("sq", u)])
                    P.op("dve", lambda e, u=u, bq=bq: e.tensor_copy(out=qf[u], in_=bq[:, :]), reads=[bqk], writes=[("qf", u)])

                    def back(u=u, ks=ks, kk=kk, sl=sl, ch=ch, tt=tt):
                        bs, bsk = pb()
                        P.op("pe", lambda e: e.matmul(bs[:, :], lhsT=blkB[:], rhs=sq[u], start=True, stop=True),
                             reads=[("sq", u), "blkB"], writes=[bsk])
                        P.op("act", lambda e: e.activation(out=rs[u], in_=bs[:, :], func=AF.Ln, scale=1.0 / 64, bias=EPS),
                             reads=[bsk], writes=[("prs", u)])
                        P.op("act", lambda e: e.activation(out=rs[u], in_=rs[u], func=AF.Exp, scale=-0.5),
                             reads=[("prs", u)], writes=[("prs", u)])
                        P.op("dve", lambda e: e.scalar_tensor_tensor(
                            out=ks[:, sl], in0=qf[u], scalar=pv[:, PV_GAIN + gaincol:PV_GAIN + gaincol + 1], in1=rs[u],
                            op0=ALU.mult, op1=ALU.mult), reads=[("qf", u), ("prs", u), pvk], writes=[kk])
                        if tt == NT - 1:
                            P.dma("sp", lambda e: e.dma_start(out=dst(ch), in_=ks), reads=[kk], writes=[("dst", colbase, ch)])
                    pend.append(back)
                    while len(pend) > 1:
                        pend.pop(0)()
            while pend:
                pend.pop(0)()

        def v_proj(w_in, colbase, hT, dst, pre=None):
            wv, wvk = pre if pre is not None else (A.bf(8, 512), "wv")
            vst = [A.bf(512) for _ in range(2)]
            if pre is None:
                P.dma("pool", lambda e: e.dma_start(out=wv, in_=w_in[:, colbase:colbase + 512].rearrange("(c p) n -> p c n", p=128)),
                      writes=["wv"])
            for i in range(16):
                u = i % 2
                bv, bvk = pb()

                def mm(e, i=i, bv=bv):
                    for kc in range(DC):
                        ins = e.matmul(bv[:, :], lhsT=hT[:, kc, i * 128:(i + 1) * 128], rhs=wv[:, kc, :],
                                       start=(kc == 0), stop=(kc == DC - 1))
                    return ins
                P.op("pe", mm, reads=[wvk, ("h", i // 4)], writes=[bvk])
                P.op("act", lambda e, u=u, bv=bv: e.activation(out=vst[u], in_=bv[:, :], func=AF.Identity),
                     reads=[bvk], writes=[("vst", u)])
                P.dma("sp", lambda e, i=i, u=u: e.dma_start(out=dst[i * 128:(i + 1) * 128, :], in_=vst[u]),
                      reads=[("vst", u)], writes=[("vdst", colbase, i)])

        def glu(w_in, hT, tts, sink, pre=None):
            wu, wuk = pre if pre is not None else (A.bf(8, 1024), "wu")
            sgf = [A.f32(TT) for _ in range(2)]
            uf = [A.f32(TT) for _ in range(2)]
            if pre is None:
                P.dma("pool", lambda e: e.dma_start(out=wu, in_=w_in[:, 3072:4096].rearrange("(c p) n -> p c n", p=128)),
                      writes=["wu"])
            k = 0
            for ch in range(4):
                for tt in tts:
                    sl = slice(tt * TT, (tt + 1) * TT)
                    u = k % 2
                    k += 1
                    b1, b1k = pb()
                    b2, b2k = pb()

                    def mm(e, col, bk, sl=sl):
                        for kc in range(DC):
                            ins = e.matmul(bk[:, :], lhsT=wu[:, kc, col:col + 128], rhs=hT[:, kc, sl],
                                           start=(kc == 0), stop=(kc == DC - 1))
                        return ins
                    P.op("pe", lambda e, ch=ch, b1=b1, mm=mm: mm(e, ch * 128, b1), reads=[wuk, ("h", tt)], writes=[b1k])
                    P.op("pe", lambda e, ch=ch, b2=b2, mm=mm: mm(e, 512 + ch * 128, b2), reads=[wuk, ("h", tt)], writes=[b2k])
                    P.op("act", lambda e, u=u, b2=b2: e.activation(out=sgf[u], in_=b2[:, :], func=AF.Sigmoid),
                         reads=[b2k], writes=[("gsg", u)])
                    P.op("dve", lambda e, u=u, b1=b1: e.tensor_tensor(out=uf[u], in0=b1[:, :], in1=sgf[u], op=ALU.mult),
                         reads=[b1k, ("gsg", u)], writes=[("guf", u)])
                    sink(ch, tt, uf[u], ("guf", u))

        def ph_kv(pv, pvk, mod, modk, scl, sclk, w_in):
            A.reset()
            P.barrier()
            hT = A.bf(8, T)
            p_ka = preload_w(w_in, 512, 512, "w_ka")
            p_kb = preload_w(w_in, 2048, 512, "w_kb")
            p_va = preload_w(w_in, 1024, 512, "w_va")
            p_vb = preload_w(w_in, 2560, 512, "w_vb")
            p_u = preload_w(w_in, 3072, 1024, "w_u")
            ph_norm(1, pv, pvk, mod, modk, scl, sclk, hT)
            mark = A.off
            proj_norm(w_in, 512, pv, pvk, 1, hT, lambda ch: O["kaT"][:, ch, :], pre=p_ka)
            fresh(mark)
            proj_norm(w_in, 2048, pv, pvk, 3, hT, lambda ch: O["kbT"][:, ch, :], pre=p_kb)
            fresh(mark)
            v_proj(w_in, 1024, hT, O["va"], pre=p_va)
            fresh(mark)
            v_proj(w_in, 2560, hT, O["vb"], pre=p_vb)
            fresh(mark)

            def sink(ch, tt, uf, key):
                P.dma("sp", lambda e, ch=ch, uf=uf: e.dma_start(out=O["ut"][:, ch, :], in_=uf[:, TT - 32:TT]),
                      reads=[key], writes=[("utout", ch)])
            glu(w_in, hT, [NT - 1], sink, pre=p_u)

        def ph_store_x():
            for c in range(DC):
                P.dma("sp", lambda e, c=c: e.dma_start(out=O["xT_out"][:, c, :], in_=xT[:, c, :]),
                      reads=[("xT", tt, c // 4) for tt in range(NT)], writes=[("xout", c)])

        def ph_ffn_full(n, pv, pvk, mod, modk, scl, sclk, gat, gatk, w_up, w_dn):
            fresh()
            hT = A.bf(8, T)
            mark = A.off
            ph_norm(n, pv, pvk, mod, modk, scl, sclk, hT)
            ph_ffn(w_up, w_dn, gat, gatk, n, hT)

        def ph_qu(pv, pvk, mod, modk, scl, sclk, w_in):
            fresh()
            hT = A.bf(8, T)
            p_qa = preload_w(w_in, 0, 512, "w_qa")
            p_qb = preload_w(w_in, 1536, 512, "w_qb")
            p_u = preload_w(w_in, 3072, 1024, "w_u")
            ph_norm(1, pv, pvk, mod, modk, scl, sclk, hT)
            for c in range(DC):
                P.dma("sp", lambda e, c=c: e.dma_start(out=S["h2T"][:, c, :], in_=hT[:, c, :]),
                      reads=[("h", tt) for tt in range(NT)], writes=[("h2Ts", c)])
            mark = A.off
            proj_norm(w_in, 0, pv, pvk, 0, hT, lambda ch: S["qaT"][:, ch, :], pre=p_qa)
            fresh(mark)
            proj_norm(w_in, 1536, pv, pvk, 2, hT, lambda ch: S["qbT"][:, ch, :], pre=p_qb)
            fresh(mark)
            utp = A.f32(4, 32)
            P.dma("sp", lambda e: e.dma_start(out=utp, in_=I["ut_p"][:, :, :]), writes=["utp"])
            P.op("dve", lambda e: e.tensor_scalar(out=utp, in0=utp, scalar1=pv[:, PV_PF:PV_PF + 1], scalar2=None, op0=ALU.mult),
                 reads=["utp", pvk], writes=["utp"])
            P.dma("sp", lambda e: e.dma_start(out=S["uT"][:, :, 0:32], in_=utp), reads=["utp"], writes=["uTs"])

            def sink(ch, tt, uf, key):
                P.dma("sp", lambda e, ch=ch, tt=tt, uf=uf: e.dma_start(out=S["uT"][:, ch, 32 + tt * TT:32 + (tt + 1) * TT], in_=uf),
                      reads=[key], writes=[("uTs", ch, tt)])
            glu(w_in, hT, list(range(NT)), sink, pre=p_u)
            for nm, src_p, src_o in (("vaf", "va_p", "va_o"), ("vbf", "vb_p", "vb_o")):
                P.dma("sp", lambda e, nm=nm, src_p=src_p: e.dma_start(out=S[nm][0:T, :], in_=I[src_p][:, :]), writes=[(nm, 0)])
                P.dma("sp", lambda e, nm=nm, src_o=src_o: e.dma_start(out=S[nm][T:2 * T, :], in_=I[src_o][:, :]), writes=[(nm, 1)])

        def ph_dil(pv, pvk):
            fresh()
            qz2 = [[A.bf(T) for _ in range(2)] for _ in range(2)]
            kh2 = [A.bf(2 * T) for _ in range(2)]
            vbuf = [A.bf(32, 256) for _ in range(2)]
            pT = [A.bf(2, 256) for _ in range(2)]
            tmp = [A.f32(2, 256) for _ in range(2)]
            acc = A.f32(2, T)
            db2 = [A.f32(2, 3, 256) for _ in range(2)]
            dbb2 = [A.bf(2, 3, 256) for _ in range(2)]
            rlow = A.f32(T)
            yn = A.bf(T)
            for s in range(2):
                P.op("pool", lambda e, s=s: e.memset(vbuf[s], 1.0), writes=[("vbuf", s, r) for r in range(16)])
                for c_ in range(2):
                    P.op("pool", lambda e, s=s, c_=c_: e.memset(qz2[s][c_], 0.0), writes=[("qh", s)])
            unit = 0
            allacc = [("acc", b) for b in range(16)]
            def load_hp(hp):
                b_ = hp % 2
                for c_ in range(2):
                    P.dma("sp", lambda e, c_=c_: e.dma_start(out=qz2[b_][c_][c_ * 64:(c_ + 1) * 64, :],
                                                             in_=S["qaT"][c_ * 64:(c_ + 1) * 64, hp, :]), writes=[("qh", b_)])
                P.dma("sp", lambda e: e.dma_start(out=kh2[b_][:, 0:T], in_=I["kaT_p"][:, hp, :]), writes=[("kh", b_, 0)])
                P.dma("sp", lambda e: e.dma_start(out=kh2[b_][:, T:2 * T], in_=I["kaT_o"][:, hp, :]), writes=[("kh", b_, 1)])
                P.dma("sp", lambda e: e.dma_start(out=db2[b_], in_=I["dbias"][:, 2 * hp:2 * hp + 2, :, :]), writes=[("db", b_)])
                P.op("dve", lambda e: e.tensor_scalar(out=dbb2[b_], in0=db2[b_], scalar1=8.0, scalar2=None, op0=ALU.mult),
                     reads=[("db", b_)], writes=[("dbb", b_)])
            load_hp(0)
            for hp in range(4):
                if hp + 1 < 4:
                    load_hp(hp + 1)
                hb_ = hp % 2
                qz, kh, dbb = qz2[hb_], kh2[hb_], dbb2[hb_]
                pend = []

                def flush(n):
                    while len(pend) > n:
                        pend.pop(0)()
                for p, (win, d) in enumerate(DIL):
                    vs = (hp * 3 + p) % 2
                    vb = vbuf[vs]
                    nb = 16 // d
                    vview = S["vaf"].rearrange("(i d) c -> i d c", d=d)
                    i0_ = T // d - 128
                    for r in range(d):
                        for h in range(2):
                            src = vview[i0_:i0_ + 128 * (nb + 1), r, hp * 128 + h * 64:hp * 128 + h * 64 + 64].rearrange(
                                "(j p) c -> p j c", p=128)
                            P.dma("sp", lambda e, vb=vb, r=r, h=h, nb=nb, src=src: e.dma_start(
                                out=vb[:, r * (nb + 1):(r + 1) * (nb + 1), h * 128:h * 128 + 64], in_=src),
                                reads=[("vaf", 0), ("vaf", 1)], writes=[("vbuf", vs, r)])
                    khv = _rs(kh, [2 * T // d, d])
                    qhv = [_rs(qz[0], [T // d, d]), _rs(qz[1], [T // d, d])]
                    for r in range(d):
                        for m in range(nb):
                            u = unit % 2
                            unit += 1
                            bS = [pb(), pb()]
                            bO, bOk = pb()

                            def st_(e, r=r, m=m, d=d, bS=bS, khv=khv, qhv=qhv, dbb=dbb, p=p):
                                for h in range(2):
                                    for jj in range(2):
                                        ki = T // d + 128 * (m - 1 + jj)
                                        ins = e.matmul(bS[h][0][:, jj * 128:(jj + 1) * 128],
                                                       lhsT=khv[:, ki:ki + 128, r],
                                                       rhs=qhv[h][:, 128 * m:128 * m + 128, r],
                                                       start=(jj == 0), stop=False, skip_group_check=True)
                                for h in range(2):
                                    for jj in range(2):
                                        ins = e.matmul(bS[h][0][:, jj * 128:(jj + 1) * 128], lhsT=identB[:],
                                                       rhs=dbb[:, h, p, jj * 128:(jj + 1) * 128],
                                                       start=False, stop=True, skip_group_check=True)
                                return ins
                            P.op("pe", st_, reads=[("qh", hb_), ("kh", hb_, 0), ("kh", hb_, 1), ("dbb", hb_), "identB"],
                                 writes=[bS[0][1], bS[1][1]])
                            for h in range(2):
                                if m == 0:
                                    P.op("act", lambda e, h=h, u=u, bS=bS: e.activation(
                                        out=pT[u][:, h, 0:128], in_=bS[h][0][:, 0:128], func=AF.Exp, scale=0.125, bias=pv[:, PV_PB:PV_PB + 1]),
                                        reads=[bS[h][1], pvk], writes=[("dpT", u, h)])
                                    P.op("act", lambda e, h=h, u=u, bS=bS: e.activation(
                                        out=pT[u][:, h, 128:256], in_=bS[h][0][:, 128:256], func=AF.Exp, scale=0.125),
                                        reads=[bS[h][1]], writes=[("dpT", u, h)])
                                else:
                                    P.op("act", lambda e, h=h, u=u, bS=bS: e.activation(out=pT[u][:, h, :], in_=bS[h][0][:, 0:256], func=AF.Exp, scale=0.125),
                                         reads=[bS[h][1]], writes=[("dpT", u, h)])

                            def back(r=r, m=m, nb=nb, u=u, vb=vb, vs=vs, bO=bO, bOk=bOk, d=d, p=p):
                                def pv_(e):
                                    for h in range(2):
                                        for jj in range(2):
                                            ins = e.matmul(bO[:, h * 128:(h + 1) * 128],
                                                           lhsT=vb[:, r * (nb + 1) + m + jj, h * 128:(h + 1) * 128],
                                                           rhs=pT[u][:, h, jj * 128:(jj + 1) * 128], start=(jj == 0), stop=(jj == 1))
                                    return ins
                                P.op("pe", pv_, reads=[("dpT", u, 0), ("dpT", u, 1), ("vbuf", vs, r)], writes=[bOk])
                                av = acc.rearrange("p h (i d) -> p h i d", d=d)[:, :, 128 * m:128 * m + 128, r]
                                ak = [("acc", b) for b in range(d * m, d * (m + 1))]
                                if p == 0:
                                    P.op("dve", lambda e: e.tensor_copy(out=av, in_=_rs(bO[:, 0:256], [2, 128])),
                                         reads=[bOk], writes=ak)
                                else:
                                    P.op("dve", lambda e: e.tensor_tensor(out=av, in0=_rs(bO[:, 0:256], [2, 128]), in1=av, op=ALU.add),
                                         reads=[bOk] + ak, writes=ak)
                            pend.append(back)
                            flush(1)
                flush(0)
                for h in range(2):
                    P.op("dve", lambda e, h=h: e.reciprocal(out=acc[64:128, h, :], in_=acc[64:128, h, :]), reads=allacc, writes=allacc)
                    P.op("dve", lambda e, h=h: e.tensor_copy(out=rlow[0:64, :], in_=acc[64:128, h, :]), reads=allacc, writes=["rlow"])
                    P.op("dve", lambda e, h=h: e.tensor_tensor(out=yn[0:64, :], in0=acc[0:64, h, :], in1=rlow[0:64, :], op=ALU.mult),
                         reads=allacc + ["rlow"], writes=["yn"])
                    P.dma("sp", lambda e, hp=hp, h=h: e.dma_start(out=S["yaT"][(hp * 2 + h) * 64:(hp * 2 + h + 1) * 64, :], in_=yn[0:64, :]),
                          reads=["yn"], writes=[("yaTs", hp, h)])

        def ph_lambda(l):
            import math
            lam_init = 0.8 - 0.6 * math.exp(-0.3 * l)
            fresh()
            lv = A.f32(256)
            t = A.f32(128)
            s2 = A.f32(2)
            P.dma("sp", lambda e: e.dma_start(out=lv, in_=I["lamA"][0:1, :].partition_broadcast(128)), writes=["lv"])
            P.op("dve", lambda e: e.tensor_tensor(out=_rs(t, [2, 64]), in0=_rs(lv, [2, 2, 64])[:, :, 0, :],
                                                  in1=_rs(lv, [2, 2, 64])[:, :, 1, :], op=ALU.mult), reads=["lv"], writes=["lvt"])
            P.op("dve", lambda e: e.tensor_reduce(out=s2, in_=_rs(t, [2, 64]), axis=mybir.AxisListType.X, op=ALU.add),
                 reads=["lvt"], writes=["lvs"])
            P.op("act", lambda e: e.activation(out=s2, in_=s2, func=AF.Exp), reads=["lvs"], writes=["lvs"])
            P.op("dve", lambda e: e.tensor_tensor(out=neglam[:, 1:2], in0=s2[:, 1:2], in1=s2[:, 0:1], op=ALU.subtract),
                 reads=["lvs"], writes=["neglam1"])
            P.op("dve", lambda e: e.tensor_scalar(out=neglam[:, 0:1], in0=neglam[:, 1:2], scalar1=-lam_init, scalar2=None, op0=ALU.add),
                 reads=["neglam1"], writes=["neglam"])
            return lam_init

        def ph_diff(pv, pvk, lam_init):
            fresh()
            qz2 = [[A.bf(T) for _ in range(2)] for _ in range(2)]
            kh2 = [A.bf(2 * T) for _ in range(2)]
            vaug2 = [A.bf(32, 130) for _ in range(2)]
            W2 = [A.f32(LW) for _ in range(2)]
            NS = 4
            tmp = [A.f32(TT) for _ in range(NS)]
            pT = [A.bf(TT) for _ in range(NS)]
            sbank = [(ps[0], ("ps", 0)), (ps[1], ("ps", 1)), (ps[6], ("ps", 6)), (ps[5], ("ps", 5))]

            def aslot(c, qb):
                a = c * 4 + qb
                return ps[2 + a // 3], ("ps", 2 + a // 3), (a % 3) * 160
            o1 = A.f32(128)
            of = A.f32(128)
            sm = A.f32(8)
            ybt = [A.bf(128) for _ in range(4)]
            ybst = A.bf(TT)
            gsub = A.f32(128)
            for b_ in range(2):
                P.op("pool", lambda e, b_=b_: e.memset(vaug2[b_], 1.0), writes=[("vaug", b_)])
                for c_ in range(2):
                    P.op("pool", lambda e, b_=b_, c_=c_: e.memset(qz2[b_][c_], 0.0), writes=[("qh", b_)])
            P.dma("sp", lambda e: e.dma_start(out=gsub, in_=I["subgA"][0:1, :].partition_broadcast(128)), writes=["gsub"])

            def load_head(h):
                b_ = h % 2
                for c_ in range(2):
                    P.dma("sp", lambda e, c_=c_: e.dma_start(out=qz2[b_][c_][c_ * 64:(c_ + 1) * 64, :],
                                                             in_=S["qbT"][c_ * 64:(c_ + 1) * 64, h, :]), writes=[("qh", b_)])
                P.dma("sp", lambda e: e.dma_start(out=kh2[b_][:, 0:T], in_=I["kbT_p"][:, h, :]), writes=[("kh", b_, 0)])
                P.dma("sp", lambda e: e.dma_start(out=kh2[b_][:, T:2 * T], in_=I["kbT_o"][:, h, :]), writes=[("kh", b_, 1)])
                P.dma("sp", lambda e: e.dma_start(
                    out=vaug2[b_][:, :, 0:128], in_=S["vbf"][:, h * 128:(h + 1) * 128].rearrange("(j p) c -> p j c", p=128)),
                    reads=[("vbf", 0), ("vbf", 1)], writes=[("vaug", b_)])
                P.dma("sp", lambda e: e.dma_start(out=W2[b_], in_=I["fbias"][:, h, :]), writes=[("W", b_)])
            P.op("dve", lambda e: e.tensor_scalar(out=gsub, in0=gsub, scalar1=(1.0 - lam_init), scalar2=None, op0=ALU.mult),
                 reads=["gsub"], writes=["gsub"])
            cnt = 0
            load_head(0)
            for h in range(4):
                if h + 1 < 4:
                    load_head(h + 1)
                hb_ = h % 2
                qz, kh, vaug, W = qz2[hb_], kh2[hb_], vaug2[hb_], W2[hb_]
                pend = []
                finb = []

                def flush(n):
                    while len(pend) > n:
                        pend.pop(0)()
                since = 0
                for g in range(4):
                    first = {}
                    for c in range(2):
                        nkb = 16 + 4 * g + 4
                        for kbi in range(nkb):
                            u = cnt % NS
                            cnt += 1
                            bS, bSk = sbank[u]
                            P.op("pe", lambda e, c=c, kbi=kbi, g=g, bS=bS, kh=kh, qz=qz: e.matmul(
                                bS[:, :], lhsT=kh[:, kbi * 128:(kbi + 1) * 128],
                                rhs=qz[c][:, g * TT:(g + 1) * TT], start=True, stop=True),
                                reads=[("qh", hb_), ("kh", hb_, 0), ("kh", hb_, 1)], writes=[bSk])
                            delta = (T + TT * g) - 128 * kbi
                            off = delta + 384 if delta < 1792 else 2176
                            P.op("dve", lambda e, u=u, off=off, bS=bS, W=W: e.scalar_tensor_tensor(
                                out=tmp[u], in0=bS[:, :], scalar=0.125, in1=W[:, off:off + TT], op0=ALU.mult, op1=ALU.add),
                                reads=[bSk, ("W", hb_)], writes=[("ftmp", u)])
                            if kbi < 16:
                                P.op("act", lambda e, u=u: e.activation(out=pT[u], in_=tmp[u], func=AF.Exp, bias=pv[:, PV_PB:PV_PB + 1]),
                                     reads=[("ftmp", u), pvk], writes=[("fpT", u)])
                            else:
                                P.op("act", lambda e, u=u: e.activation(out=pT[u], in_=tmp[u], func=AF.Exp),
                                     reads=[("ftmp", u)], writes=[("fpT", u)])
                            plan = []
                            for qb in range(4):
                                if kbi >= 16 and (4 * g + qb) < (kbi - 16):
                                    continue
                                bkx, bkk, col = aslot(c, qb)
                                stf = first.get(bkk, True)
                                first[bkk] = False
                                last = (kbi == 16 + 4 * g + qb)
                                plan.append((qb, stf, last, bkx, bkk, col))

                            def back(plan=plan, c=c, u=u, kbi=kbi, vaug=vaug, hb_=hb_):
                                def pv_(e):
                                    for qb, stf, last, bkx, bkk, col in plan:
                                        ins = e.matmul(bkx[:, col:col + 129], lhsT=pT[u][:, qb * 128:(qb + 1) * 128],
                                                       rhs=vaug[:, kbi, 0:129], start=stf, stop=last, skip_group_check=True)
                                    return ins
                                P.op("pe", pv_, reads=[("fpT", u), ("vaug", hb_)], writes=sorted(set(x[4] for x in plan)))
                            pend.append(back)
                            flush(NS - 1)
                            since += 1
                            if finb and since >= 3:
                                finb.pop(0)()
                    flush(0)
                    if finb:
                        finb.pop(0)()
                    for qb in range(4):
                        b1, b1k, col = aslot(0, qb)
                        b2, b2k, col2 = aslot(1, qb)
                        yb = ybt[qb]
                        P.op("dve", lambda e, b1=b1, col=col: e.reciprocal(out=sm[:, 0:1], in_=b1[:, col + 128:col + 129]),
                             reads=[b1k], writes=["sm0"])
                        P.op("dve", lambda e, b2=b2, col2=col2: e.reciprocal(out=sm[:, 1:2], in_=b2[:, col2 + 128:col2 + 129]),
                             reads=[b2k], writes=["sm1"])
                        P.op("dve", lambda e: e.tensor_tensor(out=sm[:, 2:3], in0=sm[:, 1:2], in1=neglam[:, 0:1], op=ALU.mult),
                             reads=["sm1", "neglam"], writes=["sm2"])
                        P.op("act", lambda e, b1=b1, col=col: e.activation(out=o1, in_=b1[:, col:col + 128], func=AF.Identity, scale=sm[:, 0:1]),
                             reads=[b1k, "sm0"], writes=["o1"])
                        P.op("dve", lambda e, b2=b2, col2=col2: e.scalar_tensor_tensor(
                            out=of, in0=b2[:, col2:col2 + 128], scalar=sm[:, 2:3], in1=o1, op0=ALU.mult, op1=ALU.add),
                            reads=[b2k, "sm2", "o1"], writes=["of"])
                        P.op("act", lambda e: e.activation(out=o1, in_=of, func=AF.Square, accum_out=sm[:, 3:4]),
                             reads=["of", "o1"], writes=["o1", "sm3"])
                        P.op("act", lambda e: e.activation(out=sm[:, 4:5], in_=sm[:, 3:4], func=AF.Ln, scale=1.0 / 128, bias=EPS),
                             reads=["sm3"], writes=["sm4"])
                        P.op("act", lambda e: e.activation(out=sm[:, 4:5], in_=sm[:, 4:5], func=AF.Exp, scale=-0.5),
                             reads=["sm4"], writes=["sm4"])
                        P.op("dve", lambda e, yb=yb: e.scalar_tensor_tensor(
                            out=yb, in0=of, scalar=sm[:, 4:5], in1=gsub, op0=ALU.mult, op1=ALU.mult),
                            reads=["of", "sm4", "gsub"], writes=[("ybt", qb)])

                    def fin_b(h=h, g=g):
                        def tr(e):
                            for qb in range(4):
                                ins = e.transpose(out=psb[:, qb * 128:(qb + 1) * 128], in_=ybt[qb], identity=identB[:])
                            return ins
                        P.op("pe", tr, reads=[("ybt", qb) for qb in range(4)] + ["identB"], writes=["psb"])
                        P.op("act", lambda e: e.activation(out=ybst, in_=psb[:, 0:TT], func=AF.Identity), reads=["psb"], writes=["ybst"])
                        P.dma("sp", lambda e: e.dma_start(out=S["ybT"][h * 128:(h + 1) * 128, g * TT:(g + 1) * TT], in_=ybst),
                              reads=["ybst"], writes=[("ybTs", h, g)])
                    finb.append(fin_b)
                    since = 0
                while finb:
                    finb.pop(0)()

        def ph_conv(pv, pvk):
            fresh()
            ubb = A.bf(4, 32 + T)
            dg = A.bf(124, 128)
            ycf = A.f32(4, T)
            ycb = A.bf(4, T)
            sqf = A.f32(4, TT)
            mf = A.f32(TT)
            vf = A.f32(TT)
            tf = [A.f32(TT) for _ in range(2)]
            HW_ = (32 + T) // 2
            for cc in range(4):
                for hh in range(2):
                    P.dma("pool", lambda e, cc=cc, hh=hh: e.dma_start(out=ubb[:, cc, hh * HW_:(hh + 1) * HW_],
                                                                      in_=S["uT"][:, cc, hh * HW_:(hh + 1) * HW_]),
                          reads=["uTs"] + [("uTs", cc, tt) for tt in range(NT)], writes=[("ubb", cc)])
            for idx in range(124):
                P.op("dve", lambda e, idx=idx: e.tensor_scalar(out=dg[:, idx, :], in0=identB[:], scalar1=pv[:, PV_CW + idx:PV_CW + idx + 1],
                                                               scalar2=None, op0=ALU.mult), reads=["identB", pvk], writes=[("dg", idx // 31)])
            for cc in range(4):
                for tt in range(NT):
                    bk, bkey = pb()

                    def mmc(e, cc=cc, tt=tt, bk=bk):
                        for j in range(31):
                            ins = e.matmul(bk[:, :], lhsT=dg[:, cc * 31 + j, :], rhs=ubb[:, cc, 2 + j + tt * TT:2 + j + (tt + 1) * TT],
                                           start=(j == 0), stop=(j == 30))
                        return ins
                    P.op("pe", mmc, reads=[("dg", cc), ("ubb", cc)], writes=[bkey])
                    P.op("act", lambda e, cc=cc, tt=tt, bk=bk: e.activation(
                        out=ycf[:, cc, tt * TT:(tt + 1) * TT], in_=bk[:, :], func=AF.Identity, bias=pv[:, PV_CB + cc:PV_CB + cc + 1]),
                        reads=[bkey, pvk], writes=[("ycf", cc)])
            k = 0
            allycf = [("ycf", cc) for cc in range(4)]
            for tt in range(NT):
                sl = slice(tt * TT, (tt + 1) * TT)
                P.op("act", lambda e, sl=sl: e.activation(out=sqf, in_=ycf[:, :, sl], func=AF.Square), reads=allycf, writes=["sqf"])
                b1, b1k = pb()
                b2, b2k = pb()

                def mm1(e, sl=sl, b1=b1):
                    for cc in range(4):
                        ins = e.matmul(b1[:, :], lhsT=onesF[:], rhs=ycf[:, cc, sl], start=(cc == 0), stop=(cc == 3))
                    return ins

                def mm2(e, b2=b2):
                    for cc in range(4):
                        ins = e.matmul(b2[:, :], lhsT=onesF[:], rhs=sqf[:, cc, :], start=(cc == 0), stop=(cc == 3))
                    return ins
                P.op("pe", mm1, reads=allycf + ["onesF"], writes=[b1k])
                P.op("pe", mm2, reads=["sqf", "onesF"], writes=[b2k])
                P.op("dve", lambda e, b1=b1: e.tensor_scalar(out=mf, in0=b1[:, :], scalar1=1.0 / 512, scalar2=None, op0=ALU.mult),
                     reads=[b1k], writes=["mf"])
                P.op("dve", lambda e: e.tensor_tensor(out=vf, in0=mf, in1=mf, op=ALU.mult), reads=["mf"], writes=["vf"])
                P.op("dve", lambda e, b2=b2: e.scalar_tensor_tensor(out=vf, in0=b2[:, :], scalar=1.0 / 512, in1=vf,
                                                                    op0=ALU.mult, op1=ALU.subtract),
                     reads=[b2k, "vf"], writes=["vf"])
                P.op("act", lambda e: e.activation(out=vf, in_=vf, func=AF.Ln, bias=EPS), reads=["vf"], writes=["vf"])
                P.op("act", lambda e: e.activation(out=vf, in_=vf, func=AF.Exp, scale=-0.5), reads=["vf"], writes=["vf"])
                for cc in range(4):
                    t_ = tf[k % 2]
                    tk = ("ctf", k % 2)
                    k += 1
                    P.op("dve", lambda e, cc=cc, sl=sl, t_=t_: e.tensor_tensor(out=t_, in0=ycf[:, cc, sl], in1=mf, op=ALU.subtract),
                         reads=allycf + ["mf"], writes=[tk])
                    P.op("dve", lambda e, t_=t_: e.tensor_tensor(out=t_, in0=t_, in1=vf, op=ALU.mult), reads=[tk, "vf"], writes=[tk])
                    P.op("act", lambda e, cc=cc, sl=sl, t_=t_: e.activation(
                        out=ycb[:, cc, sl], in_=t_, func=AF.Silu, scale=pv[:, PV_LG + cc:PV_LG + cc + 1],
                        bias=pv[:, PV_LB + cc:PV_LB + cc + 1]), reads=[tk, pvk], writes=[("ycb", cc)])
            for cc in range(4):
                P.dma("sp", lambda e, cc=cc: e.dma_start(out=S["ycT"][:, cc, :], in_=ycb[:, cc, :]), reads=[("ycb", cc)], writes=[("ycTs", cc)])

        def ph_merge(pv, pvk, gat, gatk, w_gate, w_br, w_out):
            fresh()
            wo = A.bf(8, D)
            hh = A.bf(8, 1024)
            yy = A.bf(3, 4, 1024)
            zT = A.bf(8, 1024)
            wg = [A.bf(3, 8, 128) for _ in range(3)]
            wb = [A.bf(3, 4, 128) for _ in range(3)]
            gsb = [A.f32(TT) for _ in range(2)]
            zacc = [A.f32(TT) for _ in range(2)]
            prod = [A.f32(TT) for _ in range(2)]
            P.dma("pool", lambda e: e.dma_start(out=wo, in_=w_out.rearrange("(c p) n -> p c n", p=128)), writes=["wo"])
            ysrc = [S["yaT"].rearrange("(c p) t -> p c t", p=128), S["ybT"].rearrange("(c p) t -> p c t", p=128), S["ycT"]]
            ykeys = [[("yaTs", hp, h) for hp in range(4) for h in range(2)],
                     [("ybTs", h, g) for h in range(4) for g in range(4)],
                     [("ycTs", cc) for cc in range(4)]]
            cnt = 0
            k = 0
            for half in range(2):
                hs = slice(half * 1024, (half + 1) * 1024)
                P.dma("sp", lambda e, hs=hs: e.dma_start(out=hh, in_=S["h2T"][:, :, hs]),
                      reads=[("h2Ts", c) for c in range(DC)], writes=["hh"])
                for i in range(3):
                    P.dma("sp", lambda e, i=i, hs=hs: e.dma_start(out=yy[:, i, :, :], in_=ysrc[i][:, :, hs]),
                          reads=ykeys[i], writes=[("yy", i)])
                for j in range(DC):
                    s = cnt % 3
                    cnt += 1
                    for i in range(3):
                        col = i * 1024 + j * 128
                        P.dma("pool", lambda e, s=s, i=i, col=col: e.dma_start(
                            out=wg[s][:, i, :, :], in_=w_gate[:, col:col + 128].rearrange("(c p) n -> p c n", p=128)),
                            writes=[("wg", s, i)])
                        P.dma("pool", lambda e, s=s, i=i, j=j: e.dma_start(
                            out=wb[s][:, i, :, :], in_=w_br[i * 512:(i + 1) * 512, j * 128:(j + 1) * 128].rearrange("(c p) n -> p c n", p=128)),
                            writes=[("wb", s, i)])
                    for t2 in range(2):
                        sl = slice(t2 * TT, (t2 + 1) * TT)
                        za = zacc[(2 * j + t2) % 2]
                        zk = ("zacc", (2 * j + t2) % 2)
                        for i in range(3):
                            u = k % 2
                            k += 1
                            bg, bgk = pb()
                            by, byk = pb()

                            def mg(e, s=s, i=i, sl=sl, bg=bg):
                                for kc in range(DC):
                                    ins = e.matmul(bg[:, :], lhsT=wg[s][:, i, kc, :], rhs=hh[:, kc, sl], start=(kc == 0), stop=(kc == DC - 1))
                                return ins

                            def my(e, s=s, i=i, sl=sl, by=by):
                                for kc in range(4):
                                    ins = e.matmul(by[:, :], lhsT=wb[s][:, i, kc, :], rhs=yy[:, i, kc, sl], start=(kc == 0), stop=(kc == 3))
                                return ins
                            P.op("pe", mg, reads=[("wg", s, i), "hh"], writes=[bgk])
                            P.op("pe", my, reads=[("wb", s, i), ("yy", i)], writes=[byk])
                            P.op("act", lambda e, u=u, bg=bg, i=i, j=j: e.activation(
                                out=gsb[u], in_=bg[:, :], func=AF.Sigmoid, bias=pv[:, PV_BG + i * 8 + j:PV_BG + i * 8 + j + 1]),
                                reads=[bgk, pvk], writes=[("gsb", u)])
                            if i == 0:
                                P.op("dve", lambda e, u=u, by=by, za=za: e.tensor_tensor(out=za, in0=by[:, :], in1=gsb[u], op=ALU.mult),
                                     reads=[byk, ("gsb", u)], writes=[zk])
                            else:
                                P.op("dve", lambda e, u=u, by=by: e.tensor_tensor(out=prod[u], in0=by[:, :], in1=gsb[u], op=ALU.mult),
                                     reads=[byk, ("gsb", u)], writes=[("prod", u)])
                                if i == 1:
                                    P.op("pool", lambda e, u=u, za=za: e.tensor_tensor(out=za, in0=za, in1=prod[u], op=ALU.add),
                                         reads=[zk, ("prod", u)], writes=[zk])
                                else:
                                    P.op("pool", lambda e, u=u, za=za, j=j, sl=sl: e.tensor_tensor(out=zT[:, j, sl], in0=za, in1=prod[u], op=ALU.add),
                                         reads=[zk, ("prod", u)], writes=[("zT", t2)])
                for dj in range(DC):
                    for t2 in range(2):
                        sl = slice(t2 * TT, (t2 + 1) * TT)
                        xs = slice(half * 1024 + t2 * TT, half * 1024 + (t2 + 1) * TT)
                        tt = half * 2 + t2
                        bd, bdk = pb()

                        def mo(e, dj=dj, sl=sl, bd=bd):
                            for kc in range(DC):
                                ins = e.matmul(bd[:, :], lhsT=wo[:, kc, dj * 128:(dj + 1) * 128], rhs=zT[:, kc, sl], start=(kc == 0), stop=(kc == DC - 1))
                            return ins
                        P.op("pe", mo, reads=["wo", ("zT", t2)], writes=[bdk])
                        P.op("dve", lambda e, dj=dj, xs=xs, bd=bd: e.scalar_tensor_tensor(
                            out=xT[:, dj, xs], in0=bd[:, :], scalar=gat[:, 1, dj:dj + 1], in1=xT[:, dj, xs], op0=ALU.mult, op1=ALU.add),
                            reads=[bdk, gatk, ("xT", tt, dj // 4)], writes=[("xT", tt, dj // 4)])

        def ph_out():
            fresh()
            ot = [A.f32(D) for _ in range(2)]
            for i in range(16):
                s = i % 2
                for hb in range(2):
                    bk, bkey = pb()

                    def tr(e, i=i, hb=hb, bk=bk):
                        for c4 in range(4):
                            c = hb * 4 + c4
                            ins = e.transpose(out=bk[:, c4 * 128:(c4 + 1) * 128], in_=xT[:, c, i * 128:(i + 1) * 128], identity=identF[:])
                        return ins
                    P.op("pe", tr, reads=[("xT", i // 4, hb), "identF"], writes=[bkey])
                    if hb == 0:
                        P.op("dve", lambda e, s=s, bk=bk: e.tensor_copy(out=ot[s][:, 0:512], in_=bk[:, :]), reads=[bkey], writes=[("ot", s, 0)])
                    else:
                        P.op("act", lambda e, s=s, bk=bk: e.activation(out=ot[s][:, 512:1024], in_=bk[:, :], func=AF.Identity),
                             reads=[bkey], writes=[("ot", s, 1)])
                P.dma("sp", lambda e, i=i, s=s: e.dma_start(out=O["out"][i * 128:(i + 1) * 128, :], in_=ot[s]),
                      reads=[("ot", s, 0), ("ot", s, 1)], writes=[("outd", i)])

        finals = []
        if stage == 1:
            ph_load_x()
        else:
            ph_load_xT()
            derive(pvA, "pvA", modA, "modA", sclA, gatA, "A")
            s2 = _DBG.get("s2", 9)
            ph_qu(pvA, "pvA", modA, "modA", sclA, "sclA", I["w_inA"])
            lam_init = ph_lambda(la)
            if s2 >= 2:
                ph_dil(pvA, "pvA")
            if s2 >= 3:
                ph_diff(pvA, "pvA", lam_init)
            if s2 >= 4:
                ph_conv(pvA, "pvA")
            if s2 >= 5:
                ph_merge(pvA, "pvA", gatA, "gatA", I["w_gateA"], I["w_brA"], I["w_outA"])
            if s2 >= 6:
                ph_ffn_full(2, pvA, "pvA", modA, "modA", sclA, "sclA", gatA, "gatA", I["w_fiA"], I["w_foA"])
        if lb is not None:
            up = _DBG.get("upto", 9) if _DBG.get("s2", 9) >= 7 else 0
            if up >= 1:
                ph_adaln(pvB, "pvB", I["w_adaB"], modB, "modB")
                derive(pvB, "pvB", modB, "modB", sclB, gatB, "B")
            if up >= 2:
                ph_ffn_full(0, pvB, "pvB", modB, "modB", sclB, "sclB", gatB, "gatB", I["w_fiB"], I["w_foB"])
            if up >= 3:
                ph_kv(pvB, "pvB", modB, "modB", sclB, "sclB", I["w_inB"])
            ph_store_x()
            P.dma("sp", lambda e: e.dma_start(out=O["modB"][:, :], in_=modB[:]), reads=["modB"], writes=["modBout"])
        else:
            ph_out()
        P.barrier()
        P.op("pool", lambda e: e.memset(neglam[:, 3:4], 0.0), writes=["__tail"])
        P.emit(["__tail"])
    return nc


_CACHE = {}
_DBG = {}


def _prog(stage):
    if stage not in _CACHE:
        _CACHE[stage] = build(stage)
    return _CACHE[stage]


def _mixer_inputs(inp, l, prev_res, dbias, fbias):
    maps = []
    for c in range(NCORES):
        own = prev_res[c]
        prv = prev_res[c - 1] if c % 2 == 1 else prev_res[c]
        maps.append({
            "xT_in": own["xT_out"], "modA": own["modB"], "pvA": _host_pv(inp, l, c),
            "w_inA": inp["w_in"][l], "w_gateA": inp["w_gate"][l], "w_brA": inp["w_branch"][l].reshape(1536, D),
            "w_outA": inp["w_out"][l], "w_fiA": inp["w_ffn_in"][l, 1], "w_foA": inp["w_ffn_out"][l, 1],
            "lamA": inp["lambda_vec"][l].reshape(1, 256), "subgA": inp["subln_g"][l].reshape(1, 128),
            "dbias": dbias, "fbias": fbias, "ut_p": prv["ut"],
            "kaT_o": own["kaT"], "kbT_o": own["kbT"], "va_o": own["va"], "vb_o": own["vb"],
            "kaT_p": prv["kaT"], "kbT_p": prv["kbT"], "va_p": prv["va"], "vb_p": prv["vb"],
        })
    return maps


def _ffn_kv_inputs(inp, l, c):
    return {"pvB": _host_pv(inp, l, c), "w_adaB": inp["w_ada"][l], "w_fiB": inp["w_ffn_in"][l, 0],
            "w_foB": inp["w_ffn_out"][l, 0], "w_inB": inp["w_in"][l]}


def kernel(**inputs):
    inp = {k: np.ascontiguousarray(np.asarray(v, dtype=np.float32)) for k, v in inputs.items()}
    dbias, fbias = _host_bias_tables(inp["rel_bias"])
    cores = list(range(NCORES))
    m1 = []
    for c in cores:
        b, hf = c // 2, c % 2
        d = {"x": np.ascontiguousarray(inp["x"][b, hf * T:(hf + 1) * T, :])}
        d.update(_ffn_kv_inputs(inp, 0, c))
        m1.append(d)
    r1 = run_bass_kernel_spmd(_prog(1), m1, core_ids=cores).results
    m2 = _mixer_inputs(inp, 0, r1, dbias, fbias)
    for c in cores:
        m2[c].update(_ffn_kv_inputs(inp, 1, c))
    r2 = run_bass_kernel_spmd(_prog(2), m2, core_ids=cores).results
    m3 = _mixer_inputs(inp, 1, r2, dbias, fbias)
    r3 = run_bass_kernel_spmd(_prog(3), m3, core_ids=cores).results
    out = np.empty((4, 2 * T, D), np.float32)
    for c in cores:
        out[c // 2, (c % 2) * T:(c % 2 + 1) * T, :] = r3[c]["out"]
    return out
```
